# Optimizing a Trainium2 kernel written in Bass

```python
import math
import jax
import jax.numpy as jnp
from jax import lax
import numpy as np

D_MODEL = 1024
BATCH = 4
SEQ = 4096
DEPTH = 2

HEAD_DIM = 64
NSA_HEADS = 8
NSA_KV_GROUPS = 2
SWA_HEADS = 8
SWA_KV_GROUPS = 2
FOX_HEADS = 8
MIX_WIDTH = 8 * HEAD_DIM
N_BRANCH = 3
Q_BLOCK = 128
CMP_LEN = 32
CMP_STRIDE = 16
CMP_HIDDEN = 256
SLC_LEN = 64
SLC_TOPK = 8
NSA_WINDOW = 256
SWA_WINDOW = 128
N_BUCKETS = 32
MAX_DISTANCE = 128
D_FF = 256 * ((8 * D_MODEL // 3 + 255) // 256)
CONV_WIDTH = 3
RMS_EPS = 1e-6
NEG_INF = -1e30
FORCE_SCORE = 1e9
N_QK_GAINS = 6

KV_A = NSA_KV_GROUPS * HEAD_DIM
KV_B = SWA_KV_GROUPS * HEAD_DIM
IN_SPLITS = (
    ('a_q', NSA_HEADS * HEAD_DIM), ('a_kc', KV_A), ('a_vc', KV_A), ('a_ks', KV_A), ('a_vs', KV_A),
    ('a_kw', KV_A), ('a_vw', KV_A), ('a_gate', NSA_HEADS * 3),
    ('b_q', SWA_HEADS * HEAD_DIM), ('b_k', KV_B), ('b_v', KV_B),
    ('c_q', FOX_HEADS * HEAD_DIM), ('c_k', FOX_HEADS * HEAD_DIM), ('c_v', FOX_HEADS * HEAD_DIM),
    ('c_f', FOX_HEADS),
    ('merge', N_BRANCH * D_MODEL),
)
IN_WIDTH = sum(w for _, w in IN_SPLITS)

kernel_name = 'hybrid_nsa_swa_fox_convffn'


def rms_norm(x, g):
    xf = x.astype(jnp.float32)
    return (xf * lax.rsqrt(jnp.mean(xf * xf, axis=-1, keepdims=True) + RMS_EPS)).astype(x.dtype) * g


def split_columns(h):
    cols, off = {}, 0
    for name, w in IN_SPLITS:
        cols[name] = h[..., off:off + w]
        off += w
    return cols


def to_heads(z, n_heads):
    b, t, _ = z.shape
    return z.reshape(b, t, n_heads, HEAD_DIM).transpose(0, 2, 1, 3)


def from_heads(z):
    b, h, t, d = z.shape
    return z.transpose(0, 2, 1, 3).reshape(b, t, h * d)


def t5_bucket(dist):
    max_exact = N_BUCKETS // 2
    d = jnp.maximum(dist, 0)
    log_ratio = jnp.log(jnp.maximum(d, 1).astype(jnp.float32) / max_exact) / math.log(MAX_DISTANCE / max_exact)
    large = jnp.minimum(max_exact + (log_ratio * (N_BUCKETS - max_exact)).astype(jnp.int32), N_BUCKETS - 1)
    return jnp.where(d < max_exact, d, large)


def banded_kv(z, n_prev):
    b, g, t, d = z.shape
    nb = t // Q_BLOCK
    zb = jnp.pad(z.reshape(b, g, nb, Q_BLOCK, d), ((0, 0), (0, 0), (n_prev, 0), (0, 0), (0, 0)))
    return jnp.concatenate([zb[:, :, j:j + nb] for j in range(n_prev + 1)], axis=3)


def banded_attention(q, k, v, window, tab, sinks):
    b, g, r, t, d = q.shape
    nb = t // Q_BLOCK
    n_prev = -(-(window - 1) // Q_BLOCK)
    n_keys = (n_prev + 1) * Q_BLOCK
    kb, vb = banded_kv(k, n_prev), banded_kv(v, n_prev)
    qb = q.reshape(b, g, r, nb, Q_BLOCK, d)
    qi = jnp.arange(Q_BLOCK)[:, None]
    ki = jnp.arange(n_keys)[None, :]
    dist = n_prev * Q_BLOCK + qi - ki
    s_pos = (jnp.arange(nb)[:, None, None] - n_prev) * Q_BLOCK + ki[None]
    valid = (dist >= 0) & (dist < window) & (s_pos >= 0)
    bias = jnp.transpose(tab[t5_bucket(dist)], (2, 3, 0, 1))[:, :, None]
    logits = jnp.einsum('bgrnqd,bgnkd->bgrnqk', qb, kb).astype(jnp.float32) * (HEAD_DIM ** -0.5) + bias
    logits = jnp.where(valid, logits, NEG_INF)
    if sinks is None:
        probs = jax.nn.softmax(logits, axis=-1)
    else:
        sink_col = jnp.broadcast_to(sinks.astype(jnp.float32).reshape(g, r, 1, 1, 1), logits.shape[:-1] + (1,))
        probs = jax.nn.softmax(jnp.concatenate([logits, sink_col], axis=-1), axis=-1)[..., :-1]
    out = jnp.einsum('bgrnqk,bgnkd->bgrnqd', probs.astype(v.dtype), vb)
    return out.reshape(b, g, r, t, d)


def nsa_mixer(cols, q_g, k_g, cmp_pos, cmp_w1, cmp_w2, tab_flat):
    b, t, _ = cols['a_q'].shape
    G, R = NSA_KV_GROUPS, NSA_HEADS // NSA_KV_GROUPS
    scale = HEAD_DIM ** -0.5
    tab = tab_flat.reshape(N_BUCKETS, G, R)
    q = rms_norm(to_heads(cols['a_q'], NSA_HEADS), q_g).reshape(b, G, R, t, HEAD_DIM)
    kc, vc = to_heads(cols['a_kc'], G), to_heads(cols['a_vc'], G)
    ks, vs = rms_norm(to_heads(cols['a_ks'], G), k_g), to_heads(cols['a_vs'], G)
    kw, vw = rms_norm(to_heads(cols['a_kw'], G), k_g), to_heads(cols['a_vw'], G)
    pos_t = jnp.arange(t)

    n_cmp = (t - CMP_LEN) // CMP_STRIDE + 1
    win_idx = jnp.arange(n_cmp)[:, None] * CMP_STRIDE + jnp.arange(CMP_LEN)[None, :]

    def compress(z, which):
        blk = z[:, :, win_idx] + cmp_pos[which]
        hid = jax.nn.gelu(blk.reshape(b, G, n_cmp, CMP_LEN * HEAD_DIM) @ cmp_w1[which])
        return hid @ cmp_w2[which]

    k_cmp = rms_norm(compress(kc, 0), k_g)
    v_cmp = compress(vc, 1)
    blk_end = jnp.arange(n_cmp) * CMP_STRIDE + CMP_LEN - 1
    dist_c = pos_t[:, None] - blk_end[None, :]
    valid_c = dist_c >= 0
    bias_c = jnp.transpose(tab[t5_bucket(dist_c)], (2, 3, 0, 1))
    logits_c = jnp.einsum('bgrtd,bgnd->bgrtn', q, k_cmp).astype(jnp.float32) * scale + bias_c
    logits_c = jnp.where(valid_c, logits_c, NEG_INF)
    p_cmp = jax.nn.softmax(logits_c, axis=-1) * jnp.any(valid_c, axis=-1)[:, None]
    o_cmp = jnp.einsum('bgrtn,bgnd->bgrtd', p_cmp.astype(v_cmp.dtype), v_cmp)

    n_slc = t // SLC_LEN
    n_top = min(SLC_TOPK, n_slc)
    c_start = jnp.arange(n_cmp) * CMP_STRIDE
    s_start = jnp.arange(n_slc) * SLC_LEN
    overlap = ((c_start[:, None] < s_start[None, :] + SLC_LEN)
               & (c_start[:, None] + CMP_LEN > s_start[None, :])).astype(jnp.float32)
    importance = jnp.einsum('bgtn,nj->bgtj', p_cmp.sum(axis=2), overlap)
    cur = pos_t // SLC_LEN
    jblk = jnp.arange(n_slc)[None, :]
    valid_s = jblk <= cur[:, None]
    forced = (jblk == 0) | (jblk == cur[:, None]) | (jblk == cur[:, None] - 1)
    score = jnp.where(valid_s, jnp.where(forced, FORCE_SCORE, importance), NEG_INF)
    top_score, top_idx = lax.top_k(score, n_top)
    top_ok = top_score > NEG_INF / 2

    nb = t // Q_BLOCK
    ks_blk = ks.reshape(b, G, n_slc, SLC_LEN, HEAD_DIM)
    vs_blk = vs.reshape(b, G, n_slc, SLC_LEN, HEAD_DIM)
    q_b = jnp.moveaxis(q.reshape(b, G, R, nb, Q_BLOCK, HEAD_DIM), 3, 0)
    idx_b = jnp.moveaxis(top_idx.reshape(b, G, nb, Q_BLOCK, n_top), 2, 0)
    ok_b = jnp.moveaxis(top_ok.reshape(b, G, nb, Q_BLOCK, n_top), 2, 0)
    gather = jax.vmap(jax.vmap(lambda blocks, ix: blocks[ix]))
    bias_lookup = jax.vmap(lambda tb, bk: tb[bk], in_axes=(1, 1), out_axes=1)
    n_sel = n_top * SLC_LEN

    def selected_block(args):
        i, qi, ix, ok = args
        kg = gather(ks_blk, ix).reshape(b, G, Q_BLOCK, n_sel, HEAD_DIM)
        vg = gather(vs_blk, ix).reshape(b, G, Q_BLOCK, n_sel, HEAD_DIM)
        t_pos = i * Q_BLOCK + jnp.arange(Q_BLOCK)
        s_pos = (ix[..., None] * SLC_LEN + jnp.arange(SLC_LEN)).reshape(b, G, Q_BLOCK, n_sel)
        dist = t_pos[:, None] - s_pos
        valid = (dist >= 0) & jnp.repeat(ok, SLC_LEN, axis=-1)
        bias = jnp.moveaxis(bias_lookup(tab, t5_bucket(dist)), -1, 2)
        logits = jnp.einsum('bgrqd,bgqkd->bgrqk', qi, kg).astype(jnp.float32) * scale + bias
        logits = jnp.where(valid[:, :, None], logits, NEG_INF)
        probs = jax.nn.softmax(logits, axis=-1)
        return jnp.einsum('bgrqk,bgqkd->bgrqd', probs.astype(vg.dtype), vg)

    o_slc = lax.map(selected_block, (jnp.arange(nb), q_b, idx_b, ok_b))
    o_slc = jnp.moveaxis(o_slc, 0, 3).reshape(b, G, R, t, HEAD_DIM)

    o_win = banded_attention(q, kw, vw, NSA_WINDOW, tab, None)

    gate = jax.nn.sigmoid(cols['a_gate']).reshape(b, t, NSA_HEADS, 3).transpose(0, 2, 1, 3)
    gate = gate.reshape(b, G, R, t, 3)
    o = gate[..., 0:1] * o_cmp + gate[..., 1:2] * o_slc + gate[..., 2:3] * o_win
    return from_heads(o.reshape(b, NSA_HEADS, t, HEAD_DIM))


def swa_mixer(cols, q_g, k_g, sinks, tab_flat):
    b, t, _ = cols['b_q'].shape
    G, R = SWA_KV_GROUPS, SWA_HEADS // SWA_KV_GROUPS
    q = rms_norm(to_heads(cols['b_q'], SWA_HEADS), q_g).reshape(b, G, R, t, HEAD_DIM)
    k = rms_norm(to_heads(cols['b_k'], G), k_g)
    v = to_heads(cols['b_v'], G)
    o = banded_attention(q, k, v, SWA_WINDOW, tab_flat.reshape(N_BUCKETS, G, R), sinks.reshape(G, R))
    return from_heads(o.reshape(b, SWA_HEADS, t, HEAD_DIM))


def fox_mixer(cols, q_g, k_g, forget_bias):
    b, t, _ = cols['c_q'].shape
    q = rms_norm(to_heads(cols['c_q'], FOX_HEADS), q_g)
    k = rms_norm(to_heads(cols['c_k'], FOX_HEADS), k_g)
    v = to_heads(cols['c_v'], FOX_HEADS)
    log_f = jax.nn.log_sigmoid((cols['c_f'] + forget_bias).astype(jnp.float32))
    c = jnp.cumsum(log_f, axis=1).transpose(0, 2, 1)
    nb = t // Q_BLOCK
    q_b = jnp.moveaxis(q.reshape(b, FOX_HEADS, nb, Q_BLOCK, HEAD_DIM), 2, 0)
    c_b = jnp.moveaxis(c.reshape(b, FOX_HEADS, nb, Q_BLOCK), 2, 0)
    key_pos = jnp.arange(t)

    def query_block(args):
        i, qi, ci = args
        t_pos = i * Q_BLOCK + jnp.arange(Q_BLOCK)
        logits = (jnp.einsum('bhqd,bhsd->bhqs', qi, k).astype(jnp.float32) * (HEAD_DIM ** -0.5)
                  + ci[..., None] - c[:, :, None, :])
        logits = jnp.where(t_pos[:, None] >= key_pos[None, :], logits, NEG_INF)
        probs = jax.nn.softmax(logits, axis=-1)
        return jnp.einsum('bhqs,bhsd->bhqd', probs.astype(v.dtype), v)

    o = lax.map(query_block, (jnp.arange(nb), q_b, c_b))
    o = jnp.moveaxis(o, 0, 2).reshape(b, FOX_HEADS, t, HEAD_DIM)
    return from_heads(o)


def conv_ffn(h, w_gate, w_up, conv_w, conv_b, w_down):
    a = h @ w_gate
    a = lax.conv_general_dilated(a, conv_w[:, None, :], window_strides=(1,),
                                 padding=[(CONV_WIDTH - 1, 0)],
                                 dimension_numbers=('NWC', 'WIO', 'NWC'),
                                 feature_group_count=a.shape[-1]) + conv_b
    return (jax.nn.gelu(a) * (h @ w_up)) @ w_down


def setup_inputs(seed: int = 0) -> dict:
    key = jax.random.key(seed)
    ks = jax.random.split(key, 19)
    L, D, F, d = DEPTH, D_MODEL, D_FF, HEAD_DIM
    nrm = jax.random.normal
    return {
        'x': nrm(ks[0], (BATCH, SEQ, D), jnp.float32),
        'rel_bias': 0.5 * nrm(ks[1], (N_BUCKETS, NSA_HEADS + SWA_HEADS), jnp.float32),
        'norm_mix': 1.0 + 0.05 * nrm(ks[2], (L, D), jnp.float32),
        'norm_ffn': 1.0 + 0.05 * nrm(ks[3], (L, D), jnp.float32),
        'w_in': nrm(ks[4], (L, D, IN_WIDTH), jnp.float32) * D ** -0.5,
        'forget_bias': 4.0 + 0.5 * nrm(ks[5], (L, FOX_HEADS), jnp.float32),
        'qk_gain': 1.0 + 0.05 * nrm(ks[6], (L, N_QK_GAINS, d), jnp.float32),
        'cmp_pos': 0.1 * nrm(ks[7], (L, 2, CMP_LEN, d), jnp.float32),
        'cmp_w1': nrm(ks[8], (L, 2, CMP_LEN * d, CMP_HIDDEN), jnp.float32) * (CMP_LEN * d) ** -0.5,
        'cmp_w2': nrm(ks[9], (L, 2, CMP_HIDDEN, d), jnp.float32) * CMP_HIDDEN ** -0.5,
        'sinks': nrm(ks[10], (L, SWA_HEADS), jnp.float32),
        'w_branch': nrm(ks[11], (L, N_BRANCH, MIX_WIDTH, D), jnp.float32) * MIX_WIDTH ** -0.5,
        'w_out': nrm(ks[12], (L, D, D), jnp.float32) * D ** -0.5,
        'w_gate': nrm(ks[13], (L, D, F), jnp.float32) * D ** -0.5,
        'w_up': nrm(ks[14], (L, D, F), jnp.float32) * D ** -0.5,
        'conv_w': nrm(ks[15], (L, CONV_WIDTH, F), jnp.float32) * CONV_WIDTH ** -0.5,
        'conv_b': 0.02 * nrm(ks[16], (L, F), jnp.float32),
        'w_down': nrm(ks[17], (L, F, D), jnp.float32) * F ** -0.5,
    }


def reference(x, rel_bias, norm_mix, norm_ffn, w_in, forget_bias, qk_gain, cmp_pos, cmp_w1, cmp_w2,
              sinks, w_branch, w_out, w_gate, w_up, conv_w, conv_b, w_down):
    b, t, _ = x.shape
    for l in range(DEPTH):
        h = rms_norm(x, norm_mix[l])
        cols = split_columns(h @ w_in[l])
        o_a = nsa_mixer(cols, qk_gain[l, 0], qk_gain[l, 1], cmp_pos[l], cmp_w1[l], cmp_w2[l],
                        rel_bias[:, :NSA_HEADS])
        o_b = swa_mixer(cols, qk_gain[l, 2], qk_gain[l, 3], sinks[l], rel_bias[:, NSA_HEADS:])
        o_c = fox_mixer(cols, qk_gain[l, 4], qk_gain[l, 5], forget_bias[l])
        branches = jnp.stack([o_a, o_b, o_c], axis=2)
        proj = jnp.einsum('btnm,nmd->btnd', branches, w_branch[l])
        gates = jax.nn.sigmoid(cols['merge']).reshape(b, t, N_BRANCH, D_MODEL)
        x = x + (gates * proj).sum(axis=2) @ w_out[l]
        x = x + conv_ffn(rms_norm(x, norm_ffn[l]), w_gate[l], w_up[l], conv_w[l], conv_b[l], w_down[l])
    return x
```

```python
import math
import numpy as np
import ml_dtypes
from contextlib import ExitStack
import concourse.bass as bass
import concourse.mybir as mybir
from concourse.bass_utils import run_bass_kernel_spmd

F32 = mybir.dt.float32
BF16 = mybir.dt.bfloat16
AF = mybir.ActivationFunctionType
ALU = mybir.AluOpType
AX = mybir.AxisListType
NPBF = ml_dtypes.bfloat16


import types


def _freeze(fn):
    if fn.__closure__ is None:
        return fn
    cells = []
    for c_ in fn.__closure__:
        try:
            cells.append(types.CellType(c_.cell_contents))
        except ValueError:
            cells.append(c_)
    return types.FunctionType(fn.__code__, fn.__globals__, fn.__name__, fn.__defaults__, tuple(cells))


class Buf:
    __slots__ = ("name", "w", "r", "dsem", "excl")

    REG = []

    def __init__(self, name, excl=False):
        self.name = name
        self.excl = excl
        Buf.REG.append(self)
        self.w = []
        self.r = []
        self.dsem = None


class Prog:
    ENG = ("pe", "act", "dve", "pool", "sp")

    def __init__(self, nc, es):
        self.nc = nc
        self.es = es
        self.q = {e: [] for e in self.ENG}
        self.csem = {}
        for e in ("pe", "act", "dve", "pool"):
            self.csem[e] = es.enter_context(nc.semaphore("c_" + e))
        self.cnt = {e: 0 for e in self.ENG}
        self.waited = {e: {} for e in self.ENG}
        self.dsems = []
        self.nbuf = 0
        self.same_engine_sync = True
        self.es_alloc = None
        self.free_d = {"hw": [], "sw": [], "cc": []}
        self.scope_sems = [[]]
        self.rank = {}
        self.need_rank = False
        Buf.REG.clear()

    def sbuf(self, name, shape, dtype):
        self.nalloc = getattr(self, "nalloc", 0) + 1
        t = (self.es_alloc or self.es).enter_context(self.nc.sbuf_tensor("%s_%d" % (name, self.nalloc), list(shape), dtype))
        return t, Buf(name)

    def psum(self, name, shape, dtype):
        self.nalloc = getattr(self, "nalloc", 0) + 1
        t = (self.es_alloc or self.es).enter_context(self.nc.psum_tensor("%s_%d" % (name, self.nalloc), list(shape), dtype))
        return t, Buf(name, excl=True)

    def buf(self, name="b"):
        self.nbuf += 1
        return Buf(f"{name}{self.nbuf}")

    def _dsem(self, b, eng):
        kind = "sw" if eng == "pool" else ("cc" if eng == "cc" else "hw")
        if b.dsem is None:
            b.dsem = {}
        if kind not in b.dsem:
            if self.free_d[kind]:
                d = self.free_d[kind].pop()
            else:
                s = self.es.enter_context(self.nc.semaphore("d%s_%d" % (kind, len(self.dsems))))
                d = {"sem": s, "total": 0, "dirty": False, "id": len(self.dsems), "kind": kind}
                self.dsems.append(d)
            b.dsem[kind] = d
            self.scope_sems[-1].append(d)
        return b.dsem[kind]

    def _waits_for(self, eng, reads, writes):
        toks = []
        for b in reads:
            toks += b.w
            if b.excl:
                toks += [t for t in b.r if t[1] != eng]
        for b in writes:
            toks += b.w
            toks += b.r
        waits = []
        wd = self.waited[eng]
        for t in toks:
            if t[0] == "c":
                _, e, v = t
                if e == eng and (eng == "pe" or not self.same_engine_sync):
                    continue
                key = ("c", e)
                if wd.get(key, 0) < v:
                    wd[key] = v
            else:
                _, d, v = t
                v = d["total"]
                d["dirty"] = True
                key = ("d", d["id"])
                if wd.get(key, 0) < v:
                    wd[key] = v
        return wd

    def _collect(self, eng, reads, writes):
        before = dict(self.waited[eng])
        self._waits_for(eng, reads, writes)
        after = self.waited[eng]
        out = []
        for k, v in after.items():
            if before.get(k, 0) < v:
                if k[0] == "c":
                    out.append((self.csem[k[1]], v))
                else:
                    out.append((self.dsems[k[1]]["sem"], v))
        return out

    def op(self, eng, fn, reads=(), writes=()):
        waits = self._collect(eng, reads, writes)
        fn = _freeze(fn)
        self.cnt[eng] += 1
        tok = ("c", eng, self.cnt[eng])
        sem = self.csem[eng]

        def run(e, waits=waits, fn=fn, sem=sem):
            for s, v in waits:
                e.wait_ge(s, v)
            fn(e).then_inc(sem, 1)

        self.q[eng].append(run)
        for b in writes:
            b.w = [tok]
            b.r = []
        for b in reads:
            if b not in writes:
                b.r.append(tok)
        return tok

    def dma(self, eng, out, in_, semof, reads=(), writes=(), accw=(), **kw):
        d = self._dsem(semof, eng)
        waits = self._collect(eng, reads, writes)
        for b in accw:
            if b.r:
                tmpb = Buf("tmp")
                Buf.REG.pop()
                tmpb.w = list(b.r)
                waits += self._collect(eng, [tmpb], ())
        if d["dirty"] and d["total"] > 0:
            key = ("d", d["id"])
            if self.waited[eng].get(key, 0) < d["total"]:
                self.waited[eng][key] = d["total"]
                waits.append((d["sem"], d["total"]))
            d["dirty"] = False
        d["total"] += 16
        tok = ("d", d, d["total"])
        sem = d["sem"]

        def run(e, waits=waits, out=out, in_=in_, sem=sem, kw=kw):
            for s, v in waits:
                e.wait_ge(s, v)
            o_ = out(e) if callable(out) else out
            i_ = in_(e) if callable(in_) else in_
            try:
                e.dma_start(out=o_, in_=i_, **kw).then_inc(sem, 16)
            except Exception:
                print("DMA FAIL out=", o_, "in=", i_)
                raise

        self.q[eng].append(run)
        for b in writes:
            b.w = [tok]
            b.r = []
        for b in accw:
            b.w.append(tok)
        for b in reads:
            if b not in writes:
                b.r.append(tok)
        return tok

    def coll(self, kind, ins, outs, replica_groups, reads=(), writes=()):
        eng = "pool"
        if not hasattr(self, "ccbuf"):
            self.ccbuf = Buf("ccbuf")
            self.scope_sems.append([])
            self._dsem(self.ccbuf, "cc")
            self.scope_sems.pop()
        d = self._dsem(self.ccbuf, "cc")
        waits = self._collect(eng, reads, writes)
        d["total"] += 1
        tok = ("d", d, d["total"])
        sem = d["sem"]

        def run(e, waits=waits, sem=sem):
            for s_, v in waits:
                e.wait_ge(s_, v)
            e.collective_compute(kind, mybir.AluOpType.bypass, replica_groups=replica_groups,
                                 ins=[a.opt() for a in ins], outs=[a.opt() for a in outs]).then_inc(sem)

        self.q[eng].append(run)
        for b_ in writes:
            b_.w = [tok]
            b_.r = []
        for b_ in reads:
            if b_ not in writes:
                b_.r.append(tok)
        return tok

    def barrier(self):
        for eng in self.ENG:
            waits = []
            wd = self.waited[eng]
            for e2 in ("pe", "act", "dve", "pool"):
                if self.cnt[e2] > 0 and wd.get(("c", e2), 0) < self.cnt[e2]:
                    wd[("c", e2)] = self.cnt[e2]
                    waits.append((self.csem[e2], self.cnt[e2]))
            for d in self.dsems:
                key = ("d", d["id"])
                if d["total"] > 0 and wd.get(key, 0) < d["total"]:
                    wd[key] = d["total"]
                    waits.append((d["sem"], d["total"]))
                d["dirty"] = False

            def run(e, waits=waits):
                for s_, v in waits:
                    e.wait_ge(s_, v)

            if waits:
                self.q[eng].append(run)
        for b in Buf.REG:
            b.w = []
            b.r = []

    def scope(self):
        prog = self

        class _Scope:
            def __enter__(self_):
                prog.barrier()
                self_.saved = prog.es_alloc
                self_.es = ExitStack()
                self_.es.__enter__()
                prog.es_alloc = self_.es
                prog.scope_sems.append([])
                return self_

            def __exit__(self_, *a):
                prog.barrier()
                prog.es_alloc = self_.saved
                for d in prog.scope_sems.pop():
                    prog.free_d[d["kind"]].append(d)
                return self_.es.__exit__(*a)

        return _Scope()

    def finish(self, final_bufs):
        waits = self._collect("sp", final_bufs, ())
        for d in self.dsems:
            key = ("d", d["id"])
            if self.waited["sp"].get(key, 0) < d["total"]:
                self.waited["sp"][key] = d["total"]
                waits.append((d["sem"], d["total"]))

        def run(e, waits=waits):
            for s, v in waits:
                e.wait_ge(s, v)

        self.q["sp"].append(run)
        nc = self.nc
        q = self.q
        with nc.Block() as block:
            @block.tensor
            def _(e):
                for f in q["pe"]:
                    f(e)

            @block.scalar
            def _(e):
                for f in q["act"]:
                    f(e)

            @block.vector
            def _(e):
                for f in q["dve"]:
                    f(e)

            @block.gpsimd
            def _(e):
                for f in q["pool"]:
                    f(e)

            @block.sync
            def _(e):
                if self.need_rank:
                    self.rank["sp"] = e.partition_id() % 2
                for f in q["sp"]:
                    f(e)


D = 1024
T = 4096
NB = 4
DEPTH = 2
HD = 64
F_FF = 2816
NFC = 22
EPS = 1e-6
NEG = -30000.0
IN_W = 6688
OFF = dict(a_q=0, a_kc=512, a_vc=640, a_ks=768, a_vs=896, a_kw=1024, a_vw=1152, a_gate=1280,
           b_q=1304, b_k=1816, b_v=1944, c_q=2072, c_k=2584, c_v=3096, c_f=3608, merge=3616)
NFM = 2688
NFMG = 1344
NTM = 896
NTMG = 448
FMR = dict(qa=0, ks=256, kw=320, kb=384, qb=448, qc=704, kcf=960, kc=1216, vc=1280)
TMC = dict(vs=0, vw=64, vb=128, vcf=192)


def _colperm():
    cols = []
    gain_idx = []
    qsc = []
    for g in range(2):
        def heads(base, n, gi, q):
            for h in range(n):
                hh = 4 * g + h if n == 4 else g
                c0 = base + 64 * hh
                cols.extend(range(c0, c0 + 64))
                gain_idx.append(gi)
                qsc.append(0.125 if q else 1.0)
        heads(OFF['a_q'], 4, 0, True)
        heads(OFF['a_ks'], 1, 1, False)
        heads(OFF['a_kw'], 1, 1, False)
        heads(OFF['b_k'], 1, 3, False)
        heads(OFF['b_q'], 4, 2, True)
        heads(OFF['c_q'], 4, 4, True)
        heads(OFF['c_k'], 4, 5, False)
        heads(OFF['a_kc'], 1, -1, False)
        heads(OFF['a_vc'], 1, -1, False)
    assert len(cols) == NFM
    for g in range(2):
        cols.extend(range(OFF['a_vs'] + 64 * g, OFF['a_vs'] + 64 * g + 64))
        cols.extend(range(OFF['a_vw'] + 64 * g, OFF['a_vw'] + 64 * g + 64))
        cols.extend(range(OFF['b_v'] + 64 * g, OFF['b_v'] + 64 * g + 64))
        cols.extend(range(OFF['c_v'] + 256 * g, OFF['c_v'] + 256 * g + 256))
    assert len(cols) == NFM + NTM
    cols.extend(range(OFF['a_gate'], OFF['a_gate'] + 24))
    cols.extend(range(OFF['c_f'], OFF['c_f'] + 8))
    assert len(cols) == 3616
    return np.array(cols), gain_idx, np.array(qsc, np.float32)


def build_A(NT=2048):
    nc = bass.Bass("TRN2", target_bir_lowering=False)
    ntile = NT // 128
    x = nc.dram_tensor("x", [NT, D], F32, kind="ExternalInput").ap()
    wA = nc.dram_tensor("wA", [D, 3616], F32, kind="ExternalInput").ap()
    gmix = nc.dram_tensor("gmix", [128, D], F32, kind="ExternalInput").ap()
    gainrow = nc.dram_tensor("gainrow", [128, 21], F32, kind="ExternalInput").ap()
    qscale = nc.dram_tensor("qscale", [128, 42], F32, kind="ExternalInput").ap()
    ident = nc.dram_tensor("ident", [128, 128], F32, kind="ExternalInput").ap()
    FM = nc.dram_tensor("FM", [NFM, NT], BF16, kind="ExternalOutput").ap()
    TM = nc.dram_tensor("TM", [NT, NTM], BF16, kind="ExternalOutput").ap()
    GT = nc.dram_tensor("GT", [NT, 24], F32, kind="ExternalOutput").ap()
    CF = nc.dram_tensor("CF", [8, NT], F32, kind="ExternalOutput").ap()
    with ExitStack() as es:
        P = Prog(nc, es)
        dout = phase_A(P, es, NT, x, wA, gmix, gainrow, qscale, ident, FM,
                       [TM[:, 0:NTMG], TM[:, NTMG:NTM]], [GT[:, 0:12], GT[:, 12:24]], CF)
        P.finish([dout])
    return nc


STAGE = 99


def phase_A(P, es, NT, x, wA, gmix, gainrow, qscale, ident, FM, TM, GT, CF, xdep=(), dout_in=None, chunked=None,
            chunk_bufs=None, on_chunk=None):
    ntile = NT // 128
    dout = dout_in if dout_in is not None else P.buf("doutA")
    wsb, _ = P.sbuf("A_w", [128, 8, 3616], BF16)
    banks = [(i * 512, 512) for i in range(5)] + [(2560, 128), (2688, 512), (3200, 416)]
    wgrp = [(0, 1024), (1024, 1664), (2688, 928)]
    wbg = [P.buf("A_wg") for _ in wgrp]
    for gi, (g0, gw) in enumerate(wgrp):
        P.dma("pool", wsb[:, :, g0:g0 + gw], wA[:, g0:g0 + gw].rearrange("(k p) c -> p k c", p=128), wbg[gi], writes=[wbg[gi]])
    wb = None
    gm, gmb = P.sbuf("A_gm", [128, D], F32)
    P.dma("sp", gm[:], gmix, gmb, writes=[gmb])
    gr, grb = P.sbuf("A_gr", [128, 21], F32)
    P.dma("sp", gr[:], gainrow, grb, writes=[grb])
    qs, qsb = P.sbuf("A_qs", [128, 42], F32)
    P.dma("sp", qs[:], qscale, qsb, writes=[qsb])
    idb, idbb = P.sbuf("A_idb", [128, 128], BF16)
    P.dma("pool", idb[:], ident, idbb, writes=[idbb])
    idf, idfb = P.sbuf("A_idf", [128, 128], F32)
    P.dma("sp", idf[:], ident, idfb, writes=[idfb])

    xt = [P.sbuf(f"A_x{i}", [128, D], F32) for i in range(2)]
    junk, junkb = P.sbuf("A_junk", [128, D], F32)
    ss, ssb = P.sbuf("A_ss", [128, 4], F32)
    h2 = [P.sbuf(f"A_h{i}", [128, D], BF16) for i in range(2)]
    ss2 = [P.sbuf(f"A_ss{i}", [128, 4], F32) for i in range(2)]
    hT = [P.sbuf(f"A_hT{i}", [128, 8, 128], BF16) for i in range(2)]
    raw2 = [(P.sbuf(f"A_raw{i}", [128, NFM], F32)[0], [P.buf("A_rawb") for _ in range(6)]) for i in range(2)]
    sq = [P.sbuf(f"A_sq{i}", [128, 512], F32) for i in range(2)]
    ssall2 = [P.sbuf(f"A_ssall{i}", [128, 42], F32) for i in range(2)]
    rs, rsb = P.sbuf("A_rs", [128, 42], F32)
    tmp = [P.sbuf(f"A_tmp{i}", [128, 512], F32) for i in range(2)]
    nrm, _ = P.sbuf("A_nrm", [128, NFM], BF16)
    nrmb = [P.buf("A_nrmb") for _ in range(6)]
    FMt = [P.sbuf(f"A_FMt{i}", [128, 21, 512], BF16) for i in range(2)]
    TMt = [P.sbuf(f"A_TMt{i}", [128, 4, NTM], BF16) for i in range(2)]
    gtt = [P.sbuf(f"A_gt{i}", [128, 4, 24], F32) for i in range(2)]
    cft, cftb = P.sbuf("A_cft", [128, 8], F32)
    cfT, cfTb = P.sbuf("A_cfT", [8, NT], F32)
    tp, tpb = P.psum("A_tp", [128, 8, 128], BF16)
    PB = [P.psum(f"A_PB{i}", [128, 512], F32) for i in range(3)]
    tq = [P.psum(f"A_tq{i}", [128, 8, 128], BF16) for i in range(2)]
    pcf, pcfb = P.psum("A_pcf", [8, 128], F32)

    bank_grp = [0, 0, 1, 1, 1, 1, 2, 2]
    nmm_ = [0]

    def front_norm(t):
        xs, xsb = xt[t % 2]
        h_, h_b = h2[t % 2]
        ss_, ss_b = ss2[t % 2]
        P.dma("sp", xs[:], x[t * 128:(t + 1) * 128, :], xsb, reads=list(xdep), writes=[xsb])
        _rmsnorm_tile(P, xs[:], xsb, gm, gmb, h_[:], h_b, ss_, ss_b, junk, junkb)

    def front_T(t):
        h_, h_b = h2[t % 2]
        for k in range(8):
            P.op("pe", lambda e, k=k: e.transpose(tp[:, k, :], h_[:, k * 128:(k + 1) * 128], idb[:]),
                 reads=[h_b, idbb], writes=[tpb])
        hTs, hTsb = hT[t % 2]
        P.op("act", lambda e: e.copy(out=hTs[:], in_=tp[:]), reads=[tpb], writes=[hTsb])

    def back_mm(t):
        nmm = nmm_[0]
        raw, rawb = raw2[t % 2]
        ssall, ssallb = ssall2[t % 2]
        hTs, hTsb = hT[t % 2]
        slot = (t // 4) % 2
        tt = t % 4
        TMs, TMsb = TMt[slot]
        gts, gtsb = gtt[slot]
        FMs, FMsb = FMt[slot]
        for j, (c0, w) in enumerate(banks):
            pb, pbb = PB[nmm % 3]
            nmm += 1
            for k in range(8):
                P.op("pe", lambda e, pb=pb, k=k, c0=c0, w=w, hTs=hTs: e.matmul(
                    pb[:, 0:w], lhsT=hTs[:, k, :], rhs=wsb[:, k, c0:c0 + w], start=(k == 0), stop=(k == 7)),
                    reads=[hTsb, wbg[bank_grp[j]]], writes=[pbb])
            if j < 6:
                sqs, sqsb = sq[j % 2]
                nsl = w // 64
                P.op("act", lambda e, pb=pb, c0=c0, w=w: e.copy(out=raw[:, c0:c0 + w], in_=pb[:, 0:w]),
                     reads=[pbb], writes=[rawb[j]])
                P.op("act", lambda e, pb=pb, w=w, sqs=sqs: e.activation(out=sqs[:, 0:w], in_=pb[:, 0:w], func=AF.Square),
                     reads=[pbb], writes=[sqsb])
                P.op("dve", lambda e, sqs=sqs, w=w, j=j, nsl=nsl: e.reduce_sum(
                    out=ssall[:, 8 * j:8 * j + nsl], in_=sqs[:, 0:w].rearrange("p (s d) -> p s d", d=64), axis=AX.X),
                    reads=[sqsb], writes=[ssallb])
            elif j == 6:
                P.op("act", lambda e, pb=pb, TMs=TMs, tt=tt: e.copy(out=TMs[:, tt, 0:512], in_=pb[:, 0:512]),
                     reads=[pbb], writes=[TMsb])
            else:
                P.op("act", lambda e, pb=pb, TMs=TMs, tt=tt: e.copy(out=TMs[:, tt, 512:896], in_=pb[:, 0:384]),
                     reads=[pbb], writes=[TMsb])
                P.op("act", lambda e, pb=pb, gts=gts, tt=tt: e.copy(out=gts[:, tt, :], in_=pb[:, 384:408]),
                     reads=[pbb], writes=[gtsb])
                if tt == 3:
                    P.op("act", lambda e, gts=gts: e.activation(out=gts[:], in_=gts[:], func=AF.Sigmoid), reads=[gtsb], writes=[gtsb])
        for k in range(8):
            P.op("pe", lambda e, k=k, hTs=hTs: e.matmul(pcf[:], lhsT=wsb[:, k, 3608:3616], rhs=hTs[:, k, :],
                                                        start=(k == 0), stop=(k == 7)),
                 reads=[hTsb, wbg[2]], writes=[pcfb])
        P.op("dve", lambda e, t=t: e.tensor_copy(out=cfT[:, t * 128:(t + 1) * 128], in_=pcf[:]), reads=[pcfb], writes=[cfTb])
        nmm_[0] = nmm

    def back_fin(t):
        raw, rawb = raw2[t % 2]
        ssall, ssallb = ssall2[t % 2]
        slot = (t // 4) % 2
        tt = t % 4
        TMs, TMsb = TMt[slot]
        gts, gtsb = gtt[slot]
        FMs, FMsb = FMt[slot]
        P.op("dve", lambda e: e.tensor_scalar(out=rs[:], in0=ssall[:], scalar1=1.0 / HD, scalar2=EPS,
                                              op0=ALU.mult, op1=ALU.add), reads=[ssallb], writes=[rsb])
        P.op("act", lambda e: e.activation(out=rs[:], in_=rs[:], func=AF.Sqrt), reads=[rsb], writes=[rsb])
        P.op("dve", lambda e: e.reciprocal(out=rs[:], in_=rs[:]), reads=[rsb], writes=[rsb])
        P.op("dve", lambda e: e.tensor_tensor(out=rs[:], in0=rs[:], in1=qs[:], op=ALU.mult), reads=[rsb, qsb], writes=[rsb])
        P.op("dve", lambda e: e.memset(rs[:, 19:21], 1.0), writes=[rsb])
        P.op("dve", lambda e: e.memset(rs[:, 40:42], 1.0), writes=[rsb])
        for j in range(6):
            c0, w = banks[j]
            nsl = w // 64
            P.op("pool", lambda e, c0=c0, w=w, j=j, nsl=nsl: e.tensor_tensor(
                out=nrm[:, c0:c0 + w].rearrange("p (s d) -> p s d", d=64),
                in0=raw[:, c0:c0 + w].rearrange("p (s d) -> p s d", d=64),
                in1=rs[:, 8 * j:8 * j + nsl].unsqueeze(2).to_broadcast([128, nsl, 64]), op=ALU.mult),
                reads=[rawb[j], rsb], writes=[nrmb[j]])
        for c in range(21):
            tqs, tqsb = tq[(c // 8) % 2]
            P.op("pe", lambda e, tqs=tqs, c=c: e.transpose(tqs[:, c % 8, :], nrm[:, c * 128:(c + 1) * 128], idb[:]),
                 reads=[nrmb[c // 4], idbb], writes=[tqsb])
            if c % 8 == 7 or c == 20:
                cb = (c // 8) * 8
                n = c - cb + 1
                P.op("dve", lambda e, tqs=tqs, cb=cb, n=n, FMs=FMs, tt=tt: e.tensor_tensor(
                    out=FMs[:, cb:cb + n, tt * 128:(tt + 1) * 128], in0=tqs[:, 0:n, :],
                    in1=gr[:, cb:cb + n].unsqueeze(2).to_broadcast([128, n, 128]), op=ALU.mult), reads=[tqsb, grb], writes=[FMsb])
        if tt == 3 and STAGE >= 7 and chunked is not None:
            for kk in range(2):
                k = (t - 3) // 2 + kk
                Sk = chunked[k]
                cb_ = [chunk_bufs[k]]
                ts_ = slice(kk * 256, (kk + 1) * 256)
                P.dma("sp", Sk[0:1280, :].rearrange("(c p) t -> p c t", p=128), FMs[:, 0:10, ts_], FMsb, reads=[FMsb], accw=cb_)
                P.dma("sp", Sk[1280:1344, :], FMs[0:64, 10, ts_], FMsb, reads=[FMsb], accw=cb_)
                P.dma("sp", Sk[1792:1856, :], FMs[64:128, 10, ts_], FMsb, reads=[FMsb], accw=cb_)
                P.dma("sp", Sk[1856:3136, :].rearrange("(c p) t -> p c t", p=128), FMs[:, 11:21, ts_], FMsb, reads=[FMsb], accw=cb_)
                for g in range(2):
                    P.dma("sp", Sk[1792 * g + 1344:1792 * (g + 1), :].rearrange("r x -> (r x)").rearrange("(tt p c) -> p tt c", p=128, c=NTMG),
                          TMs[:, 2 * kk:2 * kk + 2, g * NTMG:(g + 1) * NTMG], TMsb, reads=[TMsb], accw=cb_)
                if on_chunk is not None:
                    on_chunk(k)
            t0 = (t - 3) * 128
            for g in range(2):
                P.dma("sp", GT[g][t0:t0 + 512, :].rearrange("(tt p) c -> p tt c", p=128), gts[:, :, 12 * g:12 * g + 12],
                      gtsb, reads=[gtsb], writes=[dout])
        elif tt == 3 and STAGE >= 7:
            t0 = (t - 3) * 128
            for c3 in range(3):
                P.dma("sp", FM.rearrange("(c p) t -> p c t", p=128)[:, 7 * c3:7 * c3 + 7, t0:t0 + 512], FMs[:, 7 * c3:7 * c3 + 7, :],
                      FMsb, reads=[FMsb], writes=[dout])
            for g in range(2):
                P.dma("sp", TM[g][t0:t0 + 512, :].rearrange("(tt p) c -> p tt c", p=128), TMs[:, :, g * NTMG:(g + 1) * NTMG],
                      TMsb, reads=[TMsb], writes=[dout])
                P.dma("sp", GT[g][t0:t0 + 512, :].rearrange("(tt p) c -> p tt c", p=128), gts[:, :, 12 * g:12 * g + 12],
                      gtsb, reads=[gtsb], writes=[dout])
    front_norm(0)
    front_T(0)
    if ntile > 1:
        front_norm(1)
    back_mm(0)
    if ntile > 1:
        front_T(1)
    if ntile > 2:
        front_norm(2)
    for t in range(ntile):
        if t + 1 < ntile:
            back_mm(t + 1)
        if t + 2 < ntile:
            front_T(t + 2)
        if t + 3 < ntile:
            front_norm(t + 3)
        back_fin(t)
    if STAGE >= 7:
        if isinstance(CF, list):
            for g in range(2):
                P.dma("sp", CF[g], cfT[4 * g:4 * g + 4, :], cfTb, reads=[cfTb], writes=[dout])
        else:
            P.dma("sp", CF, cfT[:], cfTb, reads=[cfTb], writes=[dout])
    return dout


def prep_A(inp, l):
    perm, gain_idx, qsc = _colperm()
    wA = np.ascontiguousarray(inp["w_in"][l][:, perm])
    gr = np.ones((NFM,), np.float32)
    for s, gi in enumerate(gain_idx):
        if gi >= 0:
            gr[s * 64:(s + 1) * 64] = inp["qk_gain"][l, gi]
    return dict(
        wA=wA,
        gmix=np.ascontiguousarray(np.broadcast_to(inp["norm_mix"][l][None, :], (128, D))),
        gainrow=np.ascontiguousarray(gr.reshape(21, 128).T),
        qscale=np.ascontiguousarray(np.broadcast_to(qsc[None, :], (128, 42))),
        ident=np.eye(128, dtype=np.float32),
    )


def _rmsnorm_tile(P, xs, xsb, gm, gmb, h, hb, ss, ssb, junk, junkb):
    P.op("act", lambda e: e.activation(out=junk[:], in_=xs, func=AF.Square, accum_out=ss[:, 0:1]),
         reads=[xsb], writes=[junkb, ssb])
    P.op("dve", lambda e: e.tensor_scalar(out=ss[:, 1:2], in0=ss[:, 0:1], scalar1=1.0 / D, scalar2=EPS,
                                          op0=ALU.mult, op1=ALU.add), reads=[ssb], writes=[ssb])
    P.op("act", lambda e: e.activation(out=ss[:, 2:3], in_=ss[:, 1:2], func=AF.Sqrt), reads=[ssb], writes=[ssb])
    P.op("dve", lambda e: e.reciprocal(out=ss[:, 3:4], in_=ss[:, 2:3]), reads=[ssb], writes=[ssb])
    P.op("dve", lambda e: e.scalar_tensor_tensor(out=h, in0=xs, scalar=ss[:, 3:4], in1=gm[:],
                                                 op0=ALU.mult, op1=ALU.mult),
         reads=[xsb, ssb, gmb], writes=[hb])


def og_static(og):
    return lambda t, g: (lambda e: og[t * 128:(t + 1) * 128, :].rearrange("p (n g c) -> p n g c", n=3, g=2)[:, :, g, :])


def phase_C(P, NTC, xh, ogsrc, wM, wBr, wO, gmix, gffn, wG, wU, wD, convw, convb, ident, XM, xo, hflag,
            xdep=(), ogdep=(), dout_in=None, hnext=None):
    ntile = NTC // 128
    xdep, ogdep = list(xdep), list(ogdep)
    dout = dout_in if dout_in is not None else P.buf("doutC")
    hfl, hflb = P.sbuf("C_hfl", [128, 1], F32)
    P.dma("sp", hfl[:], hflag, hflb, writes=[hflb])
    xmb = [P.buf("C_xmd") for _ in range(ntile)]
    idb, idbb = P.sbuf("C_idb", [128, 128], BF16)
    P.dma("pool", idb[:], ident, idbb, writes=[idbb])
    ss, ssb = P.sbuf("C_ss", [128, 4], F32)
    junk, junkb = P.sbuf("C_junk", [128, D], F32)
    h, hb = P.sbuf("C_h", [128, D], BF16)
    with P.scope():
        wMs, _ = P.sbuf("C1_wM", [128, 8, 3072], BF16)
        wMb = [P.buf("C1_wMb") for _ in range(3)]
        for n_ in range(3):
            P.dma("pool", wMs[:, :, n_ * 1024:(n_ + 1) * 1024], wM[:, n_ * 1024:(n_ + 1) * 1024].rearrange("(k p) c -> p k c", p=128),
                  wMb[n_], writes=[wMb[n_]])
        wBs, wBb = P.sbuf("C1_wB", [128, 12, D], BF16)
        P.dma("pool", wBs[:], wBr.rearrange("(j p) d -> p j d", p=128), wBb, writes=[wBb])
        wOs, wOb = P.sbuf("C1_wO", [128, 8, D], BF16)
        P.dma("pool", wOs[:], wO.rearrange("(k p) d -> p k d", p=128), wOb, writes=[wOb])
        gm, gmb = P.sbuf("C1_gm", [128, D], F32)
        P.dma("sp", gm[:], gmix, gmb, writes=[gmb])
        xt = [P.sbuf(f"C1_x{i}", [128, D], F32) for i in range(2)]
        ot = [P.sbuf(f"C1_o{i}", [128, 1536], BF16) for i in range(2)]
        hT2 = [P.sbuf(f"C1_hT{i}", [128, 8, 128], BF16) for i in range(2)]
        oT2 = [P.sbuf(f"C1_oT{i}", [128, 12, 128], BF16) for i in range(2)]
        hC = [P.sbuf(f"C1_h{i}", [128, D], BF16) for i in range(2)]
        ssC = [P.sbuf(f"C1_ss{i}", [128, 4], F32) for i in range(2)]
        gs = [P.sbuf(f"C1_gs{i}", [128, 512], F32) for i in range(2)]
        tmpm = [P.sbuf(f"C1_tm{i}", [128, 512], F32) for i in range(2)]
        macc, maccb = P.sbuf("C1_macc", [128, D], F32)
        mbf, mbfb = P.sbuf("C1_mbf", [128, D], BF16)
        mT, mTb = P.sbuf("C1_mT", [128, 8, 128], BF16)
        xm = [P.sbuf(f"C1_xm{i}", [128, D], F32) for i in range(2)]
        tp, tpb = P.psum("C1_tp", [128, 8, 128], BF16)
        to = [P.psum(f"C1_to{i}", [128, 8, 128], BF16) for i in range(2)]
        PB = [P.psum(f"C1_PB{i}", [128, 512], F32) for i in range(4)]
        nb_ = [0]

        def c1_norm(t):
            xs, xsb = xt[t % 2]
            os_, osb = ot[t % 2]
            h_, h_b = hC[t % 2]
            ss_, ss_b = ssC[t % 2]
            P.dma("sp", xs[:], xh[t * 128:(t + 1) * 128, :], xsb, reads=xdep, writes=[xsb])
            for g in range(2):
                P.dma("sp", os_[:].rearrange("p (n g c) -> p n g c", n=3, g=2)[:, :, g, :], ogsrc(t, g), osb, reads=ogdep, writes=[osb])
            _rmsnorm_tile(P, xs[:], xsb, gm, gmb, h_[:], h_b, ss_, ss_b, junk, junkb)

        def c1_T(t):
            os_, osb = ot[t % 2]
            h_, h_b = hC[t % 2]
            hT, hTb = hT2[t % 2]
            oT, oTb = oT2[t % 2]
            for k in range(8):
                P.op("pe", lambda e, k=k: e.transpose(tp[:, k, :], h_[:, k * 128:(k + 1) * 128], idb[:]),
                     reads=[h_b, idbb], writes=[tpb])
            P.op("act", lambda e: e.copy(out=hT[:], in_=tp[:]), reads=[tpb], writes=[hTb])
            for j in range(12):
                tos, tosb = to[j // 8]
                P.op("pe", lambda e, tos=tos, j=j: e.transpose(tos[:, j % 8, :], os_[:, j * 128:(j + 1) * 128], idb[:]),
                     reads=[osb, idbb], writes=[tosb])
            P.op("dve", lambda e: e.tensor_copy(out=oT[:, 0:8, :], in_=to[0][0][:]), reads=[to[0][1]], writes=[oTb])
            P.op("act", lambda e: e.copy(out=oT[:, 8:12, :], in_=to[1][0][:, 0:4, :]), reads=[to[1][1]], writes=[oTb])

        def c1_back(t):
            nb = nb_[0]
            xs, xsb = xt[t % 2]
            hT, hTb = hT2[t % 2]
            oT, oTb = oT2[t % 2]
            for n in range(3):
                for half in range(2):
                    pg, pgb = PB[nb % 4]
                    pp, ppb = PB[(nb + 1) % 4]
                    nb += 2
                    c0 = n * 1024 + half * 512
                    for k in range(8):
                        P.op("pe", lambda e, pg=pg, k=k, c0=c0: e.matmul(pg[:], lhsT=hT[:, k, :], rhs=wMs[:, k, c0:c0 + 512],
                                                                         start=(k == 0), stop=(k == 7)),
                             reads=[hTb, wMb[n]], writes=[pgb])
                    for j in range(4):
                        P.op("pe", lambda e, pp=pp, j=j, n=n, half=half: e.matmul(
                            pp[:], lhsT=oT[:, 4 * n + j, :], rhs=wBs[:, 4 * n + j, half * 512:(half + 1) * 512],
                            start=(j == 0), stop=(j == 3)), reads=[oTb, wBb], writes=[ppb])
                    g_, g_b = gs[(n * 2 + half) % 2]
                    P.op("act", lambda e, pg=pg, g_=g_: e.activation(out=g_[:], in_=pg[:], func=AF.Sigmoid),
                         reads=[pgb], writes=[g_b])
                    hs = slice(half * 512, (half + 1) * 512)
                    if n == 0:
                        P.op("dve", lambda e, pp=pp, g_=g_, hs=hs: e.tensor_tensor(out=macc[:, hs], in0=pp[:], in1=g_[:], op=ALU.mult),
                             reads=[ppb, g_b], writes=[maccb])
                    else:
                        tm_, tm_b = tmpm[half]
                        P.op("dve", lambda e, pp=pp, g_=g_, tm_=tm_: e.tensor_tensor(out=tm_[:], in0=pp[:], in1=g_[:], op=ALU.mult),
                             reads=[ppb, g_b], writes=[tm_b])
                        if n == 1:
                            P.op("pool", lambda e, tm_=tm_, hs=hs: e.tensor_tensor(out=macc[:, hs], in0=macc[:, hs], in1=tm_[:], op=ALU.add),
                                 reads=[tm_b, maccb], writes=[maccb])
                        else:
                            P.op("pool", lambda e, tm_=tm_, hs=hs: e.tensor_tensor(out=mbf[:, hs], in0=macc[:, hs], in1=tm_[:], op=ALU.add),
                                 reads=[tm_b, maccb], writes=[mbfb])
            for k in range(8):
                P.op("pe", lambda e, k=k: e.transpose(tp[:, k, :], mbf[:, k * 128:(k + 1) * 128], idb[:]),
                     reads=[mbfb, idbb], writes=[tpb])
            P.op("act", lambda e: e.copy(out=mT[:], in_=tp[:]), reads=[tpb], writes=[mTb])
            xms, xmsb = xm[t % 2]
            for half in range(2):
                py, pyb = PB[nb % 4]
                nb += 1
                for k in range(8):
                    P.op("pe", lambda e, py=py, k=k, half=half: e.matmul(py[:], lhsT=mT[:, k, :], rhs=wOs[:, k, half * 512:(half + 1) * 512],
                                                                         start=(k == 0), stop=(k == 7)),
                         reads=[mTb, wOb], writes=[pyb])
                hs = slice(half * 512, (half + 1) * 512)
                P.op("dve", lambda e, py=py, xs=xs, xms=xms, hs=hs: e.tensor_tensor(out=xms[:, hs], in0=py[:], in1=xs[:, hs], op=ALU.add),
                     reads=[pyb, xsb], writes=[xmsb])
            P.dma("sp", XM[t * 128:(t + 1) * 128, :], xms[:], xmsb, reads=[xmsb], writes=[xmb[t]])
            nb_[0] = nb

        c1_norm(0)
        c1_T(0)
        if ntile > 1:
            c1_norm(1)
        for t in range(ntile):
            c1_back(t)
            if t + 1 < ntile:
                c1_T(t + 1)
            if t + 2 < ntile:
                c1_norm(t + 2)
    with P.scope():
        wGs, _ = P.sbuf("C2_wG", [128, 8, F_FF], BF16)
        wUs, _ = P.sbuf("C2_wU", [128, 8, F_FF], BF16)
        fgrp = [(0, 6), (6, 12), (12, 17), (17, 22)]
        fc_grp = [gi for gi, (a_, b_) in enumerate(fgrp) for _ in range(a_, b_)]
        wGb = [P.buf("C2_wGb") for _ in fgrp]
        wUb = [P.buf("C2_wUb") for _ in fgrp]
        for gi, (a_, b_) in enumerate(fgrp):
            P.dma("pool", wGs[:, :, a_ * 128:b_ * 128], wG[:, a_ * 128:b_ * 128].rearrange("(k p) c -> p k c", p=128), wGb[gi], writes=[wGb[gi]])
            P.dma("pool", wUs[:, :, a_ * 128:b_ * 128], wU[:, a_ * 128:b_ * 128].rearrange("(k p) c -> p k c", p=128), wUb[gi], writes=[wUb[gi]])
        wDs, _ = P.sbuf("C2_wD", [128, NFC, D], BF16)
        wDb = [P.buf("C2_wDb") for _ in range(2)]
        for i in range(2):
            P.dma("pool", wDs[:, 11 * i:11 * i + 11, :], wD[11 * i * 128:(11 * i + 11) * 128, :].rearrange("(f p) d -> p f d", p=128),
                  wDb[i], writes=[wDb[i]])
        gf, gfb = P.sbuf("C2_gf", [128, D], F32)
        P.dma("sp", gf[:], gffn, gfb, writes=[gfb])
        cw, cwb = P.sbuf("C2_cw", [128, NFC, 3], F32)
        P.dma("sp", cw[:], convw, cwb, writes=[cwb])
        cb, cbb = P.sbuf("C2_cb", [128, NFC], F32)
        P.dma("sp", cb[:], convb, cbb, writes=[cbb])
        NS = 256
        xm4 = [P.sbuf(f"C2_xm4{i}", [128, 2, D], F32) for i in range(2)]
        h2T2 = [P.sbuf(f"C2_h2T{i}", [128, 8, NS], BF16) for i in range(2)]
        Ab = [P.sbuf(f"C2_A{i}", [128, NS + 2], F32) for i in range(2)]
        cv = [P.sbuf(f"C2_cv{i}", [128, NS], F32) for i in range(2)]
        gl = [P.sbuf(f"C2_gl{i}", [128, NS], F32) for i in range(2)]
        halo, halob = P.sbuf("C2_halo", [128, NFC, 2], F32)
        gu, _ = P.sbuf("C2_gu", [128, NFC, NS], BF16)
        gub = [P.buf("C2_gub") for _ in range(NFC)]
        xot = [P.sbuf(f"C2_xo{i}", [128, D], F32) for i in range(2)]
        if hnext is not None:
            gmn, gmnb = P.sbuf("C2_gmn", [128, D], F32)
            P.dma("sp", gmn[:], hnext[0], gmnb, writes=[gmnb])
            hnx = [P.sbuf(f"C2_hn{i}", [128, D], BF16) for i in range(2)]
            ssn, ssnb = P.sbuf("C2_ssn", [128, 4], F32)
        tp, tpb = P.psum("C2_tp", [128, 8, 128], BF16)
        PA_ = [P.psum(f"C2_PA{i}", [128, 512], F32) for i in range(2)]
        PU_ = [P.psum(f"C2_PU{i}", [128, 512], F32) for i in range(3)]
        PY = [P.psum(f"C2_PY{i}", [128, 512], F32) for i in range(2)]
        sts = [(0, 1, True)] + [(1 + 2 * i, 2, False) for i in range((ntile - 1) // 2)]
        nb_ = [0, 0]

        def c2_front(si):
            t0, nt, is_halo = sts[si]
            x4, x4b = xm4[si % 2]
            h2T, h2Tb = h2T2[si % 2]
            P.dma("sp", x4[:, 0:nt, :], XM[t0 * 128:(t0 + nt) * 128, :].rearrange("(t p) d -> p t d", p=128), x4b,
                  reads=[xmb[t0 + i] for i in range(nt)], writes=[x4b])
            for tt in range(nt):
                _rmsnorm_tile(P, x4[:, tt, :], x4b, gf, gfb, h[:], hb, ss, ssb, junk, junkb)
                for k in range(8):
                    P.op("pe", lambda e, k=k: e.transpose(tp[:, k, :], h[:, k * 128:(k + 1) * 128], idb[:]),
                         reads=[hb, idbb], writes=[tpb])
                P.op("act", lambda e, tt=tt: e.copy(out=h2T[:, :, tt * 128:(tt + 1) * 128], in_=tp[:]), reads=[tpb], writes=[h2Tb])

        def c2_back(si):
            t0, nt, is_halo = sts[si]
            nb, ny = nb_
            ntok = nt * 128
            x4, x4b = xm4[si % 2]
            h2T, h2Tb = h2T2[si % 2]
            def stage_a(fc):
                pa, pab = PA_[fc % 2]
                for k in range(8):
                    P.op("pe", lambda e, pa=pa, k=k, fc=fc, ntok=ntok: e.matmul(
                        pa[:, 0:ntok], lhsT=wGs[:, k, fc * 128:(fc + 1) * 128], rhs=h2T[:, k, 0:ntok], start=(k == 0), stop=(k == 7)),
                        reads=[h2Tb, wGb[fc_grp[fc]]], writes=[pab])
                A_, A_b = Ab[fc % 2]
                if si == 0:
                    P.op("pool", lambda e, A_=A_: e.memset(A_[:, 0:2], 0.0), writes=[A_b])
                else:
                    P.op("pool", lambda e, A_=A_, fc=fc: e.tensor_copy(out=A_[:, 0:2], in_=halo[:, fc, :]), reads=[halob], writes=[A_b])
                P.op("act", lambda e, A_=A_, pa=pa, ntok=ntok: e.copy(out=A_[:, 2:2 + ntok], in_=pa[:, 0:ntok]),
                     reads=[pab], writes=[A_b])
                if is_halo:
                    P.op("pool", lambda e, A_=A_, fc=fc, ntok=ntok: e.tensor_scalar(
                        out=halo[:, fc, :], in0=A_[:, ntok:ntok + 2], scalar1=hfl[:, 0:1], scalar2=None, op0=ALU.mult),
                        reads=[A_b, hflb], writes=[halob])
                else:
                    P.op("act", lambda e, A_=A_, fc=fc, ntok=ntok: e.copy(out=halo[:, fc, :], in_=A_[:, ntok:ntok + 2]),
                         reads=[A_b], writes=[halob])

            def stage_b(fc, nb):
                A_, A_b = Ab[fc % 2]
                pu, pub = PU_[nb % 3]
                for k in range(8):
                    P.op("pe", lambda e, pu=pu, k=k, fc=fc, ntok=ntok: e.matmul(
                        pu[:, 0:ntok], lhsT=wUs[:, k, fc * 128:(fc + 1) * 128], rhs=h2T[:, k, 0:ntok], start=(k == 0), stop=(k == 7)),
                        reads=[h2Tb, wUb[fc_grp[fc]]], writes=[pub])
                cv_, cv_b = cv[fc % 2]
                P.op("pool", lambda e, A_=A_, cv_=cv_, fc=fc, ntok=ntok: e.tensor_scalar(
                    out=cv_[:, 0:ntok], in0=A_[:, 2:2 + ntok], scalar1=cw[:, fc, 2:3], scalar2=cb[:, fc:fc + 1], op0=ALU.mult, op1=ALU.add),
                    reads=[A_b, cwb, cbb], writes=[cv_b])
                P.op("dve", lambda e, A_=A_, cv_=cv_, fc=fc, ntok=ntok: e.scalar_tensor_tensor(
                    out=cv_[:, 0:ntok], in0=A_[:, 1:1 + ntok], scalar=cw[:, fc, 1:2], in1=cv_[:, 0:ntok], op0=ALU.mult, op1=ALU.add),
                    reads=[A_b, cwb, cv_b], writes=[cv_b])
                P.op("dve", lambda e, A_=A_, cv_=cv_, fc=fc, ntok=ntok: e.scalar_tensor_tensor(
                    out=cv_[:, 0:ntok], in0=A_[:, 0:ntok], scalar=cw[:, fc, 0:1], in1=cv_[:, 0:ntok], op0=ALU.mult, op1=ALU.add),
                    reads=[A_b, cwb, cv_b], writes=[cv_b])
                gl_, gl_b = gl[fc % 2]
                P.op("act", lambda e, cv_=cv_, gl_=gl_, ntok=ntok: e.activation(out=gl_[:, 0:ntok], in_=cv_[:, 0:ntok], func=AF.Gelu_apprx_tanh),
                     reads=[cv_b], writes=[gl_b])
                P.op("dve", lambda e, gl_=gl_, pu=pu, fc=fc, ntok=ntok: e.tensor_tensor(
                    out=gu[:, fc, 0:ntok], in0=pu[:, 0:ntok], in1=gl_[:, 0:ntok], op=ALU.mult),
                    reads=[pub, gl_b], writes=[gub[fc]])

            stage_a(0)
            for fc in range(NFC):
                if fc + 1 < NFC:
                    stage_a(fc + 1)
                if not is_halo:
                    stage_b(fc, nb)
                    nb += 1
            if is_halo:
                nb_[0], nb_[1] = nb, ny
                return
            for tt in range(nt):
                xos, xosb = xot[ny % 2]
                for half in range(2):
                    py, pyb = PY[ny % 2]
                    ny += 1
                    for fc in range(NFC):
                        P.op("pe", lambda e, py=py, fc=fc, tt=tt, half=half: e.matmul(
                            py[:], lhsT=gu[:, fc, tt * 128:(tt + 1) * 128], rhs=wDs[:, fc, half * 512:(half + 1) * 512],
                            start=(fc == 0), stop=(fc == NFC - 1)), reads=[gub[fc], wDb[fc // 11]], writes=[pyb])
                    hs = slice(half * 512, (half + 1) * 512)
                    P.op("dve", lambda e, py=py, xos=xos, x4=x4, tt=tt, hs=hs: e.tensor_tensor(
                        out=xos[:, hs], in0=py[:], in1=x4[:, tt, hs], op=ALU.add), reads=[pyb, x4b], writes=[xosb])
                r0 = (t0 - 1 + tt) * 128
                P.dma("sp", xo[r0:r0 + 128, :], xos[:], xosb, reads=[xosb], accw=[dout])
                if hnext is not None:
                    tl = t0 - 1 + tt
                    hn, hnb = hnx[tl % 2]
                    _rmsnorm_tile(P, xos[:], xosb, gmn, gmnb, hn[:], hnb, ssn, ssnb, junk, junkb)
                    P.dma("sp", hnext[1](tl), hn[:], hnb, reads=[hnb], accw=[hnext[2](tl)])
                    hnext[3](tl)
            nb_[0], nb_[1] = nb, ny

        c2_front(0)
        for si in range(len(sts)):
            if si + 1 < len(sts):
                c2_front(si + 1)
            c2_back(si)
    return dout


def build_C(NTC=2176):
    nc = bass.Bass("TRN2", target_bir_lowering=False)
    dt_ = lambda n, s, d=F32, k="ExternalInput": nc.dram_tensor(n, s, d, kind=k).ap()
    xh = dt_("xh", [NTC, D])
    og = dt_("og", [NTC, 1536], BF16)
    wM = dt_("wM", [D, 3072]); wBr = dt_("wBr", [1536, D]); wO = dt_("wO", [D, D])
    gmix = dt_("gmix", [128, D]); gffn = dt_("gffn", [128, D])
    wG = dt_("wG", [D, F_FF]); wU = dt_("wU", [D, F_FF]); wD = dt_("wD", [F_FF, D])
    convw = dt_("convw", [128, NFC, 3]); convb = dt_("convb", [128, NFC])
    ident = dt_("ident", [128, 128])
    hflag = dt_("hflag", [128, 1])
    XM = nc.dram_tensor("XM", [NTC, D], F32).ap()
    xo = dt_("xo", [NTC - 128, D], F32, "ExternalOutput")
    with ExitStack() as es:
        P = Prog(nc, es)
        dout = phase_C(P, NTC, xh, og_static(og), wM, wBr, wO, gmix, gffn, wG, wU, wD, convw, convb, ident, XM, xo, hflag)
        P.finish([dout])
    return nc


def prep_C(inp, l):
    return dict(
        wM=np.ascontiguousarray(inp["w_in"][l][:, OFF['merge']:]),
        wBr=np.ascontiguousarray(inp["w_branch"][l].reshape(1536, D)),
        wO=inp["w_out"][l],
        gmix=np.ascontiguousarray(np.broadcast_to(inp["norm_mix"][l][None, :], (128, D))),
        gffn=np.ascontiguousarray(np.broadcast_to(inp["norm_ffn"][l][None, :], (128, D))),
        wG=inp["w_gate"][l], wU=inp["w_up"][l], wD=inp["w_down"][l],
        convw=np.ascontiguousarray(inp["conv_w"][l].reshape(3, NFC, 128).transpose(2, 1, 0)),
        convb=np.ascontiguousarray(inp["conv_b"][l].reshape(NFC, 128).T),
        ident=np.eye(128, dtype=np.float32),
        hflag=np.ones((128, 1), np.float32),
    )


class SrcStatic:
    def __init__(self, TB, FMB, TMB, GTB, CFB):
        self.TH = TB // 2
        self.FMB, self.TMB, self.GTB, self.CFB = FMB, TMB, GTB, CFB

    def fm(self, r0, hf):
        return [(0, self.TH, lambda e: self.FMB[r0:r0 + 64, hf * self.TH:(hf + 1) * self.TH])]

    def tm(self, col0, hf):
        return [(0, self.TH, lambda e: self.TMB[hf * self.TH:(hf + 1) * self.TH, col0:col0 + 64])]

    def fm_all(self, r0):
        return lambda e: self.FMB[r0:r0 + 64, :].rearrange("p (a t) -> p a t", t=256)

    def tm_all(self):
        n = 2 * self.TH // 128
        return [(k0, min(8, n - k0), (lambda e, k0=k0: self.TMB.rearrange("(kt p) c -> p kt c", p=128)[:, k0:min(n, k0 + 8), :]))
                for k0 in range(0, n, 8)]

    def gt(self, hf):
        return lambda e: self.GTB[hf * self.TH:(hf + 1) * self.TH, :]

    def cf(self, hf):
        return lambda e: self.CFB[:, hf * self.TH:(hf + 1) * self.TH]


def phase_B(P, TB, src, c, OB, srcdep=(), dout_in=None, obuf=None, on_ob=None):
    srcdep = list(srcdep)
    obdst = OB if callable(OB) else (lambda qg, c0, c1: OB[qg * 512:(qg + 1) * 512, c0:c1])
    TH = TB // 2
    NQH = TB // 256
    NQT = TB // 128
    NQG = TB // 512
    NCP = TB // 16
    NCT = (NCP + 127) // 128
    NCW = NCT * 128
    dout = dout_in if dout_in is not None else P.buf("doutB")
    if obuf is None:
        obuf = lambda qg: dout
    idb, idbb = P.sbuf("B_idb", [128, 128], BF16)
    P.dma("pool", idb[:], c["ident"], idbb, writes=[idbb])
    idf, idfb = P.sbuf("B_idf", [128, 128], F32)
    P.dma("sp", idf[:], c["ident"], idfb, writes=[idfb])
    F = [P.psum(f"B_F{i}", [128, 512], F32) for i in range(7)]
    Tb, Tbb = P.psum("B_Tb", [128, 8, 128], BF16)
    SB3 = [F[0], F[1], F[6]]

    Vall, Vallb = P.sbuf("B_Vall", [128, NQT, NTMG], BF16)
    for (k0, nk, f) in src.tm_all():
        P.dma("sp", Vall[:, k0:k0 + nk, :], f, Vallb, reads=srcdep, accw=[Vallb])

    def load_vext(name, col0, nh):
        t, b = P.sbuf(name, [128, NQT, nh, 65], BF16)
        P.op("pool", lambda e: e.memset(t[:], 1.0), writes=[b])
        P.op("pool", lambda e: e.tensor_copy(out=t[:, :, :, 0:64],
                                             in_=Vall[:, :, col0:col0 + 64 * nh].rearrange("p k (h d) -> p k h d", d=64)),
             reads=[Vallb], writes=[b])
        return t, b

    def load_rows(dst, bufs, semb, r0):
        P.dma("sp", dst.rearrange("p (a t) -> p a t", t=256), src.fm_all(r0), semb, reads=srcdep, writes=bufs)

    def load_fm(name, r0):
        t, b = P.sbuf(name, [64, TB], BF16)
        load_rows(t[:], [b], b, r0)
        return t, b

    def finish_head(acc, accb, dst, reads, writes, extra_scalar=None, extra_b=None, addsink=None, accumulate=False,
                    gate_ap=None, gate_b=None, rz=None, rzb=None):
        if addsink is not None:
            P.op("dve", lambda e: e.tensor_tensor(out=rz[:, 0:1], in0=acc[:, 64:65], in1=addsink, op=ALU.add),
                 reads=[accb] + reads, writes=[rzb])
            P.op("dve", lambda e: e.reciprocal(out=rz[:, 0:1], in_=rz[:, 0:1]), reads=[rzb], writes=[rzb])
        else:
            P.op("dve", lambda e: e.reciprocal(out=rz[:, 0:1], in_=acc[:, 64:65]), reads=[accb], writes=[rzb])
        if gate_ap is not None:
            P.op("dve", lambda e: e.tensor_tensor(out=rz[:, 0:1], in0=rz[:, 0:1], in1=gate_ap, op=ALU.mult),
                 reads=[rzb, gate_b], writes=[rzb])
        if accumulate:
            P.op("dve", lambda e: e.scalar_tensor_tensor(out=dst, in0=acc[:, 0:64], scalar=rz[:, 0:1], in1=dst,
                                                         op0=ALU.mult, op1=ALU.add), reads=[accb, rzb] + writes, writes=writes)
        else:
            P.op("dve", lambda e: e.tensor_scalar(out=dst, in0=acc[:, 0:64], scalar1=rz[:, 0:1], scalar2=None, op0=ALU.mult),
                 reads=[accb, rzb], writes=writes)

    rzs = [P.sbuf(f"B_rz{i}", [128, 1], F32) for i in range(4)]
    PT = [P.sbuf(f"B_PT{i}", [128, 512], BF16) for i in range(3)]
    PT4 = [P.sbuf(f"B_PT4{i}", [128, 384], BF16) for i in range(4)]
    zero1, zero1b = P.sbuf("B_zero", [128, 1], F32)
    P.op("pool", lambda e: e.memset(zero1[:], 0.0), writes=[zero1b])
    nrz = [0]

    with P.scope():
        Qst = [P.sbuf(f"N_Q{h}", [128, TB], BF16)[0] for h in range(4)]
        Qb = [[P.buf(f"N_Qb{h}_") for _ in range(NQG)] for h in range(4)]
        for h in range(4):
            load_rows(Qst[h][0:64, :], Qb[h], Qb[h][0], FMR['qa'] + 64 * h)
        KE, KEb = P.sbuf("N_KE", [128, NQT, 128], BF16)
        load_rows(KE[0:64, :, :].rearrange("d kt p -> d (kt p)"), [KEb], KEb, FMR['ks'])
        P.dma("pool", KE[64:128, :, :], c["erows"][:, 0:NQT, :], KEb, writes=[KEb])
        kwT, kwTb = load_fm("N_kwT", FMR['kw'])
        Vs, Vsb = load_vext("N_Vs", TMC['vs'], 1)
        Vw, Vwb = load_vext("N_Vw", TMC['vw'], 1)
        gate, gateb = P.sbuf("N_gate", [128, NQT, 12], F32)
        for hf in range(2):
            fg = src.gt(hf)
            for k0 in range(0, NQH, 8):
                k1 = min(NQH, k0 + 8)
                P.dma("sp", gate[:, hf * NQH + k0:hf * NQH + k1, :],
                      (lambda e, fg=fg, k0=k0, k1=k1: fg(e).rearrange("(t p) c -> p t c", p=128)[:, k0:k1, :]),
                      gateb, reads=srcdep, writes=[gateb])
        bandC, bandCb = P.sbuf("N_bandC", [128, 4, 504], F32)
        P.dma("sp", bandC[:], c["bandC"], bandCb, writes=[bandCb])
        bandS, bandSb = P.sbuf("N_bandS", [128, 4, 5, 512], BF16)
        P.dma("pool", bandS[:], c["bandS"], bandSb, writes=[bandSb])
        bandW, bandWb = P.sbuf("N_bandW", [128, 4, 384], BF16)
        P.dma("pool", bandW[:], c["bandW"], bandWb, writes=[bandWb])
        tab31, tab31b = P.sbuf("N_tab31", [128, 4], F32)
        P.dma("sp", tab31[:], c["tab31"], tab31b, writes=[tab31b])
        AB, ABb = P.sbuf("N_AB", [128, 3, 126], F32)
        P.dma("sp", AB[:], c["topk"], ABb, writes=[ABb])
        ovl, ovlb = P.sbuf("N_ovl", [128, 2, 64], F32)
        P.dma("sp", ovl[:], c["overlap"], ovlb, writes=[ovlb])
        kcmpT, kcmpTb = P.sbuf("N_kcmpT", [64, NCW], BF16)
        vcmp, vcmpb = P.sbuf("N_vcmp", [128, NCT, 64], BF16)
        with P.scope():
            kg, kgb = P.sbuf("NC_kg", [128, 64], F32)
            P.dma("sp", kg[:], c["kgain"], kgb, writes=[kgb])
            hidT, hidTb = P.sbuf("NC_hidT", [128, 2, NCW], BF16)
            P.op("pool", lambda e: e.memset(hidT[:], 0.0), writes=[hidTb])
            pw, pwb = P.sbuf("NC_pw", [128, 2], F32)
            kraw, krawb = P.sbuf("NC_kraw", [128, 64], F32)
            ksq, ksqb = P.sbuf("NC_ksq", [128, 64], F32)
            kss, kssb = P.sbuf("NC_kss", [128, 4], F32)
            knb, knbb = P.sbuf("NC_knb", [128, 128], BF16)
            P.op("pool", lambda e: e.memset(knb[:], 0.0), writes=[knbb])
            for which in range(2):
                xT, xTb = load_fm(f"NC_xT{which}", FMR['kc'] if which == 0 else FMR['vc'])
                w1s, w1b = P.sbuf(f"NC_w1{which}", [64, 32, 256], BF16)
                P.dma("pool", w1s[:], c["w1"][which].rearrange("(l d) h -> d l h", d=64), w1b, writes=[w1b])
                w2s, w2b = P.sbuf(f"NC_w2{which}", [128, 2, 64], BF16)
                P.dma("pool", w2s[:], c["w2"][which].rearrange("(j p) d -> p j d", p=128), w2b, writes=[w2b])
                posT, posTb = P.sbuf(f"NC_pos{which}", [64, 32], BF16)
                P.dma("pool", posT[:], c["posT"][which], posTb, writes=[posTb])
                for hh in range(2):
                    f_, f_b = F[hh]
                    for l in range(32):
                        P.op("pe", lambda e, f_=f_, l=l, hh=hh: e.matmul(f_[:, 0:1], lhsT=w1s[:, l, hh * 128:(hh + 1) * 128],
                                                                         rhs=posT[:, l:l + 1], start=(l == 0), stop=(l == 31)),
                             reads=[w1b, posTb], writes=[f_b])
                    P.op("dve", lambda e, f_=f_, hh=hh: e.tensor_copy(out=pw[:, hh:hh + 1], in_=f_[:, 0:1]), reads=[f_b], writes=[pwb])
                ncr = NCP - 1
                for hh in range(2):
                    for c0 in range(0, ncr, 512):
                        cn = min(512, ncr - c0)
                        f_, f_b = F[2 + (hh + c0 // 512) % 2]
                        for l in range(32):
                            P.op("pe", lambda e, f_=f_, l=l, hh=hh, c0=c0, cn=cn: e.matmul(
                                f_[:, 0:cn], lhsT=w1s[:, l, hh * 128:(hh + 1) * 128],
                                rhs=xT[:, 16 * c0 + l:16 * c0 + l + 16 * (cn - 1) + 1:16], start=(l == 0), stop=(l == 31)),
                                reads=[w1b, xTb], writes=[f_b])
                        P.op("act", lambda e, f_=f_, hh=hh, c0=c0, cn=cn: e.activation(
                            out=hidT[:, hh, c0:c0 + cn], in_=f_[:, 0:cn], func=AF.Gelu_apprx_tanh, bias=pw[:, hh:hh + 1]),
                            reads=[f_b, pwb], writes=[hidTb])
                for nt in range(NCT):
                    f_, f_b = F[4 + nt % 2]
                    for hh in range(2):
                        P.op("pe", lambda e, f_=f_, hh=hh, nt=nt: e.matmul(f_[:, 0:64], lhsT=hidT[:, hh, nt * 128:(nt + 1) * 128],
                                                                           rhs=w2s[:, hh, :], start=(hh == 0), stop=(hh == 1)),
                             reads=[hidTb, w2b], writes=[f_b])
                    if which == 1:
                        P.op("act", lambda e, f_=f_, nt=nt: e.copy(out=vcmp[:, nt, :], in_=f_[:, 0:64]), reads=[f_b], writes=[vcmpb])
                    else:
                        P.op("dve", lambda e, f_=f_: e.tensor_copy(out=kraw[:], in_=f_[:, 0:64]), reads=[f_b], writes=[krawb])
                        P.op("act", lambda e: e.activation(out=ksq[:], in_=kraw[:], func=AF.Square, accum_out=kss[:, 0:1]),
                             reads=[krawb], writes=[ksqb, kssb])
                        P.op("dve", lambda e: e.tensor_scalar(out=kss[:, 1:2], in0=kss[:, 0:1], scalar1=1.0 / HD, scalar2=EPS,
                                                              op0=ALU.mult, op1=ALU.add), reads=[kssb], writes=[kssb])
                        P.op("act", lambda e: e.activation(out=kss[:, 2:3], in_=kss[:, 1:2], func=AF.Sqrt), reads=[kssb], writes=[kssb])
                        P.op("dve", lambda e: e.reciprocal(out=kss[:, 3:4], in_=kss[:, 2:3]), reads=[kssb], writes=[kssb])
                        P.op("dve", lambda e: e.scalar_tensor_tensor(out=knb[:, 0:64], in0=kraw[:], scalar=kss[:, 3:4], in1=kg[:],
                                                                     op0=ALU.mult, op1=ALU.mult), reads=[krawb, kssb, kgb], writes=[knbb])
                        P.op("pe", lambda e: e.transpose(Tb[:, 0, :], knb[:], idb[:]), reads=[knbb, idbb], writes=[Tbb])
                        P.op("act", lambda e, nt=nt: e.copy(out=kcmpT[:, nt * 128:(nt + 1) * 128], in_=Tb[0:64, 0, :]),
                             reads=[Tbb], writes=[kcmpTb])
        lg = [P.sbuf(f"N_lg{i}", [128, NCW], F32) for i in range(8)]
        pex = [P.sbuf(f"N_pex{i}", [128, NCW], F32) for i in range(8)]
        pbf = [P.sbuf(f"N_pbf{i}", [128, NCW], BF16) for i in range(8)]
        psacc = [P.sbuf(f"N_psacc{i}", [128, NCW], F32) for i in range(4)]
        pstmp = [P.sbuf(f"N_pstmp{i}", [128, NCW], F32) for i in range(2)]
        pT2 = [P.sbuf(f"N_pT{i}", [128, 4 * NCT, 128], BF16) for i in range(2)]
        psT, psTb = P.sbuf("N_psT", [128, NCT, 128], F32)
        Z = [P.sbuf(f"N_Z{i}", [128, 8], F32) for i in range(2)]
        sc, scb = P.sbuf("N_sc", [128, 64], F32)
        top8, top8b = P.sbuf("N_top8", [128, 8], F32)
        NM, NMb = P.sbuf("N_NM", [128, 128], F32)
        P.op("pool", lambda e: e.memset(NM[:], 0.0), writes=[NMb])
        OA, _ = P.sbuf("N_OA", [128, NQT, 256], F32)
        OAb = [P.buf("N_OAb") for _ in range(NQT)]
        OAo = [P.sbuf(f"N_OAo{i}", [128, 4, 256], BF16) for i in range(2)]
        rz4 = [P.sbuf(f"N_rz4{i}", [128, 4], F32) for i in range(2)]
        tmp4 = [P.sbuf(f"N_tmp4{i}", [128, 4, 64], F32) for i in range(2)]
        assert NCT <= 2

        def cmp_front(qts):
            items = [(i, qt, h) for i, qt in enumerate(qts) for h in range(4)]
            for (i, qt, h) in items:
                L, Lb = F[h]
                qs = slice(qt * 128, (qt + 1) * 128)
                P.op("pe", lambda e, L=L, h=h, i=i, qs=qs: e.matmul(L[:, i * 256:i * 256 + NCW], lhsT=Qst[h][0:64, qs], rhs=kcmpT[:, 0:NCW],
                                                                    start=True, stop=True), reads=[Qb[h][qt // 4], kcmpTb], writes=[Lb])
            for (i, qt, h) in items:
                L, Lb = F[h]
                boff = 8 * (31 - qt)
                lg_, lg_b = lg[i * 4 + h]
                P.op("dve", lambda e, L=L, h=h, i=i, lg_=lg_, boff=boff: e.tensor_tensor(
                    out=lg_[:], in0=L[:, i * 256:i * 256 + NCW], in1=bandC[:, h, boff:boff + NCW], op=ALU.add), reads=[Lb, bandCb], writes=[lg_b])
            for (i, qt, h) in items:
                lg_, lg_b = lg[i * 4 + h]
                px, pxb = pex[i * 4 + h]
                Zt, Ztb = Z[qt % 2]
                P.op("act", lambda e, lg_=lg_, px=px, Zt=Zt, h=h: e.activation(out=px[:], in_=lg_[:], func=AF.Exp, accum_out=Zt[:, h:h + 1]),
                     reads=[lg_b], writes=[pxb, Ztb])
            for qt in qts:
                Zt, Ztb = Z[qt % 2]
                P.op("dve", lambda e, Zt=Zt: e.tensor_scalar(out=Zt[:, 4:8], in0=Zt[:, 0:4], scalar1=1e-30, scalar2=None, op0=ALU.add),
                     reads=[Ztb], writes=[Ztb])
            for qt in qts:
                Zt, Ztb = Z[qt % 2]
                P.op("dve", lambda e, Zt=Zt: e.reciprocal(out=Zt[:, 4:8], in_=Zt[:, 4:8]), reads=[Ztb], writes=[Ztb])
            for (i, qt, h) in items:
                px, pxb = pex[i * 4 + h]
                Zt, Ztb = Z[qt % 2]
                P.op("dve", lambda e, px=px, Zt=Zt, h=h: e.tensor_scalar(out=px[:], in0=px[:], scalar1=Zt[:, 4 + h:5 + h], scalar2=None, op0=ALU.mult),
                     reads=[pxb, Ztb], writes=[pxb])
            for (i, qt, h) in items:
                px, pxb = pex[i * 4 + h]
                pb_, pb_b = pbf[i * 4 + h]
                P.op("act", lambda e, px=px, pb_=pb_: e.copy(out=pb_[:], in_=px[:]), reads=[pxb], writes=[pb_b])
            for i, qt in enumerate(qts):
                ps, psb = psacc[qt % 4]
                pt_, pt_b = pstmp[qt % 2]
                p0, p1, p2, p3 = [pex[i * 4 + h] for h in range(4)]
                P.op("pool", lambda e, ps=ps, p0=p0, p1=p1: e.tensor_tensor(out=ps[:], in0=p0[0][:], in1=p1[0][:], op=ALU.add),
                     reads=[p0[1], p1[1]], writes=[psb])
                P.op("pool", lambda e, pt_=pt_, p2=p2, p3=p3: e.tensor_tensor(out=pt_[:], in0=p2[0][:], in1=p3[0][:], op=ALU.add),
                     reads=[p2[1], p3[1]], writes=[pt_b])
                P.op("pool", lambda e, ps=ps, pt_=pt_: e.tensor_tensor(out=ps[:], in0=ps[:], in1=pt_[:], op=ALU.add), reads=[psb, pt_b], writes=[psb])
            for i, qt in enumerate(qts):
                pT_, pT_b = pT2[i]
                for h in range(4):
                    pb_, pb_b = pbf[i * 4 + h]
                    for nt in range(NCT):
                        P.op("pe", lambda e, h=h, nt=nt, pb_=pb_: e.transpose(Tb[:, h * NCT + nt, :], pb_[:, nt * 128:(nt + 1) * 128], idb[:]),
                             reads=[pb_b, idbb], writes=[Tbb])
                P.op("act", lambda e, pT_=pT_: e.copy(out=pT_[:], in_=Tb[:, 0:4 * NCT, :]), reads=[Tbb], writes=[pT_b])
            oc, ocb = F[4]
            for i, qt in enumerate(qts):
                pT_, pT_b = pT2[i]
                for h in range(4):
                    for nt in range(NCT):
                        P.op("pe", lambda e, h=h, nt=nt, i=i, pT_=pT_: e.matmul(
                            oc[:, i * 256 + h * 64:i * 256 + (h + 1) * 64], lhsT=pT_[:, h * NCT + nt, :], rhs=vcmp[:, nt, :],
                            start=(nt == 0), stop=(nt == NCT - 1)), reads=[pT_b, vcmpb], writes=[ocb])
            for i, qt in enumerate(qts):
                P.op("dve", lambda e, i=i, qt=qt: e.tensor_tensor(
                    out=OA[:, qt, :].rearrange("p (h d) -> p h d", h=4), in0=oc[:, i * 256:(i + 1) * 256].rearrange("p (h d) -> p h d", h=4),
                    in1=gate[:, qt, 0:12:3].unsqueeze(2).to_broadcast([128, 4, 64]), op=ALU.mult), reads=[ocb, gateb], writes=[OAb[qt]])

        def cmp_back(qt):
            qs = slice(qt * 128, (qt + 1) * 128)
            qg = qt // 4
            ps, psb = psacc[qt % 4]
            f5, f5b = F[5]
            for nt in range(NCT):
                P.op("pe", lambda e, nt=nt: e.transpose(f5[:, nt * 128:(nt + 1) * 128], ps[:, nt * 128:(nt + 1) * 128], idf[:]),
                     reads=[psb, idfb], writes=[f5b])
            P.op("act", lambda e: e.copy(out=psT[:].rearrange("p a b -> p (a b)"), in_=f5[:, 0:NCW]), reads=[f5b], writes=[psTb])
            f6, f6b = F[6]
            for nt in range(NCT):
                P.op("pe", lambda e, nt=nt: e.matmul(f6[:, 0:64], lhsT=psT[:, nt, :], rhs=ovl[:, nt, :], start=(nt == 0), stop=(nt == NCT - 1)),
                     reads=[psTb, ovlb], writes=[f6b])
            s0 = 62 - 2 * qt
            P.op("dve", lambda e: e.tensor_tensor(out=sc[:], in0=f6[:, 0:64], in1=AB[:, 0, s0:s0 + 64], op=ALU.mult), reads=[f6b, ABb], writes=[scb])
            P.op("dve", lambda e: e.tensor_tensor(out=sc[:], in0=sc[:], in1=AB[:, 1, s0:s0 + 64], op=ALU.add), reads=[scb, ABb], writes=[scb])
            P.op("dve", lambda e: e.memset(sc[:, 0:1], 1e9), writes=[scb])
            P.op("dve", lambda e: e.max(out=top8[:], in_=sc[:]), reads=[scb], writes=[top8b])
            P.op("dve", lambda e: e.tensor_scalar(out=sc[:], in0=sc[:], scalar1=top8[:, 7:8], scalar2=None, op0=ALU.is_ge),
                 reads=[scb, top8b], writes=[scb])
            P.op("dve", lambda e: e.tensor_tensor(out=sc[:], in0=sc[:], in1=AB[:, 2, s0:s0 + 64], op=ALU.mult), reads=[scb, ABb], writes=[scb])
            P.op("dve", lambda e: e.tensor_scalar(out=NM[:, 64:128], in0=sc[:], scalar1=-1.0, scalar2=-NEG, op0=ALU.add, op1=ALU.mult),
                 reads=[scb], writes=[NMb])
            P.op("pe", lambda e: e.transpose(f6[:, 128:256], NM[:], idf[:]), reads=[NMb, idfb], writes=[f6b])
            P.op("act", lambda e: e.copy(out=Qst[0][64:128, qs], in_=f6[64:128, 128:256]), reads=[f6b], writes=[Qb[0][qg]])
            for h in range(1, 4):
                P.op("pool", lambda e, h=h: e.tensor_copy(out=Qst[h][64:128, qs], in_=Qst[0][64:128, qs]), reads=[Qb[0][qg]], writes=[Qb[h][qg]])

        cmp_front([0, 1])
        for p_ in range(NQT // 2):
            if p_ + 1 < NQT // 2:
                cmp_front([2 * p_ + 2, 2 * p_ + 3])
            cmp_back(2 * p_)
            cmp_back(2 * p_ + 1)

        def finish4(accp, accpb, qt, gcol, n, sink=None, sinkb=None, dst=None, dstb=None):
            rz, rzb = rz4[n % 2]
            a3 = accp[:, 0:260].rearrange("p (h d) -> p h d", d=65)
            if sink is not None:
                P.op("dve", lambda e: e.tensor_tensor(out=rz[:], in0=a3[:, :, 64], in1=sink, op=ALU.add), reads=[accpb, sinkb], writes=[rzb])
                P.op("dve", lambda e: e.reciprocal(out=rz[:], in_=rz[:]), reads=[rzb], writes=[rzb])
            else:
                P.op("dve", lambda e: e.reciprocal(out=rz[:], in_=a3[:, :, 64]), reads=[accpb], writes=[rzb])
            if gcol is not None:
                P.op("dve", lambda e: e.tensor_tensor(out=rz[:], in0=rz[:], in1=gate[:, qt, gcol:12:3], op=ALU.mult), reads=[rzb, gateb], writes=[rzb])
            if dst is None:
                tm_, tm_b = tmp4[n % 2]
                P.op("dve", lambda e: e.tensor_tensor(out=tm_[:], in0=a3[:, :, 0:64], in1=rz[:].unsqueeze(2).to_broadcast([128, 4, 64]), op=ALU.mult),
                     reads=[accpb, rzb], writes=[tm_b])
                P.op("pool", lambda e: e.tensor_tensor(out=OA[:, qt, :], in0=OA[:, qt, :], in1=tm_[:].rearrange("p h d -> p (h d)"), op=ALU.add),
                     reads=[tm_b, OAb[qt]], writes=[OAb[qt]])
            else:
                P.op("dve", lambda e: e.tensor_tensor(out=dst, in0=a3[:, :, 0:64], in1=rz[:].unsqueeze(2).to_broadcast([128, 4, 64]), op=ALU.mult),
                     reads=[accpb, rzb], writes=[dstb])

        nfin = 0
        for qg in range(NQG):
            qcs = slice(qg * 512, (qg + 1) * 512)
            nkt = 4 * qg + 4
            for h in range(4):
                def emit_qk(kt, h=h):
                    S, Sb = SB3[kt % 3]
                    m = kt - 4 * qg + 1
                    if m >= 0:
                        P.op("pe", lambda e, S=S, m=m: e.matmul(S[:], lhsT=idb[:], rhs=bandS[:, h, m, :], start=True, stop=False),
                             reads=[idbb, bandSb], writes=[Sb])
                        P.op("pe", lambda e, S=S, kt=kt: e.matmul(S[:], lhsT=KE[:, kt, :], rhs=Qst[h][:, qcs], start=False, stop=True),
                             reads=[KEb, Qb[h][qg]], writes=[Sb])
                    else:
                        P.op("pe", lambda e, S=S, kt=kt: e.matmul(S[:], lhsT=KE[:, kt, :], rhs=Qst[h][:, qcs], start=True, stop=True),
                             reads=[KEb, Qb[h][qg]], writes=[Sb])
                emit_qk(0)
                if nkt > 1:
                    emit_qk(1)
                for kt in range(nkt):
                    if kt + 2 < nkt:
                        emit_qk(kt + 2)
                    S, Sb = SB3[kt % 3]
                    m = kt - 4 * qg + 1
                    pt, ptb = PT[kt % 3]
                    bias_ap = zero1[:, 0:1] if m >= 0 else tab31[:, h:h + 1]
                    P.op("act", lambda e, S=S, pt=pt, bias_ap=bias_ap: e.activation(out=pt[:], in_=S[:], func=AF.Exp, bias=bias_ap),
                         reads=[Sb, tab31b, zero1b], writes=[ptb])
                    for qi in range(4):
                        if kt > 4 * qg + qi:
                            continue
                        acc, accb = F[2 + qi]
                        P.op("pe", lambda e, acc=acc, pt=pt, qi=qi, kt=kt: e.matmul(
                            acc[:, 0:65], lhsT=pt[:, qi * 128:(qi + 1) * 128], rhs=Vs[:, kt, 0, :], start=(kt == 0), stop=(kt == 4 * qg + qi)),
                            reads=[ptb, Vsb], writes=[accb])
                for qi in range(4):
                    acc, accb = F[2 + qi]
                    qt = 4 * qg + qi
                    rz, rzb = rzs[nrz[0] % 4]
                    nrz[0] += 1
                    finish_head(acc, accb, OA[:, qt, h * 64:(h + 1) * 64], [], [OAb[qt]], accumulate=True,
                                gate_ap=gate[:, qt, 3 * h + 1:3 * h + 2], gate_b=gateb, rz=rz, rzb=rzb)
        for qt in range(NQT):
            qs = slice(qt * 128, (qt + 1) * 128)
            qg = qt // 4
            rr = [r for r in range(3) if qt - 2 + r >= 0]
            for h in range(4):
                S, Sb = F[h]
                P.op("pe", lambda e, S=S, h=h: e.matmul(S[:, 0:384], lhsT=idb[:], rhs=bandW[:, h, :], start=True, stop=False),
                     reads=[idbb, bandWb], writes=[Sb])
                for r in rr:
                    kt = qt - 2 + r
                    P.op("pe", lambda e, S=S, h=h, r=r, kt=kt, last=(r == rr[-1]): e.matmul(
                        S[:, r * 128:(r + 1) * 128], lhsT=kwT[:, kt * 128:(kt + 1) * 128], rhs=Qst[h][0:64, qs], start=False, stop=last),
                        reads=[kwTb, Qb[h][qg]], writes=[Sb])
            for h in range(4):
                S, Sb = F[h]
                pt, ptb = PT4[h]
                P.op("act", lambda e, S=S, pt=pt: e.activation(out=pt[:, 0:384], in_=S[:, 0:384], func=AF.Exp), reads=[Sb], writes=[ptb])
            accp, accpb = F[4 + qt % 2]
            for h in range(4):
                pt, ptb = PT4[h]
                for r in rr:
                    kt = qt - 2 + r
                    P.op("pe", lambda e, pt=pt, h=h, r=r, kt=kt, first=(r == rr[0]), last=(r == rr[-1]): e.matmul(
                        accp[:, h * 65:(h + 1) * 65], lhsT=pt[:, r * 128:(r + 1) * 128], rhs=Vw[:, kt, 0, :], start=first, stop=last),
                        reads=[ptb, Vwb], writes=[accpb])
            finish4(accp, accpb, qt, 2, qt)
            if qt % 4 == 3:
                oo, oob = OAo[qg % 2]
                P.op("act", lambda e, oo=oo: e.copy(out=oo[:], in_=OA[:, 4 * qg:4 * qg + 4, :]), reads=[OAb[4 * qg + i] for i in range(4)], writes=[oob])
                P.dma("sp", obdst(qg, 0, 256).rearrange("(t p) c -> p t c", p=128), oo[:], oob, reads=[oob], accw=[obuf(qg)])

    with P.scope():
        qT = [load_fm(f"S_q{h}", FMR['qb'] + 64 * h) for h in range(4)]
        kTs, kTsb = load_fm("S_k", FMR['kb'])
        Vb, Vbb = load_vext("S_V", TMC['vb'], 1)
        bandB, bandBb = P.sbuf("S_band", [128, 4, 256], BF16)
        P.dma("pool", bandB[:], c["bandB"], bandBb, writes=[bandBb])
        esk, eskb = P.sbuf("S_esk", [128, 4], F32)
        P.dma("sp", esk[:], c["sinkb"], eskb, writes=[eskb])
        P.op("act", lambda e: e.activation(out=esk[:], in_=esk[:], func=AF.Exp), reads=[eskb], writes=[eskb])
        OS = [P.sbuf(f"S_O{i}", [128, 4, 256], BF16) for i in range(2)]
        rz4 = [P.sbuf(f"S_rz4{i}", [128, 4], F32) for i in range(2)]
        for qt in range(NQT):
            qs = slice(qt * 128, (qt + 1) * 128)
            os_, osb = OS[(qt // 4) % 2]
            rr = [r for r in range(2) if qt - 1 + r >= 0]
            for h in range(4):
                S, Sb = F[h]
                P.op("pe", lambda e, S=S, h=h: e.matmul(S[:, 0:256], lhsT=idb[:], rhs=bandB[:, h, :], start=True, stop=False),
                     reads=[idbb, bandBb], writes=[Sb])
                for r in rr:
                    kt = qt - 1 + r
                    P.op("pe", lambda e, S=S, h=h, r=r, kt=kt, last=(r == rr[-1]): e.matmul(
                        S[:, r * 128:(r + 1) * 128], lhsT=kTs[:, kt * 128:(kt + 1) * 128], rhs=qT[h][0][:, qs], start=False, stop=last),
                        reads=[kTsb, qT[h][1]], writes=[Sb])
            for h in range(4):
                S, Sb = F[h]
                pt, ptb = PT4[h]
                P.op("act", lambda e, S=S, pt=pt: e.activation(out=pt[:, 0:256], in_=S[:, 0:256], func=AF.Exp), reads=[Sb], writes=[ptb])
            accp, accpb = F[4 + qt % 2]
            for h in range(4):
                pt, ptb = PT4[h]
                for r in rr:
                    kt = qt - 1 + r
                    P.op("pe", lambda e, pt=pt, h=h, r=r, kt=kt, first=(r == rr[0]), last=(r == rr[-1]): e.matmul(
                        accp[:, h * 65:(h + 1) * 65], lhsT=pt[:, r * 128:(r + 1) * 128], rhs=Vb[:, kt, 0, :], start=first, stop=last),
                        reads=[ptb, Vbb], writes=[accpb])
            rz, rzb = rz4[qt % 2]
            a3 = accp[:, 0:260].rearrange("p (h d) -> p h d", d=65)
            P.op("dve", lambda e, rz=rz, a3=a3: e.tensor_tensor(out=rz[:], in0=a3[:, :, 64], in1=esk[:], op=ALU.add), reads=[accpb, eskb], writes=[rzb])
            P.op("dve", lambda e, rz=rz: e.reciprocal(out=rz[:], in_=rz[:]), reads=[rzb], writes=[rzb])
            P.op("dve", lambda e, rz=rz, a3=a3, os_=os_: e.tensor_tensor(
                out=os_[:, qt % 4, :].rearrange("p (h d) -> p h d", h=4), in0=a3[:, :, 0:64],
                in1=rz[:].unsqueeze(2).to_broadcast([128, 4, 64]), op=ALU.mult), reads=[accpb, rzb], writes=[osb])
            if qt % 4 == 3:
                qg = qt // 4
                P.dma("sp", obdst(qg, 256, 512).rearrange("(t p) c -> p t c", p=128), os_[:], osb, reads=[osb], accw=[obuf(qg)])

    with P.scope():
        qT = [load_fm(f"X_q{h}", FMR['qc'] + 64 * h) for h in range(4)]
        kT = [load_fm(f"X_k{h}", FMR['kcf'] + 64 * h) for h in range(4)]
        Vc, Vcb = load_vext("X_V", TMC['vcf'], 4)
        bandF, bandFb = P.sbuf("X_band", [128, 4, 512], BF16)
        P.dma("pool", bandF[:], c["bandF"], bandFb, writes=[bandFb])
        cf, cfb = P.sbuf("X_cf", [4, TB], F32)
        for hf in range(2):
            P.dma("sp", cf[:, hf * TH:(hf + 1) * TH], src.cf(hf), cfb, reads=srcdep, writes=[cfb])
        fbt, fbb = P.sbuf("X_fb", [4, 1], F32)
        P.dma("sp", fbt[:], c["fb"], fbb, writes=[fbb])
        ones4, ones4b = P.sbuf("X_ones", [4, TB], F32)
        P.op("pool", lambda e: e.memset(ones4[:], 1.0), writes=[ones4b])
        cpos, cposb = P.sbuf("X_cpos", [4, TB], F32)
        P.op("dve", lambda e: e.tensor_scalar(out=fbt[:], in0=fbt[:], scalar1=-1.0, scalar2=None, op0=ALU.mult), reads=[fbb], writes=[fbb])
        P.op("act", lambda e: e.activation(out=cf[:], in_=cf[:], func=AF.Exp, bias=fbt[:, 0:1], scale=-1.0), reads=[cfb, fbb], writes=[cfb])
        P.op("act", lambda e: e.activation(out=cf[:], in_=cf[:], func=AF.Ln, bias=1.0), reads=[cfb], writes=[cfb])
        P.op("dve", lambda e: e.tensor_tensor_scan(out=cpos[:], data0=ones4[:], data1=cf[:], initial=0.0, op0=ALU.mult, op1=ALU.add),
             reads=[cfb, ones4b], writes=[cposb])
        cposT, cposTb = P.sbuf("X_cposT", [128, NQT, 4], F32)
        f6, f6b = F[6]
        for kt in range(NQT):
            P.op("pe", lambda e, kt=kt: e.transpose(f6[:, kt * 4:(kt + 1) * 4], cpos[:, kt * 128:(kt + 1) * 128], idf[0:4, 0:4]),
                 reads=[cposb, idfb], writes=[f6b])
        P.op("dve", lambda e: e.tensor_copy(out=cposT[:].rearrange("p a b -> p (a b)"), in_=f6[:, 0:NQT * 4]), reads=[f6b], writes=[cposTb])
        Dm, Dmb = P.sbuf("X_D", [4, 4, NQG], F32)
        P.op("dve", lambda e: e.tensor_tensor(out=Dm[:], in0=cpos[:, 0:TB:512].unsqueeze(1).to_broadcast([4, 4, NQG]),
                                              in1=idf[0:4, 0:4].unsqueeze(2).to_broadcast([4, 4, NQG]), op=ALU.mult),
             reads=[cposb, idfb], writes=[Dmb])
        P.op("pool", lambda e: e.memset(ones4[:, 0:128], 1.0), writes=[ones4b])
        f5, f5b = F[5]
        P.op("pe", lambda e: e.matmul(f5[:, 0:4 * NQG], lhsT=ones4[:, 0:128], rhs=Dm[:].rearrange("k h q -> k (h q)"), start=True, stop=True),
             reads=[ones4b, Dmb], writes=[f5b])
        crefb, crefbb = P.sbuf("X_cref", [128, 4, NQG], F32)
        P.op("dve", lambda e: e.tensor_copy(out=crefb[:].rearrange("p a b -> p (a b)"), in_=f5[:, 0:4 * NQG]), reads=[f5b], writes=[crefbb])
        bvec = [P.sbuf(f"X_bv{i}", [128, NQT], F32) for i in range(2)]
        OX = [P.sbuf(f"X_O{i}", [128, 4, 256], BF16) for i in range(2)]
        n = 0
        for qg in range(NQG):
            ox, oxb = OX[qg % 2]
            qcs = slice(qg * 512, (qg + 1) * 512)
            nkt = 4 * qg + 4
            for h in range(4):
                bv, bvb = bvec[h % 2]
                P.op("dve", lambda e, bv=bv, h=h, qg=qg, nkt=nkt: e.tensor_scalar(
                    out=bv[:, 0:nkt], in0=cposT[:, 0:nkt, h], scalar1=crefb[:, h, qg:qg + 1], scalar2=None, op0=ALU.subtract),
                    reads=[cposTb, crefbb], writes=[bvb])

                def emit_qk(kt, h=h):
                    S, Sb = SB3[kt % 3]
                    m = kt - 4 * qg + 1
                    if m >= 1:
                        P.op("pe", lambda e, S=S, m=m: e.matmul(S[:], lhsT=idb[:], rhs=bandF[:, m - 1, :], start=True, stop=False),
                             reads=[idbb, bandFb], writes=[Sb])
                        P.op("pe", lambda e, S=S, kt=kt: e.matmul(S[:], lhsT=kT[h][0][:, kt * 128:(kt + 1) * 128], rhs=qT[h][0][:, qcs],
                                                                  start=False, stop=True), reads=[kT[h][1], qT[h][1]], writes=[Sb])
                    else:
                        P.op("pe", lambda e, S=S, kt=kt: e.matmul(S[:], lhsT=kT[h][0][:, kt * 128:(kt + 1) * 128], rhs=qT[h][0][:, qcs],
                                                                  start=True, stop=True), reads=[kT[h][1], qT[h][1]], writes=[Sb])
                emit_qk(0)
                if nkt > 1:
                    emit_qk(1)
                for kt in range(nkt):
                    if kt + 2 < nkt:
                        emit_qk(kt + 2)
                    S, Sb = SB3[kt % 3]
                    pt, ptb = PT[kt % 3]
                    P.op("act", lambda e, S=S, pt=pt, bv=bv, kt=kt: e.activation(out=pt[:], in_=S[:], func=AF.Exp, bias=bv[:, kt:kt + 1]),
                         reads=[Sb, bvb], writes=[ptb])
                    for qi in range(4):
                        if kt > 4 * qg + qi:
                            continue
                        acc, accb = F[2 + qi]
                        P.op("pe", lambda e, acc=acc, pt=pt, qi=qi, kt=kt, h=h: e.matmul(
                            acc[:, 0:65], lhsT=pt[:, qi * 128:(qi + 1) * 128], rhs=Vc[:, kt, h, :], start=(kt == 0), stop=(kt == 4 * qg + qi)),
                            reads=[ptb, Vcb], writes=[accb])
                for qi in range(4):
                    acc, accb = F[2 + qi]
                    rz, rzb = rzs[n % 4]
                    n += 1
                    finish_head(acc, accb, ox[:, qi, h * 64:(h + 1) * 64], [], [oxb], rz=rz, rzb=rzb)
            P.dma("sp", obdst(qg, 512, 768).rearrange("(t p) c -> p t c", p=128), ox[:], oxb, reads=[oxb], accw=[obuf(qg)])
            if on_ob is not None:
                on_ob(qg)
    return dout


def _t5_bucket(d):
    d = np.maximum(d, 0)
    lr = np.log(np.maximum(d, 1).astype(np.float32) / np.float32(16)) / np.float32(math.log(128 / 16))
    large = np.minimum(16 + (lr * np.float32(16)).astype(np.int32), 31)
    return np.where(d < 16, d, large)


def _bias_lookup(tabcol, d, valid):
    out = np.full(d.shape, NEG, np.float32)
    b = _t5_bucket(d)
    out[valid] = tabcol[b[valid]]
    return out


def build_B(TB=4096):
    nc = bass.Bass("TRN2", target_bir_lowering=False)
    dt_ = lambda n, s, d=F32, k="ExternalInput": nc.dram_tensor(n, s, d, kind=k).ap()
    FMB = dt_("FMB", [NFMG, TB], BF16)
    TMB = dt_("TMB", [TB, NTMG], BF16)
    GTB = dt_("GTB", [TB, 12])
    CFB = dt_("CFB", [4, TB])
    c = dict(
        ident=dt_("ident", [128, 128]), erows=dt_("erows", [64, 32, 128]),
        w1=dt_("w1", [2, 2048, 256]), w2=dt_("w2", [2, 256, 64]), posT=dt_("posT", [2, 64, 32]),
        kgain=dt_("kgain", [128, 64]), bandC=dt_("bandC", [128, 4, 504]), bandS=dt_("bandS", [128, 4, 5, 512]),
        bandW=dt_("bandW", [128, 4, 384]), bandB=dt_("bandB", [128, 4, 256]), bandF=dt_("bandF", [128, 4, 512]),
        tab31=dt_("tab31", [128, 4]), topk=dt_("topk", [128, 3, 126]), overlap=dt_("overlap", [128, 2, 64]),
        sinkb=dt_("sinkb", [128, 4]), fb=dt_("fb", [4, 1]),
    )
    OB = dt_("OB", [TB, 768], BF16, "ExternalOutput")
    with ExitStack() as es:
        P = Prog(nc, es)
        dout = phase_B(P, TB, SrcStatic(TB, FMB, TMB, GTB, CFB), c, OB)
        P.finish([dout])
    return nc


def prep_B_consts():
    p = np.arange(128)
    erows = np.zeros((64, 32, 128), np.float32)
    for kt in range(32):
        erows[2 * kt, kt, 0:64] = 1.0
        erows[2 * kt + 1, kt, 64:128] = 1.0
    rel = np.arange(126)[None, :] - 62
    curr = (p[:, None] >= 64).astype(np.int64)
    valid = rel <= curr
    forced = (rel == curr) | (rel == curr - 1)
    A = (valid & ~forced).astype(np.float32)
    Bm = np.where(~valid, np.float32(-1e30), np.where(forced, np.float32(1e9), np.float32(0.0))).astype(np.float32)
    V = valid.astype(np.float32)
    topk = np.stack([A, Bm, V], axis=1).astype(np.float32)
    n = np.arange(256)
    j = np.arange(64)
    ov = ((16 * n[:, None] < 64 * j[None, :] + 64) & (16 * n[:, None] + 32 > 64 * j[None, :])).astype(np.float32)
    ov[255] = 0.0
    overlap = np.ascontiguousarray(ov.reshape(2, 128, 64).transpose(1, 0, 2))
    i = np.arange(512)
    bandF = np.zeros((128, 4, 512), np.float32)
    for m in range(1, 5):
        d = i[None, :] - 128 * (m - 1) - p[:, None]
        bandF[:, m - 1, :] = np.where(d >= 0, 0.0, NEG)
    return dict(ident=np.eye(128, dtype=np.float32), erows=erows, topk=topk, overlap=overlap, bandF=bandF)


def prep_B(inp, l, g, consts):
    tabA = inp["rel_bias"][:, 4 * g:4 * g + 4]
    tabB = inp["rel_bias"][:, 8 + 4 * g:8 + 4 * g + 4]
    p = np.arange(128)
    bandC = np.zeros((128, 4, 504), np.float32)
    m = np.arange(504)
    dC = p[:, None] - 16 * (m[None, :] - 248) - 31
    i = np.arange(512)
    bandS = np.zeros((128, 4, 5, 512), np.float32)
    bandW = np.zeros((128, 4, 384), np.float32)
    bandB = np.zeros((128, 4, 256), np.float32)
    q = np.arange(128)
    for h in range(4):
        bandC[:, h, :] = _bias_lookup(tabA[:, h], dC, dC >= 0)
        for mm in range(5):
            d = i[None, :] - 128 * (mm - 1) - p[:, None]
            bandS[:, h, mm, :] = _bias_lookup(tabA[:, h], d, d >= 0)
        for r in range(3):
            d = (2 - r) * 128 + q[None, :] - p[:, None]
            bandW[:, h, r * 128:(r + 1) * 128] = _bias_lookup(tabA[:, h], d, (d >= 0) & (d < 256))
        for r in range(2):
            d = (1 - r) * 128 + q[None, :] - p[:, None]
            bandB[:, h, r * 128:(r + 1) * 128] = _bias_lookup(tabB[:, h], d, (d >= 0) & (d < 128))
    out = dict(consts)
    out.update(
        w1=inp["cmp_w1"][l], w2=inp["cmp_w2"][l],
        posT=np.ascontiguousarray(inp["cmp_pos"][l].transpose(0, 2, 1)),
        kgain=np.ascontiguousarray(np.broadcast_to(inp["qk_gain"][l, 1][None, :], (128, 64))),
        bandC=bandC, bandS=bandS, bandW=bandW, bandB=bandB,
        tab31=np.ascontiguousarray(np.broadcast_to(tabA[31][None, :], (128, 4))),
        sinkb=np.ascontiguousarray(np.broadcast_to(inp["sinks"][l, 4 * g:4 * g + 4][None, :], (128, 4))),
        fb=np.ascontiguousarray(inp["forget_bias"][l, 4 * g:4 * g + 4].reshape(4, 1)),
    )
    return out


_NC_CACHE = {}


def _get_nc(name):
    if name not in _NC_CACHE:
        _NC_CACHE[name] = dict(A=lambda: build_A(2048), B=lambda: build_B(4096), C=lambda: build_C(2176))[name]()
    return _NC_CACHE[name]


def _run(nc, in_maps):
    res = run_bass_kernel_spmd(nc, in_maps, core_ids=list(range(8)))
    return res.results


def kernel_X(**inp):
    inp = {k: np.asarray(v) for k, v in inp.items()}
    x = np.ascontiguousarray(inp["x"], dtype=np.float32)
    consts = prep_B_consts()
    xcur = x.reshape(8, 2048, D)
    for l in range(DEPTH):
        cA = prep_A(inp, l)
        resA = _run(_get_nc("A"), [dict(cA, x=np.ascontiguousarray(xcur[c])) for c in range(8)])
        pB = [prep_B(inp, l, g, consts) for g in range(2)]
        inB = []
        for b in range(NB):
            r0, r1 = resA[2 * b], resA[2 * b + 1]
            FM = np.concatenate([np.asarray(r0["FM"]), np.asarray(r1["FM"])], axis=1)
            TM = np.concatenate([np.asarray(r0["TM"]), np.asarray(r1["TM"])], axis=0)
            GT = np.concatenate([np.asarray(r0["GT"]), np.asarray(r1["GT"])], axis=0)
            CF = np.concatenate([np.asarray(r0["CF"]), np.asarray(r1["CF"])], axis=1)
            for g in range(2):
                inB.append(dict(pB[g],
                                FMB=np.ascontiguousarray(FM[g * NFMG:(g + 1) * NFMG]),
                                TMB=np.ascontiguousarray(TM[:, g * NTMG:(g + 1) * NTMG]),
                                GTB=np.ascontiguousarray(GT[:, 12 * g:12 * g + 12]),
                                CFB=np.ascontiguousarray(CF[4 * g:4 * g + 4])))
        resB = _run(_get_nc("B"), inB)
        cC = prep_C(inp, l)
        inC = []
        for b in range(NB):
            o0, o1 = np.asarray(resB[2 * b]["OB"]), np.asarray(resB[2 * b + 1]["OB"])
            og = np.concatenate([o0[:, 0:256], o1[:, 0:256], o0[:, 256:512], o1[:, 256:512], o0[:, 512:768], o1[:, 512:768]], axis=1)
            xb = np.concatenate([xcur[2 * b], xcur[2 * b + 1]], axis=0)
            for half in range(2):
                if half == 0:
                    xh = np.concatenate([np.zeros((128, D), np.float32), xb[0:2048]], axis=0)
                    oh = np.concatenate([np.zeros((128, 1536), og.dtype), og[0:2048]], axis=0)
                else:
                    xh = xb[1920:4096]
                    oh = og[1920:4096]
                inC.append(dict(cC, xh=np.ascontiguousarray(xh), og=np.ascontiguousarray(oh)))
        resC = _run(_get_nc("C"), inC)
        xcur = np.stack([np.asarray(resC[c]["xo"]) for c in range(8)], axis=0)
    return np.ascontiguousarray(xcur.reshape(NB, T, D).astype(np.float32))


NSA_ROWS = NFM + NTM * 2048 // 2048
assert NSA_ROWS == 3584


class SrcChunks:
    def __init__(self, L, LGT, LCF):
        self.L, self.LGT, self.LCF = L, LGT, LCF

    def fm_all(self, r0):
        return lambda e: self.L[:, :, r0:r0 + 64, :].rearrange("a k p t -> p (a k) t")

    def tm_all(self):
        return [(2 * (8 * hf + k), 2, (lambda e, hf=hf, k=k: self.L[hf, k, 1344:1792, :].rearrange("r x -> (r x)").rearrange(
            "(kt p c) -> p kt c", p=128, c=NTMG))) for hf in range(2) for k in range(8)]

    def gt(self, hf):
        return lambda e: self.LGT[hf, :, :]

    def cf(self, hf):
        return lambda e: self.LCF[hf, :, :]


def build_Z(sample, ncores=8):
    nc = bass.Bass("TRN2", target_bir_lowering=False)
    I = {}
    for k, v in sample.items():
        dt = BF16 if v.dtype == NPBF else F32
        I[k] = nc.dram_tensor(k, list(v.shape), dt, kind="ExternalInput").ap()
    out = nc.dram_tensor("out", [2048, D], F32, kind="ExternalOutput").ap()
    CH = 3584
    Sk = [nc.dram_tensor("Sk%d" % k, [CH, 256], BF16).ap() for k in range(8)]
    Gk = [nc.dram_tensor("Gk%d" % k, [2 * CH, 256], BF16).ap() for k in range(8)]
    L = nc.dram_tensor("L", [2, 8, 1792, 256], BF16).ap()
    SAf = nc.dram_tensor("SAf", [32, 2048], F32).ap()
    GAf = nc.dram_tensor("GAf", [64, 2048], F32).ap()
    LGT = nc.dram_tensor("LGT", [2, 2048, 12], F32).ap()
    LCF = nc.dram_tensor("LCF", [2, 4, 2048], F32).ap()
    SBk = [nc.dram_tensor("SBk%d" % k, [1024, 768], BF16).ap() for k in range(4)]
    GBk = [nc.dram_tensor("GBk%d" % k, [2048, 768], BF16).ap() for k in range(4)]
    LO = nc.dram_tensor("LO", [2, 2176, 768], BF16).ap()
    X1 = nc.dram_tensor("X1", [2176, D], F32).ap()
    SX = nc.dram_tensor("SX", [128, D], F32).ap()
    GX = nc.dram_tensor("GX", [256, D], F32).ap()
    XM = nc.dram_tensor("XM", [2176, D], F32).ap()
    rg = [[i, i + 1] for i in range(0, ncores, 2)]
    GTv = [SAf[16 * g + 4:16 * g + 16, :].rearrange("r x -> (r x)").rearrange("(t c) -> t c", c=12) for g in range(2)]
    with ExitStack() as es:
        P = Prog(nc, es)
        P.need_rank = True
        bSA, bGA, bSB, bGB, bX1, bSX, bGX, bOut, bLA, bLO = [P.buf(n) for n in ("SA", "GA", "SB", "GB", "X1", "SX", "GX", "OUT", "LA", "LO")]
        rk = lambda: P.rank["sp"]
        bSk = [P.buf("Sk") for _ in range(8)]
        bGk = [P.buf("Gk") for _ in range(8)]
        bSBk = [P.buf("SBk") for _ in range(4)]
        bGBk = [P.buf("GBk") for _ in range(4)]
        for l in range(DEPTH):
            sfx = "_%d" % l
            xin = I["xh"] if l == 0 else X1
            xdep = [] if l == 0 else [bX1]
            def extract(k):
                for hf in range(2):
                    P.dma("sp", L[hf, k], (lambda e, hf=hf, k=k: Gk[k][bass.ds(hf * CH + rk() * 1792, 1792), :]), bLA,
                          reads=[bGk[k]], accw=[bLA])

            def on_chunk(k):
                P.coll("AllGather", [Sk[k]], [Gk[k]], rg, reads=[bSk[k]], writes=[bGk[k]])

            with P.scope():
                phase_A(P, es, 2048, xin[128:2176, :], I["wA" + sfx], I["gmix" + sfx], I["gainrow" + sfx], I["qscale" + sfx],
                        I["ident" + sfx], None, None, GTv, [SAf[16 * g:16 * g + 4, :] for g in range(2)], xdep=xdep, dout_in=bSA, chunked=Sk,
                        chunk_bufs=bSk, on_chunk=on_chunk)
            for k in range(8):
                extract(k)
            P.coll("AllGather", [SAf], [GAf], rg, reads=[bSA], writes=[bGA])
            for hf in range(2):
                P.dma("sp", LCF[hf], (lambda e, hf=hf: GAf[bass.ds(hf * 32 + rk() * 16, 4), :]), bLA, reads=[bGA], accw=[bLA])
                P.dma("sp", LGT[hf].rearrange("t c -> (t c)").rearrange("(r x) -> r x", x=2048),
                      (lambda e, hf=hf: GAf[bass.ds(hf * 32 + rk() * 16 + 4, 12), :]), bLA, reads=[bGA], accw=[bLA])
            cB = {k[:-len(sfx) - 1]: v for k, v in I.items() if k.endswith("B" + sfx)}
            with P.scope():
                obd = lambda qg, c0, c1: SBk[qg % 4][(qg // 4) * 512:(qg // 4 + 1) * 512, c0:c1]
                def extract_o(k):
                    for g in range(2):
                        P.dma("sp", LO[g, 128 + 512 * k:128 + 512 * (k + 1), :],
                              (lambda e, g=g, k=k: GBk[k][bass.ds(g * 1024 + rk() * 512, 512), :]), bLO, reads=[bGBk[k]], accw=[bLO])

                def on_ob(qg):
                    if qg >= 4:
                        k = qg - 4
                        P.coll("AllGather", [SBk[k]], [GBk[k]], rg, reads=[bSBk[k]], writes=[bGBk[k]])
                        if k >= 1:
                            extract_o(k - 1)
                phase_B(P, T, SrcChunks(L, LGT, LCF), cB, obd, srcdep=[bLA], dout_in=bSB, obuf=lambda qg: bSBk[qg % 4], on_ob=on_ob)
            extract_o(3)
            for g in range(2):
                P.dma("sp", LO[g, 0:128, :], GBk[3][g * 1024 + 384:g * 1024 + 512, :], bLO, reads=[bGBk[3]], accw=[bLO])
            with P.scope():
                xo = X1[128:2176, :] if l == 0 else out
                ogs = lambda t, g: (lambda e: LO[g, t * 128:(t + 1) * 128, :].rearrange("p (n c) -> p n c", n=3))
                phase_C(P, 2176, xin, ogs, I["wM" + sfx], I["wBr" + sfx], I["wO" + sfx], I["gmixC" + sfx], I["gffn" + sfx],
                        I["wG" + sfx], I["wU" + sfx], I["wD" + sfx], I["convw" + sfx], I["convb" + sfx], I["identC" + sfx], XM, xo,
                        I["hflag"], xdep=xdep, ogdep=[bLO], dout_in=(bX1 if l == 0 else bOut))
            if l == 0:
                P.dma("sp", SX, X1[2048:2176, :], bSX, reads=[bX1], writes=[bSX])
                P.coll("AllGather", [SX], [GX], rg, reads=[bSX], writes=[bGX])
                P.dma("sp", X1[0:128, :], GX[0:128, :], bGX, reads=[bGX], writes=[bX1])
                P.barrier()
        P.finish([bOut])
    return nc


def prep_Z(inp):
    x = np.ascontiguousarray(inp["x"], dtype=np.float32)
    consts = prep_B_consts()
    shared = {}
    perg = [dict(), dict()]
    for l in range(DEPTH):
        sfx = "_%d" % l
        for k, v in prep_A(inp, l).items():
            shared[k + sfx] = v
        for k, v in prep_C(inp, l).items():
            if k == "hflag":
                continue
            kk = {"gmix": "gmixC", "ident": "identC"}.get(k, k)
            shared[kk + sfx] = v
        for g in range(2):
            for k, v in prep_B(inp, l, g, consts).items():
                perg[g][k + "B" + sfx] = v
    maps = []
    for b in range(NB):
        for r in range(2):
            if r == 0:
                xh = np.concatenate([np.zeros((128, D), np.float32), x[b, 0:2048]], axis=0)
            else:
                xh = x[b, 1920:4096]
            m = dict(shared)
            m.update(perg[r])
            m["xh"] = np.ascontiguousarray(xh)
            m["hflag"] = np.full((128, 1), float(r), np.float32)
            maps.append(m)
    return maps


def kernel(**inp):
    inp = {k: np.asarray(v) for k, v in inp.items()}
    maps = prep_Z(inp)
    if "Z" not in _NC_CACHE:
        _NC_CACHE["Z"] = build_Z(maps[0], 8)
    res = run_bass_kernel_spmd(_NC_CACHE["Z"], maps, core_ids=list(range(8)))
    outs = [np.asarray(res.results[c]["out"]) for c in range(8)]
    return np.ascontiguousarray(np.stack(outs, axis=0).reshape(NB, T, D).astype(np.float32))


NCG = 1808


def _colperm_g(g):
    cols, gain_idx, qsc = [], [], []

    def heads(base, n, gi, q):
        for h in range(n):
            hh = 4 * g + h if n == 4 else g
            c0 = base + 64 * hh
            cols.extend(range(c0, c0 + 64))
            gain_idx.append(gi)
            qsc.append(0.125 if q else 1.0)
    heads(OFF['a_q'], 4, 0, True)
    heads(OFF['a_ks'], 1, 1, False)
    heads(OFF['a_kw'], 1, 1, False)
    heads(OFF['b_k'], 1, 3, False)
    heads(OFF['b_q'], 4, 2, True)
    heads(OFF['c_q'], 4, 4, True)
    heads(OFF['c_k'], 4, 5, False)
    heads(OFF['a_kc'], 1, -1, False)
    heads(OFF['a_vc'], 1, -1, False)
    assert len(cols) == NFMG
    cols.extend(range(OFF['a_vs'] + 64 * g, OFF['a_vs'] + 64 * g + 64))
    cols.extend(range(OFF['a_vw'] + 64 * g, OFF['a_vw'] + 64 * g + 64))
    cols.extend(range(OFF['b_v'] + 64 * g, OFF['b_v'] + 64 * g + 64))
    cols.extend(range(OFF['c_v'] + 256 * g, OFF['c_v'] + 256 * g + 256))
    cols.extend(range(OFF['a_gate'] + 12 * g, OFF['a_gate'] + 12 * g + 12))
    cols.extend(range(OFF['c_f'] + 4 * g, OFF['c_f'] + 4 * g + 4))
    assert len(cols) == NCG
    return np.array(cols), gain_idx, np.array(qsc, np.float32)


def prep_A2(inp, l, g):
    perm, gain_idx, qsc = _colperm_g(g)
    gr = np.ones((1408,), np.float32)
    for s, gi in enumerate(gain_idx):
        if gi >= 0:
            gr[s * 64:(s + 1) * 64] = inp["qk_gain"][l, gi]
    return dict(
        wA=np.ascontiguousarray(inp["w_in"][l][:, perm]),
        gmix=np.ascontiguousarray(np.broadcast_to(inp["norm_mix"][l][None, :], (128, D))),
        gainrow=np.ascontiguousarray(gr.reshape(11, 128).T),
        qscale=np.ascontiguousarray(np.broadcast_to(qsc[None, :], (128, 21))),
        ident=np.eye(128, dtype=np.float32),
    )


def phase_A2(P, NT, xsrc, x_is_h, wA, gmix, gainrow, qscale, ident, LFM, LTM, LGT, LCF, xdep=(), dout_in=None):
    ntile = NT // 128
    xdep = list(xdep)
    dout = dout_in if dout_in is not None else P.buf("doutA2")
    wsb, _ = P.sbuf("A_w", [128, 8, NCG], BF16)
    banks = [(0, 512), (512, 512), (1024, 320), (1344, 464)]
    wgrp = [(0, 1024), (1024, 784)]
    bank_grp = [0, 0, 1, 1]
    wbg = [P.buf("A_wg") for _ in wgrp]
    for gi, (g0, gw) in enumerate(wgrp):
        P.dma("pool", wsb[:, :, g0:g0 + gw], wA[:, g0:g0 + gw].rearrange("(k p) c -> p k c", p=128), wbg[gi], writes=[wbg[gi]])
    gm, gmb = P.sbuf("A_gm", [128, D], F32)
    if not x_is_h:
        P.dma("sp", gm[:], gmix, gmb, writes=[gmb])
    gr, grb = P.sbuf("A_gr", [128, 11], F32)
    P.dma("sp", gr[:], gainrow, grb, writes=[grb])
    qs, qsb = P.sbuf("A_qs", [128, 21], F32)
    P.dma("sp", qs[:], qscale, qsb, writes=[qsb])
    idb, idbb = P.sbuf("A_idb", [128, 128], BF16)
    P.dma("pool", idb[:], ident, idbb, writes=[idbb])
    xt = [P.sbuf(f"A_x{i}", [128, D], F32) for i in range(2)]
    junk, junkb = P.sbuf("A_junk", [128, D], F32)
    h2 = [P.sbuf(f"A_h{i}", [128, D], BF16) for i in range(4)]
    ss2 = [P.sbuf(f"A_ss{i}", [128, 4], F32) for i in range(4)]
    hT = [P.sbuf(f"A_hT{i}", [128, 8, 128], BF16) for i in range(4)]
    raw2 = [(P.sbuf(f"A_raw{i}", [128, NFMG], F32)[0], [P.buf("A_rawb") for _ in range(3)]) for i in range(4)]
    sq = [P.sbuf(f"A_sq{i}", [128, 512], F32) for i in range(2)]
    ssall2 = [P.sbuf(f"A_ssall{i}", [128, 21], F32) for i in range(4)]
    rs2 = [P.sbuf(f"A_rs{i}", [128, 21], F32) for i in range(2)]
    nrm2 = [(P.sbuf(f"A_nrm{i}", [128, 1408], BF16)[0], [P.buf("A_nrmb") for _ in range(3)]) for i in range(2)]
    for nrm_, nrmb_ in nrm2:
        P.op("pool", lambda e, nrm_=nrm_: e.memset(nrm_[:, NFMG:1408], 0.0), writes=[nrmb_[2]])
    FMt = [P.sbuf(f"A_FMt{i}", [128, 11, 512], BF16) for i in range(2)]
    TMt = [P.sbuf(f"A_TMt{i}", [128, 4, NTMG], BF16) for i in range(2)]
    gtt = [P.sbuf(f"A_gt{i}", [128, 4, 12], F32) for i in range(2)]
    cfT, cfTb = P.sbuf("A_cfT", [4, NT], F32)
    tp, tpb = P.psum("A_tp", [128, 8, 128], BF16)
    PB = [P.psum(f"A_PB{i}", [128, 512], F32) for i in range(3)]
    tq = [P.psum(f"A_tq{i}", [128, 8, 128], BF16) for i in range(2)]
    pcf, pcfb = P.psum("A_pcf", [4, 128], F32)
    nmm_ = [0]

    def front_norm(ts):
        if x_is_h:
            for t in ts:
                h_, h_b = h2[t % 4]
                P.dma("sp", h_[:], xsrc(t), h_b, reads=xdep, writes=[h_b])
            return
        for t in ts:
            xs, xsb = xt[t % 2]
            P.dma("sp", xs[:], xsrc(t), xsb, reads=xdep, writes=[xsb])
        for t in ts:
            xs, xsb = xt[t % 2]
            ss_, ss_b = ss2[t % 4]
            P.op("act", lambda e, xs=xs, ss_=ss_: e.activation(out=junk[:], in_=xs[:], func=AF.Square, accum_out=ss_[:, 0:1]),
                 reads=[xsb], writes=[junkb, ss_b])
        for t in ts:
            ss_, ss_b = ss2[t % 4]
            P.op("dve", lambda e, ss_=ss_: e.tensor_scalar(out=ss_[:, 1:2], in0=ss_[:, 0:1], scalar1=1.0 / D, scalar2=EPS,
                                                           op0=ALU.mult, op1=ALU.add), reads=[ss_b], writes=[ss_b])
        for t in ts:
            ss_, ss_b = ss2[t % 4]
            P.op("act", lambda e, ss_=ss_: e.activation(out=ss_[:, 2:3], in_=ss_[:, 1:2], func=AF.Sqrt), reads=[ss_b], writes=[ss_b])
        for t in ts:
            ss_, ss_b = ss2[t % 4]
            P.op("dve", lambda e, ss_=ss_: e.reciprocal(out=ss_[:, 3:4], in_=ss_[:, 2:3]), reads=[ss_b], writes=[ss_b])
        for t in ts:
            xs, xsb = xt[t % 2]
            ss_, ss_b = ss2[t % 4]
            h_, h_b = h2[t % 4]
            P.op("dve", lambda e, xs=xs, ss_=ss_, h_=h_: e.scalar_tensor_tensor(out=h_[:], in0=xs[:], scalar=ss_[:, 3:4], in1=gm[:],
                                                                                  op0=ALU.mult, op1=ALU.mult),
                 reads=[xsb, ss_b, gmb], writes=[h_b])

    def front_T(t):
        h_, h_b = h2[t % 4]
        for k in range(8):
            P.op("pe", lambda e, k=k: e.transpose(tp[:, k, :], h_[:, k * 128:(k + 1) * 128], idb[:]), reads=[h_b, idbb], writes=[tpb])
        hTs, hTsb = hT[t % 4]
        P.op("act", lambda e: e.copy(out=hTs[:], in_=tp[:]), reads=[tpb], writes=[hTsb])

    def back_mm(t):
        nmm = nmm_[0]
        raw, rawb = raw2[t % 4]
        ssall, ssallb = ssall2[t % 4]
        hTs, hTsb = hT[t % 4]
        slot = (t // 4) % 2
        tt = t % 4
        TMs, TMsb = TMt[slot]
        gts, gtsb = gtt[slot]
        for j, (c0, w) in enumerate(banks):
            pb, pbb = PB[nmm % 3]
            nmm += 1
            for k in range(8):
                P.op("pe", lambda e, pb=pb, k=k, c0=c0, w=w: e.matmul(pb[:, 0:w], lhsT=hTs[:, k, :], rhs=wsb[:, k, c0:c0 + w],
                                                                      start=(k == 0), stop=(k == 7)),
                     reads=[hTsb, wbg[bank_grp[j]]], writes=[pbb])
            if j < 3:
                sqs, sqsb = sq[j % 2]
                nsl = w // 64
                P.op("act", lambda e, pb=pb, c0=c0, w=w: e.copy(out=raw[:, c0:c0 + w], in_=pb[:, 0:w]), reads=[pbb], writes=[rawb[j]])
                P.op("act", lambda e, pb=pb, w=w, sqs=sqs: e.activation(out=sqs[:, 0:w], in_=pb[:, 0:w], func=AF.Square),
                     reads=[pbb], writes=[sqsb])
                P.op("dve", lambda e, sqs=sqs, w=w, j=j, nsl=nsl: e.reduce_sum(
                    out=ssall[:, 8 * j:8 * j + nsl], in_=sqs[:, 0:w].rearrange("p (s d) -> p s d", d=64), axis=AX.X),
                    reads=[sqsb], writes=[ssallb])
            else:
                P.op("act", lambda e, pb=pb: e.copy(out=TMs[:, tt, :], in_=pb[:, 0:NTMG]), reads=[pbb], writes=[TMsb])
                P.op("act", lambda e, pb=pb: e.copy(out=gts[:, tt, :], in_=pb[:, NTMG:NTMG + 12]), reads=[pbb], writes=[gtsb])
                if tt == 3:
                    P.op("act", lambda e: e.activation(out=gts[:], in_=gts[:], func=AF.Sigmoid), reads=[gtsb], writes=[gtsb])
        for k in range(8):
            P.op("pe", lambda e, k=k: e.matmul(pcf[:], lhsT=wsb[:, k, NCG - 4:NCG], rhs=hTs[:, k, :], start=(k == 0), stop=(k == 7)),
                 reads=[hTsb, wbg[1]], writes=[pcfb])
        P.op("dve", lambda e: e.tensor_copy(out=cfT[:, t * 128:(t + 1) * 128], in_=pcf[:]), reads=[pcfb], writes=[cfTb])
        nmm_[0] = nmm

    def back_fin(ts):
        for t in ts:
            ssall, ssallb = ssall2[t % 4]
            rs, rsb = rs2[t % 2]
            P.op("dve", lambda e, rs=rs, ssall=ssall: e.tensor_scalar(out=rs[:], in0=ssall[:], scalar1=1.0 / HD, scalar2=EPS,
                                                                     op0=ALU.mult, op1=ALU.add), reads=[ssallb], writes=[rsb])
        for t in ts:
            rs, rsb = rs2[t % 2]
            P.op("act", lambda e, rs=rs: e.activation(out=rs[:], in_=rs[:], func=AF.Sqrt), reads=[rsb], writes=[rsb])
        for t in ts:
            rs, rsb = rs2[t % 2]
            P.op("dve", lambda e, rs=rs: e.reciprocal(out=rs[:], in_=rs[:]), reads=[rsb], writes=[rsb])
        for t in ts:
            rs, rsb = rs2[t % 2]
            P.op("dve", lambda e, rs=rs: e.tensor_tensor(out=rs[:], in0=rs[:], in1=qs[:], op=ALU.mult), reads=[rsb, qsb], writes=[rsb])
        for t in ts:
            rs, rsb = rs2[t % 2]
            P.op("dve", lambda e, rs=rs: e.memset(rs[:, 19:21], 1.0), writes=[rsb])
        for t in ts:
            raw, rawb = raw2[t % 4]
            rs, rsb = rs2[t % 2]
            nrm, nrmb = nrm2[t % 2]
            for j in range(3):
                c0, w = banks[j]
                nsl = w // 64
                P.op("pool", lambda e, c0=c0, w=w, j=j, nsl=nsl, nrm=nrm, raw=raw, rs=rs: e.tensor_tensor(
                    out=nrm[:, c0:c0 + w].rearrange("p (s d) -> p s d", d=64),
                    in0=raw[:, c0:c0 + w].rearrange("p (s d) -> p s d", d=64),
                    in1=rs[:, 8 * j:8 * j + nsl].unsqueeze(2).to_broadcast([128, nsl, 64]), op=ALU.mult),
                    reads=[rawb[j], rsb], writes=[nrmb[j]])
        for t in ts:
            slot = (t // 4) % 2
            tt = t % 4
            TMs, TMsb = TMt[slot]
            gts, gtsb = gtt[slot]
            FMs, FMsb = FMt[slot]
            nrm, nrmb = nrm2[t % 2]
            for c in range(11):
                tqs, tqsb = tq[c // 8]
                P.op("pe", lambda e, tqs=tqs, c=c, nrm=nrm: e.transpose(tqs[:, c % 8, :], nrm[:, c * 128:(c + 1) * 128], idb[:]),
                     reads=[nrmb[min(2, c // 4)], idbb] + ([nrmb[2]] if c == 8 else []), writes=[tqsb])
                if c == 7 or c == 10:
                    cb = (c // 8) * 8
                    n = c - cb + 1
                    P.op("dve", lambda e, tqs=tqs, cb=cb, n=n, FMs=FMs, tt=tt: e.tensor_tensor(
                        out=FMs[:, cb:cb + n, tt * 128:(tt + 1) * 128], in0=tqs[:, 0:n, :],
                        in1=gr[:, cb:cb + n].unsqueeze(2).to_broadcast([128, n, 128]), op=ALU.mult), reads=[tqsb, grb], writes=[FMsb])
            if tt == 3:
                t0 = (t - 3) * 128
                for (ca, cb_) in ((0, 6), (6, 11)):
                    P.dma("sp", LFM[ca * 128:cb_ * 128, t0:t0 + 512].rearrange("(c p) t -> p c t", p=128), FMs[:, ca:cb_, :], FMsb,
                          reads=[FMsb], accw=[dout])
                P.dma("sp", LTM[t0:t0 + 512, :].rearrange("(tt p) c -> p tt c", p=128), TMs[:], TMsb, reads=[TMsb], accw=[dout])
                P.dma("sp", LGT[t0:t0 + 512, :].rearrange("(tt p) c -> p tt c", p=128), gts[:], gtsb, reads=[gtsb], accw=[dout])

    npair = ntile // 2
    assert ntile % 2 == 0

    def pair(fn, p):
        if 0 <= p < npair:
            if fn in (front_norm, back_fin):
                fn([2 * p, 2 * p + 1])
            else:
                fn(2 * p)
                fn(2 * p + 1)

    pair(front_norm, 0)
    pair(front_T, 0)
    pair(front_norm, 1)
    pair(back_mm, 0)
    pair(front_T, 1)
    pair(front_norm, 2)
    for p in range(npair):
        pair(back_mm, p + 1)
        pair(front_T, p + 2)
        pair(front_norm, p + 3)
        pair(back_fin, p)
    P.dma("sp", LCF, cfT[:], cfTb, reads=[cfTb], accw=[dout])
    return dout


def build_A2(NT=4096):
    nc = bass.Bass("TRN2", target_bir_lowering=False)
    dt_ = lambda n, s, d=F32, k="ExternalInput": nc.dram_tensor(n, s, d, kind=k).ap()
    x = dt_("x", [NT, D])
    wA = dt_("wA", [D, NCG]); gmix = dt_("gmix", [128, D]); gainrow = dt_("gainrow", [128, 11]); qscale = dt_("qscale", [128, 21])
    ident = dt_("ident", [128, 128])
    LFM = dt_("LFM", [1408, NT], BF16, "ExternalOutput"); LTM = dt_("LTM", [NT, NTMG], BF16, "ExternalOutput")
    LGT = dt_("LGT", [NT, 12], F32, "ExternalOutput"); LCF = dt_("LCF", [4, NT], F32, "ExternalOutput")
    with ExitStack() as es:
        P = Prog(nc, es)
        dout = phase_A2(P, NT, lambda t: x[t * 128:(t + 1) * 128, :], False, wA, gmix, gainrow, qscale, ident, LFM, LTM, LGT, LCF)
        P.finish([dout])
    return nc


def build_Z2(sample, ncores=8):
    nc = bass.Bass("TRN2", target_bir_lowering=False)
    I = {}
    for k, v in sample.items():
        dt = BF16 if v.dtype == NPBF else F32
        I[k] = nc.dram_tensor(k, list(v.shape), dt, kind="ExternalInput").ap()
    out = nc.dram_tensor("out", [2048, D], F32, kind="ExternalOutput").ap()
    LFM = nc.dram_tensor("LFM", [1408, T], BF16).ap()
    LTM = nc.dram_tensor("LTM", [T, NTMG], BF16).ap()
    LGT = nc.dram_tensor("LGT", [T, 12], F32).ap()
    LCF = nc.dram_tensor("LCF", [4, T], F32).ap()
    SBk = [nc.dram_tensor("SBk%d" % k, [1024, 768], BF16).ap() for k in range(4)]
    GBk = [nc.dram_tensor("GBk%d" % k, [2048, 768], BF16).ap() for k in range(4)]
    LO = nc.dram_tensor("LO", [2, 2176, 768], BF16).ap()
    SH = [nc.dram_tensor("SH%d" % k, [512, D], BF16).ap() for k in range(4)]
    GH = [nc.dram_tensor("GH%d" % k, [1024, D], BF16).ap() for k in range(4)]
    X1 = nc.dram_tensor("X1", [2176, D], F32).ap()
    SX = nc.dram_tensor("SX", [128, D], F32).ap()
    GX = nc.dram_tensor("GX", [256, D], F32).ap()
    XM = nc.dram_tensor("XM", [2176, D], F32).ap()
    rg = [[i, i + 1] for i in range(0, ncores, 2)]
    with ExitStack() as es:
        P = Prog(nc, es)
        P.need_rank = True
        bLA, bSB, bX1, bSX, bGX, bOut, bLO = [P.buf(n) for n in ("LA", "SB", "X1", "SX", "GX", "OUT", "LO")]
        rk = lambda: P.rank["sp"]
        bSBk = [P.buf("SBk") for _ in range(4)]
        bGBk = [P.buf("GBk") for _ in range(4)]
        bSH = [P.buf("SH") for _ in range(4)]
        bGH = [P.buf("GH") for _ in range(4)]
        for l in range(DEPTH):
            sfx = "_%d" % l
            xin = I["xh"] if l == 0 else X1
            xdep = [] if l == 0 else [bX1]
            with P.scope():
                if l == 0:
                    xsrc = lambda t: I["xfull"][t * 128:(t + 1) * 128, :]
                    adep = []
                else:
                    xsrc = lambda t: GH[(t % 16) // 4][(t // 16) * 512 + (t % 4) * 128:(t // 16) * 512 + (t % 4) * 128 + 128, :]
                    adep = bGH
                phase_A2(P, T, xsrc, l > 0, I["wA" + sfx], I["gmix" + sfx], I["gainrow" + sfx], I["qscale" + sfx], I["ident" + sfx],
                         LFM, LTM, LGT, LCF, xdep=adep, dout_in=bLA)
            cB = {k[:-len(sfx) - 1]: v for k, v in I.items() if k.endswith("B" + sfx)}
            with P.scope():
                obd = lambda qg, c0, c1: SBk[qg % 4][(qg // 4) * 512:(qg // 4 + 1) * 512, c0:c1]

                def extract_o(k):
                    for g in range(2):
                        P.dma("sp", LO[g, 128 + 512 * k:128 + 512 * (k + 1), :],
                              (lambda e, g=g, k=k: GBk[k][bass.ds(g * 1024 + rk() * 512, 512), :]), bLO, reads=[bGBk[k]], accw=[bLO])

                def on_ob(qg):
                    if qg >= 4:
                        k = qg - 4
                        P.coll("AllGather", [SBk[k]], [GBk[k]], rg, reads=[bSBk[k]], writes=[bGBk[k]])
                        if k >= 1:
                            extract_o(k - 1)
                phase_B(P, T, SrcStatic(T, LFM, LTM, LGT, LCF), cB, obd, srcdep=[bLA], dout_in=bSB, obuf=lambda qg: bSBk[qg % 4], on_ob=on_ob)
            extract_o(3)
            for g in range(2):
                P.dma("sp", LO[g, 0:128, :], GBk[3][g * 1024 + 384:g * 1024 + 512, :], bLO, reads=[bGBk[3]], accw=[bLO])
            with P.scope():
                xo = X1[128:2176, :] if l == 0 else out
                ogs = lambda t, g: (lambda e: LO[g, t * 128:(t + 1) * 128, :].rearrange("p (n c) -> p n c", n=3))
                hnext = None
                if l + 1 < DEPTH:
                    def on_h(tl):
                        if tl % 4 == 3:
                            k = tl // 4
                            P.coll("AllGather", [SH[k]], [GH[k]], rg, reads=[bSH[k]], writes=[bGH[k]])
                    hnext = (I["gmix_%d" % (l + 1)], lambda tl: SH[tl // 4][(tl % 4) * 128:(tl % 4) * 128 + 128, :],
                             lambda tl: bSH[tl // 4], on_h)
                phase_C(P, 2176, xin, ogs, I["wM" + sfx], I["wBr" + sfx], I["wO" + sfx], I["gmixC" + sfx], I["gffn" + sfx],
                        I["wG" + sfx], I["wU" + sfx], I["wD" + sfx], I["convw" + sfx], I["convb" + sfx], I["identC" + sfx], XM, xo,
                        I["hflag"], xdep=xdep, ogdep=[bLO], dout_in=(bX1 if l == 0 else bOut), hnext=hnext)
            if l == 0:
                P.dma("sp", SX, X1[2048:2176, :], bSX, reads=[bX1], writes=[bSX])
                P.coll("AllGather", [SX], [GX], rg, reads=[bSX], writes=[bGX])
                P.dma("sp", X1[0:128, :], GX[0:128, :], bGX, reads=[bGX], writes=[bX1])
                P.barrier()
        P.finish([bOut])
    return nc


def prep_Z2(inp):
    x = np.ascontiguousarray(inp["x"], dtype=np.float32)
    consts = prep_B_consts()
    shared = {}
    perg = [dict(), dict()]
    for l in range(DEPTH):
        sfx = "_%d" % l
        for k, v in prep_C(inp, l).items():
            if k == "hflag":
                continue
            kk = {"gmix": "gmixC", "ident": "identC"}.get(k, k)
            shared[kk + sfx] = v
        for g in range(2):
            for k, v in prep_A2(inp, l, g).items():
                perg[g][k + sfx] = v
            for k, v in prep_B(inp, l, g, consts).items():
                perg[g][k + "B" + sfx] = v
    maps = []
    for b in range(NB):
        for r in range(2):
            if r == 0:
                xh = np.concatenate([np.zeros((128, D), np.float32), x[b, 0:2048]], axis=0)
            else:
                xh = x[b, 1920:4096]
            m = dict(shared)
            m.update(perg[r])
            m["xh"] = np.ascontiguousarray(xh)
            m["xfull"] = x[b]
            m["hflag"] = np.full((128, 1), float(r), np.float32)
            maps.append(m)
    return maps


def kernel(**inp):
    inp = {k: np.asarray(v) for k, v in inp.items()}
    maps = prep_Z2(inp)
    if "Z2" not in _NC_CACHE:
        _NC_CACHE["Z2"] = build_Z2(maps[0], 8)
    res = run_bass_kernel_spmd(_NC_CACHE["Z2"], maps, core_ids=list(range(8)))
    outs = [np.asarray(res.results[c]["out"]) for c in range(8)]
    return np.ascontiguousarray(np.stack(outs, axis=0).reshape(NB, T, D).astype(np.float32))
```

```python
import math
import numpy as np
import ml_dtypes
from contextlib import ExitStack
import concourse.bass as bass
import concourse.mybir as mybir
from concourse.bass_utils import run_bass_kernel_spmd

F32 = mybir.dt.float32
BF16 = mybir.dt.bfloat16
AF = mybir.ActivationFunctionType
ALU = mybir.AluOpType
AX = mybir.AxisListType
NPBF = ml_dtypes.bfloat16


import types


def _freeze(fn):
    if fn.__closure__ is None:
        return fn
    cells = []
    for c_ in fn.__closure__:
        try:
            cells.append(types.CellType(c_.cell_contents))
        except ValueError:
            cells.append(c_)
    return types.FunctionType(fn.__code__, fn.__globals__, fn.__name__, fn.__defaults__, tuple(cells))


class Buf:
    __slots__ = ("name", "w", "r", "dsem", "excl")

    REG = []

    def __init__(self, name, excl=False):
        self.name = name
        self.excl = excl
        Buf.REG.append(self)
        self.w = []
        self.r = []
        self.dsem = None


class Prog:
    ENG = ("pe", "act", "dve", "pool", "sp")

    def __init__(self, nc, es):
        self.nc = nc
        self.es = es
        self.q = {e: [] for e in self.ENG}
        self.csem = {}
        for e in ("pe", "act", "dve", "pool"):
            self.csem[e] = es.enter_context(nc.semaphore("c_" + e))
        self.cnt = {e: 0 for e in self.ENG}
        self.waited = {e: {} for e in self.ENG}
        self.dsems = []
        self.nbuf = 0
        self.same_engine_sync = True
        self.es_alloc = None
        self.free_d = {"hw": [], "sw": [], "cc": []}
        self.scope_sems = [[]]
        self.rank = {}
        self.need_rank = False
        Buf.REG.clear()

    def sbuf(self, name, shape, dtype):
        self.nalloc = getattr(self, "nalloc", 0) + 1
        t = (self.es_alloc or self.es).enter_context(self.nc.sbuf_tensor("%s_%d" % (name, self.nalloc), list(shape), dtype))
        return t, Buf(name)

    def psum(self, name, shape, dtype):
        self.nalloc = getattr(self, "nalloc", 0) + 1
        t = (self.es_alloc or self.es).enter_context(self.nc.psum_tensor("%s_%d" % (name, self.nalloc), list(shape), dtype))
        return t, Buf(name, excl=True)

    def buf(self, name="b"):
        self.nbuf += 1
        return Buf(f"{name}{self.nbuf}")

    def _dsem(self, b, eng):
        kind = "sw" if eng == "pool" else ("cc" if eng == "cc" else "hw")
        if b.dsem is None:
            b.dsem = {}
        if kind not in b.dsem:
            if self.free_d[kind]:
                d = self.free_d[kind].pop()
            else:
                s = self.es.enter_context(self.nc.semaphore("d%s_%d" % (kind, len(self.dsems))))
                d = {"sem": s, "total": 0, "dirty": False, "id": len(self.dsems), "kind": kind}
                self.dsems.append(d)
            b.dsem[kind] = d
            self.scope_sems[-1].append(d)
        return b.dsem[kind]

    def _waits_for(self, eng, reads, writes):
        toks = []
        for b in reads:
            toks += b.w
            if b.excl:
                toks += [t for t in b.r if t[1] != eng]
        for b in writes:
            toks += b.w
            toks += b.r
        waits = []
        wd = self.waited[eng]
        for t in toks:
            if t[0] == "c":
                _, e, v = t
                if e == eng and (eng == "pe" or not self.same_engine_sync):
                    continue
                key = ("c", e)
                if wd.get(key, 0) < v:
                    wd[key] = v
            else:
                _, d, v = t
                v = d["total"]
                d["dirty"] = True
                key = ("d", d["id"])
                if wd.get(key, 0) < v:
                    wd[key] = v
        return wd

    def _collect(self, eng, reads, writes):
        before = dict(self.waited[eng])
        self._waits_for(eng, reads, writes)
        after = self.waited[eng]
        out = []
        for k, v in after.items():
            if before.get(k, 0) < v:
                if k[0] == "c":
                    out.append((self.csem[k[1]], v))
                else:
                    out.append((self.dsems[k[1]]["sem"], v))
        return out

    def op(self, eng, fn, reads=(), writes=()):
        waits = self._collect(eng, reads, writes)
        fn = _freeze(fn)
        self.cnt[eng] += 1
        tok = ("c", eng, self.cnt[eng])
        sem = self.csem[eng]

        def run(e, waits=waits, fn=fn, sem=sem):
            for s, v in waits:
                e.wait_ge(s, v)
            fn(e).then_inc(sem, 1)

        self.q[eng].append(run)
        for b in writes:
            b.w = [tok]
            b.r = []
        for b in reads:
            if b not in writes:
                b.r.append(tok)
        return tok

    def dma(self, eng, out, in_, semof, reads=(), writes=(), accw=(), **kw):
        d = self._dsem(semof, eng)
        waits = self._collect(eng, reads, writes)
        for b in accw:
            if b.r:
                tmpb = Buf("tmp")
                Buf.REG.pop()
                tmpb.w = list(b.r)
                waits += self._collect(eng, [tmpb], ())
        if d["dirty"] and d["total"] > 0:
            key = ("d", d["id"])
            if self.waited[eng].get(key, 0) < d["total"]:
                self.waited[eng][key] = d["total"]
                waits.append((d["sem"], d["total"]))
            d["dirty"] = False
        d["total"] += 16
        tok = ("d", d, d["total"])
        sem = d["sem"]

        def run(e, waits=waits, out=out, in_=in_, sem=sem, kw=kw):
            for s, v in waits:
                e.wait_ge(s, v)
            o_ = out(e) if callable(out) else out
            i_ = in_(e) if callable(in_) else in_
            try:
                e.dma_start(out=o_, in_=i_, **kw).then_inc(sem, 16)
            except Exception:
                print("DMA FAIL out=", o_, "in=", i_)
                raise

        self.q[eng].append(run)
        for b in writes:
            b.w = [tok]
            b.r = []
        for b in accw:
            b.w.append(tok)
        for b in reads:
            if b not in writes:
                b.r.append(tok)
        return tok

    def coll(self, kind, ins, outs, replica_groups, reads=(), writes=()):
        eng = "pool"
        if not hasattr(self, "ccbuf"):
            self.ccbuf = Buf("ccbuf")
            self.scope_sems.append([])
            self._dsem(self.ccbuf, "cc")
            self.scope_sems.pop()
        d = self._dsem(self.ccbuf, "cc")
        waits = self._collect(eng, reads, writes)
        d["total"] += 1
        tok = ("d", d, d["total"])
        sem = d["sem"]

        def run(e, waits=waits, sem=sem):
            for s_, v in waits:
                e.wait_ge(s_, v)
            e.collective_compute(kind, mybir.AluOpType.bypass, replica_groups=replica_groups,
                                 ins=[a.opt() for a in ins], outs=[a.opt() for a in outs]).then_inc(sem)

        self.q[eng].append(run)
        for b_ in writes:
            b_.w = [tok]
            b_.r = []
        for b_ in reads:
            if b_ not in writes:
                b_.r.append(tok)
        return tok

    def barrier(self):
        for eng in self.ENG:
            waits = []
            wd = self.waited[eng]
            for e2 in ("pe", "act", "dve", "pool"):
                if self.cnt[e2] > 0 and wd.get(("c", e2), 0) < self.cnt[e2]:
                    wd[("c", e2)] = self.cnt[e2]
                    waits.append((self.csem[e2], self.cnt[e2]))
            for d in self.dsems:
                key = ("d", d["id"])
                if d["total"] > 0 and wd.get(key, 0) < d["total"]:
                    wd[key] = d["total"]
                    waits.append((d["sem"], d["total"]))
                d["dirty"] = False

            def run(e, waits=waits):
                for s_, v in waits:
                    e.wait_ge(s_, v)

            if waits:
                self.q[eng].append(run)
        for b in Buf.REG:
            b.w = []
            b.r = []

    def scope(self):
        prog = self

        class _Scope:
            def __enter__(self_):
                prog.barrier()
                self_.saved = prog.es_alloc
                self_.es = ExitStack()
                self_.es.__enter__()
                prog.es_alloc = self_.es
                prog.scope_sems.append([])
                return self_

            def __exit__(self_, *a):
                prog.barrier()
                prog.es_alloc = self_.saved
                for d in prog.scope_sems.pop():
                    prog.free_d[d["kind"]].append(d)
                return self_.es.__exit__(*a)

        return _Scope()

    def finish(self, final_bufs):
        waits = self._collect("sp", final_bufs, ())
        for d in self.dsems:
            key = ("d", d["id"])
            if self.waited["sp"].get(key, 0) < d["total"]:
                self.waited["sp"][key] = d["total"]
                waits.append((d["sem"], d["total"]))

        def run(e, waits=waits):
            for s, v in waits:
                e.wait_ge(s, v)

        self.q["sp"].append(run)
        nc = self.nc
        q = self.q
        with nc.Block() as block:
            @block.tensor
            def _(e):
                for f in q["pe"]:
                    f(e)

            @block.scalar
            def _(e):
                for f in q["act"]:
                    f(e)

            @block.vector
            def _(e):
                for f in q["dve"]:
                    f(e)

            @block.gpsimd
            def _(e):
                for f in q["pool"]:
                    f(e)

            @block.sync
            def _(e):
                if self.need_rank:
                    self.rank["sp"] = e.partition_id() % 2
                for f in q["sp"]:
                    f(e)


D = 1024
T = 4096
NB = 4
DEPTH = 2
HD = 64
F_FF = 2816
NFC = 22
EPS = 1e-6
NEG = -30000.0
IN_W = 6688
OFF = dict(a_q=0, a_kc=512, a_vc=640, a_ks=768, a_vs=896, a_kw=1024, a_vw=1152, a_gate=1280,
           b_q=1304, b_k=1816, b_v=1944, c_q=2072, c_k=2584, c_v=3096, c_f=3608, merge=3616)
NFM = 2688
NFMG = 1344
NTM = 896
NTMG = 448
FMR = dict(qa=0, ks=256, kw=320, kb=384, qb=448, qc=704, kcf=960, kc=1216, vc=1280)
TMC = dict(vs=0, vw=64, vb=128, vcf=192)


def _colperm():
    cols = []
    gain_idx = []
    qsc = []
    for g in range(2):
        def heads(base, n, gi, q):
            for h in range(n):
                hh = 4 * g + h if n == 4 else g
                c0 = base + 64 * hh
                cols.extend(range(c0, c0 + 64))
                gain_idx.append(gi)
                qsc.append(0.125 if q else 1.0)
        heads(OFF['a_q'], 4, 0, True)
        heads(OFF['a_ks'], 1, 1, False)
        heads(OFF['a_kw'], 1, 1, False)
        heads(OFF['b_k'], 1, 3, False)
        heads(OFF['b_q'], 4, 2, True)
        heads(OFF['c_q'], 4, 4, True)
        heads(OFF['c_k'], 4, 5, False)
        heads(OFF['a_kc'], 1, -1, False)
        heads(OFF['a_vc'], 1, -1, False)
    assert len(cols) == NFM
    for g in range(2):
        cols.extend(range(OFF['a_vs'] + 64 * g, OFF['a_vs'] + 64 * g + 64))
        cols.extend(range(OFF['a_vw'] + 64 * g, OFF['a_vw'] + 64 * g + 64))
        cols.extend(range(OFF['b_v'] + 64 * g, OFF['b_v'] + 64 * g + 64))
        cols.extend(range(OFF['c_v'] + 256 * g, OFF['c_v'] + 256 * g + 256))
    assert len(cols) == NFM + NTM
    cols.extend(range(OFF['a_gate'], OFF['a_gate'] + 24))
    cols.extend(range(OFF['c_f'], OFF['c_f'] + 8))
    assert len(cols) == 3616
    return np.array(cols), gain_idx, np.array(qsc, np.float32)


def build_A(NT=2048):
    nc = bass.Bass("TRN2", target_bir_lowering=False)
    ntile = NT // 128
    x = nc.dram_tensor("x", [NT, D], F32, kind="ExternalInput").ap()
    wA = nc.dram_tensor("wA", [D, 3616], F32, kind="ExternalInput").ap()
    gmix = nc.dram_tensor("gmix", [128, D], F32, kind="ExternalInput").ap()
    gainrow = nc.dram_tensor("gainrow", [128, 21], F32, kind="ExternalInput").ap()
    qscale = nc.dram_tensor("qscale", [128, 42], F32, kind="ExternalInput").ap()
    ident = nc.dram_tensor("ident", [128, 128], F32, kind="ExternalInput").ap()
    FM = nc.dram_tensor("FM", [NFM, NT], BF16, kind="ExternalOutput").ap()
    TM = nc.dram_tensor("TM", [NT, NTM], BF16, kind="ExternalOutput").ap()
    GT = nc.dram_tensor("GT", [NT, 24], F32, kind="ExternalOutput").ap()
    CF = nc.dram_tensor("CF", [8, NT], F32, kind="ExternalOutput").ap()
    with ExitStack() as es:
        P = Prog(nc, es)
        dout = phase_A(P, es, NT, x, wA, gmix, gainrow, qscale, ident, FM,
                       [TM[:, 0:NTMG], TM[:, NTMG:NTM]], [GT[:, 0:12], GT[:, 12:24]], CF)
        P.finish([dout])
    return nc


STAGE = 99


def phase_A(P, es, NT, x, wA, gmix, gainrow, qscale, ident, FM, TM, GT, CF, xdep=(), dout_in=None, chunked=None,
            chunk_bufs=None, on_chunk=None):
    ntile = NT // 128
    dout = dout_in if dout_in is not None else P.buf("doutA")
    wsb, _ = P.sbuf("A_w", [128, 8, 3616], BF16)
    banks = [(i * 512, 512) for i in range(5)] + [(2560, 128), (2688, 512), (3200, 416)]
    wgrp = [(0, 1024), (1024, 1664), (2688, 928)]
    wbg = [P.buf("A_wg") for _ in wgrp]
    for gi, (g0, gw) in enumerate(wgrp):
        P.dma("pool", wsb[:, :, g0:g0 + gw], wA[:, g0:g0 + gw].rearrange("(k p) c -> p k c", p=128), wbg[gi], writes=[wbg[gi]])
    wb = None
    gm, gmb = P.sbuf("A_gm", [128, D], F32)
    P.dma("sp", gm[:], gmix, gmb, writes=[gmb])
    gr, grb = P.sbuf("A_gr", [128, 21], F32)
    P.dma("sp", gr[:], gainrow, grb, writes=[grb])
    qs, qsb = P.sbuf("A_qs", [128, 42], F32)
    P.dma("sp", qs[:], qscale, qsb, writes=[qsb])
    idb, idbb = P.sbuf("A_idb", [128, 128], BF16)
    P.dma("pool", idb[:], ident, idbb, writes=[idbb])
    idf, idfb = P.sbuf("A_idf", [128, 128], F32)
    P.dma("sp", idf[:], ident, idfb, writes=[idfb])

    xt = [P.sbuf(f"A_x{i}", [128, D], F32) for i in range(2)]
    junk, junkb = P.sbuf("A_junk", [128, D], F32)
    ss, ssb = P.sbuf("A_ss", [128, 4], F32)
    h2 = [P.sbuf(f"A_h{i}", [128, D], BF16) for i in range(2)]
    ss2 = [P.sbuf(f"A_ss{i}", [128, 4], F32) for i in range(2)]
    hT = [P.sbuf(f"A_hT{i}", [128, 8, 128], BF16) for i in range(2)]
    raw2 = [(P.sbuf(f"A_raw{i}", [128, NFM], F32)[0], [P.buf("A_rawb") for _ in range(6)]) for i in range(2)]
    sq = [P.sbuf(f"A_sq{i}", [128, 512], F32) for i in range(2)]
    ssall2 = [P.sbuf(f"A_ssall{i}", [128, 42], F32) for i in range(2)]
    rs, rsb = P.sbuf("A_rs", [128, 42], F32)
    tmp = [P.sbuf(f"A_tmp{i}", [128, 512], F32) for i in range(2)]
    nrm, _ = P.sbuf("A_nrm", [128, NFM], BF16)
    nrmb = [P.buf("A_nrmb") for _ in range(6)]
    FMt = [P.sbuf(f"A_FMt{i}", [128, 21, 512], BF16) for i in range(2)]
    TMt = [P.sbuf(f"A_TMt{i}", [128, 4, NTM], BF16) for i in range(2)]
    gtt = [P.sbuf(f"A_gt{i}", [128, 4, 24], F32) for i in range(2)]
    cft, cftb = P.sbuf("A_cft", [128, 8], F32)
    cfT, cfTb = P.sbuf("A_cfT", [8, NT], F32)
    tp, tpb = P.psum("A_tp", [128, 8, 128], BF16)
    PB = [P.psum(f"A_PB{i}", [128, 512], F32) for i in range(3)]
    tq = [P.psum(f"A_tq{i}", [128, 8, 128], BF16) for i in range(2)]
    pcf, pcfb = P.psum("A_pcf", [8, 128], F32)

    bank_grp = [0, 0, 1, 1, 1, 1, 2, 2]
    nmm_ = [0]

    def front_norm(t):
        xs, xsb = xt[t % 2]
        h_, h_b = h2[t % 2]
        ss_, ss_b = ss2[t % 2]
        P.dma("sp", xs[:], x[t * 128:(t + 1) * 128, :], xsb, reads=list(xdep), writes=[xsb])
        _rmsnorm_tile(P, xs[:], xsb, gm, gmb, h_[:], h_b, ss_, ss_b, junk, junkb)

    def front_T(t):
        h_, h_b = h2[t % 2]
        for k in range(8):
            P.op("pe", lambda e, k=k: e.transpose(tp[:, k, :], h_[:, k * 128:(k + 1) * 128], idb[:]),
                 reads=[h_b, idbb], writes=[tpb])
        hTs, hTsb = hT[t % 2]
        P.op("act", lambda e: e.copy(out=hTs[:], in_=tp[:]), reads=[tpb], writes=[hTsb])

    def back_mm(t):
        nmm = nmm_[0]
        raw, rawb = raw2[t % 2]
        ssall, ssallb = ssall2[t % 2]
        hTs, hTsb = hT[t % 2]
        slot = (t // 4) % 2
        tt = t % 4
        TMs, TMsb = TMt[slot]
        gts, gtsb = gtt[slot]
        FMs, FMsb = FMt[slot]
        for j, (c0, w) in enumerate(banks):
            pb, pbb = PB[nmm % 3]
            nmm += 1
            for k in range(8):
                P.op("pe", lambda e, pb=pb, k=k, c0=c0, w=w, hTs=hTs: e.matmul(
                    pb[:, 0:w], lhsT=hTs[:, k, :], rhs=wsb[:, k, c0:c0 + w], start=(k == 0), stop=(k == 7)),
                    reads=[hTsb, wbg[bank_grp[j]]], writes=[pbb])
            if j < 6:
                sqs, sqsb = sq[j % 2]
                nsl = w // 64
                P.op("act", lambda e, pb=pb, c0=c0, w=w: e.copy(out=raw[:, c0:c0 + w], in_=pb[:, 0:w]),
                     reads=[pbb], writes=[rawb[j]])
                P.op("act", lambda e, pb=pb, w=w, sqs=sqs: e.activation(out=sqs[:, 0:w], in_=pb[:, 0:w], func=AF.Square),
                     reads=[pbb], writes=[sqsb])
                P.op("dve", lambda e, sqs=sqs, w=w, j=j, nsl=nsl: e.reduce_sum(
                    out=ssall[:, 8 * j:8 * j + nsl], in_=sqs[:, 0:w].rearrange("p (s d) -> p s d", d=64), axis=AX.X),
                    reads=[sqsb], writes=[ssallb])
            elif j == 6:
                P.op("act", lambda e, pb=pb, TMs=TMs, tt=tt: e.copy(out=TMs[:, tt, 0:512], in_=pb[:, 0:512]),
                     reads=[pbb], writes=[TMsb])
            else:
                P.op("act", lambda e, pb=pb, TMs=TMs, tt=tt: e.copy(out=TMs[:, tt, 512:896], in_=pb[:, 0:384]),
                     reads=[pbb], writes=[TMsb])
                P.op("act", lambda e, pb=pb, gts=gts, tt=tt: e.copy(out=gts[:, tt, :], in_=pb[:, 384:408]),
                     reads=[pbb], writes=[gtsb])
                if tt == 3:
                    P.op("act", lambda e, gts=gts: e.activation(out=gts[:], in_=gts[:], func=AF.Sigmoid), reads=[gtsb], writes=[gtsb])
        for k in range(8):
            P.op("pe", lambda e, k=k, hTs=hTs: e.matmul(pcf[:], lhsT=wsb[:, k, 3608:3616], rhs=hTs[:, k, :],
                                                        start=(k == 0), stop=(k == 7)),
                 reads=[hTsb, wbg[2]], writes=[pcfb])
        P.op("dve", lambda e, t=t: e.tensor_copy(out=cfT[:, t * 128:(t + 1) * 128], in_=pcf[:]), reads=[pcfb], writes=[cfTb])
        nmm_[0] = nmm

    def back_fin(t):
        raw, rawb = raw2[t % 2]
        ssall, ssallb = ssall2[t % 2]
        slot = (t // 4) % 2
        tt = t % 4
        TMs, TMsb = TMt[slot]
        gts, gtsb = gtt[slot]
        FMs, FMsb = FMt[slot]
        P.op("dve", lambda e: e.tensor_scalar(out=rs[:], in0=ssall[:], scalar1=1.0 / HD, scalar2=EPS,
                                              op0=ALU.mult, op1=ALU.add), reads=[ssallb], writes=[rsb])
        P.op("act", lambda e: e.activation(out=rs[:], in_=rs[:], func=AF.Sqrt), reads=[rsb], writes=[rsb])
        P.op("dve", lambda e: e.reciprocal(out=rs[:], in_=rs[:]), reads=[rsb], writes=[rsb])
        P.op("dve", lambda e: e.tensor_tensor(out=rs[:], in0=rs[:], in1=qs[:], op=ALU.mult), reads=[rsb, qsb], writes=[rsb])
        P.op("dve", lambda e: e.memset(rs[:, 19:21], 1.0), writes=[rsb])
        P.op("dve", lambda e: e.memset(rs[:, 40:42], 1.0), writes=[rsb])
        for j in range(6):
            c0, w = banks[j]
            nsl = w // 64
            P.op("pool", lambda e, c0=c0, w=w, j=j, nsl=nsl: e.tensor_tensor(
                out=nrm[:, c0:c0 + w].rearrange("p (s d) -> p s d", d=64),
                in0=raw[:, c0:c0 + w].rearrange("p (s d) -> p s d", d=64),
                in1=rs[:, 8 * j:8 * j + nsl].unsqueeze(2).to_broadcast([128, nsl, 64]), op=ALU.mult),
                reads=[rawb[j], rsb], writes=[nrmb[j]])
        for c in range(21):
            tqs, tqsb = tq[(c // 8) % 2]
            P.op("pe", lambda e, tqs=tqs, c=c: e.transpose(tqs[:, c % 8, :], nrm[:, c * 128:(c + 1) * 128], idb[:]),
                 reads=[nrmb[c // 4], idbb], writes=[tqsb])
            if c % 8 == 7 or c == 20:
                cb = (c // 8) * 8
                n = c - cb + 1
                P.op("dve", lambda e, tqs=tqs, cb=cb, n=n, FMs=FMs, tt=tt: e.tensor_tensor(
                    out=FMs[:, cb:cb + n, tt * 128:(tt + 1) * 128], in0=tqs[:, 0:n, :],
                    in1=gr[:, cb:cb + n].unsqueeze(2).to_broadcast([128, n, 128]), op=ALU.mult), reads=[tqsb, grb], writes=[FMsb])
        if tt == 3 and STAGE >= 7 and chunked is not None:
            for kk in range(2):
                k = (t - 3) // 2 + kk
                Sk = chunked[k]
                cb_ = [chunk_bufs[k]]
                ts_ = slice(kk * 256, (kk + 1) * 256)
                P.dma("sp", Sk[0:1280, :].rearrange("(c p) t -> p c t", p=128), FMs[:, 0:10, ts_], FMsb, reads=[FMsb], accw=cb_)
                P.dma("sp", Sk[1280:1344, :], FMs[0:64, 10, ts_], FMsb, reads=[FMsb], accw=cb_)
                P.dma("sp", Sk[1792:1856, :], FMs[64:128, 10, ts_], FMsb, reads=[FMsb], accw=cb_)
                P.dma("sp", Sk[1856:3136, :].rearrange("(c p) t -> p c t", p=128), FMs[:, 11:21, ts_], FMsb, reads=[FMsb], accw=cb_)
                for g in range(2):
                    P.dma("sp", Sk[1792 * g + 1344:1792 * (g + 1), :].rearrange("r x -> (r x)").rearrange("(tt p c) -> p tt c", p=128, c=NTMG),
                          TMs[:, 2 * kk:2 * kk + 2, g * NTMG:(g + 1) * NTMG], TMsb, reads=[TMsb], accw=cb_)
                if on_chunk is not None:
                    on_chunk(k)
            t0 = (t - 3) * 128
            for g in range(2):
                P.dma("sp", GT[g][t0:t0 + 512, :].rearrange("(tt p) c -> p tt c", p=128), gts[:, :, 12 * g:12 * g + 12],
                      gtsb, reads=[gtsb], writes=[dout])
        elif tt == 3 and STAGE >= 7:
            t0 = (t - 3) * 128
            for c3 in range(3):
                P.dma("sp", FM.rearrange("(c p) t -> p c t", p=128)[:, 7 * c3:7 * c3 + 7, t0:t0 + 512], FMs[:, 7 * c3:7 * c3 + 7, :],
                      FMsb, reads=[FMsb], writes=[dout])
            for g in range(2):
                P.dma("sp", TM[g][t0:t0 + 512, :].rearrange("(tt p) c -> p tt c", p=128), TMs[:, :, g * NTMG:(g + 1) * NTMG],
                      TMsb, reads=[TMsb], writes=[dout])
                P.dma("sp", GT[g][t0:t0 + 512, :].rearrange("(tt p) c -> p tt c", p=128), gts[:, :, 12 * g:12 * g + 12],
                      gtsb, reads=[gtsb], writes=[dout])
    front_norm(0)
    front_T(0)
    if ntile > 1:
        front_norm(1)
    back_mm(0)
    if ntile > 1:
        front_T(1)
    if ntile > 2:
        front_norm(2)
    for t in range(ntile):
        if t + 1 < ntile:
            back_mm(t + 1)
        if t + 2 < ntile:
            front_T(t + 2)
        if t + 3 < ntile:
            front_norm(t + 3)
        back_fin(t)
    if STAGE >= 7:
        if isinstance(CF, list):
            for g in range(2):
                P.dma("sp", CF[g], cfT[4 * g:4 * g + 4, :], cfTb, reads=[cfTb], writes=[dout])
        else:
            P.dma("sp", CF, cfT[:], cfTb, reads=[cfTb], writes=[dout])
    return dout


def prep_A(inp, l):
    perm, gain_idx, qsc = _colperm()
    wA = np.ascontiguousarray(inp["w_in"][l][:, perm])
    gr = np.ones((NFM,), np.float32)
    for s, gi in enumerate(gain_idx):
        if gi >= 0:
            gr[s * 64:(s + 1) * 64] = inp["qk_gain"][l, gi]
    return dict(
        wA=wA,
        gmix=np.ascontiguousarray(np.broadcast_to(inp["norm_mix"][l][None, :], (128, D))),
        gainrow=np.ascontiguousarray(gr.reshape(21, 128).T),
        qscale=np.ascontiguousarray(np.broadcast_to(qsc[None, :], (128, 42))),
        ident=np.eye(128, dtype=np.float32),
    )


def _rmsnorm_tile(P, xs, xsb, gm, gmb, h, hb, ss, ssb, junk, junkb):
    P.op("act", lambda e: e.activation(out=junk[:], in_=xs, func=AF.Square, accum_out=ss[:, 0:1]),
         reads=[xsb], writes=[junkb, ssb])
    P.op("dve", lambda e: e.tensor_scalar(out=ss[:, 1:2], in0=ss[:, 0:1], scalar1=1.0 / D, scalar2=EPS,
                                          op0=ALU.mult, op1=ALU.add), reads=[ssb], writes=[ssb])
    P.op("act", lambda e: e.activation(out=ss[:, 2:3], in_=ss[:, 1:2], func=AF.Sqrt), reads=[ssb], writes=[ssb])
    P.op("dve", lambda e: e.reciprocal(out=ss[:, 3:4], in_=ss[:, 2:3]), reads=[ssb], writes=[ssb])
    P.op("dve", lambda e: e.scalar_tensor_tensor(out=h, in0=xs, scalar=ss[:, 3:4], in1=gm[:],
                                                 op0=ALU.mult, op1=ALU.mult),
         reads=[xsb, ssb, gmb], writes=[hb])


def og_static(og):
    return lambda t, g: (lambda e: og[t * 128:(t + 1) * 128, :].rearrange("p (n g c) -> p n g c", n=3, g=2)[:, :, g, :])


def phase_C(P, NTC, xh, ogsrc, wM, wBr, wO, gmix, gffn, wG, wU, wD, convw, convb, ident, XM, xo, hflag,
            xdep=(), ogdep=(), dout_in=None, hnext=None):
    ntile = NTC // 128
    xdep, ogdep = list(xdep), list(ogdep)
    dout = dout_in if dout_in is not None else P.buf("doutC")
    hfl, hflb = P.sbuf("C_hfl", [128, 1], F32)
    P.dma("sp", hfl[:], hflag, hflb, writes=[hflb])
    xmb = [P.buf("C_xmd") for _ in range(ntile)]
    idb, idbb = P.sbuf("C_idb", [128, 128], BF16)
    P.dma("pool", idb[:], ident, idbb, writes=[idbb])
    ss, ssb = P.sbuf("C_ss", [128, 4], F32)
    junk, junkb = P.sbuf("C_junk", [128, D], F32)
    h, hb = P.sbuf("C_h", [128, D], BF16)
    with P.scope():
        wMs, _ = P.sbuf("C1_wM", [128, 8, 3072], BF16)
        wMb = [P.buf("C1_wMb") for _ in range(3)]
        for n_ in range(3):
            P.dma("pool", wMs[:, :, n_ * 1024:(n_ + 1) * 1024], wM[:, n_ * 1024:(n_ + 1) * 1024].rearrange("(k p) c -> p k c", p=128),
                  wMb[n_], writes=[wMb[n_]])
        wBs, wBb = P.sbuf("C1_wB", [128, 12, D], BF16)
        P.dma("pool", wBs[:], wBr.rearrange("(j p) d -> p j d", p=128), wBb, writes=[wBb])
        wOs, wOb = P.sbuf("C1_wO", [128, 8, D], BF16)
        P.dma("pool", wOs[:], wO.rearrange("(k p) d -> p k d", p=128), wOb, writes=[wOb])
        gm, gmb = P.sbuf("C1_gm", [128, D], F32)
        P.dma("sp", gm[:], gmix, gmb, writes=[gmb])
        xt = [P.sbuf(f"C1_x{i}", [128, D], F32) for i in range(2)]
        ot = [P.sbuf(f"C1_o{i}", [128, 1536], BF16) for i in range(2)]
        hT2 = [P.sbuf(f"C1_hT{i}", [128, 8, 128], BF16) for i in range(2)]
        oT2 = [P.sbuf(f"C1_oT{i}", [128, 12, 128], BF16) for i in range(2)]
        hC = [P.sbuf(f"C1_h{i}", [128, D], BF16) for i in range(2)]
        ssC = [P.sbuf(f"C1_ss{i}", [128, 4], F32) for i in range(2)]
        gs = [P.sbuf(f"C1_gs{i}", [128, 512], F32) for i in range(2)]
        tmpm = [P.sbuf(f"C1_tm{i}", [128, 512], F32) for i in range(2)]
        macc, maccb = P.sbuf("C1_macc", [128, D], F32)
        mbf, mbfb = P.sbuf("C1_mbf", [128, D], BF16)
        mT, mTb = P.sbuf("C1_mT", [128, 8, 128], BF16)
        xm = [P.sbuf(f"C1_xm{i}", [128, D], F32) for i in range(2)]
        tp, tpb = P.psum("C1_tp", [128, 8, 128], BF16)
        to = [P.psum(f"C1_to{i}", [128, 8, 128], BF16) for i in range(2)]
        PB = [P.psum(f"C1_PB{i}", [128, 512], F32) for i in range(4)]
        nb_ = [0]

        def c1_norm(t):
            xs, xsb = xt[t % 2]
            os_, osb = ot[t % 2]
            h_, h_b = hC[t % 2]
            ss_, ss_b = ssC[t % 2]
            P.dma("sp", xs[:], xh[t * 128:(t + 1) * 128, :], xsb, reads=xdep, writes=[xsb])
            for g in range(2):
                P.dma("sp", os_[:].rearrange("p (n g c) -> p n g c", n=3, g=2)[:, :, g, :], ogsrc(t, g), osb, reads=ogdep, writes=[osb])
            _rmsnorm_tile(P, xs[:], xsb, gm, gmb, h_[:], h_b, ss_, ss_b, junk, junkb)

        def c1_T(t):
            os_, osb = ot[t % 2]
            h_, h_b = hC[t % 2]
            hT, hTb = hT2[t % 2]
            oT, oTb = oT2[t % 2]
            for k in range(8):
                P.op("pe", lambda e, k=k: e.transpose(tp[:, k, :], h_[:, k * 128:(k + 1) * 128], idb[:]),
                     reads=[h_b, idbb], writes=[tpb])
            P.op("act", lambda e: e.copy(out=hT[:], in_=tp[:]), reads=[tpb], writes=[hTb])
            for j in range(12):
                tos, tosb = to[j // 8]
                P.op("pe", lambda e, tos=tos, j=j: e.transpose(tos[:, j % 8, :], os_[:, j * 128:(j + 1) * 128], idb[:]),
                     reads=[osb, idbb], writes=[tosb])
            P.op("dve", lambda e: e.tensor_copy(out=oT[:, 0:8, :], in_=to[0][0][:]), reads=[to[0][1]], writes=[oTb])
            P.op("act", lambda e: e.copy(out=oT[:, 8:12, :], in_=to[1][0][:, 0:4, :]), reads=[to[1][1]], writes=[oTb])

        def c1_back(t):
            nb = nb_[0]
            xs, xsb = xt[t % 2]
            hT, hTb = hT2[t % 2]
            oT, oTb = oT2[t % 2]
            for n in range(3):
                for half in range(2):
                    pg, pgb = PB[nb % 4]
                    pp, ppb = PB[(nb + 1) % 4]
                    nb += 2
                    c0 = n * 1024 + half * 512
                    for k in range(8):
                        P.op("pe", lambda e, pg=pg, k=k, c0=c0: e.matmul(pg[:], lhsT=hT[:, k, :], rhs=wMs[:, k, c0:c0 + 512],
                                                                         start=(k == 0), stop=(k == 7)),
                             reads=[hTb, wMb[n]], writes=[pgb])
                    for j in range(4):
                        P.op("pe", lambda e, pp=pp, j=j, n=n, half=half: e.matmul(
                            pp[:], lhsT=oT[:, 4 * n + j, :], rhs=wBs[:, 4 * n + j, half * 512:(half + 1) * 512],
                            start=(j == 0), stop=(j == 3)), reads=[oTb, wBb], writes=[ppb])
                    g_, g_b = gs[(n * 2 + half) % 2]
                    P.op("act", lambda e, pg=pg, g_=g_: e.activation(out=g_[:], in_=pg[:], func=AF.Sigmoid),
                         reads=[pgb], writes=[g_b])
                    hs = slice(half * 512, (half + 1) * 512)
                    if n == 0:
                        P.op("dve", lambda e, pp=pp, g_=g_, hs=hs: e.tensor_tensor(out=macc[:, hs], in0=pp[:], in1=g_[:], op=ALU.mult),
                             reads=[ppb, g_b], writes=[maccb])
                    else:
                        tm_, tm_b = tmpm[half]
                        P.op("dve", lambda e, pp=pp, g_=g_, tm_=tm_: e.tensor_tensor(out=tm_[:], in0=pp[:], in1=g_[:], op=ALU.mult),
                             reads=[ppb, g_b], writes=[tm_b])
                        if n == 1:
                            P.op("pool", lambda e, tm_=tm_, hs=hs: e.tensor_tensor(out=macc[:, hs], in0=macc[:, hs], in1=tm_[:], op=ALU.add),
                                 reads=[tm_b, maccb], writes=[maccb])
                        else:
                            P.op("pool", lambda e, tm_=tm_, hs=hs: e.tensor_tensor(out=mbf[:, hs], in0=macc[:, hs], in1=tm_[:], op=ALU.add),
                                 reads=[tm_b, maccb], writes=[mbfb])
            for k in range(8):
                P.op("pe", lambda e, k=k: e.transpose(tp[:, k, :], mbf[:, k * 128:(k + 1) * 128], idb[:]),
                     reads=[mbfb, idbb], writes=[tpb])
            P.op("act", lambda e: e.copy(out=mT[:], in_=tp[:]), reads=[tpb], writes=[mTb])
            xms, xmsb = xm[t % 2]
            for half in range(2):
                py, pyb = PB[nb % 4]
                nb += 1
                for k in range(8):
                    P.op("pe", lambda e, py=py, k=k, half=half: e.matmul(py[:], lhsT=mT[:, k, :], rhs=wOs[:, k, half * 512:(half + 1) * 512],
                                                                         start=(k == 0), stop=(k == 7)),
                         reads=[mTb, wOb], writes=[pyb])
                hs = slice(half * 512, (half + 1) * 512)
                P.op("dve", lambda e, py=py, xs=xs, xms=xms, hs=hs: e.tensor_tensor(out=xms[:, hs], in0=py[:], in1=xs[:, hs], op=ALU.add),
                     reads=[pyb, xsb], writes=[xmsb])
            P.dma("sp", XM[t * 128:(t + 1) * 128, :], xms[:], xmsb, reads=[xmsb], writes=[xmb[t]])
            nb_[0] = nb

        c1_norm(0)
        c1_T(0)
        if ntile > 1:
            c1_norm(1)
        for t in range(ntile):
            c1_back(t)
            if t + 1 < ntile:
                c1_T(t + 1)
            if t + 2 < ntile:
                c1_norm(t + 2)
    with P.scope():
        wGs, _ = P.sbuf("C2_wG", [128, 8, F_FF], BF16)
        wUs, _ = P.sbuf("C2_wU", [128, 8, F_FF], BF16)
        fgrp = [(0, 6), (6, 12), (12, 17), (17, 22)]
        fc_grp = [gi for gi, (a_, b_) in enumerate(fgrp) for _ in range(a_, b_)]
        wGb = [P.buf("C2_wGb") for _ in fgrp]
        wUb = [P.buf("C2_wUb") for _ in fgrp]
        for gi, (a_, b_) in enumerate(fgrp):
            P.dma("pool", wGs[:, :, a_ * 128:b_ * 128], wG[:, a_ * 128:b_ * 128].rearrange("(k p) c -> p k c", p=128), wGb[gi], writes=[wGb[gi]])
            P.dma("pool", wUs[:, :, a_ * 128:b_ * 128], wU[:, a_ * 128:b_ * 128].rearrange("(k p) c -> p k c", p=128), wUb[gi], writes=[wUb[gi]])
        wDs, _ = P.sbuf("C2_wD", [128, NFC, D], BF16)
        wDb = [P.buf("C2_wDb") for _ in range(2)]
        for i in range(2):
            P.dma("pool", wDs[:, 11 * i:11 * i + 11, :], wD[11 * i * 128:(11 * i + 11) * 128, :].rearrange("(f p) d -> p f d", p=128),
                  wDb[i], writes=[wDb[i]])
        gf, gfb = P.sbuf("C2_gf", [128, D], F32)
        P.dma("sp", gf[:], gffn, gfb, writes=[gfb])
        cw, cwb = P.sbuf("C2_cw", [128, NFC, 3], F32)
        P.dma("sp", cw[:], convw, cwb, writes=[cwb])
        cb, cbb = P.sbuf("C2_cb", [128, NFC], F32)
        P.dma("sp", cb[:], convb, cbb, writes=[cbb])
        NS = 256
        xm4 = [P.sbuf(f"C2_xm4{i}", [128, 2, D], F32) for i in range(2)]
        h2T2 = [P.sbuf(f"C2_h2T{i}", [128, 8, NS], BF16) for i in range(2)]
        Ab = [P.sbuf(f"C2_A{i}", [128, NS + 2], F32) for i in range(2)]
        cv = [P.sbuf(f"C2_cv{i}", [128, NS], F32) for i in range(2)]
        gl = [P.sbuf(f"C2_gl{i}", [128, NS], F32) for i in range(2)]
        halo, halob = P.sbuf("C2_halo", [128, NFC, 2], F32)
        gu, _ = P.sbuf("C2_gu", [128, NFC, NS], BF16)
        gub = [P.buf("C2_gub") for _ in range(NFC)]
        xot = [P.sbuf(f"C2_xo{i}", [128, D], F32) for i in range(2)]
        if hnext is not None:
            gmn, gmnb = P.sbuf("C2_gmn", [128, D], F32)
            P.dma("sp", gmn[:], hnext[0], gmnb, writes=[gmnb])
            hnx = [P.sbuf(f"C2_hn{i}", [128, D], BF16) for i in range(2)]
            ssn, ssnb = P.sbuf("C2_ssn", [128, 4], F32)
        tp, tpb = P.psum("C2_tp", [128, 8, 128], BF16)
        PA_ = [P.psum(f"C2_PA{i}", [128, 512], F32) for i in range(2)]
        PU_ = [P.psum(f"C2_PU{i}", [128, 512], F32) for i in range(3)]
        PY = [P.psum(f"C2_PY{i}", [128, 512], F32) for i in range(2)]
        sts = [(0, 1, True)] + [(1 + 2 * i, 2, False) for i in range((ntile - 1) // 2)]
        nb_ = [0, 0]

        def c2_front(si):
            t0, nt, is_halo = sts[si]
            x4, x4b = xm4[si % 2]
            h2T, h2Tb = h2T2[si % 2]
            P.dma("sp", x4[:, 0:nt, :], XM[t0 * 128:(t0 + nt) * 128, :].rearrange("(t p) d -> p t d", p=128), x4b,
                  reads=[xmb[t0 + i] for i in range(nt)], writes=[x4b])
            for tt in range(nt):
                _rmsnorm_tile(P, x4[:, tt, :], x4b, gf, gfb, h[:], hb, ss, ssb, junk, junkb)
                for k in range(8):
                    P.op("pe", lambda e, k=k: e.transpose(tp[:, k, :], h[:, k * 128:(k + 1) * 128], idb[:]),
                         reads=[hb, idbb], writes=[tpb])
                P.op("act", lambda e, tt=tt: e.copy(out=h2T[:, :, tt * 128:(tt + 1) * 128], in_=tp[:]), reads=[tpb], writes=[h2Tb])

        def c2_back(si):
            t0, nt, is_halo = sts[si]
            nb, ny = nb_
            ntok = nt * 128
            x4, x4b = xm4[si % 2]
            h2T, h2Tb = h2T2[si % 2]
            def stage_a(fc):
                pa, pab = PA_[fc % 2]
                for k in range(8):
                    P.op("pe", lambda e, pa=pa, k=k, fc=fc, ntok=ntok: e.matmul(
                        pa[:, 0:ntok], lhsT=wGs[:, k, fc * 128:(fc + 1) * 128], rhs=h2T[:, k, 0:ntok], start=(k == 0), stop=(k == 7)),
                        reads=[h2Tb, wGb[fc_grp[fc]]], writes=[pab])
                A_, A_b = Ab[fc % 2]
                if si == 0:
                    P.op("pool", lambda e, A_=A_: e.memset(A_[:, 0:2], 0.0), writes=[A_b])
                else:
                    P.op("pool", lambda e, A_=A_, fc=fc: e.tensor_copy(out=A_[:, 0:2], in_=halo[:, fc, :]), reads=[halob], writes=[A_b])
                P.op("act", lambda e, A_=A_, pa=pa, ntok=ntok: e.copy(out=A_[:, 2:2 + ntok], in_=pa[:, 0:ntok]),
                     reads=[pab], writes=[A_b])
                if is_halo:
                    P.op("pool", lambda e, A_=A_, fc=fc, ntok=ntok: e.tensor_scalar(
                        out=halo[:, fc, :], in0=A_[:, ntok:ntok + 2], scalar1=hfl[:, 0:1], scalar2=None, op0=ALU.mult),
                        reads=[A_b, hflb], writes=[halob])
                else:
                    P.op("act", lambda e, A_=A_, fc=fc, ntok=ntok: e.copy(out=halo[:, fc, :], in_=A_[:, ntok:ntok + 2]),
                         reads=[A_b], writes=[halob])

            def stage_b(fc, nb):
                A_, A_b = Ab[fc % 2]
                pu, pub = PU_[nb % 3]
                for k in range(8):
                    P.op("pe", lambda e, pu=pu, k=k, fc=fc, ntok=ntok: e.matmul(
                        pu[:, 0:ntok], lhsT=wUs[:, k, fc * 128:(fc + 1) * 128], rhs=h2T[:, k, 0:ntok], start=(k == 0), stop=(k == 7)),
                        reads=[h2Tb, wUb[fc_grp[fc]]], writes=[pub])
                cv_, cv_b = cv[fc % 2]
                P.op("pool", lambda e, A_=A_, cv_=cv_, fc=fc, ntok=ntok: e.tensor_scalar(
                    out=cv_[:, 0:ntok], in0=A_[:, 2:2 + ntok], scalar1=cw[:, fc, 2:3], scalar2=cb[:, fc:fc + 1], op0=ALU.mult, op1=ALU.add),
                    reads=[A_b, cwb, cbb], writes=[cv_b])
                P.op("dve", lambda e, A_=A_, cv_=cv_, fc=fc, ntok=ntok: e.scalar_tensor_tensor(
                    out=cv_[:, 0:ntok], in0=A_[:, 1:1 + ntok], scalar=cw[:, fc, 1:2], in1=cv_[:, 0:ntok], op0=ALU.mult, op1=ALU.add),
                    reads=[A_b, cwb, cv_b], writes=[cv_b])
                P.op("dve", lambda e, A_=A_, cv_=cv_, fc=fc, ntok=ntok: e.scalar_tensor_tensor(
                    out=cv_[:, 0:ntok], in0=A_[:, 0:ntok], scalar=cw[:, fc, 0:1], in1=cv_[:, 0:ntok], op0=ALU.mult, op1=ALU.add),
                    reads=[A_b, cwb, cv_b], writes=[cv_b])
                gl_, gl_b = gl[fc % 2]
                P.op("act", lambda e, cv_=cv_, gl_=gl_, ntok=ntok: e.activation(out=gl_[:, 0:ntok], in_=cv_[:, 0:ntok], func=AF.Gelu_apprx_tanh),
                     reads=[cv_b], writes=[gl_b])
                P.op("dve", lambda e, gl_=gl_, pu=pu, fc=fc, ntok=ntok: e.tensor_tensor(
                    out=gu[:, fc, 0:ntok], in0=pu[:, 0:ntok], in1=gl_[:, 0:ntok], op=ALU.mult),
                    reads=[pub, gl_b], writes=[gub[fc]])

            stage_a(0)
            for fc in range(NFC):
                if fc + 1 < NFC:
                    stage_a(fc + 1)
                if not is_halo:
                    stage_b(fc, nb)
                    nb += 1
            if is_halo:
                nb_[0], nb_[1] = nb, ny
                return
            for tt in range(nt):
                xos, xosb = xot[ny % 2]
                for half in range(2):
                    py, pyb = PY[ny % 2]
                    ny += 1
                    for fc in range(NFC):
                        P.op("pe", lambda e, py=py, fc=fc, tt=tt, half=half: e.matmul(
                            py[:], lhsT=gu[:, fc, tt * 128:(tt + 1) * 128], rhs=wDs[:, fc, half * 512:(half + 1) * 512],
                            start=(fc == 0), stop=(fc == NFC - 1)), reads=[gub[fc], wDb[fc // 11]], writes=[pyb])
                    hs = slice(half * 512, (half + 1) * 512)
                    P.op("dve", lambda e, py=py, xos=xos, x4=x4, tt=tt, hs=hs: e.tensor_tensor(
                        out=xos[:, hs], in0=py[:], in1=x4[:, tt, hs], op=ALU.add), reads=[pyb, x4b], writes=[xosb])
                r0 = (t0 - 1 + tt) * 128
                P.dma("sp", xo[r0:r0 + 128, :], xos[:], xosb, reads=[xosb], accw=[dout])
                if hnext is not None:
                    tl = t0 - 1 + tt
                    hn, hnb = hnx[tl % 2]
                    _rmsnorm_tile(P, xos[:], xosb, gmn, gmnb, hn[:], hnb, ssn, ssnb, junk, junkb)
                    P.dma("sp", hnext[1](tl), hn[:], hnb, reads=[hnb], accw=[hnext[2](tl)])
                    hnext[3](tl)
            nb_[0], nb_[1] = nb, ny

        c2_front(0)
        for si in range(len(sts)):
            if si + 1 < len(sts):
                c2_front(si + 1)
            c2_back(si)
    return dout


def build_C(NTC=2176):
    nc = bass.Bass("TRN2", target_bir_lowering=False)
    dt_ = lambda n, s, d=F32, k="ExternalInput": nc.dram_tensor(n, s, d, kind=k).ap()
    xh = dt_("xh", [NTC, D])
    og = dt_("og", [NTC, 1536], BF16)
    wM = dt_("wM", [D, 3072]); wBr = dt_("wBr", [1536, D]); wO = dt_("wO", [D, D])
    gmix = dt_("gmix", [128, D]); gffn = dt_("gffn", [128, D])
    wG = dt_("wG", [D, F_FF]); wU = dt_("wU", [D, F_FF]); wD = dt_("wD", [F_FF, D])
    convw = dt_("convw", [128, NFC, 3]); convb = dt_("convb", [128, NFC])
    ident = dt_("ident", [128, 128])
    hflag = dt_("hflag", [128, 1])
    XM = nc.dram_tensor("XM", [NTC, D], F32).ap()
    xo = dt_("xo", [NTC - 128, D], F32, "ExternalOutput")
    with ExitStack() as es:
        P = Prog(nc, es)
        dout = phase_C(P, NTC, xh, og_static(og), wM, wBr, wO, gmix, gffn, wG, wU, wD, convw, convb, ident, XM, xo, hflag)
        P.finish([dout])
    return nc


def prep_C(inp, l):
    return dict(
        wM=np.ascontiguousarray(inp["w_in"][l][:, OFF['merge']:]),
        wBr=np.ascontiguousarray(inp["w_branch"][l].reshape(1536, D)),
        wO=inp["w_out"][l],
        gmix=np.ascontiguousarray(np.broadcast_to(inp["norm_mix"][l][None, :], (128, D))),
        gffn=np.ascontiguousarray(np.broadcast_to(inp["norm_ffn"][l][None, :], (128, D))),
        wG=inp["w_gate"][l], wU=inp["w_up"][l], wD=inp["w_down"][l],
        convw=np.ascontiguousarray(inp["conv_w"][l].reshape(3, NFC, 128).transpose(2, 1, 0)),
        convb=np.ascontiguousarray(inp["conv_b"][l].reshape(NFC, 128).T),
        ident=np.eye(128, dtype=np.float32),
        hflag=np.ones((128, 1), np.float32),
    )


class SrcStatic:
    def __init__(self, TB, FMB, TMB, GTB, CFB):
        self.TH = TB // 2
        self.FMB, self.TMB, self.GTB, self.CFB = FMB, TMB, GTB, CFB

    def fm(self, r0, hf):
        return [(0, self.TH, lambda e: self.FMB[r0:r0 + 64, hf * self.TH:(hf + 1) * self.TH])]

    def tm(self, col0, hf):
        return [(0, self.TH, lambda e: self.TMB[hf * self.TH:(hf + 1) * self.TH, col0:col0 + 64])]

    def fm_all(self, r0):
        return lambda e: self.FMB[r0:r0 + 64, :].rearrange("p (a t) -> p a t", t=256)

    def tm_all(self):
        n = 2 * self.TH // 128
        return [(k0, min(8, n - k0), (lambda e, k0=k0: self.TMB.rearrange("(kt p) c -> p kt c", p=128)[:, k0:min(n, k0 + 8), :]))
                for k0 in range(0, n, 8)]

    def gt(self, hf):
        return lambda e: self.GTB[hf * self.TH:(hf + 1) * self.TH, :]

    def cf(self, hf):
        return lambda e: self.CFB[:, hf * self.TH:(hf + 1) * self.TH]


def phase_B(P, TB, src, c, OB, srcdep=(), dout_in=None, obuf=None, on_ob=None):
    srcdep = list(srcdep)
    obdst = OB if callable(OB) else (lambda qg, c0, c1: OB[qg * 512:(qg + 1) * 512, c0:c1])
    TH = TB // 2
    NQH = TB // 256
    NQT = TB // 128
    NQG = TB // 512
    NCP = TB // 16
    NCT = (NCP + 127) // 128
    NCW = NCT * 128
    dout = dout_in if dout_in is not None else P.buf("doutB")
    if obuf is None:
        obuf = lambda qg: dout
    idb, idbb = P.sbuf("B_idb", [128, 128], BF16)
    P.dma("pool", idb[:], c["ident"], idbb, writes=[idbb])
    idf, idfb = P.sbuf("B_idf", [128, 128], F32)
    P.dma("sp", idf[:], c["ident"], idfb, writes=[idfb])
    F = [P.psum(f"B_F{i}", [128, 512], F32) for i in range(7)]
    Tb, Tbb = P.psum("B_Tb", [128, 8, 128], BF16)
    SB3 = [F[0], F[1], F[6]]

    Vall, Vallb = P.sbuf("B_Vall", [128, NQT, NTMG], BF16)
    for (k0, nk, f) in src.tm_all():
        P.dma("sp", Vall[:, k0:k0 + nk, :], f, Vallb, reads=srcdep, accw=[Vallb])

    def load_vext(name, col0, nh):
        t, b = P.sbuf(name, [128, NQT, nh, 65], BF16)
        P.op("pool", lambda e: e.memset(t[:], 1.0), writes=[b])
        P.op("pool", lambda e: e.tensor_copy(out=t[:, :, :, 0:64],
                                             in_=Vall[:, :, col0:col0 + 64 * nh].rearrange("p k (h d) -> p k h d", d=64)),
             reads=[Vallb], writes=[b])
        return t, b

    def load_rows(dst, bufs, semb, r0):
        P.dma("sp", dst.rearrange("p (a t) -> p a t", t=256), src.fm_all(r0), semb, reads=srcdep, writes=bufs)

    def load_fm(name, r0):
        t, b = P.sbuf(name, [64, TB], BF16)
        load_rows(t[:], [b], b, r0)
        return t, b

    def finish_head(acc, accb, dst, reads, writes, extra_scalar=None, extra_b=None, addsink=None, accumulate=False,
                    gate_ap=None, gate_b=None, rz=None, rzb=None):
        if addsink is not None:
            P.op("dve", lambda e: e.tensor_tensor(out=rz[:, 0:1], in0=acc[:, 64:65], in1=addsink, op=ALU.add),
                 reads=[accb] + reads, writes=[rzb])
            P.op("dve", lambda e: e.reciprocal(out=rz[:, 0:1], in_=rz[:, 0:1]), reads=[rzb], writes=[rzb])
        else:
            P.op("dve", lambda e: e.reciprocal(out=rz[:, 0:1], in_=acc[:, 64:65]), reads=[accb], writes=[rzb])
        if gate_ap is not None:
            P.op("dve", lambda e: e.tensor_tensor(out=rz[:, 0:1], in0=rz[:, 0:1], in1=gate_ap, op=ALU.mult),
                 reads=[rzb, gate_b], writes=[rzb])
        if accumulate:
            P.op("dve", lambda e: e.scalar_tensor_tensor(out=dst, in0=acc[:, 0:64], scalar=rz[:, 0:1], in1=dst,
                                                         op0=ALU.mult, op1=ALU.add), reads=[accb, rzb] + writes, writes=writes)
        else:
            P.op("dve", lambda e: e.tensor_scalar(out=dst, in0=acc[:, 0:64], scalar1=rz[:, 0:1], scalar2=None, op0=ALU.mult),
                 reads=[accb, rzb], writes=writes)

    rzs = [P.sbuf(f"B_rz{i}", [128, 1], F32) for i in range(4)]
    PT = [P.sbuf(f"B_PT{i}", [128, 512], BF16) for i in range(3)]
    PT4 = [P.sbuf(f"B_PT4{i}", [128, 384], BF16) for i in range(4)]
    zero1, zero1b = P.sbuf("B_zero", [128, 1], F32)
    P.op("pool", lambda e: e.memset(zero1[:], 0.0), writes=[zero1b])
    nrz = [0]

    with P.scope():
        Qst = [P.sbuf(f"N_Q{h}", [128, TB], BF16)[0] for h in range(4)]
        Qb = [[P.buf(f"N_Qb{h}_") for _ in range(NQG)] for h in range(4)]
        for h in range(4):
            load_rows(Qst[h][0:64, :], Qb[h], Qb[h][0], FMR['qa'] + 64 * h)
        KE, KEb = P.sbuf("N_KE", [128, NQT, 128], BF16)
        load_rows(KE[0:64, :, :].rearrange("d kt p -> d (kt p)"), [KEb], KEb, FMR['ks'])
        P.dma("pool", KE[64:128, :, :], c["erows"][:, 0:NQT, :], KEb, writes=[KEb])
        kwT, kwTb = load_fm("N_kwT", FMR['kw'])
        Vs, Vsb = load_vext("N_Vs", TMC['vs'], 1)
        Vw, Vwb = load_vext("N_Vw", TMC['vw'], 1)
        gate, gateb = P.sbuf("N_gate", [128, NQT, 12], F32)
        for hf in range(2):
            fg = src.gt(hf)
            for k0 in range(0, NQH, 8):
                k1 = min(NQH, k0 + 8)
                P.dma("sp", gate[:, hf * NQH + k0:hf * NQH + k1, :],
                      (lambda e, fg=fg, k0=k0, k1=k1: fg(e).rearrange("(t p) c -> p t c", p=128)[:, k0:k1, :]),
                      gateb, reads=srcdep, writes=[gateb])
        bandC, bandCb = P.sbuf("N_bandC", [128, 4, 504], F32)
        P.dma("sp", bandC[:], c["bandC"], bandCb, writes=[bandCb])
        bandS, bandSb = P.sbuf("N_bandS", [128, 4, 5, 512], BF16)
        P.dma("pool", bandS[:], c["bandS"], bandSb, writes=[bandSb])
        bandW, bandWb = P.sbuf("N_bandW", [128, 4, 384], BF16)
        P.dma("pool", bandW[:], c["bandW"], bandWb, writes=[bandWb])
        tab31, tab31b = P.sbuf("N_tab31", [128, 4], F32)
        P.dma("sp", tab31[:], c["tab31"], tab31b, writes=[tab31b])
        AB, ABb = P.sbuf("N_AB", [128, 3, 126], F32)
        P.dma("sp", AB[:], c["topk"], ABb, writes=[ABb])
        ovl, ovlb = P.sbuf("N_ovl", [128, 2, 64], F32)
        P.dma("sp", ovl[:], c["overlap"], ovlb, writes=[ovlb])
        kcmpT, kcmpTb = P.sbuf("N_kcmpT", [64, NCW], BF16)
        vcmp, vcmpb = P.sbuf("N_vcmp", [128, NCT, 64], BF16)
        with P.scope():
            kg, kgb = P.sbuf("NC_kg", [128, 64], F32)
            P.dma("sp", kg[:], c["kgain"], kgb, writes=[kgb])
            hidT, hidTb = P.sbuf("NC_hidT", [128, 2, NCW], BF16)
            P.op("pool", lambda e: e.memset(hidT[:], 0.0), writes=[hidTb])
            pw, pwb = P.sbuf("NC_pw", [128, 2], F32)
            kraw, krawb = P.sbuf("NC_kraw", [128, 64], F32)
            ksq, ksqb = P.sbuf("NC_ksq", [128, 64], F32)
            kss, kssb = P.sbuf("NC_kss", [128, 4], F32)
            knb, knbb = P.sbuf("NC_knb", [128, 128], BF16)
            P.op("pool", lambda e: e.memset(knb[:], 0.0), writes=[knbb])
            for which in range(2):
                xT, xTb = load_fm(f"NC_xT{which}", FMR['kc'] if which == 0 else FMR['vc'])
                w1s, w1b = P.sbuf(f"NC_w1{which}", [64, 32, 256], BF16)
                P.dma("pool", w1s[:], c["w1"][which].rearrange("(l d) h -> d l h", d=64), w1b, writes=[w1b])
                w2s, w2b = P.sbuf(f"NC_w2{which}", [128, 2, 64], BF16)
                P.dma("pool", w2s[:], c["w2"][which].rearrange("(j p) d -> p j d", p=128), w2b, writes=[w2b])
                posT, posTb = P.sbuf(f"NC_pos{which}", [64, 32], BF16)
                P.dma("pool", posT[:], c["posT"][which], posTb, writes=[posTb])
                for hh in range(2):
                    f_, f_b = F[hh]
                    for l in range(32):
                        P.op("pe", lambda e, f_=f_, l=l, hh=hh: e.matmul(f_[:, 0:1], lhsT=w1s[:, l, hh * 128:(hh + 1) * 128],
                                                                         rhs=posT[:, l:l + 1], start=(l == 0), stop=(l == 31)),
                             reads=[w1b, posTb], writes=[f_b])
                    P.op("dve", lambda e, f_=f_, hh=hh: e.tensor_copy(out=pw[:, hh:hh + 1], in_=f_[:, 0:1]), reads=[f_b], writes=[pwb])
                ncr = NCP - 1
                for hh in range(2):
                    for c0 in range(0, ncr, 512):
                        cn = min(512, ncr - c0)
                        f_, f_b = F[2 + (hh + c0 // 512) % 2]
                        for l in range(32):
                            P.op("pe", lambda e, f_=f_, l=l, hh=hh, c0=c0, cn=cn: e.matmul(
                                f_[:, 0:cn], lhsT=w1s[:, l, hh * 128:(hh + 1) * 128],
                                rhs=xT[:, 16 * c0 + l:16 * c0 + l + 16 * (cn - 1) + 1:16], start=(l == 0), stop=(l == 31)),
                                reads=[w1b, xTb], writes=[f_b])
                        P.op("act", lambda e, f_=f_, hh=hh, c0=c0, cn=cn: e.activation(
                            out=hidT[:, hh, c0:c0 + cn], in_=f_[:, 0:cn], func=AF.Gelu_apprx_tanh, bias=pw[:, hh:hh + 1]),
                            reads=[f_b, pwb], writes=[hidTb])
                for nt in range(NCT):
                    f_, f_b = F[4 + nt % 2]
                    for hh in range(2):
                        P.op("pe", lambda e, f_=f_, hh=hh, nt=nt: e.matmul(f_[:, 0:64], lhsT=hidT[:, hh, nt * 128:(nt + 1) * 128],
                                                                           rhs=w2s[:, hh, :], start=(hh == 0), stop=(hh == 1)),
                             reads=[hidTb, w2b], writes=[f_b])
                    if which == 1:
                        P.op("act", lambda e, f_=f_, nt=nt: e.copy(out=vcmp[:, nt, :], in_=f_[:, 0:64]), reads=[f_b], writes=[vcmpb])
                    else:
                        P.op("dve", lambda e, f_=f_: e.tensor_copy(out=kraw[:], in_=f_[:, 0:64]), reads=[f_b], writes=[krawb])
                        P.op("act", lambda e: e.activation(out=ksq[:], in_=kraw[:], func=AF.Square, accum_out=kss[:, 0:1]),
                             reads=[krawb], writes=[ksqb, kssb])
                        P.op("dve", lambda e: e.tensor_scalar(out=kss[:, 1:2], in0=kss[:, 0:1], scalar1=1.0 / HD, scalar2=EPS,
                                                              op0=ALU.mult, op1=ALU.add), reads=[kssb], writes=[kssb])
                        P.op("act", lambda e: e.activation(out=kss[:, 2:3], in_=kss[:, 1:2], func=AF.Sqrt), reads=[kssb], writes=[kssb])
                        P.op("dve", lambda e: e.reciprocal(out=kss[:, 3:4], in_=kss[:, 2:3]), reads=[kssb], writes=[kssb])
                        P.op("dve", lambda e: e.scalar_tensor_tensor(out=knb[:, 0:64], in0=kraw[:], scalar=kss[:, 3:4], in1=kg[:],
                                                                     op0=ALU.mult, op1=ALU.mult), reads=[krawb, kssb, kgb], writes=[knbb])
                        P.op("pe", lambda e: e.transpose(Tb[:, 0, :], knb[:], idb[:]), reads=[knbb, idbb], writes=[Tbb])
                        P.op("act", lambda e, nt=nt: e.copy(out=kcmpT[:, nt * 128:(nt + 1) * 128], in_=Tb[0:64, 0, :]),
                             reads=[Tbb], writes=[kcmpTb])
        lg = [P.sbuf(f"N_lg{i}", [128, NCW], F32) for i in range(8)]
        pex = [P.sbuf(f"N_pex{i}", [128, NCW], F32) for i in range(8)]
        pbf = [P.sbuf(f"N_pbf{i}", [128, NCW], BF16) for i in range(8)]
        psacc = [P.sbuf(f"N_psacc{i}", [128, NCW], F32) for i in range(4)]
        pstmp = [P.sbuf(f"N_pstmp{i}", [128, NCW], F32) for i in range(2)]
        pT2 = [P.sbuf(f"N_pT{i}", [128, 4 * NCT, 128], BF16) for i in range(2)]
        psT, psTb = P.sbuf("N_psT", [128, NCT, 128], F32)
        Z = [P.sbuf(f"N_Z{i}", [128, 8], F32) for i in range(2)]
        sc, scb = P.sbuf("N_sc", [128, 64], F32)
        top8, top8b = P.sbuf("N_top8", [128, 8], F32)
        NM, NMb = P.sbuf("N_NM", [128, 128], F32)
        P.op("pool", lambda e: e.memset(NM[:], 0.0), writes=[NMb])
        OA, _ = P.sbuf("N_OA", [128, NQT, 256], F32)
        OAb = [P.buf("N_OAb") for _ in range(NQT)]
        OAo = [P.sbuf(f"N_OAo{i}", [128, 4, 256], BF16) for i in range(2)]
        rz4 = [P.sbuf(f"N_rz4{i}", [128, 4], F32) for i in range(2)]
        tmp4 = [P.sbuf(f"N_tmp4{i}", [128, 4, 64], F32) for i in range(2)]
        assert NCT <= 2

        def cmp_front(qts):
            items = [(i, qt, h) for i, qt in enumerate(qts) for h in range(4)]
            for (i, qt, h) in items:
                L, Lb = F[h]
                qs = slice(qt * 128, (qt + 1) * 128)
                P.op("pe", lambda e, L=L, h=h, i=i, qs=qs: e.matmul(L[:, i * 256:i * 256 + NCW], lhsT=Qst[h][0:64, qs], rhs=kcmpT[:, 0:NCW],
                                                                    start=True, stop=True), reads=[Qb[h][qt // 4], kcmpTb], writes=[Lb])
            for (i, qt, h) in items:
                L, Lb = F[h]
                boff = 8 * (31 - qt)
                lg_, lg_b = lg[i * 4 + h]
                P.op("dve", lambda e, L=L, h=h, i=i, lg_=lg_, boff=boff: e.tensor_tensor(
                    out=lg_[:], in0=L[:, i * 256:i * 256 + NCW], in1=bandC[:, h, boff:boff + NCW], op=ALU.add), reads=[Lb, bandCb], writes=[lg_b])
            for (i, qt, h) in items:
                lg_, lg_b = lg[i * 4 + h]
                px, pxb = pex[i * 4 + h]
                Zt, Ztb = Z[qt % 2]
                P.op("act", lambda e, lg_=lg_, px=px, Zt=Zt, h=h: e.activation(out=px[:], in_=lg_[:], func=AF.Exp, accum_out=Zt[:, h:h + 1]),
                     reads=[lg_b], writes=[pxb, Ztb])
            for qt in qts:
                Zt, Ztb = Z[qt % 2]
                P.op("dve", lambda e, Zt=Zt: e.tensor_scalar(out=Zt[:, 4:8], in0=Zt[:, 0:4], scalar1=1e-30, scalar2=None, op0=ALU.add),
                     reads=[Ztb], writes=[Ztb])
            for qt in qts:
                Zt, Ztb = Z[qt % 2]
                P.op("dve", lambda e, Zt=Zt: e.reciprocal(out=Zt[:, 4:8], in_=Zt[:, 4:8]), reads=[Ztb], writes=[Ztb])
            for (i, qt, h) in items:
                px, pxb = pex[i * 4 + h]
                Zt, Ztb = Z[qt % 2]
                P.op("dve", lambda e, px=px, Zt=Zt, h=h: e.tensor_scalar(out=px[:], in0=px[:], scalar1=Zt[:, 4 + h:5 + h], scalar2=None, op0=ALU.mult),
                     reads=[pxb, Ztb], writes=[pxb])
            for (i, qt, h) in items:
                px, pxb = pex[i * 4 + h]
                pb_, pb_b = pbf[i * 4 + h]
                P.op("act", lambda e, px=px, pb_=pb_: e.copy(out=pb_[:], in_=px[:]), reads=[pxb], writes=[pb_b])
            for i, qt in enumerate(qts):
                ps, psb = psacc[qt % 4]
                pt_, pt_b = pstmp[qt % 2]
                p0, p1, p2, p3 = [pex[i * 4 + h] for h in range(4)]
                P.op("pool", lambda e, ps=ps, p0=p0, p1=p1: e.tensor_tensor(out=ps[:], in0=p0[0][:], in1=p1[0][:], op=ALU.add),
                     reads=[p0[1], p1[1]], writes=[psb])
                P.op("pool", lambda e, pt_=pt_, p2=p2, p3=p3: e.tensor_tensor(out=pt_[:], in0=p2[0][:], in1=p3[0][:], op=ALU.add),
                     reads=[p2[1], p3[1]], writes=[pt_b])
                P.op("pool", lambda e, ps=ps, pt_=pt_: e.tensor_tensor(out=ps[:], in0=ps[:], in1=pt_[:], op=ALU.add), reads=[psb, pt_b], writes=[psb])
            for i, qt in enumerate(qts):
                pT_, pT_b = pT2[i]
                for h in range(4):
                    pb_, pb_b = pbf[i * 4 + h]
                    for nt in range(NCT):
                        P.op("pe", lambda e, h=h, nt=nt, pb_=pb_: e.transpose(Tb[:, h * NCT + nt, :], pb_[:, nt * 128:(nt + 1) * 128], idb[:]),
                             reads=[pb_b, idbb], writes=[Tbb])
                P.op("act", lambda e, pT_=pT_: e.copy(out=pT_[:], in_=Tb[:, 0:4 * NCT, :]), reads=[Tbb], writes=[pT_b])
            oc, ocb = F[4]
            for i, qt in enumerate(qts):
                pT_, pT_b = pT2[i]
                for h in range(4):
                    for nt in range(NCT):
                        P.op("pe", lambda e, h=h, nt=nt, i=i, pT_=pT_: e.matmul(
                            oc[:, i * 256 + h * 64:i * 256 + (h + 1) * 64], lhsT=pT_[:, h * NCT + nt, :], rhs=vcmp[:, nt, :],
                            start=(nt == 0), stop=(nt == NCT - 1)), reads=[pT_b, vcmpb], writes=[ocb])
            for i, qt in enumerate(qts):
                P.op("dve", lambda e, i=i, qt=qt: e.tensor_tensor(
                    out=OA[:, qt, :].rearrange("p (h d) -> p h d", h=4), in0=oc[:, i * 256:(i + 1) * 256].rearrange("p (h d) -> p h d", h=4),
                    in1=gate[:, qt, 0:12:3].unsqueeze(2).to_broadcast([128, 4, 64]), op=ALU.mult), reads=[ocb, gateb], writes=[OAb[qt]])

        def cmp_back(qt):
            qs = slice(qt * 128, (qt + 1) * 128)
            qg = qt // 4
            ps, psb = psacc[qt % 4]
            f5, f5b = F[5]
            for nt in range(NCT):
                P.op("pe", lambda e, nt=nt: e.transpose(f5[:, nt * 128:(nt + 1) * 128], ps[:, nt * 128:(nt + 1) * 128], idf[:]),
                     reads=[psb, idfb], writes=[f5b])
            P.op("act", lambda e: e.copy(out=psT[:].rearrange("p a b -> p (a b)"), in_=f5[:, 0:NCW]), reads=[f5b], writes=[psTb])
            f6, f6b = F[6]
            for nt in range(NCT):
                P.op("pe", lambda e, nt=nt: e.matmul(f6[:, 0:64], lhsT=psT[:, nt, :], rhs=ovl[:, nt, :], start=(nt == 0), stop=(nt == NCT - 1)),
                     reads=[psTb, ovlb], writes=[f6b])
            s0 = 62 - 2 * qt
            P.op("dve", lambda e: e.tensor_tensor(out=sc[:], in0=f6[:, 0:64], in1=AB[:, 0, s0:s0 + 64], op=ALU.mult), reads=[f6b, ABb], writes=[scb])
            P.op("dve", lambda e: e.tensor_tensor(out=sc[:], in0=sc[:], in1=AB[:, 1, s0:s0 + 64], op=ALU.add), reads=[scb, ABb], writes=[scb])
            P.op("dve", lambda e: e.memset(sc[:, 0:1], 1e9), writes=[scb])
            P.op("dve", lambda e: e.max(out=top8[:], in_=sc[:]), reads=[scb], writes=[top8b])
            P.op("dve", lambda e: e.tensor_scalar(out=sc[:], in0=sc[:], scalar1=top8[:, 7:8], scalar2=None, op0=ALU.is_ge),
                 reads=[scb, top8b], writes=[scb])
            P.op("dve", lambda e: e.tensor_tensor(out=sc[:], in0=sc[:], in1=AB[:, 2, s0:s0 + 64], op=ALU.mult), reads=[scb, ABb], writes=[scb])
            P.op("dve", lambda e: e.tensor_scalar(out=NM[:, 64:128], in0=sc[:], scalar1=-1.0, scalar2=-NEG, op0=ALU.add, op1=ALU.mult),
                 reads=[scb], writes=[NMb])
            P.op("pe", lambda e: e.transpose(f6[:, 128:256], NM[:], idf[:]), reads=[NMb, idfb], writes=[f6b])
            P.op("act", lambda e: e.copy(out=Qst[0][64:128, qs], in_=f6[64:128, 128:256]), reads=[f6b], writes=[Qb[0][qg]])
            for h in range(1, 4):
                P.op("pool", lambda e, h=h: e.tensor_copy(out=Qst[h][64:128, qs], in_=Qst[0][64:128, qs]), reads=[Qb[0][qg]], writes=[Qb[h][qg]])

        cmp_front([0, 1])
        for p_ in range(NQT // 2):
            if p_ + 1 < NQT // 2:
                cmp_front([2 * p_ + 2, 2 * p_ + 3])
            cmp_back(2 * p_)
            cmp_back(2 * p_ + 1)

        def finish4(accp, accpb, qt, gcol, n, sink=None, sinkb=None, dst=None, dstb=None):
            rz, rzb = rz4[n % 2]
            a3 = accp[:, 0:260].rearrange("p (h d) -> p h d", d=65)
            if sink is not None:
                P.op("dve", lambda e: e.tensor_tensor(out=rz[:], in0=a3[:, :, 64], in1=sink, op=ALU.add), reads=[accpb, sinkb], writes=[rzb])
                P.op("dve", lambda e: e.reciprocal(out=rz[:], in_=rz[:]), reads=[rzb], writes=[rzb])
            else:
                P.op("dve", lambda e: e.reciprocal(out=rz[:], in_=a3[:, :, 64]), reads=[accpb], writes=[rzb])
            if gcol is not None:
                P.op("dve", lambda e: e.tensor_tensor(out=rz[:], in0=rz[:], in1=gate[:, qt, gcol:12:3], op=ALU.mult), reads=[rzb, gateb], writes=[rzb])
            if dst is None:
                tm_, tm_b = tmp4[n % 2]
                P.op("dve", lambda e: e.tensor_tensor(out=tm_[:], in0=a3[:, :, 0:64], in1=rz[:].unsqueeze(2).to_broadcast([128, 4, 64]), op=ALU.mult),
                     reads=[accpb, rzb], writes=[tm_b])
                P.op("pool", lambda e: e.tensor_tensor(out=OA[:, qt, :], in0=OA[:, qt, :], in1=tm_[:].rearrange("p h d -> p (h d)"), op=ALU.add),
                     reads=[tm_b, OAb[qt]], writes=[OAb[qt]])
            else:
                P.op("dve", lambda e: e.tensor_tensor(out=dst, in0=a3[:, :, 0:64], in1=rz[:].unsqueeze(2).to_broadcast([128, 4, 64]), op=ALU.mult),
                     reads=[accpb, rzb], writes=[dstb])

        nfin = 0
        for qg in range(NQG):
            qcs = slice(qg * 512, (qg + 1) * 512)
            nkt = 4 * qg + 4
            for h in range(4):
                def emit_qk(kt, h=h):
                    S, Sb = SB3[kt % 3]
                    m = kt - 4 * qg + 1
                    if m >= 0:
                        P.op("pe", lambda e, S=S, m=m: e.matmul(S[:], lhsT=idb[:], rhs=bandS[:, h, m, :], start=True, stop=False),
                             reads=[idbb, bandSb], writes=[Sb])
                        P.op("pe", lambda e, S=S, kt=kt: e.matmul(S[:], lhsT=KE[:, kt, :], rhs=Qst[h][:, qcs], start=False, stop=True),
                             reads=[KEb, Qb[h][qg]], writes=[Sb])
                    else:
                        P.op("pe", lambda e, S=S, kt=kt: e.matmul(S[:], lhsT=KE[:, kt, :], rhs=Qst[h][:, qcs], start=True, stop=True),
                             reads=[KEb, Qb[h][qg]], writes=[Sb])
                emit_qk(0)
                if nkt > 1:
                    emit_qk(1)
                for kt in range(nkt):
                    if kt + 2 < nkt:
                        emit_qk(kt + 2)
                    S, Sb = SB3[kt % 3]
                    m = kt - 4 * qg + 1
                    pt, ptb = PT[kt % 3]
                    bias_ap = zero1[:, 0:1] if m >= 0 else tab31[:, h:h + 1]
                    P.op("act", lambda e, S=S, pt=pt, bias_ap=bias_ap: e.activation(out=pt[:], in_=S[:], func=AF.Exp, bias=bias_ap),
                         reads=[Sb, tab31b, zero1b], writes=[ptb])
                    for qi in range(4):
                        if kt > 4 * qg + qi:
                            continue
                        acc, accb = F[2 + qi]
                        P.op("pe", lambda e, acc=acc, pt=pt, qi=qi, kt=kt: e.matmul(
                            acc[:, 0:65], lhsT=pt[:, qi * 128:(qi + 1) * 128], rhs=Vs[:, kt, 0, :], start=(kt == 0), stop=(kt == 4 * qg + qi)),
                            reads=[ptb, Vsb], writes=[accb])
                for qi in range(4):
                    acc, accb = F[2 + qi]
                    qt = 4 * qg + qi
                    rz, rzb = rzs[nrz[0] % 4]
                    nrz[0] += 1
                    finish_head(acc, accb, OA[:, qt, h * 64:(h + 1) * 64], [], [OAb[qt]], accumulate=True,
                                gate_ap=gate[:, qt, 3 * h + 1:3 * h + 2], gate_b=gateb, rz=rz, rzb=rzb)
        for qt in range(NQT):
            qs = slice(qt * 128, (qt + 1) * 128)
            qg = qt // 4
            rr = [r for r in range(3) if qt - 2 + r >= 0]
            for h in range(4):
                S, Sb = F[h]
                P.op("pe", lambda e, S=S, h=h: e.matmul(S[:, 0:384], lhsT=idb[:], rhs=bandW[:, h, :], start=True, stop=False),
                     reads=[idbb, bandWb], writes=[Sb])
                for r in rr:
                    kt = qt - 2 + r
                    P.op("pe", lambda e, S=S, h=h, r=r, kt=kt, last=(r == rr[-1]): e.matmul(
                        S[:, r * 128:(r + 1) * 128], lhsT=kwT[:, kt * 128:(kt + 1) * 128], rhs=Qst[h][0:64, qs], start=False, stop=last),
                        reads=[kwTb, Qb[h][qg]], writes=[Sb])
            for h in range(4):
                S, Sb = F[h]
                pt, ptb = PT4[h]
                P.op("act", lambda e, S=S, pt=pt: e.activation(out=pt[:, 0:384], in_=S[:, 0:384], func=AF.Exp), reads=[Sb], writes=[ptb])
            accp, accpb = F[4 + qt % 2]
            for h in range(4):
                pt, ptb = PT4[h]
                for r in rr:
                    kt = qt - 2 + r
                    P.op("pe", lambda e, pt=pt, h=h, r=r, kt=kt, first=(r == rr[0]), last=(r == rr[-1]): e.matmul(
                        accp[:, h * 65:(h + 1) * 65], lhsT=pt[:, r * 128:(r + 1) * 128], rhs=Vw[:, kt, 0, :], start=first, stop=last),
                        reads=[ptb, Vwb], writes=[accpb])
            finish4(accp, accpb, qt, 2, qt)
            if qt % 4 == 3:
                oo, oob = OAo[qg % 2]
                P.op("act", lambda e, oo=oo: e.copy(out=oo[:], in_=OA[:, 4 * qg:4 * qg + 4, :]), reads=[OAb[4 * qg + i] for i in range(4)], writes=[oob])
                P.dma("sp", obdst(qg, 0, 256).rearrange("(t p) c -> p t c", p=128), oo[:], oob, reads=[oob], accw=[obuf(qg)])

    with P.scope():
        qT = [load_fm(f"S_q{h}", FMR['qb'] + 64 * h) for h in range(4)]
        kTs, kTsb = load_fm("S_k", FMR['kb'])
        Vb, Vbb = load_vext("S_V", TMC['vb'], 1)
        bandB, bandBb = P.sbuf("S_band", [128, 4, 256], BF16)
        P.dma("pool", bandB[:], c["bandB"], bandBb, writes=[bandBb])
        esk, eskb = P.sbuf("S_esk", [128, 4], F32)
        P.dma("sp", esk[:], c["sinkb"], eskb, writes=[eskb])
        P.op("act", lambda e: e.activation(out=esk[:], in_=esk[:], func=AF.Exp), reads=[eskb], writes=[eskb])
        OS = [P.sbuf(f"S_O{i}", [128, 4, 256], BF16) for i in range(2)]
        rz4 = [P.sbuf(f"S_rz4{i}", [128, 4], F32) for i in range(2)]
        for qt in range(NQT):
            qs = slice(qt * 128, (qt + 1) * 128)
            os_, osb = OS[(qt // 4) % 2]
            rr = [r for r in range(2) if qt - 1 + r >= 0]
            for h in range(4):
                S, Sb = F[h]
                P.op("pe", lambda e, S=S, h=h: e.matmul(S[:, 0:256], lhsT=idb[:], rhs=bandB[:, h, :], start=True, stop=False),
                     reads=[idbb, bandBb], writes=[Sb])
                for r in rr:
                    kt = qt - 1 + r
                    P.op("pe", lambda e, S=S, h=h, r=r, kt=kt, last=(r == rr[-1]): e.matmul(
                        S[:, r * 128:(r + 1) * 128], lhsT=kTs[:, kt * 128:(kt + 1) * 128], rhs=qT[h][0][:, qs], start=False, stop=last),
                        reads=[kTsb, qT[h][1]], writes=[Sb])
            for h in range(4):
                S, Sb = F[h]
                pt, ptb = PT4[h]
                P.op("act", lambda e, S=S, pt=pt: e.activation(out=pt[:, 0:256], in_=S[:, 0:256], func=AF.Exp), reads=[Sb], writes=[ptb])
            accp, accpb = F[4 + qt % 2]
            for h in range(4):
                pt, ptb = PT4[h]
                for r in rr:
                    kt = qt - 1 + r
                    P.op("pe", lambda e, pt=pt, h=h, r=r, kt=kt, first=(r == rr[0]), last=(r == rr[-1]): e.matmul(
                        accp[:, h * 65:(h + 1) * 65], lhsT=pt[:, r * 128:(r + 1) * 128], rhs=Vb[:, kt, 0, :], start=first, stop=last),
                        reads=[ptb, Vbb], writes=[accpb])
            rz, rzb = rz4[qt % 2]
            a3 = accp[:, 0:260].rearrange("p (h d) -> p h d", d=65)
            P.op("dve", lambda e, rz=rz, a3=a3: e.tensor_tensor(out=rz[:], in0=a3[:, :, 64], in1=esk[:], op=ALU.add), reads=[accpb, eskb], writes=[rzb])
            P.op("dve", lambda e, rz=rz: e.reciprocal(out=rz[:], in_=rz[:]), reads=[rzb], writes=[rzb])
            P.op("dve", lambda e, rz=rz, a3=a3, os_=os_: e.tensor_tensor(
                out=os_[:, qt % 4, :].rearrange("p (h d) -> p h d", h=4), in0=a3[:, :, 0:64],
                in1=rz[:].unsqueeze(2).to_broadcast([128, 4, 64]), op=ALU.mult), reads=[accpb, rzb], writes=[osb])
            if qt % 4 == 3:
                qg = qt // 4
                P.dma("sp", obdst(qg, 256, 512).rearrange("(t p) c -> p t c", p=128), os_[:], osb, reads=[osb], accw=[obuf(qg)])

    with P.scope():
        qT = [load_fm(f"X_q{h}", FMR['qc'] + 64 * h) for h in range(4)]
        kT = [load_fm(f"X_k{h}", FMR['kcf'] + 64 * h) for h in range(4)]
        Vc, Vcb = load_vext("X_V", TMC['vcf'], 4)
        bandF, bandFb = P.sbuf("X_band", [128, 4, 512], BF16)
        P.dma("pool", bandF[:], c["bandF"], bandFb, writes=[bandFb])
        cf, cfb = P.sbuf("X_cf", [4, TB], F32)
        for hf in range(2):
            P.dma("sp", cf[:, hf * TH:(hf + 1) * TH], src.cf(hf), cfb, reads=srcdep, writes=[cfb])
        fbt, fbb = P.sbuf("X_fb", [4, 1], F32)
        P.dma("sp", fbt[:], c["fb"], fbb, writes=[fbb])
        ones4, ones4b = P.sbuf("X_ones", [4, TB], F32)
        P.op("pool", lambda e: e.memset(ones4[:], 1.0), writes=[ones4b])
        cpos, cposb = P.sbuf("X_cpos", [4, TB], F32)
        P.op("dve", lambda e: e.tensor_scalar(out=fbt[:], in0=fbt[:], scalar1=-1.0, scalar2=None, op0=ALU.mult), reads=[fbb], writes=[fbb])
        P.op("act", lambda e: e.activation(out=cf[:], in_=cf[:], func=AF.Exp, bias=fbt[:, 0:1], scale=-1.0), reads=[cfb, fbb], writes=[cfb])
        P.op("act", lambda e: e.activation(out=cf[:], in_=cf[:], func=AF.Ln, bias=1.0), reads=[cfb], writes=[cfb])
        P.op("dve", lambda e: e.tensor_tensor_scan(out=cpos[:], data0=ones4[:], data1=cf[:], initial=0.0, op0=ALU.mult, op1=ALU.add),
             reads=[cfb, ones4b], writes=[cposb])
        cposT, cposTb = P.sbuf("X_cposT", [128, NQT, 4], F32)
        f6, f6b = F[6]
        for kt in range(NQT):
            P.op("pe", lambda e, kt=kt: e.transpose(f6[:, kt * 4:(kt + 1) * 4], cpos[:, kt * 128:(kt + 1) * 128], idf[0:4, 0:4]),
                 reads=[cposb, idfb], writes=[f6b])
        P.op("dve", lambda e: e.tensor_copy(out=cposT[:].rearrange("p a b -> p (a b)"), in_=f6[:, 0:NQT * 4]), reads=[f6b], writes=[cposTb])
        Dm, Dmb = P.sbuf("X_D", [4, 4, NQG], F32)
        P.op("dve", lambda e: e.tensor_tensor(out=Dm[:], in0=cpos[:, 0:TB:512].unsqueeze(1).to_broadcast([4, 4, NQG]),
                                              in1=idf[0:4, 0:4].unsqueeze(2).to_broadcast([4, 4, NQG]), op=ALU.mult),
             reads=[cposb, idfb], writes=[Dmb])
        P.op("pool", lambda e: e.memset(ones4[:, 0:128], 1.0), writes=[ones4b])
        f5, f5b = F[5]
        P.op("pe", lambda e: e.matmul(f5[:, 0:4 * NQG], lhsT=ones4[:, 0:128], rhs=Dm[:].rearrange("k h q -> k (h q)"), start=True, stop=True),
             reads=[ones4b, Dmb], writes=[f5b])
        crefb, crefbb = P.sbuf("X_cref", [128, 4, NQG], F32)
        P.op("dve", lambda e: e.tensor_copy(out=crefb[:].rearrange("p a b -> p (a b)"), in_=f5[:, 0:4 * NQG]), reads=[f5b], writes=[crefbb])
        bvec = [P.sbuf(f"X_bv{i}", [128, NQT], F32) for i in range(2)]
        OX = [P.sbuf(f"X_O{i}", [128, 4, 256], BF16) for i in range(2)]
        n = 0
        for qg in range(NQG):
            ox, oxb = OX[qg % 2]
            qcs = slice(qg * 512, (qg + 1) * 512)
            nkt = 4 * qg + 4
            for h in range(4):
                bv, bvb = bvec[h % 2]
                P.op("dve", lambda e, bv=bv, h=h, qg=qg, nkt=nkt: e.tensor_scalar(
                    out=bv[:, 0:nkt], in0=cposT[:, 0:nkt, h], scalar1=crefb[:, h, qg:qg + 1], scalar2=None, op0=ALU.subtract),
                    reads=[cposTb, crefbb], writes=[bvb])

                def emit_qk(kt, h=h):
                    S, Sb = SB3[kt % 3]
                    m = kt - 4 * qg + 1
                    if m >= 1:
                        P.op("pe", lambda e, S=S, m=m: e.matmul(S[:], lhsT=idb[:], rhs=bandF[:, m - 1, :], start=True, stop=False),
                             reads=[idbb, bandFb], writes=[Sb])
                        P.op("pe", lambda e, S=S, kt=kt: e.matmul(S[:], lhsT=kT[h][0][:, kt * 128:(kt + 1) * 128], rhs=qT[h][0][:, qcs],
                                                                  start=False, stop=True), reads=[kT[h][1], qT[h][1]], writes=[Sb])
                    else:
                        P.op("pe", lambda e, S=S, kt=kt: e.matmul(S[:], lhsT=kT[h][0][:, kt * 128:(kt + 1) * 128], rhs=qT[h][0][:, qcs],
                                                                  start=True, stop=True), reads=[kT[h][1], qT[h][1]], writes=[Sb])
                emit_qk(0)
                if nkt > 1:
                    emit_qk(1)
                for kt in range(nkt):
                    if kt + 2 < nkt:
                        emit_qk(kt + 2)
                    S, Sb = SB3[kt % 3]
                    pt, ptb = PT[kt % 3]
                    P.op("act", lambda e, S=S, pt=pt, bv=bv, kt=kt: e.activation(out=pt[:], in_=S[:], func=AF.Exp, bias=bv[:, kt:kt + 1]),
                         reads=[Sb, bvb], writes=[ptb])
                    for qi in range(4):
                        if kt > 4 * qg + qi:
                            continue
                        acc, accb = F[2 + qi]
                        P.op("pe", lambda e, acc=acc, pt=pt, qi=qi, kt=kt, h=h: e.matmul(
                            acc[:, 0:65], lhsT=pt[:, qi * 128:(qi + 1) * 128], rhs=Vc[:, kt, h, :], start=(kt == 0), stop=(kt == 4 * qg + qi)),
                            reads=[ptb, Vcb], writes=[accb])
                for qi in range(4):
                    acc, accb = F[2 + qi]
                    rz, rzb = rzs[n % 4]
                    n += 1
                    finish_head(acc, accb, ox[:, qi, h * 64:(h + 1) * 64], [], [oxb], rz=rz, rzb=rzb)
            P.dma("sp", obdst(qg, 512, 768).rearrange("(t p) c -> p t c", p=128), ox[:], oxb, reads=[oxb], accw=[obuf(qg)])
            if on_ob is not None:
                on_ob(qg)
    return dout


def _t5_bucket(d):
    d = np.maximum(d, 0)
    lr = np.log(np.maximum(d, 1).astype(np.float32) / np.float32(16)) / np.float32(math.log(128 / 16))
    large = np.minimum(16 + (lr * np.float32(16)).astype(np.int32), 31)
    return np.where(d < 16, d, large)


def _bias_lookup(tabcol, d, valid):
    out = np.full(d.shape, NEG, np.float32)
    b = _t5_bucket(d)
    out[valid] = tabcol[b[valid]]
    return out


def build_B(TB=4096):
    nc = bass.Bass("TRN2", target_bir_lowering=False)
    dt_ = lambda n, s, d=F32, k="ExternalInput": nc.dram_tensor(n, s, d, kind=k).ap()
    FMB = dt_("FMB", [NFMG, TB], BF16)
    TMB = dt_("TMB", [TB, NTMG], BF16)
    GTB = dt_("GTB", [TB, 12])
    CFB = dt_("CFB", [4, TB])
    c = dict(
        ident=dt_("ident", [128, 128]), erows=dt_("erows", [64, 32, 128]),
        w1=dt_("w1", [2, 2048, 256]), w2=dt_("w2", [2, 256, 64]), posT=dt_("posT", [2, 64, 32]),
        kgain=dt_("kgain", [128, 64]), bandC=dt_("bandC", [128, 4, 504]), bandS=dt_("bandS", [128, 4, 5, 512]),
        bandW=dt_("bandW", [128, 4, 384]), bandB=dt_("bandB", [128, 4, 256]), bandF=dt_("bandF", [128, 4, 512]),
        tab31=dt_("tab31", [128, 4]), topk=dt_("topk", [128, 3, 126]), overlap=dt_("overlap", [128, 2, 64]),
        sinkb=dt_("sinkb", [128, 4]), fb=dt_("fb", [4, 1]),
    )
    OB = dt_("OB", [TB, 768], BF16, "ExternalOutput")
    with ExitStack() as es:
        P = Prog(nc, es)
        dout = phase_B(P, TB, SrcStatic(TB, FMB, TMB, GTB, CFB), c, OB)
        P.finish([dout])
    return nc


def prep_B_consts():
    p = np.arange(128)
    erows = np.zeros((64, 32, 128), np.float32)
    for kt in range(32):
        erows[2 * kt, kt, 0:64] = 1.0
        erows[2 * kt + 1, kt, 64:128] = 1.0
    rel = np.arange(126)[None, :] - 62
    curr = (p[:, None] >= 64).astype(np.int64)
    valid = rel <= curr
    forced = (rel == curr) | (rel == curr - 1)
    A = (valid & ~forced).astype(np.float32)
    Bm = np.where(~valid, np.float32(-1e30), np.where(forced, np.float32(1e9), np.float32(0.0))).astype(np.float32)
    V = valid.astype(np.float32)
    topk = np.stack([A, Bm, V], axis=1).astype(np.float32)
    n = np.arange(256)
    j = np.arange(64)
    ov = ((16 * n[:, None] < 64 * j[None, :] + 64) & (16 * n[:, None] + 32 > 64 * j[None, :])).astype(np.float32)
    ov[255] = 0.0
    overlap = np.ascontiguousarray(ov.reshape(2, 128, 64).transpose(1, 0, 2))
    i = np.arange(512)
    bandF = np.zeros((128, 4, 512), np.float32)
    for m in range(1, 5):
        d = i[None, :] - 128 * (m - 1) - p[:, None]
        bandF[:, m - 1, :] = np.where(d >= 0, 0.0, NEG)
    return dict(ident=np.eye(128, dtype=np.float32), erows=erows, topk=topk, overlap=overlap, bandF=bandF)


def prep_B(inp, l, g, consts):
    tabA = inp["rel_bias"][:, 4 * g:4 * g + 4]
    tabB = inp["rel_bias"][:, 8 + 4 * g:8 + 4 * g + 4]
    p = np.arange(128)
    bandC = np.zeros((128, 4, 504), np.float32)
    m = np.arange(504)
    dC = p[:, None] - 16 * (m[None, :] - 248) - 31
    i = np.arange(512)
    bandS = np.zeros((128, 4, 5, 512), np.float32)
    bandW = np.zeros((128, 4, 384), np.float32)
    bandB = np.zeros((128, 4, 256), np.float32)
    q = np.arange(128)
    for h in range(4):
        bandC[:, h, :] = _bias_lookup(tabA[:, h], dC, dC >= 0)
        for mm in range(5):
            d = i[None, :] - 128 * (mm - 1) - p[:, None]
            bandS[:, h, mm, :] = _bias_lookup(tabA[:, h], d, d >= 0)
        for r in range(3):
            d = (2 - r) * 128 + q[None, :] - p[:, None]
            bandW[:, h, r * 128:(r + 1) * 128] = _bias_lookup(tabA[:, h], d, (d >= 0) & (d < 256))
        for r in range(2):
            d = (1 - r) * 128 + q[None, :] - p[:, None]
            bandB[:, h, r * 128:(r + 1) * 128] = _bias_lookup(tabB[:, h], d, (d >= 0) & (d < 128))
    out = dict(consts)
    out.update(
        w1=inp["cmp_w1"][l], w2=inp["cmp_w2"][l],
        posT=np.ascontiguousarray(inp["cmp_pos"][l].transpose(0, 2, 1)),
        kgain=np.ascontiguousarray(np.broadcast_to(inp["qk_gain"][l, 1][None, :], (128, 64))),
        bandC=bandC, bandS=bandS, bandW=bandW, bandB=bandB,
        tab31=np.ascontiguousarray(np.broadcast_to(tabA[31][None, :], (128, 4))),
        sinkb=np.ascontiguousarray(np.broadcast_to(inp["sinks"][l, 4 * g:4 * g + 4][None, :], (128, 4))),
        fb=np.ascontiguousarray(inp["forget_bias"][l, 4 * g:4 * g + 4].reshape(4, 1)),
    )
    return out


_NC_CACHE = {}


def _get_nc(name):
    if name not in _NC_CACHE:
        _NC_CACHE[name] = dict(A=lambda: build_A(2048), B=lambda: build_B(4096), C=lambda: build_C(2176))[name]()
    return _NC_CACHE[name]


def _run(nc, in_maps):
    res = run_bass_kernel_spmd(nc, in_maps, core_ids=list(range(8)))
    return res.results


def kernel_X(**inp):
    inp = {k: np.asarray(v) for k, v in inp.items()}
    x = np.ascontiguousarray(inp["x"], dtype=np.float32)
    consts = prep_B_consts()
    xcur = x.reshape(8, 2048, D)
    for l in range(DEPTH):
        cA = prep_A(inp, l)
        resA = _run(_get_nc("A"), [dict(cA, x=np.ascontiguousarray(xcur[c])) for c in range(8)])
        pB = [prep_B(inp, l, g, consts) for g in range(2)]
        inB = []
        for b in range(NB):
            r0, r1 = resA[2 * b], resA[2 * b + 1]
            FM = np.concatenate([np.asarray(r0["FM"]), np.asarray(r1["FM"])], axis=1)
            TM = np.concatenate([np.asarray(r0["TM"]), np.asarray(r1["TM"])], axis=0)
            GT = np.concatenate([np.asarray(r0["GT"]), np.asarray(r1["GT"])], axis=0)
            CF = np.concatenate([np.asarray(r0["CF"]), np.asarray(r1["CF"])], axis=1)
            for g in range(2):
                inB.append(dict(pB[g],
                                FMB=np.ascontiguousarray(FM[g * NFMG:(g + 1) * NFMG]),
                                TMB=np.ascontiguousarray(TM[:, g * NTMG:(g + 1) * NTMG]),
                                GTB=np.ascontiguousarray(GT[:, 12 * g:12 * g + 12]),
                                CFB=np.ascontiguousarray(CF[4 * g:4 * g + 4])))
        resB = _run(_get_nc("B"), inB)
        cC = prep_C(inp, l)
        inC = []
        for b in range(NB):
            o0, o1 = np.asarray(resB[2 * b]["OB"]), np.asarray(resB[2 * b + 1]["OB"])
            og = np.concatenate([o0[:, 0:256], o1[:, 0:256], o0[:, 256:512], o1[:, 256:512], o0[:, 512:768], o1[:, 512:768]], axis=1)
            xb = np.concatenate([xcur[2 * b], xcur[2 * b + 1]], axis=0)
            for half in range(2):
                if half == 0:
                    xh = np.concatenate([np.zeros((128, D), np.float32), xb[0:2048]], axis=0)
                    oh = np.concatenate([np.zeros((128, 1536), og.dtype), og[0:2048]], axis=0)
                else:
                    xh = xb[1920:4096]
                    oh = og[1920:4096]
                inC.append(dict(cC, xh=np.ascontiguousarray(xh), og=np.ascontiguousarray(oh)))
        resC = _run(_get_nc("C"), inC)
        xcur = np.stack([np.asarray(resC[c]["xo"]) for c in range(8)], axis=0)
    return np.ascontiguousarray(xcur.reshape(NB, T, D).astype(np.float32))


NSA_ROWS = NFM + NTM * 2048 // 2048
assert NSA_ROWS == 3584


class SrcChunks:
    def __init__(self, L, LGT, LCF):
        self.L, self.LGT, self.LCF = L, LGT, LCF

    def fm_all(self, r0):
        return lambda e: self.L[:, :, r0:r0 + 64, :].rearrange("a k p t -> p (a k) t")

    def tm_all(self):
        return [(2 * (8 * hf + k), 2, (lambda e, hf=hf, k=k: self.L[hf, k, 1344:1792, :].rearrange("r x -> (r x)").rearrange(
            "(kt p c) -> p kt c", p=128, c=NTMG))) for hf in range(2) for k in range(8)]

    def gt(self, hf):
        return lambda e: self.LGT[hf, :, :]

    def cf(self, hf):
        return lambda e: self.LCF[hf, :, :]


def build_Z(sample, ncores=8):
    nc = bass.Bass("TRN2", target_bir_lowering=False)
    I = {}
    for k, v in sample.items():
        dt = BF16 if v.dtype == NPBF else F32
        I[k] = nc.dram_tensor(k, list(v.shape), dt, kind="ExternalInput").ap()
    out = nc.dram_tensor("out", [2048, D], F32, kind="ExternalOutput").ap()
    CH = 3584
    Sk = [nc.dram_tensor("Sk%d" % k, [CH, 256], BF16).ap() for k in range(8)]
    Gk = [nc.dram_tensor("Gk%d" % k, [2 * CH, 256], BF16).ap() for k in range(8)]
    L = nc.dram_tensor("L", [2, 8, 1792, 256], BF16).ap()
    SAf = nc.dram_tensor("SAf", [32, 2048], F32).ap()
    GAf = nc.dram_tensor("GAf", [64, 2048], F32).ap()
    LGT = nc.dram_tensor("LGT", [2, 2048, 12], F32).ap()
    LCF = nc.dram_tensor("LCF", [2, 4, 2048], F32).ap()
    SBk = [nc.dram_tensor("SBk%d" % k, [1024, 768], BF16).ap() for k in range(4)]
    GBk = [nc.dram_tensor("GBk%d" % k, [2048, 768], BF16).ap() for k in range(4)]
    LO = nc.dram_tensor("LO", [2, 2176, 768], BF16).ap()
    X1 = nc.dram_tensor("X1", [2176, D], F32).ap()
    SX = nc.dram_tensor("SX", [128, D], F32).ap()
    GX = nc.dram_tensor("GX", [256, D], F32).ap()
    XM = nc.dram_tensor("XM", [2176, D], F32).ap()
    rg = [[i, i + 1] for i in range(0, ncores, 2)]
    GTv = [SAf[16 * g + 4:16 * g + 16, :].rearrange("r x -> (r x)").rearrange("(t c) -> t c", c=12) for g in range(2)]
    with ExitStack() as es:
        P = Prog(nc, es)
        P.need_rank = True
        bSA, bGA, bSB, bGB, bX1, bSX, bGX, bOut, bLA, bLO = [P.buf(n) for n in ("SA", "GA", "SB", "GB", "X1", "SX", "GX", "OUT", "LA", "LO")]
        rk = lambda: P.rank["sp"]
        bSk = [P.buf("Sk") for _ in range(8)]
        bGk = [P.buf("Gk") for _ in range(8)]
        bSBk = [P.buf("SBk") for _ in range(4)]
        bGBk = [P.buf("GBk") for _ in range(4)]
        for l in range(DEPTH):
            sfx = "_%d" % l
            xin = I["xh"] if l == 0 else X1
            xdep = [] if l == 0 else [bX1]
            def extract(k):
                for hf in range(2):
                    P.dma("sp", L[hf, k], (lambda e, hf=hf, k=k: Gk[k][bass.ds(hf * CH + rk() * 1792, 1792), :]), bLA,
                          reads=[bGk[k]], accw=[bLA])

            def on_chunk(k):
                P.coll("AllGather", [Sk[k]], [Gk[k]], rg, reads=[bSk[k]], writes=[bGk[k]])

            with P.scope():
                phase_A(P, es, 2048, xin[128:2176, :], I["wA" + sfx], I["gmix" + sfx], I["gainrow" + sfx], I["qscale" + sfx],
                        I["ident" + sfx], None, None, GTv, [SAf[16 * g:16 * g + 4, :] for g in range(2)], xdep=xdep, dout_in=bSA, chunked=Sk,
                        chunk_bufs=bSk, on_chunk=on_chunk)
            for k in range(8):
                extract(k)
            P.coll("AllGather", [SAf], [GAf], rg, reads=[bSA], writes=[bGA])
            for hf in range(2):
                P.dma("sp", LCF[hf], (lambda e, hf=hf: GAf[bass.ds(hf * 32 + rk() * 16, 4), :]), bLA, reads=[bGA], accw=[bLA])
                P.dma("sp", LGT[hf].rearrange("t c -> (t c)").rearrange("(r x) -> r x", x=2048),
                      (lambda e, hf=hf: GAf[bass.ds(hf * 32 + rk() * 16 + 4, 12), :]), bLA, reads=[bGA], accw=[bLA])
            cB = {k[:-len(sfx) - 1]: v for k, v in I.items() if k.endswith("B" + sfx)}
            with P.scope():
                obd = lambda qg, c0, c1: SBk[qg % 4][(qg // 4) * 512:(qg // 4 + 1) * 512, c0:c1]
                def extract_o(k):
                    for g in range(2):
                        P.dma("sp", LO[g, 128 + 512 * k:128 + 512 * (k + 1), :],
                              (lambda e, g=g, k=k: GBk[k][bass.ds(g * 1024 + rk() * 512, 512), :]), bLO, reads=[bGBk[k]], accw=[bLO])

                def on_ob(qg):
                    if qg >= 4:
                        k = qg - 4
                        P.coll("AllGather", [SBk[k]], [GBk[k]], rg, reads=[bSBk[k]], writes=[bGBk[k]])
                        if k >= 1:
                            extract_o(k - 1)
                phase_B(P, T, SrcChunks(L, LGT, LCF), cB, obd, srcdep=[bLA], dout_in=bSB, obuf=lambda qg: bSBk[qg % 4], on_ob=on_ob)
            extract_o(3)
            for g in range(2):
                P.dma("sp", LO[g, 0:128, :], GBk[3][g * 1024 + 384:g * 1024 + 512, :], bLO, reads=[bGBk[3]], accw=[bLO])
            with P.scope():
                xo = X1[128:2176, :] if l == 0 else out
                ogs = lambda t, g: (lambda e: LO[g, t * 128:(t + 1) * 128, :].rearrange("p (n c) -> p n c", n=3))
                phase_C(P, 2176, xin, ogs, I["wM" + sfx], I["wBr" + sfx], I["wO" + sfx], I["gmixC" + sfx], I["gffn" + sfx],
                        I["wG" + sfx], I["wU" + sfx], I["wD" + sfx], I["convw" + sfx], I["convb" + sfx], I["identC" + sfx], XM, xo,
                        I["hflag"], xdep=xdep, ogdep=[bLO], dout_in=(bX1 if l == 0 else bOut))
            if l == 0:
                P.dma("sp", SX, X1[2048:2176, :], bSX, reads=[bX1], writes=[bSX])
                P.coll("AllGather", [SX], [GX], rg, reads=[bSX], writes=[bGX])
                P.dma("sp", X1[0:128, :], GX[0:128, :], bGX, reads=[bGX], writes=[bX1])
                P.barrier()
        P.finish([bOut])
    return nc


def prep_Z(inp):
    x = np.ascontiguousarray(inp["x"], dtype=np.float32)
    consts = prep_B_consts()
    shared = {}
    perg = [dict(), dict()]
    for l in range(DEPTH):
        sfx = "_%d" % l
        for k, v in prep_A(inp, l).items():
            shared[k + sfx] = v
        for k, v in prep_C(inp, l).items():
            if k == "hflag":
                continue
            kk = {"gmix": "gmixC", "ident": "identC"}.get(k, k)
            shared[kk + sfx] = v
        for g in range(2):
            for k, v in prep_B(inp, l, g, consts).items():
                perg[g][k + "B" + sfx] = v
    maps = []
    for b in range(NB):
        for r in range(2):
            if r == 0:
                xh = np.concatenate([np.zeros((128, D), np.float32), x[b, 0:2048]], axis=0)
            else:
                xh = x[b, 1920:4096]
            m = dict(shared)
            m.update(perg[r])
            m["xh"] = np.ascontiguousarray(xh)
            m["hflag"] = np.full((128, 1), float(r), np.float32)
            maps.append(m)
    return maps


def kernel(**inp):
    inp = {k: np.asarray(v) for k, v in inp.items()}
    maps = prep_Z(inp)
    if "Z" not in _NC_CACHE:
        _NC_CACHE["Z"] = build_Z(maps[0], 8)
    res = run_bass_kernel_spmd(_NC_CACHE["Z"], maps, core_ids=list(range(8)))
    outs = [np.asarray(res.results[c]["out"]) for c in range(8)]
    return np.ascontiguousarray(np.stack(outs, axis=0).reshape(NB, T, D).astype(np.float32))


NCG = 1808


def _colperm_g(g):
    cols, gain_idx, qsc = [], [], []

    def heads(base, n, gi, q):
        for h in range(n):
            hh = 4 * g + h if n == 4 else g
            c0 = base + 64 * hh
            cols.extend(range(c0, c0 + 64))
            gain_idx.append(gi)
            qsc.append(0.125 if q else 1.0)
    heads(OFF['a_q'], 4, 0, True)
    heads(OFF['a_ks'], 1, 1, False)
    heads(OFF['a_kw'], 1, 1, False)
    heads(OFF['b_k'], 1, 3, False)
    heads(OFF['b_q'], 4, 2, True)
    heads(OFF['c_q'], 4, 4, True)
    heads(OFF['c_k'], 4, 5, False)
    heads(OFF['a_kc'], 1, -1, False)
    heads(OFF['a_vc'], 1, -1, False)
    assert len(cols) == NFMG
    cols.extend(range(OFF['a_vs'] + 64 * g, OFF['a_vs'] + 64 * g + 64))
    cols.extend(range(OFF['a_vw'] + 64 * g, OFF['a_vw'] + 64 * g + 64))
    cols.extend(range(OFF['b_v'] + 64 * g, OFF['b_v'] + 64 * g + 64))
    cols.extend(range(OFF['c_v'] + 256 * g, OFF['c_v'] + 256 * g + 256))
    cols.extend(range(OFF['a_gate'] + 12 * g, OFF['a_gate'] + 12 * g + 12))
    cols.extend(range(OFF['c_f'] + 4 * g, OFF['c_f'] + 4 * g + 4))
    assert len(cols) == NCG
    return np.array(cols), gain_idx, np.array(qsc, np.float32)


def prep_A2(inp, l, g):
    perm, gain_idx, qsc = _colperm_g(g)
    gr = np.ones((1408,), np.float32)
    for s, gi in enumerate(gain_idx):
        if gi >= 0:
            gr[s * 64:(s + 1) * 64] = inp["qk_gain"][l, gi]
    return dict(
        wA=np.ascontiguousarray(inp["w_in"][l][:, perm]),
        gmix=np.ascontiguousarray(np.broadcast_to(inp["norm_mix"][l][None, :], (128, D))),
        gainrow=np.ascontiguousarray(gr.reshape(11, 128).T),
        qscale=np.ascontiguousarray(np.broadcast_to(qsc[None, :], (128, 21))),
        ident=np.eye(128, dtype=np.float32),
    )


def phase_A2(P, NT, xsrc, x_is_h, wA, gmix, gainrow, qscale, ident, LFM, LTM, LGT, LCF, xdep=(), dout_in=None):
    ntile = NT // 128
    xdep = list(xdep)
    dout = dout_in if dout_in is not None else P.buf("doutA2")
    wsb, _ = P.sbuf("A_w", [128, 8, NCG], BF16)
    banks = [(0, 512), (512, 512), (1024, 320), (1344, 464)]
    wgrp = [(0, 1024), (1024, 784)]
    bank_grp = [0, 0, 1, 1]
    wbg = [P.buf("A_wg") for _ in wgrp]
    for gi, (g0, gw) in enumerate(wgrp):
        P.dma("pool", wsb[:, :, g0:g0 + gw], wA[:, g0:g0 + gw].rearrange("(k p) c -> p k c", p=128), wbg[gi], writes=[wbg[gi]])
    gm, gmb = P.sbuf("A_gm", [128, D], F32)
    if not x_is_h:
        P.dma("sp", gm[:], gmix, gmb, writes=[gmb])
    gr, grb = P.sbuf("A_gr", [128, 11], F32)
    P.dma("sp", gr[:], gainrow, grb, writes=[grb])
    qs, qsb = P.sbuf("A_qs", [128, 21], F32)
    P.dma("sp", qs[:], qscale, qsb, writes=[qsb])
    idb, idbb = P.sbuf("A_idb", [128, 128], BF16)
    P.dma("pool", idb[:], ident, idbb, writes=[idbb])
    xt = [P.sbuf(f"A_x{i}", [128, D], F32) for i in range(2)]
    junk, junkb = P.sbuf("A_junk", [128, D], F32)
    h2 = [P.sbuf(f"A_h{i}", [128, D], BF16) for i in range(4)]
    ss2 = [P.sbuf(f"A_ss{i}", [128, 4], F32) for i in range(4)]
    hT = [P.sbuf(f"A_hT{i}", [128, 8, 128], BF16) for i in range(4)]
    raw2 = [(P.sbuf(f"A_raw{i}", [128, NFMG], F32)[0], [P.buf("A_rawb") for _ in range(3)]) for i in range(4)]
    sq = [P.sbuf(f"A_sq{i}", [128, 512], F32) for i in range(2)]
    ssall2 = [P.sbuf(f"A_ssall{i}", [128, 21], F32) for i in range(4)]
    rs2 = [P.sbuf(f"A_rs{i}", [128, 21], F32) for i in range(2)]
    nrm2 = [(P.sbuf(f"A_nrm{i}", [128, 1408], BF16)[0], [P.buf("A_nrmb") for _ in range(3)]) for i in range(2)]
    for nrm_, nrmb_ in nrm2:
        P.op("pool", lambda e, nrm_=nrm_: e.memset(nrm_[:, NFMG:1408], 0.0), writes=[nrmb_[2]])
    FMt = [P.sbuf(f"A_FMt{i}", [128, 11, 512], BF16) for i in range(2)]
    TMt = [P.sbuf(f"A_TMt{i}", [128, 4, NTMG], BF16) for i in range(2)]
    gtt = [P.sbuf(f"A_gt{i}", [128, 4, 12], F32) for i in range(2)]
    cfT, cfTb = P.sbuf("A_cfT", [4, NT], F32)
    tp, tpb = P.psum("A_tp", [128, 8, 128], BF16)
    PB = [P.psum(f"A_PB{i}", [128, 512], F32) for i in range(3)]
    tq = [P.psum(f"A_tq{i}", [128, 8, 128], BF16) for i in range(2)]
    pcf, pcfb = P.psum("A_pcf", [4, 128], F32)
    nmm_ = [0]

    def front_norm(ts):
        if x_is_h:
            for t in ts:
                h_, h_b = h2[t % 4]
                P.dma("sp", h_[:], xsrc(t), h_b, reads=xdep, writes=[h_b])
            return
        for t in ts:
            xs, xsb = xt[t % 2]
            P.dma("sp", xs[:], xsrc(t), xsb, reads=xdep, writes=[xsb])
        for t in ts:
            xs, xsb = xt[t % 2]
            ss_, ss_b = ss2[t % 4]
            P.op("act", lambda e, xs=xs, ss_=ss_: e.activation(out=junk[:], in_=xs[:], func=AF.Square, accum_out=ss_[:, 0:1]),
                 reads=[xsb], writes=[junkb, ss_b])
        for t in ts:
            ss_, ss_b = ss2[t % 4]
            P.op("dve", lambda e, ss_=ss_: e.tensor_scalar(out=ss_[:, 1:2], in0=ss_[:, 0:1], scalar1=1.0 / D, scalar2=EPS,
                                                           op0=ALU.mult, op1=ALU.add), reads=[ss_b], writes=[ss_b])
        for t in ts:
            ss_, ss_b = ss2[t % 4]
            P.op("act", lambda e, ss_=ss_: e.activation(out=ss_[:, 2:3], in_=ss_[:, 1:2], func=AF.Sqrt), reads=[ss_b], writes=[ss_b])
        for t in ts:
            ss_, ss_b = ss2[t % 4]
            P.op("dve", lambda e, ss_=ss_: e.reciprocal(out=ss_[:, 3:4], in_=ss_[:, 2:3]), reads=[ss_b], writes=[ss_b])
        for t in ts:
            xs, xsb = xt[t % 2]
            ss_, ss_b = ss2[t % 4]
            h_, h_b = h2[t % 4]
            P.op("dve", lambda e, xs=xs, ss_=ss_, h_=h_: e.scalar_tensor_tensor(out=h_[:], in0=xs[:], scalar=ss_[:, 3:4], in1=gm[:],
                                                                                  op0=ALU.mult, op1=ALU.mult),
                 reads=[xsb, ss_b, gmb], writes=[h_b])

    def front_T(t):
        h_, h_b = h2[t % 4]
        for k in range(8):
            P.op("pe", lambda e, k=k: e.transpose(tp[:, k, :], h_[:, k * 128:(k + 1) * 128], idb[:]), reads=[h_b, idbb], writes=[tpb])
        hTs, hTsb = hT[t % 4]
        P.op("act", lambda e: e.copy(out=hTs[:], in_=tp[:]), reads=[tpb], writes=[hTsb])

    def back_mm(t):
        nmm = nmm_[0]
        raw, rawb = raw2[t % 4]
        ssall, ssallb = ssall2[t % 4]
        hTs, hTsb = hT[t % 4]
        slot = (t // 4) % 2
        tt = t % 4
        TMs, TMsb = TMt[slot]
        gts, gtsb = gtt[slot]
        for j, (c0, w) in enumerate(banks):
            pb, pbb = PB[nmm % 3]
            nmm += 1
            for k in range(8):
                P.op("pe", lambda e, pb=pb, k=k, c0=c0, w=w: e.matmul(pb[:, 0:w], lhsT=hTs[:, k, :], rhs=wsb[:, k, c0:c0 + w],
                                                                      start=(k == 0), stop=(k == 7)),
                     reads=[hTsb, wbg[bank_grp[j]]], writes=[pbb])
            if j < 3:
                sqs, sqsb = sq[j % 2]
                nsl = w // 64
                P.op("act", lambda e, pb=pb, c0=c0, w=w: e.copy(out=raw[:, c0:c0 + w], in_=pb[:, 0:w]), reads=[pbb], writes=[rawb[j]])
                P.op("act", lambda e, pb=pb, w=w, sqs=sqs: e.activation(out=sqs[:, 0:w], in_=pb[:, 0:w], func=AF.Square),
                     reads=[pbb], writes=[sqsb])
                P.op("dve", lambda e, sqs=sqs, w=w, j=j, nsl=nsl: e.reduce_sum(
                    out=ssall[:, 8 * j:8 * j + nsl], in_=sqs[:, 0:w].rearrange("p (s d) -> p s d", d=64), axis=AX.X),
                    reads=[sqsb], writes=[ssallb])
            else:
                P.op("act", lambda e, pb=pb: e.copy(out=TMs[:, tt, :], in_=pb[:, 0:NTMG]), reads=[pbb], writes=[TMsb])
                P.op("act", lambda e, pb=pb: e.copy(out=gts[:, tt, :], in_=pb[:, NTMG:NTMG + 12]), reads=[pbb], writes=[gtsb])
                if tt == 3:
                    P.op("act", lambda e: e.activation(out=gts[:], in_=gts[:], func=AF.Sigmoid), reads=[gtsb], writes=[gtsb])
        for k in range(8):
            P.op("pe", lambda e, k=k: e.matmul(pcf[:], lhsT=wsb[:, k, NCG - 4:NCG], rhs=hTs[:, k, :], start=(k == 0), stop=(k == 7)),
                 reads=[hTsb, wbg[1]], writes=[pcfb])
        P.op("dve", lambda e: e.tensor_copy(out=cfT[:, t * 128:(t + 1) * 128], in_=pcf[:]), reads=[pcfb], writes=[cfTb])
        nmm_[0] = nmm

    def fin_a(ts):
        for t in ts:
            ssall, ssallb = ssall2[t % 4]
            rs, rsb = rs2[t % 2]
            P.op("dve", lambda e, rs=rs, ssall=ssall: e.tensor_scalar(out=rs[:], in0=ssall[:], scalar1=1.0 / HD, scalar2=EPS,
                                                                     op0=ALU.mult, op1=ALU.add), reads=[ssallb], writes=[rsb])
        for t in ts:
            rs, rsb = rs2[t % 2]
            P.op("act", lambda e, rs=rs: e.activation(out=rs[:], in_=rs[:], func=AF.Sqrt), reads=[rsb], writes=[rsb])
        for t in ts:
            rs, rsb = rs2[t % 2]
            P.op("dve", lambda e, rs=rs: e.reciprocal(out=rs[:], in_=rs[:]), reads=[rsb], writes=[rsb])
        for t in ts:
            rs, rsb = rs2[t % 2]
            P.op("dve", lambda e, rs=rs: e.tensor_tensor(out=rs[:], in0=rs[:], in1=qs[:], op=ALU.mult), reads=[rsb, qsb], writes=[rsb])
        for t in ts:
            rs, rsb = rs2[t % 2]
            P.op("dve", lambda e, rs=rs: e.memset(rs[:, 19:21], 1.0), writes=[rsb])
        for t in ts:
            raw, rawb = raw2[t % 4]
            rs, rsb = rs2[t % 2]
            nrm, nrmb = nrm2[t % 2]
            for j in range(3):
                c0, w = banks[j]
                nsl = w // 64
                P.op("pool", lambda e, c0=c0, w=w, j=j, nsl=nsl, nrm=nrm, raw=raw, rs=rs: e.tensor_tensor(
                    out=nrm[:, c0:c0 + w].rearrange("p (s d) -> p s d", d=64),
                    in0=raw[:, c0:c0 + w].rearrange("p (s d) -> p s d", d=64),
                    in1=rs[:, 8 * j:8 * j + nsl].unsqueeze(2).to_broadcast([128, nsl, 64]), op=ALU.mult),
                    reads=[rawb[j], rsb], writes=[nrmb[j]])

    def fin_b(ts):
        for t in ts:
            slot = (t // 4) % 2
            tt = t % 4
            TMs, TMsb = TMt[slot]
            gts, gtsb = gtt[slot]
            FMs, FMsb = FMt[slot]
            nrm, nrmb = nrm2[t % 2]
            for c in range(11):
                tqs, tqsb = tq[c // 8]
                P.op("pe", lambda e, tqs=tqs, c=c, nrm=nrm: e.transpose(tqs[:, c % 8, :], nrm[:, c * 128:(c + 1) * 128], idb[:]),
                     reads=[nrmb[min(2, c // 4)], idbb] + ([nrmb[2]] if c == 8 else []), writes=[tqsb])
                if c == 7 or c == 10:
                    cb = (c // 8) * 8
                    n = c - cb + 1
                    P.op("dve", lambda e, tqs=tqs, cb=cb, n=n, FMs=FMs, tt=tt: e.tensor_tensor(
                        out=FMs[:, cb:cb + n, tt * 128:(tt + 1) * 128], in0=tqs[:, 0:n, :],
                        in1=gr[:, cb:cb + n].unsqueeze(2).to_broadcast([128, n, 128]), op=ALU.mult), reads=[tqsb, grb], writes=[FMsb])
            if tt == 3:
                t0 = (t - 3) * 128
                for (ca, cb_) in ((0, 6), (6, 11)):
                    P.dma("sp", LFM[ca * 128:cb_ * 128, t0:t0 + 512].rearrange("(c p) t -> p c t", p=128), FMs[:, ca:cb_, :], FMsb,
                          reads=[FMsb], accw=[dout])
                P.dma("sp", LTM[t0:t0 + 512, :].rearrange("(tt p) c -> p tt c", p=128), TMs[:], TMsb, reads=[TMsb], accw=[dout])
                P.dma("sp", LGT[t0:t0 + 512, :].rearrange("(tt p) c -> p tt c", p=128), gts[:], gtsb, reads=[gtsb], accw=[dout])

    npair = ntile // 2
    assert ntile % 2 == 0

    def pair(fn, p):
        if 0 <= p < npair:
            if fn in (front_norm, fin_a, fin_b):
                fn([2 * p, 2 * p + 1])
            else:
                fn(2 * p)
                fn(2 * p + 1)

    pair(front_norm, 0)
    pair(front_T, 0)
    pair(front_norm, 1)
    pair(back_mm, 0)
    pair(front_T, 1)
    pair(front_norm, 2)
    for p in range(npair):
        pair(fin_a, p)
        pair(back_mm, p + 1)
        pair(front_T, p + 2)
        pair(front_norm, p + 3)
        pair(fin_b, p)
    P.dma("sp", LCF, cfT[:], cfTb, reads=[cfTb], accw=[dout])
    return dout


def build_A2(NT=4096):
    nc = bass.Bass("TRN2", target_bir_lowering=False)
    dt_ = lambda n, s, d=F32, k="ExternalInput": nc.dram_tensor(n, s, d, kind=k).ap()
    x = dt_("x", [NT, D])
    wA = dt_("wA", [D, NCG]); gmix = dt_("gmix", [128, D]); gainrow = dt_("gainrow", [128, 11]); qscale = dt_("qscale", [128, 21])
    ident = dt_("ident", [128, 128])
    LFM = dt_("LFM", [1408, NT], BF16, "ExternalOutput"); LTM = dt_("LTM", [NT, NTMG], BF16, "ExternalOutput")
    LGT = dt_("LGT", [NT, 12], F32, "ExternalOutput"); LCF = dt_("LCF", [4, NT], F32, "ExternalOutput")
    with ExitStack() as es:
        P = Prog(nc, es)
        dout = phase_A2(P, NT, lambda t: x[t * 128:(t + 1) * 128, :], False, wA, gmix, gainrow, qscale, ident, LFM, LTM, LGT, LCF)
        P.finish([dout])
    return nc


def build_Z2(sample, ncores=8):
    nc = bass.Bass("TRN2", target_bir_lowering=False)
    I = {}
    for k, v in sample.items():
        dt = BF16 if v.dtype == NPBF else F32
        I[k] = nc.dram_tensor(k, list(v.shape), dt, kind="ExternalInput").ap()
    out = nc.dram_tensor("out", [2048, D], F32, kind="ExternalOutput").ap()
    LFM = nc.dram_tensor("LFM", [1408, T], BF16).ap()
    LTM = nc.dram_tensor("LTM", [T, NTMG], BF16).ap()
    LGT = nc.dram_tensor("LGT", [T, 12], F32).ap()
    LCF = nc.dram_tensor("LCF", [4, T], F32).ap()
    SBk = [nc.dram_tensor("SBk%d" % k, [1024, 768], BF16).ap() for k in range(4)]
    GBk = [nc.dram_tensor("GBk%d" % k, [2048, 768], BF16).ap() for k in range(4)]
    LO = nc.dram_tensor("LO", [2, 2176, 768], BF16).ap()
    SH = [nc.dram_tensor("SH%d" % k, [512, D], BF16).ap() for k in range(4)]
    GH = [nc.dram_tensor("GH%d" % k, [1024, D], BF16).ap() for k in range(4)]
    X1 = nc.dram_tensor("X1", [2176, D], F32).ap()
    SX = nc.dram_tensor("SX", [128, D], F32).ap()
    GX = nc.dram_tensor("GX", [256, D], F32).ap()
    XM = nc.dram_tensor("XM", [2176, D], F32).ap()
    rg = [[i, i + 1] for i in range(0, ncores, 2)]
    with ExitStack() as es:
        P = Prog(nc, es)
        P.need_rank = True
        bLA, bSB, bX1, bSX, bGX, bOut, bLO = [P.buf(n) for n in ("LA", "SB", "X1", "SX", "GX", "OUT", "LO")]
        rk = lambda: P.rank["sp"]
        bSBk = [P.buf("SBk") for _ in range(4)]
        bGBk = [P.buf("GBk") for _ in range(4)]
        bSH = [P.buf("SH") for _ in range(4)]
        bGH = [P.buf("GH") for _ in range(4)]
        for l in range(DEPTH):
            sfx = "_%d" % l
            xin = I["xh"] if l == 0 else X1
            xdep = [] if l == 0 else [bX1]
            with P.scope():
                if l == 0:
                    xsrc = lambda t: I["xfull"][t * 128:(t + 1) * 128, :]
                    adep = []
                else:
                    xsrc = lambda t: GH[(t % 16) // 4][(t // 16) * 512 + (t % 4) * 128:(t // 16) * 512 + (t % 4) * 128 + 128, :]
                    adep = bGH
                phase_A2(P, T, xsrc, l > 0, I["wA" + sfx], I["gmix" + sfx], I["gainrow" + sfx], I["qscale" + sfx], I["ident" + sfx],
                         LFM, LTM, LGT, LCF, xdep=adep, dout_in=bLA)
            cB = {k[:-len(sfx) - 1]: v for k, v in I.items() if k.endswith("B" + sfx)}
            with P.scope():
                obd = lambda qg, c0, c1: SBk[qg % 4][(qg // 4) * 512:(qg // 4 + 1) * 512, c0:c1]

                def extract_o(k):
                    for g in range(2):
                        P.dma("sp", LO[g, 128 + 512 * k:128 + 512 * (k + 1), :],
                              (lambda e, g=g, k=k: GBk[k][bass.ds(g * 1024 + rk() * 512, 512), :]), bLO, reads=[bGBk[k]], accw=[bLO])

                def on_ob(qg):
                    if qg >= 4:
                        k = qg - 4
                        P.coll("AllGather", [SBk[k]], [GBk[k]], rg, reads=[bSBk[k]], writes=[bGBk[k]])
                        if k >= 1:
                            extract_o(k - 1)
                phase_B(P, T, SrcStatic(T, LFM, LTM, LGT, LCF), cB, obd, srcdep=[bLA], dout_in=bSB, obuf=lambda qg: bSBk[qg % 4], on_ob=on_ob)
            extract_o(3)
            for g in range(2):
                P.dma("sp", LO[g, 0:128, :], GBk[3][g * 1024 + 384:g * 1024 + 512, :], bLO, reads=[bGBk[3]], accw=[bLO])
            with P.scope():
                xo = X1[128:2176, :] if l == 0 else out
                ogs = lambda t, g: (lambda e: LO[g, t * 128:(t + 1) * 128, :].rearrange("p (n c) -> p n c", n=3))
                hnext = None
                if l + 1 < DEPTH:
                    def on_h(tl):
                        if tl % 4 == 3:
                            k = tl // 4
                            P.coll("AllGather", [SH[k]], [GH[k]], rg, reads=[bSH[k]], writes=[bGH[k]])
                    hnext = (I["gmix_%d" % (l + 1)], lambda tl: SH[tl // 4][(tl % 4) * 128:(tl % 4) * 128 + 128, :],
                             lambda tl: bSH[tl // 4], on_h)
                phase_C(P, 2176, xin, ogs, I["wM" + sfx], I["wBr" + sfx], I["wO" + sfx], I["gmixC" + sfx], I["gffn" + sfx],
                        I["wG" + sfx], I["wU" + sfx], I["wD" + sfx], I["convw" + sfx], I["convb" + sfx], I["identC" + sfx], XM, xo,
                        I["hflag"], xdep=xdep, ogdep=[bLO], dout_in=(bX1 if l == 0 else bOut), hnext=hnext)
            if l == 0:
                P.dma("sp", SX, X1[2048:2176, :], bSX, reads=[bX1], writes=[bSX])
                P.coll("AllGather", [SX], [GX], rg, reads=[bSX], writes=[bGX])
                P.dma("sp", X1[0:128, :], GX[0:128, :], bGX, reads=[bGX], writes=[bX1])
                P.barrier()
        P.finish([bOut])
    return nc


def prep_Z2(inp):
    x = np.ascontiguousarray(inp["x"], dtype=np.float32)
    consts = prep_B_consts()
    shared = {}
    perg = [dict(), dict()]
    for l in range(DEPTH):
        sfx = "_%d" % l
        for k, v in prep_C(inp, l).items():
            if k == "hflag":
                continue
            kk = {"gmix": "gmixC", "ident": "identC"}.get(k, k)
            shared[kk + sfx] = v
        for g in range(2):
            for k, v in prep_A2(inp, l, g).items():
                perg[g][k + sfx] = v
            for k, v in prep_B(inp, l, g, consts).items():
                perg[g][k + "B" + sfx] = v
    maps = []
    for b in range(NB):
        for r in range(2):
            if r == 0:
                xh = np.concatenate([np.zeros((128, D), np.float32), x[b, 0:2048]], axis=0)
            else:
                xh = x[b, 1920:4096]
            m = dict(shared)
            m.update(perg[r])
            m["xh"] = np.ascontiguousarray(xh)
            m["xfull"] = x[b]
            m["hflag"] = np.full((128, 1), float(r), np.float32)
            maps.append(m)
    return maps


def kernel(**inp):
    inp = {k: np.asarray(v) for k, v in inp.items()}
    maps = prep_Z2(inp)
    if "Z2" not in _NC_CACHE:
        _NC_CACHE["Z2"] = build_Z2(maps[0], 8)
    res = run_bass_kernel_spmd(_NC_CACHE["Z2"], maps, core_ids=list(range(8)))
    outs = [np.asarray(res.results[c]["out"]) for c in range(8)]
    return np.ascontiguousarray(np.stack(outs, axis=0).reshape(NB, T, D).astype(np.float32))
```

```python
import math
import numpy as np
import ml_dtypes
from contextlib import ExitStack
import concourse.bass as bass
import concourse.mybir as mybir
from concourse.bass_utils import run_bass_kernel_spmd

F32 = mybir.dt.float32
BF16 = mybir.dt.bfloat16
AF = mybir.ActivationFunctionType
ALU = mybir.AluOpType
AX = mybir.AxisListType
NPBF = ml_dtypes.bfloat16


import types


def _freeze(fn):
    if fn.__closure__ is None:
        return fn
    cells = []
    for c_ in fn.__closure__:
        try:
            cells.append(types.CellType(c_.cell_contents))
        except ValueError:
            cells.append(c_)
    return types.FunctionType(fn.__code__, fn.__globals__, fn.__name__, fn.__defaults__, tuple(cells))


class Buf:
    __slots__ = ("name", "w", "r", "dsem", "excl")

    REG = []

    def __init__(self, name, excl=False):
        self.name = name
        self.excl = excl
        Buf.REG.append(self)
        self.w = []
        self.r = []
        self.dsem = None


class Prog:
    ENG = ("pe", "act", "dve", "pool", "sp")

    def __init__(self, nc, es):
        self.nc = nc
        self.es = es
        self.q = {e: [] for e in self.ENG}
        self.csem = {}
        for e in ("pe", "act", "dve", "pool"):
            self.csem[e] = es.enter_context(nc.semaphore("c_" + e))
        self.cnt = {e: 0 for e in self.ENG}
        self.waited = {e: {} for e in self.ENG}
        self.dsems = []
        self.nbuf = 0
        self.same_engine_sync = True
        self.es_alloc = None
        self.free_d = {"hw": [], "sw": [], "cc": []}
        self.scope_sems = [[]]
        self.rank = {}
        self.need_rank = False
        Buf.REG.clear()

    def sbuf(self, name, shape, dtype):
        self.nalloc = getattr(self, "nalloc", 0) + 1
        t = (self.es_alloc or self.es).enter_context(self.nc.sbuf_tensor("%s_%d" % (name, self.nalloc), list(shape), dtype))
        return t, Buf(name)

    def psum(self, name, shape, dtype):
        self.nalloc = getattr(self, "nalloc", 0) + 1
        t = (self.es_alloc or self.es).enter_context(self.nc.psum_tensor("%s_%d" % (name, self.nalloc), list(shape), dtype))
        return t, Buf(name, excl=True)

    def buf(self, name="b"):
        self.nbuf += 1
        return Buf(f"{name}{self.nbuf}")

    def _dsem(self, b, eng):
        kind = "sw" if eng == "pool" else ("cc" if eng == "cc" else "hw")
        if b.dsem is None:
            b.dsem = {}
        if kind not in b.dsem:
            if self.free_d[kind]:
                d = self.free_d[kind].pop()
            else:
                s = self.es.enter_context(self.nc.semaphore("d%s_%d" % (kind, len(self.dsems))))
                d = {"sem": s, "total": 0, "dirty": False, "id": len(self.dsems), "kind": kind}
                self.dsems.append(d)
            b.dsem[kind] = d
            self.scope_sems[-1].append(d)
        return b.dsem[kind]

    def _waits_for(self, eng, reads, writes):
        toks = []
        for b in reads:
            toks += b.w
            if b.excl:
                toks += [t for t in b.r if t[1] != eng]
        for b in writes:
            toks += b.w
            toks += b.r
        waits = []
        wd = self.waited[eng]
        for t in toks:
            if t[0] == "c":
                _, e, v = t
                if e == eng and (eng == "pe" or not self.same_engine_sync):
                    continue
                key = ("c", e)
                if wd.get(key, 0) < v:
                    wd[key] = v
            else:
                _, d, v = t
                v = d["total"]
                d["dirty"] = True
                key = ("d", d["id"])
                if wd.get(key, 0) < v:
                    wd[key] = v
        return wd

    def _collect(self, eng, reads, writes):
        before = dict(self.waited[eng])
        self._waits_for(eng, reads, writes)
        after = self.waited[eng]
        out = []
        for k, v in after.items():
            if before.get(k, 0) < v:
                if k[0] == "c":
                    out.append((self.csem[k[1]], v))
                else:
                    out.append((self.dsems[k[1]]["sem"], v))
        return out

    def op(self, eng, fn, reads=(), writes=()):
        waits = self._collect(eng, reads, writes)
        fn = _freeze(fn)
        self.cnt[eng] += 1
        tok = ("c", eng, self.cnt[eng])
        sem = self.csem[eng]

        def run(e, waits=waits, fn=fn, sem=sem):
            for s, v in waits:
                e.wait_ge(s, v)
            fn(e).then_inc(sem, 1)

        self.q[eng].append(run)
        for b in writes:
            b.w = [tok]
            b.r = []
        for b in reads:
            if b not in writes:
                b.r.append(tok)
        return tok

    def dma(self, eng, out, in_, semof, reads=(), writes=(), accw=(), **kw):
        d = self._dsem(semof, eng)
        waits = self._collect(eng, reads, writes)
        for b in accw:
            if b.r:
                tmpb = Buf("tmp")
                Buf.REG.pop()
                tmpb.w = list(b.r)
                waits += self._collect(eng, [tmpb], ())
        if d["dirty"] and d["total"] > 0:
            key = ("d", d["id"])
            if self.waited[eng].get(key, 0) < d["total"]:
                self.waited[eng][key] = d["total"]
                waits.append((d["sem"], d["total"]))
            d["dirty"] = False
        d["total"] += 16
        tok = ("d", d, d["total"])
        sem = d["sem"]

        def run(e, waits=waits, out=out, in_=in_, sem=sem, kw=kw):
            for s, v in waits:
                e.wait_ge(s, v)
            o_ = out(e) if callable(out) else out
            i_ = in_(e) if callable(in_) else in_
            try:
                e.dma_start(out=o_, in_=i_, **kw).then_inc(sem, 16)
            except Exception:
                print("DMA FAIL out=", o_, "in=", i_)
                raise

        self.q[eng].append(run)
        for b in writes:
            b.w = [tok]
            b.r = []
        for b in accw:
            b.w.append(tok)
        for b in reads:
            if b not in writes:
                b.r.append(tok)
        return tok

    def coll(self, kind, ins, outs, replica_groups, reads=(), writes=()):
        eng = "pool"
        if not hasattr(self, "ccbuf"):
            self.ccbuf = Buf("ccbuf")
            self.scope_sems.append([])
            self._dsem(self.ccbuf, "cc")
            self.scope_sems.pop()
        d = self._dsem(self.ccbuf, "cc")
        waits = self._collect(eng, reads, writes)
        d["total"] += 1
        tok = ("d", d, d["total"])
        sem = d["sem"]

        def run(e, waits=waits, sem=sem):
            for s_, v in waits:
                e.wait_ge(s_, v)
            e.collective_compute(kind, mybir.AluOpType.bypass, replica_groups=replica_groups,
                                 ins=[a.opt() for a in ins], outs=[a.opt() for a in outs]).then_inc(sem)

        self.q[eng].append(run)
        for b_ in writes:
            b_.w = [tok]
            b_.r = []
        for b_ in reads:
            if b_ not in writes:
                b_.r.append(tok)
        return tok

    def barrier(self):
        for eng in self.ENG:
            waits = []
            wd = self.waited[eng]
            for e2 in ("pe", "act", "dve", "pool"):
                if self.cnt[e2] > 0 and wd.get(("c", e2), 0) < self.cnt[e2]:
                    wd[("c", e2)] = self.cnt[e2]
                    waits.append((self.csem[e2], self.cnt[e2]))
            for d in self.dsems:
                key = ("d", d["id"])
                if d["total"] > 0 and wd.get(key, 0) < d["total"]:
                    wd[key] = d["total"]
                    waits.append((d["sem"], d["total"]))
                d["dirty"] = False

            def run(e, waits=waits):
                for s_, v in waits:
                    e.wait_ge(s_, v)

            if waits:
                self.q[eng].append(run)
        for b in Buf.REG:
            b.w = []
            b.r = []

    def scope(self):
        prog = self

        class _Scope:
            def __enter__(self_):
                prog.barrier()
                self_.saved = prog.es_alloc
                self_.es = ExitStack()
                self_.es.__enter__()
                prog.es_alloc = self_.es
                prog.scope_sems.append([])
                return self_

            def __exit__(self_, *a):
                prog.barrier()
                prog.es_alloc = self_.saved
                for d in prog.scope_sems.pop():
                    prog.free_d[d["kind"]].append(d)
                return self_.es.__exit__(*a)

        return _Scope()

    def finish(self, final_bufs):
        waits = self._collect("sp", final_bufs, ())
        for d in self.dsems:
            key = ("d", d["id"])
            if self.waited["sp"].get(key, 0) < d["total"]:
                self.waited["sp"][key] = d["total"]
                waits.append((d["sem"], d["total"]))

        def run(e, waits=waits):
            for s, v in waits:
                e.wait_ge(s, v)

        self.q["sp"].append(run)
        nc = self.nc
        q = self.q
        with nc.Block() as block:
            @block.tensor
            def _(e):
                for f in q["pe"]:
                    f(e)

            @block.scalar
            def _(e):
                for f in q["act"]:
                    f(e)

            @block.vector
            def _(e):
                for f in q["dve"]:
                    f(e)

            @block.gpsimd
            def _(e):
                for f in q["pool"]:
                    f(e)

            @block.sync
            def _(e):
                if self.need_rank:
                    self.rank["sp"] = e.partition_id() % 2
                for f in q["sp"]:
                    f(e)


D = 1024
T = 4096
NB = 4
DEPTH = 2
HD = 64
F_FF = 2816
NFC = 22
EPS = 1e-6
NEG = -30000.0
IN_W = 6688
OFF = dict(a_q=0, a_kc=512, a_vc=640, a_ks=768, a_vs=896, a_kw=1024, a_vw=1152, a_gate=1280,
           b_q=1304, b_k=1816, b_v=1944, c_q=2072, c_k=2584, c_v=3096, c_f=3608, merge=3616)
NFM = 2688
NFMG = 1344
NTM = 896
NTMG = 448
FMR = dict(qa=0, ks=256, kw=320, kb=384, qb=448, qc=704, kcf=960, kc=1216, vc=1280)
TMC = dict(vs=0, vw=64, vb=128, vcf=192)


def _colperm():
    cols = []
    gain_idx = []
    qsc = []
    for g in range(2):
        def heads(base, n, gi, q):
            for h in range(n):
                hh = 4 * g + h if n == 4 else g
                c0 = base + 64 * hh
                cols.extend(range(c0, c0 + 64))
                gain_idx.append(gi)
                qsc.append(0.125 if q else 1.0)
        heads(OFF['a_q'], 4, 0, True)
        heads(OFF['a_ks'], 1, 1, False)
        heads(OFF['a_kw'], 1, 1, False)
        heads(OFF['b_k'], 1, 3, False)
        heads(OFF['b_q'], 4, 2, True)
        heads(OFF['c_q'], 4, 4, True)
        heads(OFF['c_k'], 4, 5, False)
        heads(OFF['a_kc'], 1, -1, False)
        heads(OFF['a_vc'], 1, -1, False)
    assert len(cols) == NFM
    for g in range(2):
        cols.extend(range(OFF['a_vs'] + 64 * g, OFF['a_vs'] + 64 * g + 64))
        cols.extend(range(OFF['a_vw'] + 64 * g, OFF['a_vw'] + 64 * g + 64))
        cols.extend(range(OFF['b_v'] + 64 * g, OFF['b_v'] + 64 * g + 64))
        cols.extend(range(OFF['c_v'] + 256 * g, OFF['c_v'] + 256 * g + 256))
    assert len(cols) == NFM + NTM
    cols.extend(range(OFF['a_gate'], OFF['a_gate'] + 24))
    cols.extend(range(OFF['c_f'], OFF['c_f'] + 8))
    assert len(cols) == 3616
    return np.array(cols), gain_idx, np.array(qsc, np.float32)


def build_A(NT=2048):
    nc = bass.Bass("TRN2", target_bir_lowering=False)
    ntile = NT // 128
    x = nc.dram_tensor("x", [NT, D], F32, kind="ExternalInput").ap()
    wA = nc.dram_tensor("wA", [D, 3616], F32, kind="ExternalInput").ap()
    gmix = nc.dram_tensor("gmix", [128, D], F32, kind="ExternalInput").ap()
    gainrow = nc.dram_tensor("gainrow", [128, 21], F32, kind="ExternalInput").ap()
    qscale = nc.dram_tensor("qscale", [128, 42], F32, kind="ExternalInput").ap()
    ident = nc.dram_tensor("ident", [128, 128], F32, kind="ExternalInput").ap()
    FM = nc.dram_tensor("FM", [NFM, NT], BF16, kind="ExternalOutput").ap()
    TM = nc.dram_tensor("TM", [NT, NTM], BF16, kind="ExternalOutput").ap()
    GT = nc.dram_tensor("GT", [NT, 24], F32, kind="ExternalOutput").ap()
    CF = nc.dram_tensor("CF", [8, NT], F32, kind="ExternalOutput").ap()
    with ExitStack() as es:
        P = Prog(nc, es)
        dout = phase_A(P, es, NT, x, wA, gmix, gainrow, qscale, ident, FM,
                       [TM[:, 0:NTMG], TM[:, NTMG:NTM]], [GT[:, 0:12], GT[:, 12:24]], CF)
        P.finish([dout])
    return nc


STAGE = 99


def phase_A(P, es, NT, x, wA, gmix, gainrow, qscale, ident, FM, TM, GT, CF, xdep=(), dout_in=None, chunked=None,
            chunk_bufs=None, on_chunk=None):
    ntile = NT // 128
    dout = dout_in if dout_in is not None else P.buf("doutA")
    wsb, _ = P.sbuf("A_w", [128, 8, 3616], BF16)
    banks = [(i * 512, 512) for i in range(5)] + [(2560, 128), (2688, 512), (3200, 416)]
    wgrp = [(0, 1024), (1024, 1664), (2688, 928)]
    wbg = [P.buf("A_wg") for _ in wgrp]
    for gi, (g0, gw) in enumerate(wgrp):
        P.dma("pool", wsb[:, :, g0:g0 + gw], wA[:, g0:g0 + gw].rearrange("(k p) c -> p k c", p=128), wbg[gi], writes=[wbg[gi]])
    wb = None
    gm, gmb = P.sbuf("A_gm", [128, D], F32)
    P.dma("sp", gm[:], gmix, gmb, writes=[gmb])
    gr, grb = P.sbuf("A_gr", [128, 21], F32)
    P.dma("sp", gr[:], gainrow, grb, writes=[grb])
    qs, qsb = P.sbuf("A_qs", [128, 42], F32)
    P.dma("sp", qs[:], qscale, qsb, writes=[qsb])
    idb, idbb = P.sbuf("A_idb", [128, 128], BF16)
    P.dma("pool", idb[:], ident, idbb, writes=[idbb])
    idf, idfb = P.sbuf("A_idf", [128, 128], F32)
    P.dma("sp", idf[:], ident, idfb, writes=[idfb])

    xt = [P.sbuf(f"A_x{i}", [128, D], F32) for i in range(2)]
    junk, junkb = P.sbuf("A_junk", [128, D], F32)
    ss, ssb = P.sbuf("A_ss", [128, 4], F32)
    h2 = [P.sbuf(f"A_h{i}", [128, D], BF16) for i in range(2)]
    ss2 = [P.sbuf(f"A_ss{i}", [128, 4], F32) for i in range(2)]
    hT = [P.sbuf(f"A_hT{i}", [128, 8, 128], BF16) for i in range(2)]
    raw2 = [(P.sbuf(f"A_raw{i}", [128, NFM], F32)[0], [P.buf("A_rawb") for _ in range(6)]) for i in range(2)]
    sq = [P.sbuf(f"A_sq{i}", [128, 512], F32) for i in range(2)]
    ssall2 = [P.sbuf(f"A_ssall{i}", [128, 42], F32) for i in range(2)]
    rs, rsb = P.sbuf("A_rs", [128, 42], F32)
    tmp = [P.sbuf(f"A_tmp{i}", [128, 512], F32) for i in range(2)]
    nrm, _ = P.sbuf("A_nrm", [128, NFM], BF16)
    nrmb = [P.buf("A_nrmb") for _ in range(6)]
    FMt = [P.sbuf(f"A_FMt{i}", [128, 21, 512], BF16) for i in range(2)]
    TMt = [P.sbuf(f"A_TMt{i}", [128, 4, NTM], BF16) for i in range(2)]
    gtt = [P.sbuf(f"A_gt{i}", [128, 4, 24], F32) for i in range(2)]
    cft, cftb = P.sbuf("A_cft", [128, 8], F32)
    cfT, cfTb = P.sbuf("A_cfT", [8, NT], F32)
    tp, tpb = P.psum("A_tp", [128, 8, 128], BF16)
    PB = [P.psum(f"A_PB{i}", [128, 512], F32) for i in range(3)]
    tq = [P.psum(f"A_tq{i}", [128, 8, 128], BF16) for i in range(2)]
    pcf, pcfb = P.psum("A_pcf", [8, 128], F32)

    bank_grp = [0, 0, 1, 1, 1, 1, 2, 2]
    nmm_ = [0]

    def front_norm(t):
        xs, xsb = xt[t % 2]
        h_, h_b = h2[t % 2]
        ss_, ss_b = ss2[t % 2]
        P.dma("sp", xs[:], x[t * 128:(t + 1) * 128, :], xsb, reads=list(xdep), writes=[xsb])
        _rmsnorm_tile(P, xs[:], xsb, gm, gmb, h_[:], h_b, ss_, ss_b, junk, junkb)

    def front_T(t):
        h_, h_b = h2[t % 2]
        for k in range(8):
            P.op("pe", lambda e, k=k: e.transpose(tp[:, k, :], h_[:, k * 128:(k + 1) * 128], idb[:]),
                 reads=[h_b, idbb], writes=[tpb])
        hTs, hTsb = hT[t % 2]
        P.op("act", lambda e: e.copy(out=hTs[:], in_=tp[:]), reads=[tpb], writes=[hTsb])

    def back_mm(t):
        nmm = nmm_[0]
        raw, rawb = raw2[t % 2]
        ssall, ssallb = ssall2[t % 2]
        hTs, hTsb = hT[t % 2]
        slot = (t // 4) % 2
        tt = t % 4
        TMs, TMsb = TMt[slot]
        gts, gtsb = gtt[slot]
        FMs, FMsb = FMt[slot]
        for j, (c0, w) in enumerate(banks):
            pb, pbb = PB[nmm % 3]
            nmm += 1
            for k in range(8):
                P.op("pe", lambda e, pb=pb, k=k, c0=c0, w=w, hTs=hTs: e.matmul(
                    pb[:, 0:w], lhsT=hTs[:, k, :], rhs=wsb[:, k, c0:c0 + w], start=(k == 0), stop=(k == 7)),
                    reads=[hTsb, wbg[bank_grp[j]]], writes=[pbb])
            if j < 6:
                sqs, sqsb = sq[j % 2]
                nsl = w // 64
                P.op("act", lambda e, pb=pb, c0=c0, w=w: e.copy(out=raw[:, c0:c0 + w], in_=pb[:, 0:w]),
                     reads=[pbb], writes=[rawb[j]])
                P.op("act", lambda e, pb=pb, w=w, sqs=sqs: e.activation(out=sqs[:, 0:w], in_=pb[:, 0:w], func=AF.Square),
                     reads=[pbb], writes=[sqsb])
                P.op("dve", lambda e, sqs=sqs, w=w, j=j, nsl=nsl: e.reduce_sum(
                    out=ssall[:, 8 * j:8 * j + nsl], in_=sqs[:, 0:w].rearrange("p (s d) -> p s d", d=64), axis=AX.X),
                    reads=[sqsb], writes=[ssallb])
            elif j == 6:
                P.op("act", lambda e, pb=pb, TMs=TMs, tt=tt: e.copy(out=TMs[:, tt, 0:512], in_=pb[:, 0:512]),
                     reads=[pbb], writes=[TMsb])
            else:
                P.op("act", lambda e, pb=pb, TMs=TMs, tt=tt: e.copy(out=TMs[:, tt, 512:896], in_=pb[:, 0:384]),
                     reads=[pbb], writes=[TMsb])
                P.op("act", lambda e, pb=pb, gts=gts, tt=tt: e.copy(out=gts[:, tt, :], in_=pb[:, 384:408]),
                     reads=[pbb], writes=[gtsb])
                if tt == 3:
                    P.op("act", lambda e, gts=gts: e.activation(out=gts[:], in_=gts[:], func=AF.Sigmoid), reads=[gtsb], writes=[gtsb])
        for k in range(8):
            P.op("pe", lambda e, k=k, hTs=hTs: e.matmul(pcf[:], lhsT=wsb[:, k, 3608:3616], rhs=hTs[:, k, :],
                                                        start=(k == 0), stop=(k == 7)),
                 reads=[hTsb, wbg[2]], writes=[pcfb])
        P.op("dve", lambda e, t=t: e.tensor_copy(out=cfT[:, t * 128:(t + 1) * 128], in_=pcf[:]), reads=[pcfb], writes=[cfTb])
        nmm_[0] = nmm

    def back_fin(t):
        raw, rawb = raw2[t % 2]
        ssall, ssallb = ssall2[t % 2]
        slot = (t // 4) % 2
        tt = t % 4
        TMs, TMsb = TMt[slot]
        gts, gtsb = gtt[slot]
        FMs, FMsb = FMt[slot]
        P.op("dve", lambda e: e.tensor_scalar(out=rs[:], in0=ssall[:], scalar1=1.0 / HD, scalar2=EPS,
                                              op0=ALU.mult, op1=ALU.add), reads=[ssallb], writes=[rsb])
        P.op("act", lambda e: e.activation(out=rs[:], in_=rs[:], func=AF.Sqrt), reads=[rsb], writes=[rsb])
        P.op("dve", lambda e: e.reciprocal(out=rs[:], in_=rs[:]), reads=[rsb], writes=[rsb])
        P.op("dve", lambda e: e.tensor_tensor(out=rs[:], in0=rs[:], in1=qs[:], op=ALU.mult), reads=[rsb, qsb], writes=[rsb])
        P.op("dve", lambda e: e.memset(rs[:, 19:21], 1.0), writes=[rsb])
        P.op("dve", lambda e: e.memset(rs[:, 40:42], 1.0), writes=[rsb])
        for j in range(6):
            c0, w = banks[j]
            nsl = w // 64
            P.op("pool", lambda e, c0=c0, w=w, j=j, nsl=nsl: e.tensor_tensor(
                out=nrm[:, c0:c0 + w].rearrange("p (s d) -> p s d", d=64),
                in0=raw[:, c0:c0 + w].rearrange("p (s d) -> p s d", d=64),
                in1=rs[:, 8 * j:8 * j + nsl].unsqueeze(2).to_broadcast([128, nsl, 64]), op=ALU.mult),
                reads=[rawb[j], rsb], writes=[nrmb[j]])
        for c in range(21):
            tqs, tqsb = tq[(c // 8) % 2]
            P.op("pe", lambda e, tqs=tqs, c=c: e.transpose(tqs[:, c % 8, :], nrm[:, c * 128:(c + 1) * 128], idb[:]),
                 reads=[nrmb[c // 4], idbb], writes=[tqsb])
            if c % 8 == 7 or c == 20:
                cb = (c // 8) * 8
                n = c - cb + 1
                P.op("dve", lambda e, tqs=tqs, cb=cb, n=n, FMs=FMs, tt=tt: e.tensor_tensor(
                    out=FMs[:, cb:cb + n, tt * 128:(tt + 1) * 128], in0=tqs[:, 0:n, :],
                    in1=gr[:, cb:cb + n].unsqueeze(2).to_broadcast([128, n, 128]), op=ALU.mult), reads=[tqsb, grb], writes=[FMsb])
        if tt == 3 and STAGE >= 7 and chunked is not None:
            for kk in range(2):
                k = (t - 3) // 2 + kk
                Sk = chunked[k]
                cb_ = [chunk_bufs[k]]
                ts_ = slice(kk * 256, (kk + 1) * 256)
                P.dma("sp", Sk[0:1280, :].rearrange("(c p) t -> p c t", p=128), FMs[:, 0:10, ts_], FMsb, reads=[FMsb], accw=cb_)
                P.dma("sp", Sk[1280:1344, :], FMs[0:64, 10, ts_], FMsb, reads=[FMsb], accw=cb_)
                P.dma("sp", Sk[1792:1856, :], FMs[64:128, 10, ts_], FMsb, reads=[FMsb], accw=cb_)
                P.dma("sp", Sk[1856:3136, :].rearrange("(c p) t -> p c t", p=128), FMs[:, 11:21, ts_], FMsb, reads=[FMsb], accw=cb_)
                for g in range(2):
                    P.dma("sp", Sk[1792 * g + 1344:1792 * (g + 1), :].rearrange("r x -> (r x)").rearrange("(tt p c) -> p tt c", p=128, c=NTMG),
                          TMs[:, 2 * kk:2 * kk + 2, g * NTMG:(g + 1) * NTMG], TMsb, reads=[TMsb], accw=cb_)
                if on_chunk is not None:
                    on_chunk(k)
            t0 = (t - 3) * 128
            for g in range(2):
                P.dma("sp", GT[g][t0:t0 + 512, :].rearrange("(tt p) c -> p tt c", p=128), gts[:, :, 12 * g:12 * g + 12],
                      gtsb, reads=[gtsb], writes=[dout])
        elif tt == 3 and STAGE >= 7:
            t0 = (t - 3) * 128
            for c3 in range(3):
                P.dma("sp", FM.rearrange("(c p) t -> p c t", p=128)[:, 7 * c3:7 * c3 + 7, t0:t0 + 512], FMs[:, 7 * c3:7 * c3 + 7, :],
                      FMsb, reads=[FMsb], writes=[dout])
            for g in range(2):
                P.dma("sp", TM[g][t0:t0 + 512, :].rearrange("(tt p) c -> p tt c", p=128), TMs[:, :, g * NTMG:(g + 1) * NTMG],
                      TMsb, reads=[TMsb], writes=[dout])
                P.dma("sp", GT[g][t0:t0 + 512, :].rearrange("(tt p) c -> p tt c", p=128), gts[:, :, 12 * g:12 * g + 12],
                      gtsb, reads=[gtsb], writes=[dout])
    front_norm(0)
    front_T(0)
    if ntile > 1:
        front_norm(1)
    back_mm(0)
    if ntile > 1:
        front_T(1)
    if ntile > 2:
        front_norm(2)
    for t in range(ntile):
        if t + 1 < ntile:
            back_mm(t + 1)
        if t + 2 < ntile:
            front_T(t + 2)
        if t + 3 < ntile:
            front_norm(t + 3)
        back_fin(t)
    if STAGE >= 7:
        if isinstance(CF, list):
            for g in range(2):
                P.dma("sp", CF[g], cfT[4 * g:4 * g + 4, :], cfTb, reads=[cfTb], writes=[dout])
        else:
            P.dma("sp", CF, cfT[:], cfTb, reads=[cfTb], writes=[dout])
    return dout


def prep_A(inp, l):
    perm, gain_idx, qsc = _colperm()
    wA = np.ascontiguousarray(inp["w_in"][l][:, perm])
    gr = np.ones((NFM,), np.float32)
    for s, gi in enumerate(gain_idx):
        if gi >= 0:
            gr[s * 64:(s + 1) * 64] = inp["qk_gain"][l, gi]
    return dict(
        wA=wA,
        gmix=np.ascontiguousarray(np.broadcast_to(inp["norm_mix"][l][None, :], (128, D))),
        gainrow=np.ascontiguousarray(gr.reshape(21, 128).T),
        qscale=np.ascontiguousarray(np.broadcast_to(qsc[None, :], (128, 42))),
        ident=np.eye(128, dtype=np.float32),
    )


def _rmsnorm_tile(P, xs, xsb, gm, gmb, h, hb, ss, ssb, junk, junkb):
    P.op("act", lambda e: e.activation(out=junk[:], in_=xs, func=AF.Square, accum_out=ss[:, 0:1]),
         reads=[xsb], writes=[junkb, ssb])
    P.op("dve", lambda e: e.tensor_scalar(out=ss[:, 1:2], in0=ss[:, 0:1], scalar1=1.0 / D, scalar2=EPS,
                                          op0=ALU.mult, op1=ALU.add), reads=[ssb], writes=[ssb])
    P.op("act", lambda e: e.activation(out=ss[:, 2:3], in_=ss[:, 1:2], func=AF.Sqrt), reads=[ssb], writes=[ssb])
    P.op("dve", lambda e: e.reciprocal(out=ss[:, 3:4], in_=ss[:, 2:3]), reads=[ssb], writes=[ssb])
    P.op("dve", lambda e: e.scalar_tensor_tensor(out=h, in0=xs, scalar=ss[:, 3:4], in1=gm[:],
                                                 op0=ALU.mult, op1=ALU.mult),
         reads=[xsb, ssb, gmb], writes=[hb])


def og_static(og):
    return lambda t, g: (lambda e: og[t * 128:(t + 1) * 128, :].rearrange("p (n g c) -> p n g c", n=3, g=2)[:, :, g, :])


def phase_C(P, NTC, xh, ogsrc, wM, wBr, wO, gmix, gffn, wG, wU, wD, convw, convb, ident, XM, xo, hflag,
            xdep=(), ogdep=(), dout_in=None, hnext=None):
    ntile = NTC // 128
    xdep, ogdep = list(xdep), list(ogdep)
    dout = dout_in if dout_in is not None else P.buf("doutC")
    hfl, hflb = P.sbuf("C_hfl", [128, 1], F32)
    P.dma("sp", hfl[:], hflag, hflb, writes=[hflb])
    xmb = [P.buf("C_xmd") for _ in range(ntile)]
    idb, idbb = P.sbuf("C_idb", [128, 128], BF16)
    P.dma("pool", idb[:], ident, idbb, writes=[idbb])
    ss, ssb = P.sbuf("C_ss", [128, 4], F32)
    junk, junkb = P.sbuf("C_junk", [128, D], F32)
    h, hb = P.sbuf("C_h", [128, D], BF16)
    with P.scope():
        wMs, _ = P.sbuf("C1_wM", [128, 8, 3072], BF16)
        wMb = [P.buf("C1_wMb") for _ in range(3)]
        for n_ in range(3):
            P.dma("pool", wMs[:, :, n_ * 1024:(n_ + 1) * 1024], wM[:, n_ * 1024:(n_ + 1) * 1024].rearrange("(k p) c -> p k c", p=128),
                  wMb[n_], writes=[wMb[n_]])
        wBs, wBb = P.sbuf("C1_wB", [128, 12, D], BF16)
        P.dma("pool", wBs[:], wBr.rearrange("(j p) d -> p j d", p=128), wBb, writes=[wBb])
        wOs, wOb = P.sbuf("C1_wO", [128, 8, D], BF16)
        P.dma("pool", wOs[:], wO.rearrange("(k p) d -> p k d", p=128), wOb, writes=[wOb])
        gm, gmb = P.sbuf("C1_gm", [128, D], F32)
        P.dma("sp", gm[:], gmix, gmb, writes=[gmb])
        xt = [P.sbuf(f"C1_x{i}", [128, D], F32) for i in range(2)]
        ot = [P.sbuf(f"C1_o{i}", [128, 1536], BF16) for i in range(2)]
        hT2 = [P.sbuf(f"C1_hT{i}", [128, 8, 128], BF16) for i in range(2)]
        oT2 = [P.sbuf(f"C1_oT{i}", [128, 12, 128], BF16) for i in range(2)]
        hC = [P.sbuf(f"C1_h{i}", [128, D], BF16) for i in range(2)]
        ssC = [P.sbuf(f"C1_ss{i}", [128, 4], F32) for i in range(2)]
        gs = [P.sbuf(f"C1_gs{i}", [128, 512], F32) for i in range(2)]
        tmpm = [P.sbuf(f"C1_tm{i}", [128, 512], F32) for i in range(2)]
        macc, maccb = P.sbuf("C1_macc", [128, D], F32)
        mbf, mbfb = P.sbuf("C1_mbf", [128, D], BF16)
        mT, mTb = P.sbuf("C1_mT", [128, 8, 128], BF16)
        xm = [P.sbuf(f"C1_xm{i}", [128, D], F32) for i in range(2)]
        tp, tpb = P.psum("C1_tp", [128, 8, 128], BF16)
        to = [P.psum(f"C1_to{i}", [128, 8, 128], BF16) for i in range(2)]
        PB = [P.psum(f"C1_PB{i}", [128, 512], F32) for i in range(4)]
        nb_ = [0]

        def c1_norm(t):
            xs, xsb = xt[t % 2]
            os_, osb = ot[t % 2]
            h_, h_b = hC[t % 2]
            ss_, ss_b = ssC[t % 2]
            P.dma("sp", xs[:], xh[t * 128:(t + 1) * 128, :], xsb, reads=xdep, writes=[xsb])
            for g in range(2):
                P.dma("sp", os_[:].rearrange("p (n g c) -> p n g c", n=3, g=2)[:, :, g, :], ogsrc(t, g), osb, reads=ogdep, writes=[osb])
            _rmsnorm_tile(P, xs[:], xsb, gm, gmb, h_[:], h_b, ss_, ss_b, junk, junkb)

        def c1_T(t):
            os_, osb = ot[t % 2]
            h_, h_b = hC[t % 2]
            hT, hTb = hT2[t % 2]
            oT, oTb = oT2[t % 2]
            for k in range(8):
                P.op("pe", lambda e, k=k: e.transpose(tp[:, k, :], h_[:, k * 128:(k + 1) * 128], idb[:]),
                     reads=[h_b, idbb], writes=[tpb])
            P.op("act", lambda e: e.copy(out=hT[:], in_=tp[:]), reads=[tpb], writes=[hTb])
            for j in range(12):
                tos, tosb = to[j // 8]
                P.op("pe", lambda e, tos=tos, j=j: e.transpose(tos[:, j % 8, :], os_[:, j * 128:(j + 1) * 128], idb[:]),
                     reads=[osb, idbb], writes=[tosb])
            P.op("dve", lambda e: e.tensor_copy(out=oT[:, 0:8, :], in_=to[0][0][:]), reads=[to[0][1]], writes=[oTb])
            P.op("act", lambda e: e.copy(out=oT[:, 8:12, :], in_=to[1][0][:, 0:4, :]), reads=[to[1][1]], writes=[oTb])

        def c1_back(t):
            nb = nb_[0]
            xs, xsb = xt[t % 2]
            hT, hTb = hT2[t % 2]
            oT, oTb = oT2[t % 2]
            for n in range(3):
                for half in range(2):
                    pg, pgb = PB[nb % 4]
                    pp, ppb = PB[(nb + 1) % 4]
                    nb += 2
                    c0 = n * 1024 + half * 512
                    for k in range(8):
                        P.op("pe", lambda e, pg=pg, k=k, c0=c0: e.matmul(pg[:], lhsT=hT[:, k, :], rhs=wMs[:, k, c0:c0 + 512],
                                                                         start=(k == 0), stop=(k == 7)),
                             reads=[hTb, wMb[n]], writes=[pgb])
                    for j in range(4):
                        P.op("pe", lambda e, pp=pp, j=j, n=n, half=half: e.matmul(
                            pp[:], lhsT=oT[:, 4 * n + j, :], rhs=wBs[:, 4 * n + j, half * 512:(half + 1) * 512],
                            start=(j == 0), stop=(j == 3)), reads=[oTb, wBb], writes=[ppb])
                    g_, g_b = gs[(n * 2 + half) % 2]
                    P.op("act", lambda e, pg=pg, g_=g_: e.activation(out=g_[:], in_=pg[:], func=AF.Sigmoid),
                         reads=[pgb], writes=[g_b])
                    hs = slice(half * 512, (half + 1) * 512)
                    if n == 0:
                        P.op("dve", lambda e, pp=pp, g_=g_, hs=hs: e.tensor_tensor(out=macc[:, hs], in0=pp[:], in1=g_[:], op=ALU.mult),
                             reads=[ppb, g_b], writes=[maccb])
                    else:
                        tm_, tm_b = tmpm[half]
                        P.op("dve", lambda e, pp=pp, g_=g_, tm_=tm_: e.tensor_tensor(out=tm_[:], in0=pp[:], in1=g_[:], op=ALU.mult),
                             reads=[ppb, g_b], writes=[tm_b])
                        if n == 1:
                            P.op("pool", lambda e, tm_=tm_, hs=hs: e.tensor_tensor(out=macc[:, hs], in0=macc[:, hs], in1=tm_[:], op=ALU.add),
                                 reads=[tm_b, maccb], writes=[maccb])
                        else:
                            P.op("pool", lambda e, tm_=tm_, hs=hs: e.tensor_tensor(out=mbf[:, hs], in0=macc[:, hs], in1=tm_[:], op=ALU.add),
                                 reads=[tm_b, maccb], writes=[mbfb])
            for k in range(8):
                P.op("pe", lambda e, k=k: e.transpose(tp[:, k, :], mbf[:, k * 128:(k + 1) * 128], idb[:]),
                     reads=[mbfb, idbb], writes=[tpb])
            P.op("act", lambda e: e.copy(out=mT[:], in_=tp[:]), reads=[tpb], writes=[mTb])
            xms, xmsb = xm[t % 2]
            for half in range(2):
                py, pyb = PB[nb % 4]
                nb += 1
                for k in range(8):
                    P.op("pe", lambda e, py=py, k=k, half=half: e.matmul(py[:], lhsT=mT[:, k, :], rhs=wOs[:, k, half * 512:(half + 1) * 512],
                                                                         start=(k == 0), stop=(k == 7)),
                         reads=[mTb, wOb], writes=[pyb])
                hs = slice(half * 512, (half + 1) * 512)
                P.op("dve", lambda e, py=py, xs=xs, xms=xms, hs=hs: e.tensor_tensor(out=xms[:, hs], in0=py[:], in1=xs[:, hs], op=ALU.add),
                     reads=[pyb, xsb], writes=[xmsb])
            P.dma("sp", XM[t * 128:(t + 1) * 128, :], xms[:], xmsb, reads=[xmsb], writes=[xmb[t]])
            nb_[0] = nb

        c1_norm(0)
        c1_T(0)
        if ntile > 1:
            c1_norm(1)
        for t in range(ntile):
            c1_back(t)
            if t + 1 < ntile:
                c1_T(t + 1)
            if t + 2 < ntile:
                c1_norm(t + 2)
    with P.scope():
        wGs, _ = P.sbuf("C2_wG", [128, 8, F_FF], BF16)
        wUs, _ = P.sbuf("C2_wU", [128, 8, F_FF], BF16)
        fgrp = [(0, 6), (6, 12), (12, 17), (17, 22)]
        fc_grp = [gi for gi, (a_, b_) in enumerate(fgrp) for _ in range(a_, b_)]
        wGb = [P.buf("C2_wGb") for _ in fgrp]
        wUb = [P.buf("C2_wUb") for _ in fgrp]
        for gi, (a_, b_) in enumerate(fgrp):
            P.dma("pool", wGs[:, :, a_ * 128:b_ * 128], wG[:, a_ * 128:b_ * 128].rearrange("(k p) c -> p k c", p=128), wGb[gi], writes=[wGb[gi]])
            P.dma("pool", wUs[:, :, a_ * 128:b_ * 128], wU[:, a_ * 128:b_ * 128].rearrange("(k p) c -> p k c", p=128), wUb[gi], writes=[wUb[gi]])
        wDs, _ = P.sbuf("C2_wD", [128, NFC, D], BF16)
        wDb = [P.buf("C2_wDb") for _ in range(2)]
        for i in range(2):
            P.dma("pool", wDs[:, 11 * i:11 * i + 11, :], wD[11 * i * 128:(11 * i + 11) * 128, :].rearrange("(f p) d -> p f d", p=128),
                  wDb[i], writes=[wDb[i]])
        gf, gfb = P.sbuf("C2_gf", [128, D], F32)
        P.dma("sp", gf[:], gffn, gfb, writes=[gfb])
        cw, cwb = P.sbuf("C2_cw", [128, NFC, 3], F32)
        P.dma("sp", cw[:], convw, cwb, writes=[cwb])
        cb, cbb = P.sbuf("C2_cb", [128, NFC], F32)
        P.dma("sp", cb[:], convb, cbb, writes=[cbb])
        NS = 256
        xm4 = [P.sbuf(f"C2_xm4{i}", [128, 2, D], F32) for i in range(2)]
        h2T2 = [P.sbuf(f"C2_h2T{i}", [128, 8, NS], BF16) for i in range(2)]
        Ab = [P.sbuf(f"C2_A{i}", [128, NS + 2], F32) for i in range(2)]
        cv = [P.sbuf(f"C2_cv{i}", [128, NS], F32) for i in range(2)]
        gl = [P.sbuf(f"C2_gl{i}", [128, NS], F32) for i in range(2)]
        halo, halob = P.sbuf("C2_halo", [128, NFC, 2], F32)
        gu, _ = P.sbuf("C2_gu", [128, NFC, NS], BF16)
        gub = [P.buf("C2_gub") for _ in range(NFC)]
        xot = [P.sbuf(f"C2_xo{i}", [128, D], F32) for i in range(2)]
        if hnext is not None:
            gmn, gmnb = P.sbuf("C2_gmn", [128, D], F32)
            P.dma("sp", gmn[:], hnext[0], gmnb, writes=[gmnb])
            hnx = [P.sbuf(f"C2_hn{i}", [128, D], BF16) for i in range(2)]
            ssn, ssnb = P.sbuf("C2_ssn", [128, 4], F32)
        tp, tpb = P.psum("C2_tp", [128, 8, 128], BF16)
        PA_ = [P.psum(f"C2_PA{i}", [128, 512], F32) for i in range(2)]
        PU_ = [P.psum(f"C2_PU{i}", [128, 512], F32) for i in range(3)]
        PY = [P.psum(f"C2_PY{i}", [128, 512], F32) for i in range(2)]
        sts = [(0, 1, True)] + [(1 + 2 * i, 2, False) for i in range((ntile - 1) // 2)]
        nb_ = [0, 0]

        def c2_front(si):
            t0, nt, is_halo = sts[si]
            x4, x4b = xm4[si % 2]
            h2T, h2Tb = h2T2[si % 2]
            P.dma("sp", x4[:, 0:nt, :], XM[t0 * 128:(t0 + nt) * 128, :].rearrange("(t p) d -> p t d", p=128), x4b,
                  reads=[xmb[t0 + i] for i in range(nt)], writes=[x4b])
            for tt in range(nt):
                _rmsnorm_tile(P, x4[:, tt, :], x4b, gf, gfb, h[:], hb, ss, ssb, junk, junkb)
                for k in range(8):
                    P.op("pe", lambda e, k=k: e.transpose(tp[:, k, :], h[:, k * 128:(k + 1) * 128], idb[:]),
                         reads=[hb, idbb], writes=[tpb])
                P.op("act", lambda e, tt=tt: e.copy(out=h2T[:, :, tt * 128:(tt + 1) * 128], in_=tp[:]), reads=[tpb], writes=[h2Tb])

        def c2_back(si):
            t0, nt, is_halo = sts[si]
            nb, ny = nb_
            ntok = nt * 128
            x4, x4b = xm4[si % 2]
            h2T, h2Tb = h2T2[si % 2]
            def stage_a(fc):
                pa, pab = PA_[fc % 2]
                for k in range(8):
                    P.op("pe", lambda e, pa=pa, k=k, fc=fc, ntok=ntok: e.matmul(
                        pa[:, 0:ntok], lhsT=wGs[:, k, fc * 128:(fc + 1) * 128], rhs=h2T[:, k, 0:ntok], start=(k == 0), stop=(k == 7)),
                        reads=[h2Tb, wGb[fc_grp[fc]]], writes=[pab])
                A_, A_b = Ab[fc % 2]
                if si == 0:
                    P.op("pool", lambda e, A_=A_: e.memset(A_[:, 0:2], 0.0), writes=[A_b])
                else:
                    P.op("pool", lambda e, A_=A_, fc=fc: e.tensor_copy(out=A_[:, 0:2], in_=halo[:, fc, :]), reads=[halob], writes=[A_b])
                P.op("act", lambda e, A_=A_, pa=pa, ntok=ntok: e.copy(out=A_[:, 2:2 + ntok], in_=pa[:, 0:ntok]),
                     reads=[pab], writes=[A_b])
                if is_halo:
                    P.op("pool", lambda e, A_=A_, fc=fc, ntok=ntok: e.tensor_scalar(
                        out=halo[:, fc, :], in0=A_[:, ntok:ntok + 2], scalar1=hfl[:, 0:1], scalar2=None, op0=ALU.mult),
                        reads=[A_b, hflb], writes=[halob])
                else:
                    P.op("act", lambda e, A_=A_, fc=fc, ntok=ntok: e.copy(out=halo[:, fc, :], in_=A_[:, ntok:ntok + 2]),
                         reads=[A_b], writes=[halob])

            def stage_b(fc, nb):
                A_, A_b = Ab[fc % 2]
                pu, pub = PU_[nb % 3]
                for k in range(8):
                    P.op("pe", lambda e, pu=pu, k=k, fc=fc, ntok=ntok: e.matmul(
                        pu[:, 0:ntok], lhsT=wUs[:, k, fc * 128:(fc + 1) * 128], rhs=h2T[:, k, 0:ntok], start=(k == 0), stop=(k == 7)),
                        reads=[h2Tb, wUb[fc_grp[fc]]], writes=[pub])
                cv_, cv_b = cv[fc % 2]
                P.op("pool", lambda e, A_=A_, cv_=cv_, fc=fc, ntok=ntok: e.tensor_scalar(
                    out=cv_[:, 0:ntok], in0=A_[:, 2:2 + ntok], scalar1=cw[:, fc, 2:3], scalar2=cb[:, fc:fc + 1], op0=ALU.mult, op1=ALU.add),
                    reads=[A_b, cwb, cbb], writes=[cv_b])
                P.op("dve", lambda e, A_=A_, cv_=cv_, fc=fc, ntok=ntok: e.scalar_tensor_tensor(
                    out=cv_[:, 0:ntok], in0=A_[:, 1:1 + ntok], scalar=cw[:, fc, 1:2], in1=cv_[:, 0:ntok], op0=ALU.mult, op1=ALU.add),
                    reads=[A_b, cwb, cv_b], writes=[cv_b])
                P.op("dve", lambda e, A_=A_, cv_=cv_, fc=fc, ntok=ntok: e.scalar_tensor_tensor(
                    out=cv_[:, 0:ntok], in0=A_[:, 0:ntok], scalar=cw[:, fc, 0:1], in1=cv_[:, 0:ntok], op0=ALU.mult, op1=ALU.add),
                    reads=[A_b, cwb, cv_b], writes=[cv_b])
                gl_, gl_b = gl[fc % 2]
                P.op("act", lambda e, cv_=cv_, gl_=gl_, ntok=ntok: e.activation(out=gl_[:, 0:ntok], in_=cv_[:, 0:ntok], func=AF.Gelu_apprx_tanh),
                     reads=[cv_b], writes=[gl_b])
                P.op("dve", lambda e, gl_=gl_, pu=pu, fc=fc, ntok=ntok: e.tensor_tensor(
                    out=gu[:, fc, 0:ntok], in0=pu[:, 0:ntok], in1=gl_[:, 0:ntok], op=ALU.mult),
                    reads=[pub, gl_b], writes=[gub[fc]])

            stage_a(0)
            for fc in range(NFC):
                if fc + 1 < NFC:
                    stage_a(fc + 1)
                if not is_halo:
                    stage_b(fc, nb)
                    nb += 1
            if is_halo:
                nb_[0], nb_[1] = nb, ny
                return
            for tt in range(nt):
                xos, xosb = xot[ny % 2]
                for half in range(2):
                    py, pyb = PY[ny % 2]
                    ny += 1
                    for fc in range(NFC):
                        P.op("pe", lambda e, py=py, fc=fc, tt=tt, half=half: e.matmul(
                            py[:], lhsT=gu[:, fc, tt * 128:(tt + 1) * 128], rhs=wDs[:, fc, half * 512:(half + 1) * 512],
                            start=(fc == 0), stop=(fc == NFC - 1)), reads=[gub[fc], wDb[fc // 11]], writes=[pyb])
                    hs = slice(half * 512, (half + 1) * 512)
                    P.op("dve", lambda e, py=py, xos=xos, x4=x4, tt=tt, hs=hs: e.tensor_tensor(
                        out=xos[:, hs], in0=py[:], in1=x4[:, tt, hs], op=ALU.add), reads=[pyb, x4b], writes=[xosb])
                r0 = (t0 - 1 + tt) * 128
                P.dma("sp", xo[r0:r0 + 128, :], xos[:], xosb, reads=[xosb], accw=[dout])
                if hnext is not None:
                    tl = t0 - 1 + tt
                    hn, hnb = hnx[tl % 2]
                    _rmsnorm_tile(P, xos[:], xosb, gmn, gmnb, hn[:], hnb, ssn, ssnb, junk, junkb)
                    P.dma("sp", hnext[1](tl), hn[:], hnb, reads=[hnb], accw=[hnext[2](tl)])
                    hnext[3](tl)
            nb_[0], nb_[1] = nb, ny

        c2_front(0)
        for si in range(len(sts)):
            if si + 1 < len(sts):
                c2_front(si + 1)
            c2_back(si)
    return dout


def build_C(NTC=2176):
    nc = bass.Bass("TRN2", target_bir_lowering=False)
    dt_ = lambda n, s, d=F32, k="ExternalInput": nc.dram_tensor(n, s, d, kind=k).ap()
    xh = dt_("xh", [NTC, D])
    og = dt_("og", [NTC, 1536], BF16)
    wM = dt_("wM", [D, 3072]); wBr = dt_("wBr", [1536, D]); wO = dt_("wO", [D, D])
    gmix = dt_("gmix", [128, D]); gffn = dt_("gffn", [128, D])
    wG = dt_("wG", [D, F_FF]); wU = dt_("wU", [D, F_FF]); wD = dt_("wD", [F_FF, D])
    convw = dt_("convw", [128, NFC, 3]); convb = dt_("convb", [128, NFC])
    ident = dt_("ident", [128, 128])
    hflag = dt_("hflag", [128, 1])
    XM = nc.dram_tensor("XM", [NTC, D], F32).ap()
    xo = dt_("xo", [NTC - 128, D], F32, "ExternalOutput")
    with ExitStack() as es:
        P = Prog(nc, es)
        dout = phase_C(P, NTC, xh, og_static(og), wM, wBr, wO, gmix, gffn, wG, wU, wD, convw, convb, ident, XM, xo, hflag)
        P.finish([dout])
    return nc


def prep_C(inp, l):
    return dict(
        wM=np.ascontiguousarray(inp["w_in"][l][:, OFF['merge']:]),
        wBr=np.ascontiguousarray(inp["w_branch"][l].reshape(1536, D)),
        wO=inp["w_out"][l],
        gmix=np.ascontiguousarray(np.broadcast_to(inp["norm_mix"][l][None, :], (128, D))),
        gffn=np.ascontiguousarray(np.broadcast_to(inp["norm_ffn"][l][None, :], (128, D))),
        wG=inp["w_gate"][l], wU=inp["w_up"][l], wD=inp["w_down"][l],
        convw=np.ascontiguousarray(inp["conv_w"][l].reshape(3, NFC, 128).transpose(2, 1, 0)),
        convb=np.ascontiguousarray(inp["conv_b"][l].reshape(NFC, 128).T),
        ident=np.eye(128, dtype=np.float32),
        hflag=np.ones((128, 1), np.float32),
    )


class SrcStatic:
    def __init__(self, TB, FMB, TMB, GTB, CFB):
        self.TH = TB // 2
        self.FMB, self.TMB, self.GTB, self.CFB = FMB, TMB, GTB, CFB

    def fm(self, r0, hf):
        return [(0, self.TH, lambda e: self.FMB[r0:r0 + 64, hf * self.TH:(hf + 1) * self.TH])]

    def tm(self, col0, hf):
        return [(0, self.TH, lambda e: self.TMB[hf * self.TH:(hf + 1) * self.TH, col0:col0 + 64])]

    def fm_all(self, r0):
        return lambda e: self.FMB[r0:r0 + 64, :].rearrange("p (a t) -> p a t", t=256)

    def tm_all(self):
        n = 2 * self.TH // 128
        return [(k0, min(8, n - k0), (lambda e, k0=k0: self.TMB.rearrange("(kt p) c -> p kt c", p=128)[:, k0:min(n, k0 + 8), :]))
                for k0 in range(0, n, 8)]

    def gt(self, hf):
        return lambda e: self.GTB[hf * self.TH:(hf + 1) * self.TH, :]

    def cf(self, hf):
        return lambda e: self.CFB[:, hf * self.TH:(hf + 1) * self.TH]


def phase_B(P, TB, src, c, OB, srcdep=(), dout_in=None, obuf=None, on_ob=None):
    srcdep = list(srcdep)
    obdst = OB if callable(OB) else (lambda qg, c0, c1: OB[qg * 512:(qg + 1) * 512, c0:c1])
    TH = TB // 2
    NQH = TB // 256
    NQT = TB // 128
    NQG = TB // 512
    NCP = TB // 16
    NCT = (NCP + 127) // 128
    NCW = NCT * 128
    dout = dout_in if dout_in is not None else P.buf("doutB")
    if obuf is None:
        obuf = lambda qg: dout
    idb, idbb = P.sbuf("B_idb", [128, 128], BF16)
    P.dma("pool", idb[:], c["ident"], idbb, writes=[idbb])
    idf, idfb = P.sbuf("B_idf", [128, 128], F32)
    P.dma("sp", idf[:], c["ident"], idfb, writes=[idfb])
    F = [P.psum(f"B_F{i}", [128, 512], F32) for i in range(7)]
    Tb, Tbb = P.psum("B_Tb", [128, 8, 128], BF16)
    SB3 = [F[0], F[1], F[6]]

    Vall, Vallb = P.sbuf("B_Vall", [128, NQT, NTMG], BF16)
    for (k0, nk, f) in src.tm_all():
        P.dma("sp", Vall[:, k0:k0 + nk, :], f, Vallb, reads=srcdep, accw=[Vallb])

    def load_vext(name, col0, nh):
        t, b = P.sbuf(name, [128, NQT, nh, 65], BF16)
        P.op("pool", lambda e: e.memset(t[:], 1.0), writes=[b])
        P.op("pool", lambda e: e.tensor_copy(out=t[:, :, :, 0:64],
                                             in_=Vall[:, :, col0:col0 + 64 * nh].rearrange("p k (h d) -> p k h d", d=64)),
             reads=[Vallb], writes=[b])
        return t, b

    def load_rows(dst, bufs, semb, r0):
        P.dma("sp", dst.rearrange("p (a t) -> p a t", t=256), src.fm_all(r0), semb, reads=srcdep, writes=bufs)

    def load_fm(name, r0):
        t, b = P.sbuf(name, [64, TB], BF16)
        load_rows(t[:], [b], b, r0)
        return t, b

    def finish_head(acc, accb, dst, reads, writes, extra_scalar=None, extra_b=None, addsink=None, accumulate=False,
                    gate_ap=None, gate_b=None, rz=None, rzb=None):
        if addsink is not None:
            P.op("dve", lambda e: e.tensor_tensor(out=rz[:, 0:1], in0=acc[:, 64:65], in1=addsink, op=ALU.add),
                 reads=[accb] + reads, writes=[rzb])
            P.op("dve", lambda e: e.reciprocal(out=rz[:, 0:1], in_=rz[:, 0:1]), reads=[rzb], writes=[rzb])
        else:
            P.op("dve", lambda e: e.reciprocal(out=rz[:, 0:1], in_=acc[:, 64:65]), reads=[accb], writes=[rzb])
        if gate_ap is not None:
            P.op("dve", lambda e: e.tensor_tensor(out=rz[:, 0:1], in0=rz[:, 0:1], in1=gate_ap, op=ALU.mult),
                 reads=[rzb, gate_b], writes=[rzb])
        if accumulate:
            P.op("dve", lambda e: e.scalar_tensor_tensor(out=dst, in0=acc[:, 0:64], scalar=rz[:, 0:1], in1=dst,
                                                         op0=ALU.mult, op1=ALU.add), reads=[accb, rzb] + writes, writes=writes)
        else:
            P.op("dve", lambda e: e.tensor_scalar(out=dst, in0=acc[:, 0:64], scalar1=rz[:, 0:1], scalar2=None, op0=ALU.mult),
                 reads=[accb, rzb], writes=writes)

    rzs = [P.sbuf(f"B_rz{i}", [128, 1], F32) for i in range(4)]
    PT = [P.sbuf(f"B_PT{i}", [128, 512], BF16) for i in range(3)]
    PT4 = [P.sbuf(f"B_PT4{i}", [128, 384], BF16) for i in range(4)]
    zero1, zero1b = P.sbuf("B_zero", [128, 1], F32)
    P.op("pool", lambda e: e.memset(zero1[:], 0.0), writes=[zero1b])
    nrz = [0]

    with P.scope():
        Qst = [P.sbuf(f"N_Q{h}", [128, TB], BF16)[0] for h in range(4)]
        Qb = [[P.buf(f"N_Qb{h}_") for _ in range(NQG)] for h in range(4)]
        for h in range(4):
            load_rows(Qst[h][0:64, :], Qb[h], Qb[h][0], FMR['qa'] + 64 * h)
        KE, KEb = P.sbuf("N_KE", [128, NQT, 128], BF16)
        load_rows(KE[0:64, :, :].rearrange("d kt p -> d (kt p)"), [KEb], KEb, FMR['ks'])
        kwT, kwTb = load_fm("N_kwT", FMR['kw'])
        Vs, Vsb = load_vext("N_Vs", TMC['vs'], 1)
        Vw, Vwb = load_vext("N_Vw", TMC['vw'], 1)
        gate, gateb = P.sbuf("N_gate", [128, NQT, 12], F32)
        for hf in range(2):
            fg = src.gt(hf)
            for k0 in range(0, NQH, 8):
                k1 = min(NQH, k0 + 8)
                P.dma("sp", gate[:, hf * NQH + k0:hf * NQH + k1, :],
                      (lambda e, fg=fg, k0=k0, k1=k1: fg(e).rearrange("(t p) c -> p t c", p=128)[:, k0:k1, :]),
                      gateb, reads=srcdep, writes=[gateb])
        bandC, bandCb = P.sbuf("N_bandC", [128, 4, 504], F32)
        P.dma("sp", bandC[:], c["bandC"], bandCb, writes=[bandCb])
        bandS, bandSb = P.sbuf("N_bandS", [128, 4, 5, 512], BF16)
        bandW, bandWb = P.sbuf("N_bandW", [128, 4, 384], BF16)
        tab31, tab31b = P.sbuf("N_tab31", [128, 4], F32)
        P.dma("sp", tab31[:], c["tab31"], tab31b, writes=[tab31b])
        AB, ABb = P.sbuf("N_AB", [128, 3, 126], F32)
        P.dma("sp", AB[:], c["topk"], ABb, writes=[ABb])
        ovl, ovlb = P.sbuf("N_ovl", [128, 2, 64], F32)
        P.dma("sp", ovl[:], c["overlap"], ovlb, writes=[ovlb])
        kcmpT, kcmpTb = P.sbuf("N_kcmpT", [64, NCW], BF16)
        vcmp, vcmpb = P.sbuf("N_vcmp", [128, NCT, 64], BF16)
        with P.scope():
            kg, kgb = P.sbuf("NC_kg", [128, 64], F32)
            P.dma("sp", kg[:], c["kgain"], kgb, writes=[kgb])
            hidT, hidTb = P.sbuf("NC_hidT", [128, 2, NCW], BF16)
            P.op("pool", lambda e: e.memset(hidT[:], 0.0), writes=[hidTb])
            pw, pwb = P.sbuf("NC_pw", [128, 2], F32)
            kraw, krawb = P.sbuf("NC_kraw", [128, 64], F32)
            ksq, ksqb = P.sbuf("NC_ksq", [128, 64], F32)
            kss, kssb = P.sbuf("NC_kss", [128, 4], F32)
            knb, knbb = P.sbuf("NC_knb", [128, 128], BF16)
            P.op("pool", lambda e: e.memset(knb[:], 0.0), writes=[knbb])
            for which in range(2):
                xT, xTb = load_fm(f"NC_xT{which}", FMR['kc'] if which == 0 else FMR['vc'])
                w1s, w1b = P.sbuf(f"NC_w1{which}", [64, 32, 256], BF16)
                P.dma("pool", w1s[:], c["w1"][which].rearrange("(l d) h -> d l h", d=64), w1b, writes=[w1b])
                w2s, w2b = P.sbuf(f"NC_w2{which}", [128, 2, 64], BF16)
                P.dma("pool", w2s[:], c["w2"][which].rearrange("(j p) d -> p j d", p=128), w2b, writes=[w2b])
                posT, posTb = P.sbuf(f"NC_pos{which}", [64, 32], BF16)
                P.dma("pool", posT[:], c["posT"][which], posTb, writes=[posTb])
                for hh in range(2):
                    f_, f_b = F[hh]
                    for l in range(32):
                        P.op("pe", lambda e, f_=f_, l=l, hh=hh: e.matmul(f_[:, 0:1], lhsT=w1s[:, l, hh * 128:(hh + 1) * 128],
                                                                         rhs=posT[:, l:l + 1], start=(l == 0), stop=(l == 31)),
                             reads=[w1b, posTb], writes=[f_b])
                    P.op("dve", lambda e, f_=f_, hh=hh: e.tensor_copy(out=pw[:, hh:hh + 1], in_=f_[:, 0:1]), reads=[f_b], writes=[pwb])
                ncr = NCP - 1
                for hh in range(2):
                    for c0 in range(0, ncr, 512):
                        cn = min(512, ncr - c0)
                        f_, f_b = F[2 + (hh + c0 // 512) % 2]
                        for l in range(32):
                            P.op("pe", lambda e, f_=f_, l=l, hh=hh, c0=c0, cn=cn: e.matmul(
                                f_[:, 0:cn], lhsT=w1s[:, l, hh * 128:(hh + 1) * 128],
                                rhs=xT[:, 16 * c0 + l:16 * c0 + l + 16 * (cn - 1) + 1:16], start=(l == 0), stop=(l == 31)),
                                reads=[w1b, xTb], writes=[f_b])
                        P.op("act", lambda e, f_=f_, hh=hh, c0=c0, cn=cn: e.activation(
                            out=hidT[:, hh, c0:c0 + cn], in_=f_[:, 0:cn], func=AF.Gelu_apprx_tanh, bias=pw[:, hh:hh + 1]),
                            reads=[f_b, pwb], writes=[hidTb])
                for nt in range(NCT):
                    f_, f_b = F[4 + nt % 2]
                    for hh in range(2):
                        P.op("pe", lambda e, f_=f_, hh=hh, nt=nt: e.matmul(f_[:, 0:64], lhsT=hidT[:, hh, nt * 128:(nt + 1) * 128],
                                                                           rhs=w2s[:, hh, :], start=(hh == 0), stop=(hh == 1)),
                             reads=[hidTb, w2b], writes=[f_b])
                    if which == 1:
                        P.op("act", lambda e, f_=f_, nt=nt: e.copy(out=vcmp[:, nt, :], in_=f_[:, 0:64]), reads=[f_b], writes=[vcmpb])
                    else:
                        P.op("dve", lambda e, f_=f_: e.tensor_copy(out=kraw[:], in_=f_[:, 0:64]), reads=[f_b], writes=[krawb])
                        P.op("act", lambda e: e.activation(out=ksq[:], in_=kraw[:], func=AF.Square, accum_out=kss[:, 0:1]),
                             reads=[krawb], writes=[ksqb, kssb])
                        P.op("dve", lambda e: e.tensor_scalar(out=kss[:, 1:2], in0=kss[:, 0:1], scalar1=1.0 / HD, scalar2=EPS,
                                                              op0=ALU.mult, op1=ALU.add), reads=[kssb], writes=[kssb])
                        P.op("act", lambda e: e.activation(out=kss[:, 2:3], in_=kss[:, 1:2], func=AF.Sqrt), reads=[kssb], writes=[kssb])
                        P.op("dve", lambda e: e.reciprocal(out=kss[:, 3:4], in_=kss[:, 2:3]), reads=[kssb], writes=[kssb])
                        P.op("dve", lambda e: e.scalar_tensor_tensor(out=knb[:, 0:64], in0=kraw[:], scalar=kss[:, 3:4], in1=kg[:],
                                                                     op0=ALU.mult, op1=ALU.mult), reads=[krawb, kssb, kgb], writes=[knbb])
                        P.op("pe", lambda e: e.transpose(Tb[:, 0, :], knb[:], idb[:]), reads=[knbb, idbb], writes=[Tbb])
                        P.op("act", lambda e, nt=nt: e.copy(out=kcmpT[:, nt * 128:(nt + 1) * 128], in_=Tb[0:64, 0, :]),
                             reads=[Tbb], writes=[kcmpTb])
        P.dma("pool", KE[64:128, :, :], c["erows"][:, 0:NQT, :], KEb, accw=[KEb])
        P.dma("pool", bandS[:], c["bandS"], bandSb, writes=[bandSb])
        P.dma("pool", bandW[:], c["bandW"], bandWb, writes=[bandWb])
        lg = [P.sbuf(f"N_lg{i}", [128, NCW], F32) for i in range(8)]
        pex = [P.sbuf(f"N_pex{i}", [128, NCW], F32) for i in range(8)]
        pbf = [P.sbuf(f"N_pbf{i}", [128, NCW], BF16) for i in range(8)]
        psacc = [P.sbuf(f"N_psacc{i}", [128, NCW], F32) for i in range(4)]
        pstmp = [P.sbuf(f"N_pstmp{i}", [128, NCW], F32) for i in range(2)]
        pT2 = [P.sbuf(f"N_pT{i}", [128, 4 * NCT, 128], BF16) for i in range(2)]
        psT, psTb = P.sbuf("N_psT", [128, NCT, 128], F32)
        Z = [P.sbuf(f"N_Z{i}", [128, 8], F32) for i in range(2)]
        sc, scb = P.sbuf("N_sc", [128, 64], F32)
        top8, top8b = P.sbuf("N_top8", [128, 8], F32)
        NM, NMb = P.sbuf("N_NM", [128, 128], F32)
        P.op("pool", lambda e: e.memset(NM[:], 0.0), writes=[NMb])
        OA, _ = P.sbuf("N_OA", [128, NQT, 256], F32)
        OAb = [P.buf("N_OAb") for _ in range(NQT)]
        OAo = [P.sbuf(f"N_OAo{i}", [128, 4, 256], BF16) for i in range(2)]
        rz4 = [P.sbuf(f"N_rz4{i}", [128, 4], F32) for i in range(2)]
        tmp4 = [P.sbuf(f"N_tmp4{i}", [128, 4, 64], F32) for i in range(2)]
        assert NCT <= 2

        def cmp_front(qts):
            items = [(i, qt, h) for i, qt in enumerate(qts) for h in range(4)]
            for (i, qt, h) in items:
                L, Lb = F[h]
                qs = slice(qt * 128, (qt + 1) * 128)
                P.op("pe", lambda e, L=L, h=h, i=i, qs=qs: e.matmul(L[:, i * 256:i * 256 + NCW], lhsT=Qst[h][0:64, qs], rhs=kcmpT[:, 0:NCW],
                                                                    start=True, stop=True), reads=[Qb[h][qt // 4], kcmpTb], writes=[Lb])
            for (i, qt, h) in items:
                L, Lb = F[h]
                boff = 8 * (31 - qt)
                lg_, lg_b = lg[i * 4 + h]
                P.op("dve", lambda e, L=L, h=h, i=i, lg_=lg_, boff=boff: e.tensor_tensor(
                    out=lg_[:], in0=L[:, i * 256:i * 256 + NCW], in1=bandC[:, h, boff:boff + NCW], op=ALU.add), reads=[Lb, bandCb], writes=[lg_b])
            for (i, qt, h) in items:
                lg_, lg_b = lg[i * 4 + h]
                px, pxb = pex[i * 4 + h]
                Zt, Ztb = Z[qt % 2]
                P.op("act", lambda e, lg_=lg_, px=px, Zt=Zt, h=h: e.activation(out=px[:], in_=lg_[:], func=AF.Exp, accum_out=Zt[:, h:h + 1]),
                     reads=[lg_b], writes=[pxb, Ztb])
            for qt in qts:
                Zt, Ztb = Z[qt % 2]
                P.op("dve", lambda e, Zt=Zt: e.tensor_scalar(out=Zt[:, 4:8], in0=Zt[:, 0:4], scalar1=1e-30, scalar2=None, op0=ALU.add),
                     reads=[Ztb], writes=[Ztb])
            for qt in qts:
                Zt, Ztb = Z[qt % 2]
                P.op("dve", lambda e, Zt=Zt: e.reciprocal(out=Zt[:, 4:8], in_=Zt[:, 4:8]), reads=[Ztb], writes=[Ztb])
            for (i, qt, h) in items:
                px, pxb = pex[i * 4 + h]
                Zt, Ztb = Z[qt % 2]
                P.op("dve", lambda e, px=px, Zt=Zt, h=h: e.tensor_scalar(out=px[:], in0=px[:], scalar1=Zt[:, 4 + h:5 + h], scalar2=None, op0=ALU.mult),
                     reads=[pxb, Ztb], writes=[pxb])
            for (i, qt, h) in items:
                px, pxb = pex[i * 4 + h]
                pb_, pb_b = pbf[i * 4 + h]
                P.op("act", lambda e, px=px, pb_=pb_: e.copy(out=pb_[:], in_=px[:]), reads=[pxb], writes=[pb_b])
            for i, qt in enumerate(qts):
                ps, psb = psacc[qt % 4]
                pt_, pt_b = pstmp[qt % 2]
                p0, p1, p2, p3 = [pex[i * 4 + h] for h in range(4)]
                P.op("pool", lambda e, ps=ps, p0=p0, p1=p1: e.tensor_tensor(out=ps[:], in0=p0[0][:], in1=p1[0][:], op=ALU.add),
                     reads=[p0[1], p1[1]], writes=[psb])
                P.op("pool", lambda e, pt_=pt_, p2=p2, p3=p3: e.tensor_tensor(out=pt_[:], in0=p2[0][:], in1=p3[0][:], op=ALU.add),
                     reads=[p2[1], p3[1]], writes=[pt_b])
                P.op("pool", lambda e, ps=ps, pt_=pt_: e.tensor_tensor(out=ps[:], in0=ps[:], in1=pt_[:], op=ALU.add), reads=[psb, pt_b], writes=[psb])
            for i, qt in enumerate(qts):
                pT_, pT_b = pT2[i]
                for h in range(4):
                    pb_, pb_b = pbf[i * 4 + h]
                    for nt in range(NCT):
                        P.op("pe", lambda e, h=h, nt=nt, pb_=pb_: e.transpose(Tb[:, h * NCT + nt, :], pb_[:, nt * 128:(nt + 1) * 128], idb[:]),
                             reads=[pb_b, idbb], writes=[Tbb])
                P.op("act", lambda e, pT_=pT_: e.copy(out=pT_[:], in_=Tb[:, 0:4 * NCT, :]), reads=[Tbb], writes=[pT_b])
            oc, ocb = F[4]
            for i, qt in enumerate(qts):
                pT_, pT_b = pT2[i]
                for h in range(4):
                    for nt in range(NCT):
                        P.op("pe", lambda e, h=h, nt=nt, i=i, pT_=pT_: e.matmul(
                            oc[:, i * 256 + h * 64:i * 256 + (h + 1) * 64], lhsT=pT_[:, h * NCT + nt, :], rhs=vcmp[:, nt, :],
                            start=(nt == 0), stop=(nt == NCT - 1)), reads=[pT_b, vcmpb], writes=[ocb])
            for i, qt in enumerate(qts):
                P.op("dve", lambda e, i=i, qt=qt: e.tensor_tensor(
                    out=OA[:, qt, :].rearrange("p (h d) -> p h d", h=4), in0=oc[:, i * 256:(i + 1) * 256].rearrange("p (h d) -> p h d", h=4),
                    in1=gate[:, qt, 0:12:3].unsqueeze(2).to_broadcast([128, 4, 64]), op=ALU.mult), reads=[ocb, gateb], writes=[OAb[qt]])

        def cmp_back(qt):
            qs = slice(qt * 128, (qt + 1) * 128)
            qg = qt // 4
            ps, psb = psacc[qt % 4]
            f5, f5b = F[5]
            for nt in range(NCT):
                P.op("pe", lambda e, nt=nt: e.transpose(f5[:, nt * 128:(nt + 1) * 128], ps[:, nt * 128:(nt + 1) * 128], idf[:]),
                     reads=[psb, idfb], writes=[f5b])
            P.op("act", lambda e: e.copy(out=psT[:].rearrange("p a b -> p (a b)"), in_=f5[:, 0:NCW]), reads=[f5b], writes=[psTb])
            f6, f6b = F[6]
            for nt in range(NCT):
                P.op("pe", lambda e, nt=nt: e.matmul(f6[:, 0:64], lhsT=psT[:, nt, :], rhs=ovl[:, nt, :], start=(nt == 0), stop=(nt == NCT - 1)),
                     reads=[psTb, ovlb], writes=[f6b])
            s0 = 62 - 2 * qt
            P.op("dve", lambda e: e.tensor_tensor(out=sc[:], in0=f6[:, 0:64], in1=AB[:, 0, s0:s0 + 64], op=ALU.mult), reads=[f6b, ABb], writes=[scb])
            P.op("dve", lambda e: e.tensor_tensor(out=sc[:], in0=sc[:], in1=AB[:, 1, s0:s0 + 64], op=ALU.add), reads=[scb, ABb], writes=[scb])
            P.op("dve", lambda e: e.memset(sc[:, 0:1], 1e9), writes=[scb])
            P.op("dve", lambda e: e.max(out=top8[:], in_=sc[:]), reads=[scb], writes=[top8b])
            P.op("dve", lambda e: e.tensor_scalar(out=sc[:], in0=sc[:], scalar1=top8[:, 7:8], scalar2=None, op0=ALU.is_ge),
                 reads=[scb, top8b], writes=[scb])
            P.op("dve", lambda e: e.tensor_tensor(out=sc[:], in0=sc[:], in1=AB[:, 2, s0:s0 + 64], op=ALU.mult), reads=[scb, ABb], writes=[scb])
            P.op("dve", lambda e: e.tensor_scalar(out=NM[:, 64:128], in0=sc[:], scalar1=-1.0, scalar2=-NEG, op0=ALU.add, op1=ALU.mult),
                 reads=[scb], writes=[NMb])
            P.op("pe", lambda e: e.transpose(f6[:, 128:256], NM[:], idf[:]), reads=[NMb, idfb], writes=[f6b])
            P.op("act", lambda e: e.copy(out=Qst[0][64:128, qs], in_=f6[64:128, 128:256]), reads=[f6b], writes=[Qb[0][qg]])
            for h in range(1, 4):
                P.op("pool", lambda e, h=h: e.tensor_copy(out=Qst[h][64:128, qs], in_=Qst[0][64:128, qs]), reads=[Qb[0][qg]], writes=[Qb[h][qg]])

        cmp_front([0, 1])
        for p_ in range(NQT // 2):
            if p_ + 1 < NQT // 2:
                cmp_front([2 * p_ + 2, 2 * p_ + 3])
            cmp_back(2 * p_)
            cmp_back(2 * p_ + 1)

        def finish4(accp, accpb, qt, gcol, n, sink=None, sinkb=None, dst=None, dstb=None):
            rz, rzb = rz4[n % 2]
            a3 = accp[:, 0:260].rearrange("p (h d) -> p h d", d=65)
            if sink is not None:
                P.op("dve", lambda e: e.tensor_tensor(out=rz[:], in0=a3[:, :, 64], in1=sink, op=ALU.add), reads=[accpb, sinkb], writes=[rzb])
                P.op("dve", lambda e: e.reciprocal(out=rz[:], in_=rz[:]), reads=[rzb], writes=[rzb])
            else:
                P.op("dve", lambda e: e.reciprocal(out=rz[:], in_=a3[:, :, 64]), reads=[accpb], writes=[rzb])
            if gcol is not None:
                P.op("dve", lambda e: e.tensor_tensor(out=rz[:], in0=rz[:], in1=gate[:, qt, gcol:12:3], op=ALU.mult), reads=[rzb, gateb], writes=[rzb])
            if dst is None:
                tm_, tm_b = tmp4[n % 2]
                P.op("dve", lambda e: e.tensor_tensor(out=tm_[:], in0=a3[:, :, 0:64], in1=rz[:].unsqueeze(2).to_broadcast([128, 4, 64]), op=ALU.mult),
                     reads=[accpb, rzb], writes=[tm_b])
                P.op("pool", lambda e: e.tensor_tensor(out=OA[:, qt, :], in0=OA[:, qt, :], in1=tm_[:].rearrange("p h d -> p (h d)"), op=ALU.add),
                     reads=[tm_b, OAb[qt]], writes=[OAb[qt]])
            else:
                P.op("dve", lambda e: e.tensor_tensor(out=dst, in0=a3[:, :, 0:64], in1=rz[:].unsqueeze(2).to_broadcast([128, 4, 64]), op=ALU.mult),
                     reads=[accpb, rzb], writes=[dstb])

        nfin = 0
        for qg in range(NQG):
            qcs = slice(qg * 512, (qg + 1) * 512)
            nkt = 4 * qg + 4
            for h in range(4):
                def emit_qk(kt, h=h):
                    S, Sb = SB3[kt % 3]
                    m = kt - 4 * qg + 1
                    if m >= 0:
                        P.op("pe", lambda e, S=S, m=m: e.matmul(S[:], lhsT=idb[:], rhs=bandS[:, h, m, :], start=True, stop=False),
                             reads=[idbb, bandSb], writes=[Sb])
                        P.op("pe", lambda e, S=S, kt=kt: e.matmul(S[:], lhsT=KE[:, kt, :], rhs=Qst[h][:, qcs], start=False, stop=True),
                             reads=[KEb, Qb[h][qg]], writes=[Sb])
                    else:
                        P.op("pe", lambda e, S=S, kt=kt: e.matmul(S[:], lhsT=KE[:, kt, :], rhs=Qst[h][:, qcs], start=True, stop=True),
                             reads=[KEb, Qb[h][qg]], writes=[Sb])
                emit_qk(0)
                if nkt > 1:
                    emit_qk(1)
                for kt in range(nkt):
                    if kt + 2 < nkt:
                        emit_qk(kt + 2)
                    S, Sb = SB3[kt % 3]
                    m = kt - 4 * qg + 1
                    pt, ptb = PT[kt % 3]
                    bias_ap = zero1[:, 0:1] if m >= 0 else tab31[:, h:h + 1]
                    P.op("act", lambda e, S=S, pt=pt, bias_ap=bias_ap: e.activation(out=pt[:], in_=S[:], func=AF.Exp, bias=bias_ap),
                         reads=[Sb, tab31b, zero1b], writes=[ptb])
                    for qi in range(4):
                        if kt > 4 * qg + qi:
                            continue
                        acc, accb = F[2 + qi]
                        P.op("pe", lambda e, acc=acc, pt=pt, qi=qi, kt=kt: e.matmul(
                            acc[:, 0:65], lhsT=pt[:, qi * 128:(qi + 1) * 128], rhs=Vs[:, kt, 0, :], start=(kt == 0), stop=(kt == 4 * qg + qi)),
                            reads=[ptb, Vsb], writes=[accb])
                for qi in range(4):
                    acc, accb = F[2 + qi]
                    qt = 4 * qg + qi
                    rz, rzb = rzs[nrz[0] % 4]
                    nrz[0] += 1
                    finish_head(acc, accb, OA[:, qt, h * 64:(h + 1) * 64], [], [OAb[qt]], accumulate=True,
                                gate_ap=gate[:, qt, 3 * h + 1:3 * h + 2], gate_b=gateb, rz=rz, rzb=rzb)
        for qt in range(NQT):
            qs = slice(qt * 128, (qt + 1) * 128)
            qg = qt // 4
            rr = [r for r in range(3) if qt - 2 + r >= 0]
            for h in range(4):
                S, Sb = F[h]
                P.op("pe", lambda e, S=S, h=h: e.matmul(S[:, 0:384], lhsT=idb[:], rhs=bandW[:, h, :], start=True, stop=False),
                     reads=[idbb, bandWb], writes=[Sb])
                for r in rr:
                    kt = qt - 2 + r
                    P.op("pe", lambda e, S=S, h=h, r=r, kt=kt, last=(r == rr[-1]): e.matmul(
                        S[:, r * 128:(r + 1) * 128], lhsT=kwT[:, kt * 128:(kt + 1) * 128], rhs=Qst[h][0:64, qs], start=False, stop=last),
                        reads=[kwTb, Qb[h][qg]], writes=[Sb])
            for h in range(4):
                S, Sb = F[h]
                pt, ptb = PT4[h]
                P.op("act", lambda e, S=S, pt=pt: e.activation(out=pt[:, 0:384], in_=S[:, 0:384], func=AF.Exp), reads=[Sb], writes=[ptb])
            accp, accpb = F[4 + qt % 2]
            for h in range(4):
                pt, ptb = PT4[h]
                for r in rr:
                    kt = qt - 2 + r
                    P.op("pe", lambda e, pt=pt, h=h, r=r, kt=kt, first=(r == rr[0]), last=(r == rr[-1]): e.matmul(
                        accp[:, h * 65:(h + 1) * 65], lhsT=pt[:, r * 128:(r + 1) * 128], rhs=Vw[:, kt, 0, :], start=first, stop=last),
                        reads=[ptb, Vwb], writes=[accpb])
            finish4(accp, accpb, qt, 2, qt)
            if qt % 4 == 3:
                oo, oob = OAo[qg % 2]
                P.op("act", lambda e, oo=oo: e.copy(out=oo[:], in_=OA[:, 4 * qg:4 * qg + 4, :]), reads=[OAb[4 * qg + i] for i in range(4)], writes=[oob])
                P.dma("sp", obdst(qg, 0, 256).rearrange("(t p) c -> p t c", p=128), oo[:], oob, reads=[oob], accw=[obuf(qg)])

    with P.scope():
        qT = [load_fm(f"S_q{h}", FMR['qb'] + 64 * h) for h in range(4)]
        kTs, kTsb = load_fm("S_k", FMR['kb'])
        Vb, Vbb = load_vext("S_V", TMC['vb'], 1)
        bandB, bandBb = P.sbuf("S_band", [128, 4, 256], BF16)
        P.dma("pool", bandB[:], c["bandB"], bandBb, writes=[bandBb])
        esk, eskb = P.sbuf("S_esk", [128, 4], F32)
        P.dma("sp", esk[:], c["sinkb"], eskb, writes=[eskb])
        P.op("act", lambda e: e.activation(out=esk[:], in_=esk[:], func=AF.Exp), reads=[eskb], writes=[eskb])
        OS = [P.sbuf(f"S_O{i}", [128, 4, 256], BF16) for i in range(2)]
        rz4 = [P.sbuf(f"S_rz4{i}", [128, 4], F32) for i in range(2)]
        for qt in range(NQT):
            qs = slice(qt * 128, (qt + 1) * 128)
            os_, osb = OS[(qt // 4) % 2]
            rr = [r for r in range(2) if qt - 1 + r >= 0]
            for h in range(4):
                S, Sb = F[h]
                P.op("pe", lambda e, S=S, h=h: e.matmul(S[:, 0:256], lhsT=idb[:], rhs=bandB[:, h, :], start=True, stop=False),
                     reads=[idbb, bandBb], writes=[Sb])
                for r in rr:
                    kt = qt - 1 + r
                    P.op("pe", lambda e, S=S, h=h, r=r, kt=kt, last=(r == rr[-1]): e.matmul(
                        S[:, r * 128:(r + 1) * 128], lhsT=kTs[:, kt * 128:(kt + 1) * 128], rhs=qT[h][0][:, qs], start=False, stop=last),
                        reads=[kTsb, qT[h][1]], writes=[Sb])
            for h in range(4):
                S, Sb = F[h]
                pt, ptb = PT4[h]
                P.op("act", lambda e, S=S, pt=pt: e.activation(out=pt[:, 0:256], in_=S[:, 0:256], func=AF.Exp), reads=[Sb], writes=[ptb])
            accp, accpb = F[4 + qt % 2]
            for h in range(4):
                pt, ptb = PT4[h]
                for r in rr:
                    kt = qt - 1 + r
                    P.op("pe", lambda e, pt=pt, h=h, r=r, kt=kt, first=(r == rr[0]), last=(r == rr[-1]): e.matmul(
                        accp[:, h * 65:(h + 1) * 65], lhsT=pt[:, r * 128:(r + 1) * 128], rhs=Vb[:, kt, 0, :], start=first, stop=last),
                        reads=[ptb, Vbb], writes=[accpb])
            rz, rzb = rz4[qt % 2]
            a3 = accp[:, 0:260].rearrange("p (h d) -> p h d", d=65)
            P.op("dve", lambda e, rz=rz, a3=a3: e.tensor_tensor(out=rz[:], in0=a3[:, :, 64], in1=esk[:], op=ALU.add), reads=[accpb, eskb], writes=[rzb])
            P.op("dve", lambda e, rz=rz: e.reciprocal(out=rz[:], in_=rz[:]), reads=[rzb], writes=[rzb])
            P.op("dve", lambda e, rz=rz, a3=a3, os_=os_: e.tensor_tensor(
                out=os_[:, qt % 4, :].rearrange("p (h d) -> p h d", h=4), in0=a3[:, :, 0:64],
                in1=rz[:].unsqueeze(2).to_broadcast([128, 4, 64]), op=ALU.mult), reads=[accpb, rzb], writes=[osb])
            if qt % 4 == 3:
                qg = qt // 4
                P.dma("sp", obdst(qg, 256, 512).rearrange("(t p) c -> p t c", p=128), os_[:], osb, reads=[osb], accw=[obuf(qg)])

    with P.scope():
        qT = [load_fm(f"X_q{h}", FMR['qc'] + 64 * h) for h in range(4)]
        kT = [load_fm(f"X_k{h}", FMR['kcf'] + 64 * h) for h in range(4)]
        Vc, Vcb = load_vext("X_V", TMC['vcf'], 4)
        bandF, bandFb = P.sbuf("X_band", [128, 4, 512], BF16)
        P.dma("pool", bandF[:], c["bandF"], bandFb, writes=[bandFb])
        cf, cfb = P.sbuf("X_cf", [4, TB], F32)
        for hf in range(2):
            P.dma("sp", cf[:, hf * TH:(hf + 1) * TH], src.cf(hf), cfb, reads=srcdep, writes=[cfb])
        fbt, fbb = P.sbuf("X_fb", [4, 1], F32)
        P.dma("sp", fbt[:], c["fb"], fbb, writes=[fbb])
        ones4, ones4b = P.sbuf("X_ones", [4, TB], F32)
        P.op("pool", lambda e: e.memset(ones4[:], 1.0), writes=[ones4b])
        cpos, cposb = P.sbuf("X_cpos", [4, TB], F32)
        P.op("dve", lambda e: e.tensor_scalar(out=fbt[:], in0=fbt[:], scalar1=-1.0, scalar2=None, op0=ALU.mult), reads=[fbb], writes=[fbb])
        P.op("act", lambda e: e.activation(out=cf[:], in_=cf[:], func=AF.Exp, bias=fbt[:, 0:1], scale=-1.0), reads=[cfb, fbb], writes=[cfb])
        P.op("act", lambda e: e.activation(out=cf[:], in_=cf[:], func=AF.Ln, bias=1.0), reads=[cfb], writes=[cfb])
        P.op("dve", lambda e: e.tensor_tensor_scan(out=cpos[:], data0=ones4[:], data1=cf[:], initial=0.0, op0=ALU.mult, op1=ALU.add),
             reads=[cfb, ones4b], writes=[cposb])
        cposT, cposTb = P.sbuf("X_cposT", [128, NQT, 4], F32)
        f6, f6b = F[6]
        for kt in range(NQT):
            P.op("pe", lambda e, kt=kt: e.transpose(f6[:, kt * 4:(kt + 1) * 4], cpos[:, kt * 128:(kt + 1) * 128], idf[0:4, 0:4]),
                 reads=[cposb, idfb], writes=[f6b])
        P.op("dve", lambda e: e.tensor_copy(out=cposT[:].rearrange("p a b -> p (a b)"), in_=f6[:, 0:NQT * 4]), reads=[f6b], writes=[cposTb])
        Dm, Dmb = P.sbuf("X_D", [4, 4, NQG], F32)
        P.op("dve", lambda e: e.tensor_tensor(out=Dm[:], in0=cpos[:, 0:TB:512].unsqueeze(1).to_broadcast([4, 4, NQG]),
                                              in1=idf[0:4, 0:4].unsqueeze(2).to_broadcast([4, 4, NQG]), op=ALU.mult),
             reads=[cposb, idfb], writes=[Dmb])
        P.op("pool", lambda e: e.memset(ones4[:, 0:128], 1.0), writes=[ones4b])
        f5, f5b = F[5]
        P.op("pe", lambda e: e.matmul(f5[:, 0:4 * NQG], lhsT=ones4[:, 0:128], rhs=Dm[:].rearrange("k h q -> k (h q)"), start=True, stop=True),
             reads=[ones4b, Dmb], writes=[f5b])
        crefb, crefbb = P.sbuf("X_cref", [128, 4, NQG], F32)
        P.op("dve", lambda e: e.tensor_copy(out=crefb[:].rearrange("p a b -> p (a b)"), in_=f5[:, 0:4 * NQG]), reads=[f5b], writes=[crefbb])
        bvec = [P.sbuf(f"X_bv{i}", [128, NQT], F32) for i in range(2)]
        OX = [P.sbuf(f"X_O{i}", [128, 4, 256], BF16) for i in range(2)]
        n = 0
        for qg in range(NQG):
            ox, oxb = OX[qg % 2]
            qcs = slice(qg * 512, (qg + 1) * 512)
            nkt = 4 * qg + 4
            for h in range(4):
                bv, bvb = bvec[h % 2]
                P.op("dve", lambda e, bv=bv, h=h, qg=qg, nkt=nkt: e.tensor_scalar(
                    out=bv[:, 0:nkt], in0=cposT[:, 0:nkt, h], scalar1=crefb[:, h, qg:qg + 1], scalar2=None, op0=ALU.subtract),
                    reads=[cposTb, crefbb], writes=[bvb])

                def emit_qk(kt, h=h):
                    S, Sb = SB3[kt % 3]
                    m = kt - 4 * qg + 1
                    if m >= 1:
                        P.op("pe", lambda e, S=S, m=m: e.matmul(S[:], lhsT=idb[:], rhs=bandF[:, m - 1, :], start=True, stop=False),
                             reads=[idbb, bandFb], writes=[Sb])
                        P.op("pe", lambda e, S=S, kt=kt: e.matmul(S[:], lhsT=kT[h][0][:, kt * 128:(kt + 1) * 128], rhs=qT[h][0][:, qcs],
                                                                  start=False, stop=True), reads=[kT[h][1], qT[h][1]], writes=[Sb])
                    else:
                        P.op("pe", lambda e, S=S, kt=kt: e.matmul(S[:], lhsT=kT[h][0][:, kt * 128:(kt + 1) * 128], rhs=qT[h][0][:, qcs],
                                                                  start=True, stop=True), reads=[kT[h][1], qT[h][1]], writes=[Sb])
                emit_qk(0)
                if nkt > 1:
                    emit_qk(1)
                for kt in range(nkt):
                    if kt + 2 < nkt:
                        emit_qk(kt + 2)
                    S, Sb = SB3[kt % 3]
                    pt, ptb = PT[kt % 3]
                    P.op("act", lambda e, S=S, pt=pt, bv=bv, kt=kt: e.activation(out=pt[:], in_=S[:], func=AF.Exp, bias=bv[:, kt:kt + 1]),
                         reads=[Sb, bvb], writes=[ptb])
                    for qi in range(4):
                        if kt > 4 * qg + qi:
                            continue
                        acc, accb = F[2 + qi]
                        P.op("pe", lambda e, acc=acc, pt=pt, qi=qi, kt=kt, h=h: e.matmul(
                            acc[:, 0:65], lhsT=pt[:, qi * 128:(qi + 1) * 128], rhs=Vc[:, kt, h, :], start=(kt == 0), stop=(kt == 4 * qg + qi)),
                            reads=[ptb, Vcb], writes=[accb])
                for qi in range(4):
                    acc, accb = F[2 + qi]
                    rz, rzb = rzs[n % 4]
                    n += 1
                    finish_head(acc, accb, ox[:, qi, h * 64:(h + 1) * 64], [], [oxb], rz=rz, rzb=rzb)
            P.dma("sp", obdst(qg, 512, 768).rearrange("(t p) c -> p t c", p=128), ox[:], oxb, reads=[oxb], accw=[obuf(qg)])
            if on_ob is not None:
                on_ob(qg)
    return dout


def _t5_bucket(d):
    d = np.maximum(d, 0)
    lr = np.log(np.maximum(d, 1).astype(np.float32) / np.float32(16)) / np.float32(math.log(128 / 16))
    large = np.minimum(16 + (lr * np.float32(16)).astype(np.int32), 31)
    return np.where(d < 16, d, large)


def _bias_lookup(tabcol, d, valid):
    out = np.full(d.shape, NEG, np.float32)
    b = _t5_bucket(d)
    out[valid] = tabcol[b[valid]]
    return out


def build_B(TB=4096):
    nc = bass.Bass("TRN2", target_bir_lowering=False)
    dt_ = lambda n, s, d=F32, k="ExternalInput": nc.dram_tensor(n, s, d, kind=k).ap()
    FMB = dt_("FMB", [NFMG, TB], BF16)
    TMB = dt_("TMB", [TB, NTMG], BF16)
    GTB = dt_("GTB", [TB, 12])
    CFB = dt_("CFB", [4, TB])
    c = dict(
        ident=dt_("ident", [128, 128]), erows=dt_("erows", [64, 32, 128]),
        w1=dt_("w1", [2, 2048, 256]), w2=dt_("w2", [2, 256, 64]), posT=dt_("posT", [2, 64, 32]),
        kgain=dt_("kgain", [128, 64]), bandC=dt_("bandC", [128, 4, 504]), bandS=dt_("bandS", [128, 4, 5, 512]),
        bandW=dt_("bandW", [128, 4, 384]), bandB=dt_("bandB", [128, 4, 256]), bandF=dt_("bandF", [128, 4, 512]),
        tab31=dt_("tab31", [128, 4]), topk=dt_("topk", [128, 3, 126]), overlap=dt_("overlap", [128, 2, 64]),
        sinkb=dt_("sinkb", [128, 4]), fb=dt_("fb", [4, 1]),
    )
    OB = dt_("OB", [TB, 768], BF16, "ExternalOutput")
    with ExitStack() as es:
        P = Prog(nc, es)
        dout = phase_B(P, TB, SrcStatic(TB, FMB, TMB, GTB, CFB), c, OB)
        P.finish([dout])
    return nc


def prep_B_consts():
    p = np.arange(128)
    erows = np.zeros((64, 32, 128), np.float32)
    for kt in range(32):
        erows[2 * kt, kt, 0:64] = 1.0
        erows[2 * kt + 1, kt, 64:128] = 1.0
    rel = np.arange(126)[None, :] - 62
    curr = (p[:, None] >= 64).astype(np.int64)
    valid = rel <= curr
    forced = (rel == curr) | (rel == curr - 1)
    A = (valid & ~forced).astype(np.float32)
    Bm = np.where(~valid, np.float32(-1e30), np.where(forced, np.float32(1e9), np.float32(0.0))).astype(np.float32)
    V = valid.astype(np.float32)
    topk = np.stack([A, Bm, V], axis=1).astype(np.float32)
    n = np.arange(256)
    j = np.arange(64)
    ov = ((16 * n[:, None] < 64 * j[None, :] + 64) & (16 * n[:, None] + 32 > 64 * j[None, :])).astype(np.float32)
    ov[255] = 0.0
    overlap = np.ascontiguousarray(ov.reshape(2, 128, 64).transpose(1, 0, 2))
    i = np.arange(512)
    bandF = np.zeros((128, 4, 512), np.float32)
    for m in range(1, 5):
        d = i[None, :] - 128 * (m - 1) - p[:, None]
        bandF[:, m - 1, :] = np.where(d >= 0, 0.0, NEG)
    return dict(ident=np.eye(128, dtype=np.float32), erows=erows, topk=topk, overlap=overlap, bandF=bandF)


def prep_B(inp, l, g, consts):
    tabA = inp["rel_bias"][:, 4 * g:4 * g + 4]
    tabB = inp["rel_bias"][:, 8 + 4 * g:8 + 4 * g + 4]
    p = np.arange(128)
    bandC = np.zeros((128, 4, 504), np.float32)
    m = np.arange(504)
    dC = p[:, None] - 16 * (m[None, :] - 248) - 31
    i = np.arange(512)
    bandS = np.zeros((128, 4, 5, 512), np.float32)
    bandW = np.zeros((128, 4, 384), np.float32)
    bandB = np.zeros((128, 4, 256), np.float32)
    q = np.arange(128)
    for h in range(4):
        bandC[:, h, :] = _bias_lookup(tabA[:, h], dC, dC >= 0)
        for mm in range(5):
            d = i[None, :] - 128 * (mm - 1) - p[:, None]
            bandS[:, h, mm, :] = _bias_lookup(tabA[:, h], d, d >= 0)
        for r in range(3):
            d = (2 - r) * 128 + q[None, :] - p[:, None]
            bandW[:, h, r * 128:(r + 1) * 128] = _bias_lookup(tabA[:, h], d, (d >= 0) & (d < 256))
        for r in range(2):
            d = (1 - r) * 128 + q[None, :] - p[:, None]
            bandB[:, h, r * 128:(r + 1) * 128] = _bias_lookup(tabB[:, h], d, (d >= 0) & (d < 128))
    out = dict(consts)
    out.update(
        w1=inp["cmp_w1"][l], w2=inp["cmp_w2"][l],
        posT=np.ascontiguousarray(inp["cmp_pos"][l].transpose(0, 2, 1)),
        kgain=np.ascontiguousarray(np.broadcast_to(inp["qk_gain"][l, 1][None, :], (128, 64))),
        bandC=bandC, bandS=bandS, bandW=bandW, bandB=bandB,
        tab31=np.ascontiguousarray(np.broadcast_to(tabA[31][None, :], (128, 4))),
        sinkb=np.ascontiguousarray(np.broadcast_to(inp["sinks"][l, 4 * g:4 * g + 4][None, :], (128, 4))),
        fb=np.ascontiguousarray(inp["forget_bias"][l, 4 * g:4 * g + 4].reshape(4, 1)),
    )
    return out


_NC_CACHE = {}


def _get_nc(name):
    if name not in _NC_CACHE:
        _NC_CACHE[name] = dict(A=lambda: build_A(2048), B=lambda: build_B(4096), C=lambda: build_C(2176))[name]()
    return _NC_CACHE[name]


def _run(nc, in_maps):
    res = run_bass_kernel_spmd(nc, in_maps, core_ids=list(range(8)))
    return res.results


def kernel_X(**inp):
    inp = {k: np.asarray(v) for k, v in inp.items()}
    x = np.ascontiguousarray(inp["x"], dtype=np.float32)
    consts = prep_B_consts()
    xcur = x.reshape(8, 2048, D)
    for l in range(DEPTH):
        cA = prep_A(inp, l)
        resA = _run(_get_nc("A"), [dict(cA, x=np.ascontiguousarray(xcur[c])) for c in range(8)])
        pB = [prep_B(inp, l, g, consts) for g in range(2)]
        inB = []
        for b in range(NB):
            r0, r1 = resA[2 * b], resA[2 * b + 1]
            FM = np.concatenate([np.asarray(r0["FM"]), np.asarray(r1["FM"])], axis=1)
            TM = np.concatenate([np.asarray(r0["TM"]), np.asarray(r1["TM"])], axis=0)
            GT = np.concatenate([np.asarray(r0["GT"]), np.asarray(r1["GT"])], axis=0)
            CF = np.concatenate([np.asarray(r0["CF"]), np.asarray(r1["CF"])], axis=1)
            for g in range(2):
                inB.append(dict(pB[g],
                                FMB=np.ascontiguousarray(FM[g * NFMG:(g + 1) * NFMG]),
                                TMB=np.ascontiguousarray(TM[:, g * NTMG:(g + 1) * NTMG]),
                                GTB=np.ascontiguousarray(GT[:, 12 * g:12 * g + 12]),
                                CFB=np.ascontiguousarray(CF[4 * g:4 * g + 4])))
        resB = _run(_get_nc("B"), inB)
        cC = prep_C(inp, l)
        inC = []
        for b in range(NB):
            o0, o1 = np.asarray(resB[2 * b]["OB"]), np.asarray(resB[2 * b + 1]["OB"])
            og = np.concatenate([o0[:, 0:256], o1[:, 0:256], o0[:, 256:512], o1[:, 256:512], o0[:, 512:768], o1[:, 512:768]], axis=1)
            xb = np.concatenate([xcur[2 * b], xcur[2 * b + 1]], axis=0)
            for half in range(2):
                if half == 0:
                    xh = np.concatenate([np.zeros((128, D), np.float32), xb[0:2048]], axis=0)
                    oh = np.concatenate([np.zeros((128, 1536), og.dtype), og[0:2048]], axis=0)
                else:
                    xh = xb[1920:4096]
                    oh = og[1920:4096]
                inC.append(dict(cC, xh=np.ascontiguousarray(xh), og=np.ascontiguousarray(oh)))
        resC = _run(_get_nc("C"), inC)
        xcur = np.stack([np.asarray(resC[c]["xo"]) for c in range(8)], axis=0)
    return np.ascontiguousarray(xcur.reshape(NB, T, D).astype(np.float32))


NSA_ROWS = NFM + NTM * 2048 // 2048
assert NSA_ROWS == 3584


class SrcChunks:
    def __init__(self, L, LGT, LCF):
        self.L, self.LGT, self.LCF = L, LGT, LCF

    def fm_all(self, r0):
        return lambda e: self.L[:, :, r0:r0 + 64, :].rearrange("a k p t -> p (a k) t")

    def tm_all(self):
        return [(2 * (8 * hf + k), 2, (lambda e, hf=hf, k=k: self.L[hf, k, 1344:1792, :].rearrange("r x -> (r x)").rearrange(
            "(kt p c) -> p kt c", p=128, c=NTMG))) for hf in range(2) for k in range(8)]

    def gt(self, hf):
        return lambda e: self.LGT[hf, :, :]

    def cf(self, hf):
        return lambda e: self.LCF[hf, :, :]


def build_Z(sample, ncores=8):
    nc = bass.Bass("TRN2", target_bir_lowering=False)
    I = {}
    for k, v in sample.items():
        dt = BF16 if v.dtype == NPBF else F32
        I[k] = nc.dram_tensor(k, list(v.shape), dt, kind="ExternalInput").ap()
    out = nc.dram_tensor("out", [2048, D], F32, kind="ExternalOutput").ap()
    CH = 3584
    Sk = [nc.dram_tensor("Sk%d" % k, [CH, 256], BF16).ap() for k in range(8)]
    Gk = [nc.dram_tensor("Gk%d" % k, [2 * CH, 256], BF16).ap() for k in range(8)]
    L = nc.dram_tensor("L", [2, 8, 1792, 256], BF16).ap()
    SAf = nc.dram_tensor("SAf", [32, 2048], F32).ap()
    GAf = nc.dram_tensor("GAf", [64, 2048], F32).ap()
    LGT = nc.dram_tensor("LGT", [2, 2048, 12], F32).ap()
    LCF = nc.dram_tensor("LCF", [2, 4, 2048], F32).ap()
    SBk = [nc.dram_tensor("SBk%d" % k, [1024, 768], BF16).ap() for k in range(4)]
    GBk = [nc.dram_tensor("GBk%d" % k, [2048, 768], BF16).ap() for k in range(4)]
    LO = nc.dram_tensor("LO", [2, 2176, 768], BF16).ap()
    X1 = nc.dram_tensor("X1", [2176, D], F32).ap()
    SX = nc.dram_tensor("SX", [128, D], F32).ap()
    GX = nc.dram_tensor("GX", [256, D], F32).ap()
    XM = nc.dram_tensor("XM", [2176, D], F32).ap()
    rg = [[i, i + 1] for i in range(0, ncores, 2)]
    GTv = [SAf[16 * g + 4:16 * g + 16, :].rearrange("r x -> (r x)").rearrange("(t c) -> t c", c=12) for g in range(2)]
    with ExitStack() as es:
        P = Prog(nc, es)
        P.need_rank = True
        bSA, bGA, bSB, bGB, bX1, bSX, bGX, bOut, bLA, bLO = [P.buf(n) for n in ("SA", "GA", "SB", "GB", "X1", "SX", "GX", "OUT", "LA", "LO")]
        rk = lambda: P.rank["sp"]
        bSk = [P.buf("Sk") for _ in range(8)]
        bGk = [P.buf("Gk") for _ in range(8)]
        bSBk = [P.buf("SBk") for _ in range(4)]
        bGBk = [P.buf("GBk") for _ in range(4)]
        for l in range(DEPTH):
            sfx = "_%d" % l
            xin = I["xh"] if l == 0 else X1
            xdep = [] if l == 0 else [bX1]
            def extract(k):
                for hf in range(2):
                    P.dma("sp", L[hf, k], (lambda e, hf=hf, k=k: Gk[k][bass.ds(hf * CH + rk() * 1792, 1792), :]), bLA,
                          reads=[bGk[k]], accw=[bLA])

            def on_chunk(k):
                P.coll("AllGather", [Sk[k]], [Gk[k]], rg, reads=[bSk[k]], writes=[bGk[k]])

            with P.scope():
                phase_A(P, es, 2048, xin[128:2176, :], I["wA" + sfx], I["gmix" + sfx], I["gainrow" + sfx], I["qscale" + sfx],
                        I["ident" + sfx], None, None, GTv, [SAf[16 * g:16 * g + 4, :] for g in range(2)], xdep=xdep, dout_in=bSA, chunked=Sk,
                        chunk_bufs=bSk, on_chunk=on_chunk)
            for k in range(8):
                extract(k)
            P.coll("AllGather", [SAf], [GAf], rg, reads=[bSA], writes=[bGA])
            for hf in range(2):
                P.dma("sp", LCF[hf], (lambda e, hf=hf: GAf[bass.ds(hf * 32 + rk() * 16, 4), :]), bLA, reads=[bGA], accw=[bLA])
                P.dma("sp", LGT[hf].rearrange("t c -> (t c)").rearrange("(r x) -> r x", x=2048),
                      (lambda e, hf=hf: GAf[bass.ds(hf * 32 + rk() * 16 + 4, 12), :]), bLA, reads=[bGA], accw=[bLA])
            cB = {k[:-len(sfx) - 1]: v for k, v in I.items() if k.endswith("B" + sfx)}
            with P.scope():
                obd = lambda qg, c0, c1: SBk[qg % 4][(qg // 4) * 512:(qg // 4 + 1) * 512, c0:c1]
                def extract_o(k):
                    for g in range(2):
                        P.dma("sp", LO[g, 128 + 512 * k:128 + 512 * (k + 1), :],
                              (lambda e, g=g, k=k: GBk[k][bass.ds(g * 1024 + rk() * 512, 512), :]), bLO, reads=[bGBk[k]], accw=[bLO])

                def on_ob(qg):
                    if qg >= 4:
                        k = qg - 4
                        P.coll("AllGather", [SBk[k]], [GBk[k]], rg, reads=[bSBk[k]], writes=[bGBk[k]])
                        if k >= 1:
                            extract_o(k - 1)
                phase_B(P, T, SrcChunks(L, LGT, LCF), cB, obd, srcdep=[bLA], dout_in=bSB, obuf=lambda qg: bSBk[qg % 4], on_ob=on_ob)
            extract_o(3)
            for g in range(2):
                P.dma("sp", LO[g, 0:128, :], GBk[3][g * 1024 + 384:g * 1024 + 512, :], bLO, reads=[bGBk[3]], accw=[bLO])
            with P.scope():
                xo = X1[128:2176, :] if l == 0 else out
                ogs = lambda t, g: (lambda e: LO[g, t * 128:(t + 1) * 128, :].rearrange("p (n c) -> p n c", n=3))
                phase_C(P, 2176, xin, ogs, I["wM" + sfx], I["wBr" + sfx], I["wO" + sfx], I["gmixC" + sfx], I["gffn" + sfx],
                        I["wG" + sfx], I["wU" + sfx], I["wD" + sfx], I["convw" + sfx], I["convb" + sfx], I["identC" + sfx], XM, xo,
                        I["hflag"], xdep=xdep, ogdep=[bLO], dout_in=(bX1 if l == 0 else bOut))
            if l == 0:
                P.dma("sp", SX, X1[2048:2176, :], bSX, reads=[bX1], writes=[bSX])
                P.coll("AllGather", [SX], [GX], rg, reads=[bSX], writes=[bGX])
                P.dma("sp", X1[0:128, :], GX[0:128, :], bGX, reads=[bGX], writes=[bX1])
                P.barrier()
        P.finish([bOut])
    return nc


def prep_Z(inp):
    x = np.ascontiguousarray(inp["x"], dtype=np.float32)
    consts = prep_B_consts()
    shared = {}
    perg = [dict(), dict()]
    for l in range(DEPTH):
        sfx = "_%d" % l
        for k, v in prep_A(inp, l).items():
            shared[k + sfx] = v
        for k, v in prep_C(inp, l).items():
            if k == "hflag":
                continue
            kk = {"gmix": "gmixC", "ident": "identC"}.get(k, k)
            shared[kk + sfx] = v
        for g in range(2):
            for k, v in prep_B(inp, l, g, consts).items():
                perg[g][k + "B" + sfx] = v
    maps = []
    for b in range(NB):
        for r in range(2):
            if r == 0:
                xh = np.concatenate([np.zeros((128, D), np.float32), x[b, 0:2048]], axis=0)
            else:
                xh = x[b, 1920:4096]
            m = dict(shared)
            m.update(perg[r])
            m["xh"] = np.ascontiguousarray(xh)
            m["hflag"] = np.full((128, 1), float(r), np.float32)
            maps.append(m)
    return maps


def kernel(**inp):
    inp = {k: np.asarray(v) for k, v in inp.items()}
    maps = prep_Z(inp)
    if "Z" not in _NC_CACHE:
        _NC_CACHE["Z"] = build_Z(maps[0], 8)
    res = run_bass_kernel_spmd(_NC_CACHE["Z"], maps, core_ids=list(range(8)))
    outs = [np.asarray(res.results[c]["out"]) for c in range(8)]
    return np.ascontiguousarray(np.stack(outs, axis=0).reshape(NB, T, D).astype(np.float32))


NCG = 1808


def _colperm_g(g):
    cols, gain_idx, qsc = [], [], []

    def heads(base, n, gi, q):
        for h in range(n):
            hh = 4 * g + h if n == 4 else g
            c0 = base + 64 * hh
            cols.extend(range(c0, c0 + 64))
            gain_idx.append(gi)
            qsc.append(0.125 if q else 1.0)
    heads(OFF['a_q'], 4, 0, True)
    heads(OFF['a_ks'], 1, 1, False)
    heads(OFF['a_kw'], 1, 1, False)
    heads(OFF['b_k'], 1, 3, False)
    heads(OFF['b_q'], 4, 2, True)
    heads(OFF['c_q'], 4, 4, True)
    heads(OFF['c_k'], 4, 5, False)
    heads(OFF['a_kc'], 1, -1, False)
    heads(OFF['a_vc'], 1, -1, False)
    assert len(cols) == NFMG
    cols.extend(range(OFF['a_vs'] + 64 * g, OFF['a_vs'] + 64 * g + 64))
    cols.extend(range(OFF['a_vw'] + 64 * g, OFF['a_vw'] + 64 * g + 64))
    cols.extend(range(OFF['b_v'] + 64 * g, OFF['b_v'] + 64 * g + 64))
    cols.extend(range(OFF['c_v'] + 256 * g, OFF['c_v'] + 256 * g + 256))
    cols.extend(range(OFF['a_gate'] + 12 * g, OFF['a_gate'] + 12 * g + 12))
    cols.extend(range(OFF['c_f'] + 4 * g, OFF['c_f'] + 4 * g + 4))
    assert len(cols) == NCG
    return np.array(cols), gain_idx, np.array(qsc, np.float32)


def prep_A2(inp, l, g):
    perm, gain_idx, qsc = _colperm_g(g)
    gr = np.ones((1408,), np.float32)
    for s, gi in enumerate(gain_idx):
        if gi >= 0:
            gr[s * 64:(s + 1) * 64] = inp["qk_gain"][l, gi]
    return dict(
        wA=np.ascontiguousarray(inp["w_in"][l][:, perm]),
        gmix=np.ascontiguousarray(np.broadcast_to(inp["norm_mix"][l][None, :], (128, D))),
        gainrow=np.ascontiguousarray(gr.reshape(11, 128).T),
        qscale=np.ascontiguousarray(np.broadcast_to(qsc[None, :], (128, 21))),
        ident=np.eye(128, dtype=np.float32),
    )


def phase_A2(P, NT, xsrc, x_is_h, wA, gmix, gainrow, qscale, ident, LFM, LTM, LGT, LCF, xdep=(), dout_in=None):
    ntile = NT // 128
    xdep = list(xdep)
    dout = dout_in if dout_in is not None else P.buf("doutA2")
    wsb, _ = P.sbuf("A_w", [128, 8, NCG], BF16)
    banks = [(0, 512), (512, 512), (1024, 320), (1344, 464)]
    wgrp = [(0, 1024), (1024, 784)]
    bank_grp = [0, 0, 1, 1]
    wbg = [P.buf("A_wg") for _ in wgrp]
    for gi, (g0, gw) in enumerate(wgrp):
        P.dma("pool", wsb[:, :, g0:g0 + gw], wA[:, g0:g0 + gw].rearrange("(k p) c -> p k c", p=128), wbg[gi], writes=[wbg[gi]])
    gm, gmb = P.sbuf("A_gm", [128, D], F32)
    if not x_is_h:
        P.dma("sp", gm[:], gmix, gmb, writes=[gmb])
    gr, grb = P.sbuf("A_gr", [128, 11], F32)
    P.dma("sp", gr[:], gainrow, grb, writes=[grb])
    qs, qsb = P.sbuf("A_qs", [128, 21], F32)
    P.dma("sp", qs[:], qscale, qsb, writes=[qsb])
    idb, idbb = P.sbuf("A_idb", [128, 128], BF16)
    P.dma("pool", idb[:], ident, idbb, writes=[idbb])
    xt = [P.sbuf(f"A_x{i}", [128, D], F32) for i in range(2)]
    junk, junkb = P.sbuf("A_junk", [128, D], F32)
    h2 = [P.sbuf(f"A_h{i}", [128, D], BF16) for i in range(4)]
    ss2 = [P.sbuf(f"A_ss{i}", [128, 4], F32) for i in range(4)]
    hT = [P.sbuf(f"A_hT{i}", [128, 8, 128], BF16) for i in range(4)]
    raw2 = [(P.sbuf(f"A_raw{i}", [128, NFMG], F32)[0], [P.buf("A_rawb") for _ in range(3)]) for i in range(4)]
    sq = [P.sbuf(f"A_sq{i}", [128, 512], F32) for i in range(2)]
    ssall2 = [P.sbuf(f"A_ssall{i}", [128, 21], F32) for i in range(4)]
    rs2 = [P.sbuf(f"A_rs{i}", [128, 21], F32) for i in range(2)]
    nrm2 = [(P.sbuf(f"A_nrm{i}", [128, 1408], BF16)[0], [P.buf("A_nrmb") for _ in range(3)]) for i in range(2)]
    for nrm_, nrmb_ in nrm2:
        P.op("pool", lambda e, nrm_=nrm_: e.memset(nrm_[:, NFMG:1408], 0.0), writes=[nrmb_[2]])
    FMt = [P.sbuf(f"A_FMt{i}", [128, 11, 512], BF16) for i in range(2)]
    TMt = [P.sbuf(f"A_TMt{i}", [128, 4, NTMG], BF16) for i in range(2)]
    gtt = [P.sbuf(f"A_gt{i}", [128, 4, 12], F32) for i in range(2)]
    cfT, cfTb = P.sbuf("A_cfT", [4, NT], F32)
    tp, tpb = P.psum("A_tp", [128, 8, 128], BF16)
    PB = [P.psum(f"A_PB{i}", [128, 512], F32) for i in range(3)]
    tq = [P.psum(f"A_tq{i}", [128, 8, 128], BF16) for i in range(2)]
    pcf, pcfb = P.psum("A_pcf", [4, 128], F32)
    nmm_ = [0]

    def front_norm(ts):
        if x_is_h:
            for t in ts:
                h_, h_b = h2[t % 4]
                P.dma("sp", h_[:], xsrc(t), h_b, reads=xdep, writes=[h_b])
            return
        for t in ts:
            xs, xsb = xt[t % 2]
            P.dma("sp", xs[:], xsrc(t), xsb, reads=xdep, writes=[xsb])
        for t in ts:
            xs, xsb = xt[t % 2]
            ss_, ss_b = ss2[t % 4]
            P.op("act", lambda e, xs=xs, ss_=ss_: e.activation(out=junk[:], in_=xs[:], func=AF.Square, accum_out=ss_[:, 0:1]),
                 reads=[xsb], writes=[junkb, ss_b])
        for t in ts:
            ss_, ss_b = ss2[t % 4]
            P.op("dve", lambda e, ss_=ss_: e.tensor_scalar(out=ss_[:, 1:2], in0=ss_[:, 0:1], scalar1=1.0 / D, scalar2=EPS,
                                                           op0=ALU.mult, op1=ALU.add), reads=[ss_b], writes=[ss_b])
        for t in ts:
            ss_, ss_b = ss2[t % 4]
            P.op("act", lambda e, ss_=ss_: e.activation(out=ss_[:, 2:3], in_=ss_[:, 1:2], func=AF.Sqrt), reads=[ss_b], writes=[ss_b])
        for t in ts:
            ss_, ss_b = ss2[t % 4]
            P.op("dve", lambda e, ss_=ss_: e.reciprocal(out=ss_[:, 3:4], in_=ss_[:, 2:3]), reads=[ss_b], writes=[ss_b])
        for t in ts:
            xs, xsb = xt[t % 2]
            ss_, ss_b = ss2[t % 4]
            h_, h_b = h2[t % 4]
            P.op("dve", lambda e, xs=xs, ss_=ss_, h_=h_: e.scalar_tensor_tensor(out=h_[:], in0=xs[:], scalar=ss_[:, 3:4], in1=gm[:],
                                                                                  op0=ALU.mult, op1=ALU.mult),
                 reads=[xsb, ss_b, gmb], writes=[h_b])

    def front_T(t):
        h_, h_b = h2[t % 4]
        for k in range(8):
            P.op("pe", lambda e, k=k: e.transpose(tp[:, k, :], h_[:, k * 128:(k + 1) * 128], idb[:]), reads=[h_b, idbb], writes=[tpb])
        hTs, hTsb = hT[t % 4]
        P.op("act", lambda e: e.copy(out=hTs[:], in_=tp[:]), reads=[tpb], writes=[hTsb])

    def back_mm(t):
        nmm = nmm_[0]
        raw, rawb = raw2[t % 4]
        ssall, ssallb = ssall2[t % 4]
        hTs, hTsb = hT[t % 4]
        slot = (t // 4) % 2
        tt = t % 4
        TMs, TMsb = TMt[slot]
        gts, gtsb = gtt[slot]
        for j, (c0, w) in enumerate(banks):
            pb, pbb = PB[nmm % 3]
            nmm += 1
            for k in range(8):
                P.op("pe", lambda e, pb=pb, k=k, c0=c0, w=w: e.matmul(pb[:, 0:w], lhsT=hTs[:, k, :], rhs=wsb[:, k, c0:c0 + w],
                                                                      start=(k == 0), stop=(k == 7)),
                     reads=[hTsb, wbg[bank_grp[j]]], writes=[pbb])
            if j < 3:
                sqs, sqsb = sq[j % 2]
                nsl = w // 64
                P.op("act", lambda e, pb=pb, c0=c0, w=w: e.copy(out=raw[:, c0:c0 + w], in_=pb[:, 0:w]), reads=[pbb], writes=[rawb[j]])
                P.op("act", lambda e, pb=pb, w=w, sqs=sqs: e.activation(out=sqs[:, 0:w], in_=pb[:, 0:w], func=AF.Square),
                     reads=[pbb], writes=[sqsb])
                P.op("dve", lambda e, sqs=sqs, w=w, j=j, nsl=nsl: e.reduce_sum(
                    out=ssall[:, 8 * j:8 * j + nsl], in_=sqs[:, 0:w].rearrange("p (s d) -> p s d", d=64), axis=AX.X),
                    reads=[sqsb], writes=[ssallb])
            else:
                P.op("act", lambda e, pb=pb: e.copy(out=TMs[:, tt, :], in_=pb[:, 0:NTMG]), reads=[pbb], writes=[TMsb])
                P.op("act", lambda e, pb=pb: e.copy(out=gts[:, tt, :], in_=pb[:, NTMG:NTMG + 12]), reads=[pbb], writes=[gtsb])
                if tt == 3:
                    P.op("act", lambda e: e.activation(out=gts[:], in_=gts[:], func=AF.Sigmoid), reads=[gtsb], writes=[gtsb])
        for k in range(8):
            P.op("pe", lambda e, k=k: e.matmul(pcf[:], lhsT=wsb[:, k, NCG - 4:NCG], rhs=hTs[:, k, :], start=(k == 0), stop=(k == 7)),
                 reads=[hTsb, wbg[1]], writes=[pcfb])
        P.op("dve", lambda e: e.tensor_copy(out=cfT[:, t * 128:(t + 1) * 128], in_=pcf[:]), reads=[pcfb], writes=[cfTb])
        nmm_[0] = nmm

    def fin_a(ts):
        for t in ts:
            ssall, ssallb = ssall2[t % 4]
            rs, rsb = rs2[t % 2]
            P.op("dve", lambda e, rs=rs, ssall=ssall: e.tensor_scalar(out=rs[:], in0=ssall[:], scalar1=1.0 / HD, scalar2=EPS,
                                                                     op0=ALU.mult, op1=ALU.add), reads=[ssallb], writes=[rsb])
        for t in ts:
            rs, rsb = rs2[t % 2]
            P.op("act", lambda e, rs=rs: e.activation(out=rs[:], in_=rs[:], func=AF.Sqrt), reads=[rsb], writes=[rsb])
        for t in ts:
            rs, rsb = rs2[t % 2]
            P.op("dve", lambda e, rs=rs: e.reciprocal(out=rs[:], in_=rs[:]), reads=[rsb], writes=[rsb])
        for t in ts:
            rs, rsb = rs2[t % 2]
            P.op("dve", lambda e, rs=rs: e.tensor_tensor(out=rs[:], in0=rs[:], in1=qs[:], op=ALU.mult), reads=[rsb, qsb], writes=[rsb])
        for t in ts:
            rs, rsb = rs2[t % 2]
            P.op("dve", lambda e, rs=rs: e.memset(rs[:, 19:21], 1.0), writes=[rsb])
        for t in ts:
            raw, rawb = raw2[t % 4]
            rs, rsb = rs2[t % 2]
            nrm, nrmb = nrm2[t % 2]
            for j in range(3):
                c0, w = banks[j]
                nsl = w // 64
                P.op("pool", lambda e, c0=c0, w=w, j=j, nsl=nsl, nrm=nrm, raw=raw, rs=rs: e.tensor_tensor(
                    out=nrm[:, c0:c0 + w].rearrange("p (s d) -> p s d", d=64),
                    in0=raw[:, c0:c0 + w].rearrange("p (s d) -> p s d", d=64),
                    in1=rs[:, 8 * j:8 * j + nsl].unsqueeze(2).to_broadcast([128, nsl, 64]), op=ALU.mult),
                    reads=[rawb[j], rsb], writes=[nrmb[j]])

    def fin_b(ts):
        for t in ts:
            slot = (t // 4) % 2
            tt = t % 4
            TMs, TMsb = TMt[slot]
            gts, gtsb = gtt[slot]
            FMs, FMsb = FMt[slot]
            nrm, nrmb = nrm2[t % 2]
            for c in range(11):
                tqs, tqsb = tq[c // 8]
                P.op("pe", lambda e, tqs=tqs, c=c, nrm=nrm: e.transpose(tqs[:, c % 8, :], nrm[:, c * 128:(c + 1) * 128], idb[:]),
                     reads=[nrmb[min(2, c // 4)], idbb] + ([nrmb[2]] if c == 8 else []), writes=[tqsb])
                if c == 7 or c == 10:
                    cb = (c // 8) * 8
                    n = c - cb + 1
                    P.op("dve", lambda e, tqs=tqs, cb=cb, n=n, FMs=FMs, tt=tt: e.tensor_tensor(
                        out=FMs[:, cb:cb + n, tt * 128:(tt + 1) * 128], in0=tqs[:, 0:n, :],
                        in1=gr[:, cb:cb + n].unsqueeze(2).to_broadcast([128, n, 128]), op=ALU.mult), reads=[tqsb, grb], writes=[FMsb])
            if tt == 3:
                t0 = (t - 3) * 128
                for (ca, cb_) in ((0, 6), (6, 11)):
                    P.dma("sp", LFM[ca * 128:cb_ * 128, t0:t0 + 512].rearrange("(c p) t -> p c t", p=128), FMs[:, ca:cb_, :], FMsb,
                          reads=[FMsb], accw=[dout])
                P.dma("sp", LTM[t0:t0 + 512, :].rearrange("(tt p) c -> p tt c", p=128), TMs[:], TMsb, reads=[TMsb], accw=[dout])
                P.dma("sp", LGT[t0:t0 + 512, :].rearrange("(tt p) c -> p tt c", p=128), gts[:], gtsb, reads=[gtsb], accw=[dout])

    npair = ntile // 2
    assert ntile % 2 == 0

    def pair(fn, p):
        if 0 <= p < npair:
            if fn in (front_norm, fin_a, fin_b):
                fn([2 * p, 2 * p + 1])
            else:
                fn(2 * p)
                fn(2 * p + 1)

    pair(front_norm, 0)
    pair(front_T, 0)
    pair(front_norm, 1)
    pair(back_mm, 0)
    pair(front_T, 1)
    pair(front_norm, 2)
    for p in range(npair):
        pair(fin_a, p)
        pair(back_mm, p + 1)
        pair(front_T, p + 2)
        pair(front_norm, p + 3)
        pair(fin_b, p)
    P.dma("sp", LCF, cfT[:], cfTb, reads=[cfTb], accw=[dout])
    return dout


def build_A2(NT=4096):
    nc = bass.Bass("TRN2", target_bir_lowering=False)
    dt_ = lambda n, s, d=F32, k="ExternalInput": nc.dram_tensor(n, s, d, kind=k).ap()
    x = dt_("x", [NT, D])
    wA = dt_("wA", [D, NCG]); gmix = dt_("gmix", [128, D]); gainrow = dt_("gainrow", [128, 11]); qscale = dt_("qscale", [128, 21])
    ident = dt_("ident", [128, 128])
    LFM = dt_("LFM", [1408, NT], BF16, "ExternalOutput"); LTM = dt_("LTM", [NT, NTMG], BF16, "ExternalOutput")
    LGT = dt_("LGT", [NT, 12], F32, "ExternalOutput"); LCF = dt_("LCF", [4, NT], F32, "ExternalOutput")
    with ExitStack() as es:
        P = Prog(nc, es)
        dout = phase_A2(P, NT, lambda t: x[t * 128:(t + 1) * 128, :], False, wA, gmix, gainrow, qscale, ident, LFM, LTM, LGT, LCF)
        P.finish([dout])
    return nc


def build_Z2(sample, ncores=8):
    nc = bass.Bass("TRN2", target_bir_lowering=False)
    I = {}
    for k, v in sample.items():
        dt = BF16 if v.dtype == NPBF else F32
        I[k] = nc.dram_tensor(k, list(v.shape), dt, kind="ExternalInput").ap()
    out = nc.dram_tensor("out", [2048, D], F32, kind="ExternalOutput").ap()
    LFM = nc.dram_tensor("LFM", [1408, T], BF16).ap()
    LTM = nc.dram_tensor("LTM", [T, NTMG], BF16).ap()
    LGT = nc.dram_tensor("LGT", [T, 12], F32).ap()
    LCF = nc.dram_tensor("LCF", [4, T], F32).ap()
    SBk = [nc.dram_tensor("SBk%d" % k, [1024, 768], BF16).ap() for k in range(4)]
    GBk = [nc.dram_tensor("GBk%d" % k, [2048, 768], BF16).ap() for k in range(4)]
    LO = nc.dram_tensor("LO", [2, 2176, 768], BF16).ap()
    SH = [nc.dram_tensor("SH%d" % k, [512, D], BF16).ap() for k in range(4)]
    GH = [nc.dram_tensor("GH%d" % k, [1024, D], BF16).ap() for k in range(4)]
    X1 = nc.dram_tensor("X1", [2176, D], F32).ap()
    SX = nc.dram_tensor("SX", [128, D], F32).ap()
    GX = nc.dram_tensor("GX", [256, D], F32).ap()
    XM = nc.dram_tensor("XM", [2176, D], F32).ap()
    rg = [[i, i + 1] for i in range(0, ncores, 2)]
    with ExitStack() as es:
        P = Prog(nc, es)
        P.need_rank = True
        bLA, bSB, bX1, bSX, bGX, bOut, bLO = [P.buf(n) for n in ("LA", "SB", "X1", "SX", "GX", "OUT", "LO")]
        rk = lambda: P.rank["sp"]
        bSBk = [P.buf("SBk") for _ in range(4)]
        bGBk = [P.buf("GBk") for _ in range(4)]
        bSH = [P.buf("SH") for _ in range(4)]
        bGH = [P.buf("GH") for _ in range(4)]
        for l in range(DEPTH):
            sfx = "_%d" % l
            xin = I["xh"] if l == 0 else X1
            xdep = [] if l == 0 else [bX1]
            with P.scope():
                if l == 0:
                    xsrc = lambda t: I["xfull"][t * 128:(t + 1) * 128, :]
                    adep = []
                else:
                    xsrc = lambda t: GH[(t % 16) // 4][(t // 16) * 512 + (t % 4) * 128:(t // 16) * 512 + (t % 4) * 128 + 128, :]
                    adep = bGH
                phase_A2(P, T, xsrc, l > 0, I["wA" + sfx], I["gmix" + sfx], I["gainrow" + sfx], I["qscale" + sfx], I["ident" + sfx],
                         LFM, LTM, LGT, LCF, xdep=adep, dout_in=bLA)
            cB = {k[:-len(sfx) - 1]: v for k, v in I.items() if k.endswith("B" + sfx)}
            with P.scope():
                obd = lambda qg, c0, c1: SBk[qg % 4][(qg // 4) * 512:(qg // 4 + 1) * 512, c0:c1]

                def extract_o(k):
                    for g in range(2):
                        P.dma("sp", LO[g, 128 + 512 * k:128 + 512 * (k + 1), :],
                              (lambda e, g=g, k=k: GBk[k][bass.ds(g * 1024 + rk() * 512, 512), :]), bLO, reads=[bGBk[k]], accw=[bLO])

                def on_ob(qg):
                    if qg >= 4:
                        k = qg - 4
                        P.coll("AllGather", [SBk[k]], [GBk[k]], rg, reads=[bSBk[k]], writes=[bGBk[k]])
                        if k >= 1:
                            extract_o(k - 1)
                phase_B(P, T, SrcStatic(T, LFM, LTM, LGT, LCF), cB, obd, srcdep=[bLA], dout_in=bSB, obuf=lambda qg: bSBk[qg % 4], on_ob=on_ob)
            extract_o(3)
            for g in range(2):
                P.dma("sp", LO[g, 0:128, :], GBk[3][g * 1024 + 384:g * 1024 + 512, :], bLO, reads=[bGBk[3]], accw=[bLO])
            with P.scope():
                xo = X1[128:2176, :] if l == 0 else out
                ogs = lambda t, g: (lambda e: LO[g, t * 128:(t + 1) * 128, :].rearrange("p (n c) -> p n c", n=3))
                hnext = None
                if l + 1 < DEPTH:
                    def on_h(tl):
                        if tl % 4 == 3:
                            k = tl // 4
                            P.coll("AllGather", [SH[k]], [GH[k]], rg, reads=[bSH[k]], writes=[bGH[k]])
                    hnext = (I["gmix_%d" % (l + 1)], lambda tl: SH[tl // 4][(tl % 4) * 128:(tl % 4) * 128 + 128, :],
                             lambda tl: bSH[tl // 4], on_h)
                phase_C(P, 2176, xin, ogs, I["wM" + sfx], I["wBr" + sfx], I["wO" + sfx], I["gmixC" + sfx], I["gffn" + sfx],
                        I["wG" + sfx], I["wU" + sfx], I["wD" + sfx], I["convw" + sfx], I["convb" + sfx], I["identC" + sfx], XM, xo,
                        I["hflag"], xdep=xdep, ogdep=[bLO], dout_in=(bX1 if l == 0 else bOut), hnext=hnext)
            if l == 0:
                P.dma("sp", SX, X1[2048:2176, :], bSX, reads=[bX1], writes=[bSX])
                P.coll("AllGather", [SX], [GX], rg, reads=[bSX], writes=[bGX])
                P.dma("sp", X1[0:128, :], GX[0:128, :], bGX, reads=[bGX], writes=[bX1])
                P.barrier()
        P.finish([bOut])
    return nc


def prep_Z2(inp):
    x = np.ascontiguousarray(inp["x"], dtype=np.float32)
    consts = prep_B_consts()
    shared = {}
    perg = [dict(), dict()]
    for l in range(DEPTH):
        sfx = "_%d" % l
        for k, v in prep_C(inp, l).items():
            if k == "hflag":
                continue
            kk = {"gmix": "gmixC", "ident": "identC"}.get(k, k)
            shared[kk + sfx] = v
        for g in range(2):
            for k, v in prep_A2(inp, l, g).items():
                perg[g][k + sfx] = v
            for k, v in prep_B(inp, l, g, consts).items():
                perg[g][k + "B" + sfx] = v
    maps = []
    for b in range(NB):
        for r in range(2):
            if r == 0:
                xh = np.concatenate([np.zeros((128, D), np.float32), x[b, 0:2048]], axis=0)
            else:
                xh = x[b, 1920:4096]
            m = dict(shared)
            m.update(perg[r])
            m["xh"] = np.ascontiguousarray(xh)
            m["xfull"] = x[b]
            m["hflag"] = np.full((128, 1), float(r), np.float32)
            maps.append(m)
    return maps


def kernel(**inp):
    inp = {k: np.asarray(v) for k, v in inp.items()}
    maps = prep_Z2(inp)
    if "Z2" not in _NC_CACHE:
        _NC_CACHE["Z2"] = build_Z2(maps[0], 8)
    res = run_bass_kernel_spmd(_NC_CACHE["Z2"], maps, core_ids=list(range(8)))
    outs = [np.asarray(res.results[c]["out"]) for c in range(8)]
    return np.ascontiguousarray(np.stack(outs, axis=0).reshape(NB, T, D).astype(np.float32))
```
